# Optimizing a Trainium2 kernel written in Bass

```python
import math
import jax, jax.numpy as jnp
from jax import lax
import numpy as np

D_MODEL = 1024
BATCH = 2
SEQ = 8192
DEPTH = 4

CTX_LEN = 256
GRID_W = 64
EPS = 1e-6

BR_WIDTH = D_MODEL // 2
N_BRANCH = 3

DA_SUB = 64
DA_VDIM = 2 * DA_SUB
DA_HEADS = BR_WIDTH // DA_VDIM
DA_QK = DA_HEADS * 2 * DA_SUB
BLOCK_Q = 128
ROPE_BASE = 10000.0
ROPE_NF = DA_SUB // 4

HY_WIDTH = BR_WIDTH
HY_SHORT = 3
HY_BANDS = 16
HY_EMB = 1 + 2 * HY_BANDS
HY_HIDDEN = 64
HY_TARGET = 1e-2
HY_FAST_DECAY = 0.3
HY_SLOW_DECAY = 1.5
HY_MIN_DECAY = math.log(HY_TARGET) / HY_SLOW_DECAY
HY_MAX_DECAY = math.log(HY_TARGET) / HY_FAST_DECAY

GM_WIDTH = BR_WIDTH
GM_GROUPS = 8
GM_CHUNK = 128

COL_K = 0
COL_V = COL_K + DA_QK
COL_Q = COL_V + BR_WIDTH
COL_GA = COL_Q + DA_QK
COL_HY = COL_GA + BR_WIDTH
COL_GB = COL_HY + 3 * HY_WIDTH
COL_GM = COL_GB + BR_WIDTH
COL_GC = COL_GM + 2 * GM_WIDTH
COL_MG = COL_GC + BR_WIDTH
COL_END = COL_MG + N_BRANCH * D_MODEL

kernel_name = "hybrid_diffattn_hyena_gmlp_prefix_dit"


def rms_norm(x, g):
    xf = x.astype(jnp.float32)
    y = xf * lax.rsqrt(jnp.mean(xf * xf, axis=-1, keepdims=True) + EPS)
    return (y * g.astype(jnp.float32)).astype(x.dtype)


def layer_norm(x, g, b):
    xf = x.astype(jnp.float32)
    mu = jnp.mean(xf, axis=-1, keepdims=True)
    var = jnp.mean(jnp.square(xf - mu), axis=-1, keepdims=True)
    y = (xf - mu) * lax.rsqrt(var + EPS)
    return (y * g.astype(jnp.float32) + b.astype(jnp.float32)).astype(x.dtype)


def modulation(cond, w, b):
    m = jax.nn.silu(cond) @ w + b
    return jnp.split(m, 3, axis=-1)


def axial_rope(row, col):
    inv = ROPE_BASE ** (-jnp.arange(ROPE_NF, dtype=jnp.float32) / ROPE_NF)
    ang = jnp.stack([row[:, None] * inv, col[:, None] * inv], axis=1)
    return jnp.cos(ang), jnp.sin(ang)


def apply_rope(x, cos, sin):
    xr = x.reshape(*x.shape[:-1], 2, 2, ROPE_NF)
    x1, x2 = xr[..., 0, :], xr[..., 1, :]
    cs, sn = cos.astype(x.dtype), sin.astype(x.dtype)
    out = jnp.stack([x1 * cs - x2 * sn, x2 * cs + x1 * sn], axis=-2)
    return out.reshape(x.shape)


def qk_heads(zs):
    b, n, _ = zs.shape
    return zs.reshape(b, n, DA_HEADS, 2, DA_SUB).transpose(0, 2, 3, 1, 4)


def v_heads(zs):
    b, n, _ = zs.shape
    return zs.reshape(b, n, DA_HEADS, DA_VDIM).transpose(0, 2, 1, 3)


def diff_attend(q, k, v, lam):
    b, h, _, n, _ = q.shape
    nb = n // BLOCK_Q
    qb = q.reshape(b, h, 2, nb, BLOCK_Q, DA_SUB).transpose(3, 0, 1, 2, 4, 5)
    scale = DA_SUB ** -0.5

    def one(qi):
        s = jnp.einsum('bhmqd,bhmkd->bhmqk', qi, k).astype(jnp.float32) * scale
        p = jax.nn.softmax(s, axis=-1)
        w = p[:, :, 0] - lam * p[:, :, 1]
        return jnp.einsum('bhqk,bhkd->bhqd', w.astype(v.dtype), v)

    o = lax.map(one, qb)
    return o.transpose(1, 2, 0, 3, 4).reshape(b, h, n, DA_VDIM)


def diff_head_out(o, g, lam_init):
    o = rms_norm(o, g) * (1.0 - lam_init)
    b, _, n, _ = o.shape
    return o.transpose(0, 2, 1, 3).reshape(b, n, BR_WIDTH)


def hyena_filter(L, w1, b1, w2, b2, w3, freq):
    f32 = jnp.float32
    t = jnp.linspace(0.0, 1.0, L, dtype=f32)[:, None]
    wpos = (2.0 * math.pi / L) * jnp.arange(L, dtype=f32)[:, None]
    bands = jnp.linspace(1e-4, HY_BANDS - 1, HY_BANDS, dtype=f32)[None, :]
    z = jnp.concatenate([t, jnp.cos(bands * wpos), -jnp.sin(bands * wpos)], axis=-1)
    hid = jnp.sin(freq[0].astype(f32) * (z @ w1.astype(f32) + b1.astype(f32)))
    hid = jnp.sin(freq[1].astype(f32) * (hid @ w2.astype(f32) + b2.astype(f32)))
    h = (hid @ w3.astype(f32)).reshape(L, 2, HY_WIDTH)
    deltas = jnp.abs(jnp.linspace(HY_MIN_DECAY, HY_MAX_DECAY, HY_WIDTH, dtype=f32))
    return h * jnp.exp(-t * deltas)[:, None, :]


def long_conv_bidir(u, h, bias):
    L = u.shape[1]
    f32 = jnp.float32
    k = jnp.concatenate([h[:, 0], jnp.zeros((1, HY_WIDTH), f32), h[:0:-1, 1]], axis=0)
    k = k * lax.rsqrt(jnp.sum(k * k, axis=0, keepdims=True))
    kf = jnp.fft.rfft(k, n=2 * L, axis=0)
    uf = jnp.fft.rfft(u.astype(f32), n=2 * L, axis=1)
    y = jnp.fft.irfft(uf * kf[None], n=2 * L, axis=1)[:, :L]
    return (y + u.astype(f32) * bias.astype(f32)).astype(u.dtype)


def short_conv(z, w, b):
    n = z.shape[1]
    pad = HY_SHORT // 2
    zp = jnp.pad(z, ((0, 0), (pad, pad), (0, 0)))
    return sum(zp[:, j:j + n] * w[j] for j in range(HY_SHORT)) + b


def hyena_branch(zb, sw, sb, filt, bias):
    zb = short_conv(zb, sw, sb)
    x0, x1, v = jnp.split(zb, 3, axis=-1)
    return x0 * long_conv_bidir(x1 * v, filt, bias)


def gmlp_branch(zg, ln_g, ln_b, ws, bs):
    u, v = jnp.split(jax.nn.gelu(zg, approximate=False), 2, axis=-1)
    v = layer_norm(v, ln_g, ln_b)
    b, n, _ = v.shape
    vr = v.reshape(b, n // GM_CHUNK, GM_CHUNK, GM_GROUPS, GM_WIDTH // GM_GROUPS)
    vm = jnp.einsum('gpq,bcqgd->bcpgd', ws, vr) + bs.T[:, :, None]
    return u * vm.reshape(b, n, GM_WIDTH)


def merge_branches(z, ys, wb, wo):
    out = None
    for i, (y, g0) in enumerate(zip(ys, (COL_GA, COL_GB, COL_GC))):
        gated = y * jax.nn.silu(z[..., g0:g0 + BR_WIDTH])
        sel = jax.nn.sigmoid(z[..., COL_MG + i * D_MODEL:COL_MG + (i + 1) * D_MODEL])
        term = sel * (gated @ wb[i])
        out = term if out is None else out + term
    return out @ wo


def setup_inputs(seed: int = 0) -> dict:
    key = jax.random.key(seed)
    ks = jax.random.split(key, 26)
    f32 = jnp.float32
    nrm = lambda k, shape, s: jax.random.normal(k, shape, f32) * s
    D = D_MODEL
    return {
        "x": nrm(ks[0], (BATCH, SEQ, D), 1.0),
        "c": nrm(ks[1], (BATCH, D), 1.0),
        "ctx": nrm(ks[2], (BATCH, CTX_LEN, D), 1.0),
        "c_ctx": nrm(ks[3], (D,), 1.0),
        "ada_w": nrm(ks[4], (DEPTH, D, 3 * D), 0.5 * D ** -0.5),
        "ada_b": nrm(ks[5], (DEPTH, 3 * D), 0.01),
        "norm_pre": 1.0 + nrm(ks[6], (DEPTH, D), 0.05),
        "norm_post": 1.0 + nrm(ks[7], (DEPTH, D), 0.05),
        "w_in": nrm(ks[8], (DEPTH, D, COL_END), D ** -0.5),
        "da_lambda": nrm(ks[9], (DEPTH, 4, DA_SUB), 0.1),
        "da_subln": 1.0 + nrm(ks[10], (DEPTH, DA_VDIM), 0.05),
        "hy_short_w": nrm(ks[11], (DEPTH, HY_SHORT, 3 * HY_WIDTH), HY_SHORT ** -0.5),
        "hy_short_b": nrm(ks[12], (DEPTH, 3 * HY_WIDTH), 0.02),
        "hy_f_w1": nrm(ks[13], (DEPTH, HY_EMB, HY_HIDDEN), HY_EMB ** -0.5),
        "hy_f_b1": nrm(ks[14], (DEPTH, HY_HIDDEN), 0.1),
        "hy_f_w2": nrm(ks[15], (DEPTH, HY_HIDDEN, HY_HIDDEN), HY_HIDDEN ** -0.5),
        "hy_f_b2": nrm(ks[16], (DEPTH, HY_HIDDEN), 0.1),
        "hy_f_w3": nrm(ks[17], (DEPTH, HY_HIDDEN, 2 * HY_WIDTH), HY_HIDDEN ** -0.5),
        "hy_f_freq": 1.0 + nrm(ks[18], (DEPTH, 2, HY_HIDDEN), 0.1),
        "hy_bias": nrm(ks[19], (DEPTH, HY_WIDTH), 0.5),
        "gm_ln_g": 1.0 + nrm(ks[20], (DEPTH, GM_WIDTH), 0.05),
        "gm_ln_b": nrm(ks[21], (DEPTH, GM_WIDTH), 0.02),
        "gm_ws": nrm(ks[22], (DEPTH, GM_GROUPS, GM_CHUNK, GM_CHUNK), 0.5 * GM_CHUNK ** -0.5),
        "gm_bs": 1.0 + nrm(ks[23], (DEPTH, GM_GROUPS, GM_CHUNK), 0.1),
        "w_branch": nrm(ks[24], (DEPTH, N_BRANCH, BR_WIDTH, D), BR_WIDTH ** -0.5),
        "w_out": nrm(ks[25], (DEPTH, D, D), D ** -0.5),
    }


def reference(x, c, ctx, c_ctx, ada_w, ada_b, norm_pre, norm_post, w_in, da_lambda, da_subln,
              hy_short_w, hy_short_b, hy_f_w1, hy_f_b1, hy_f_w2, hy_f_b2, hy_f_w3, hy_f_freq, hy_bias,
              gm_ln_g, gm_ln_b, gm_ws, gm_bs, w_branch, w_out):
    n = x.shape[1]
    n_ctx = ctx.shape[1]
    rows = n // GRID_W
    row = jnp.repeat(jnp.arange(rows, dtype=jnp.float32), GRID_W)
    col = jnp.tile(jnp.arange(GRID_W, dtype=jnp.float32), rows)
    cos, sin = axial_rope(row, col)
    xc = ctx
    for l in range(DEPTH):
        last = l == DEPTH - 1
        lam_init = 0.8 - 0.6 * math.exp(-0.3 * l)
        lp = da_lambda[l].astype(jnp.float32)
        lam = jnp.exp(jnp.sum(lp[0] * lp[1])) - jnp.exp(jnp.sum(lp[2] * lp[3])) + lam_init

        sh, sc, gt = modulation(c, ada_w[l], ada_b[l])
        shc, scc, gtc = modulation(c_ctx, ada_w[l], ada_b[l])
        h = rms_norm(x, norm_pre[l]) * (1.0 + sc[:, None]) + sh[:, None]
        hc = rms_norm(xc, norm_pre[l]) * (1.0 + scc) + shc

        z = h @ w_in[l]
        zc = hc @ (w_in[l][:, :COL_Q] if last else w_in[l])
        kc = qk_heads(zc[..., COL_K:COL_V])
        vc = v_heads(zc[..., COL_V:COL_Q])

        q = apply_rope(qk_heads(z[..., COL_Q:COL_GA]), cos, sin)
        k = apply_rope(qk_heads(z[..., COL_K:COL_V]), cos, sin)
        v = v_heads(z[..., COL_V:COL_Q])
        k_all = jnp.concatenate([k, kc], axis=3)
        v_all = jnp.concatenate([v, vc], axis=2)
        y_a = diff_head_out(diff_attend(q, k_all, v_all, lam), da_subln[l], lam_init)
        filt = hyena_filter(n, hy_f_w1[l], hy_f_b1[l], hy_f_w2[l], hy_f_b2[l], hy_f_w3[l], hy_f_freq[l])
        y_b = hyena_branch(z[..., COL_HY:COL_GB], hy_short_w[l], hy_short_b[l], filt, hy_bias[l])
        y_c = gmlp_branch(z[..., COL_GM:COL_GC], gm_ln_g[l], gm_ln_b[l], gm_ws[l], gm_bs[l])
        out = merge_branches(z, (y_a, y_b, y_c), w_branch[l], w_out[l])
        x_new = x + gt[:, None] * rms_norm(out, norm_post[l])

        if not last:
            qc = qk_heads(zc[..., COL_Q:COL_GA])
            yc_a = diff_head_out(diff_attend(qc, kc, vc, lam), da_subln[l], lam_init)
            filt_c = hyena_filter(n_ctx, hy_f_w1[l], hy_f_b1[l], hy_f_w2[l], hy_f_b2[l], hy_f_w3[l], hy_f_freq[l])
            yc_b = hyena_branch(zc[..., COL_HY:COL_GB], hy_short_w[l], hy_short_b[l], filt_c, hy_bias[l])
            yc_c = gmlp_branch(zc[..., COL_GM:COL_GC], gm_ln_g[l], gm_ln_b[l], gm_ws[l], gm_bs[l])
            outc = merge_branches(zc, (yc_a, yc_b, yc_c), w_branch[l], w_out[l])
            xc = xc + gtc * rms_norm(outc, norm_post[l])
        x = x_new
    return x
```

```python
import contextlib
import math
import numpy as np
import ml_dtypes
import concourse.bass as bass
import concourse.mybir as mybir
from concourse.bass_utils import run_bass_kernel_spmd

F32 = mybir.dt.float32
BF16 = mybir.dt.bfloat16
AF = mybir.ActivationFunctionType
ALU = mybir.AluOpType
AX = mybir.AxisListType

D = 1024
SEQ = 8192
NB = 2
DEPTH = 4
CTX = 256
TOK = 2048
NT = TOK // 128
EPS = 1e-6
COL_K, COL_V, COL_Q, COL_GA, COL_HY, COL_GB, COL_GM, COL_GC, COL_MG, COL_END = (
    0, 512, 1024, 1536, 2048, 3584, 4096, 5120, 5632, 8704)
PI = math.pi


class Prog:
    ENG = ("pe", "act", "dve", "pool", "sp")

    def __init__(self, nc):
        self.nc = nc
        self.ops = {e: [] for e in self.ENG}
        self.cnt = {e: 0 for e in self.ENG}
        self.semidx = {e: 0 for e in self.ENG}
        self.known = {e: {} for e in self.ENG}
        self.lastw = {}
        self.reads = {}
        self.ndma = 0
        self.dma_uses = {}
        self.NDMASEM = 32
        self.semnames = set()

    def _need(self, eng, reads, writes):
        toks = []
        for b in reads:
            t = self.lastw.get(b)
            if t is not None:
                toks.append(t)
        for b in writes:
            t = self.lastw.get(b)
            if t is not None:
                toks.append(t)
            toks.extend(self.reads.get(b, ()))
        need = {}
        for (s, v, e) in toks:
            if e == "pe" and eng == "pe":
                continue
            if v > need.get(s, 0):
                need[s] = v
        kn = self.known[eng]
        out = []
        for s, v in need.items():
            if kn.get(s, 0) >= v:
                continue
            kn[s] = v
            out.append((s, v))
        return out

    def _commit(self, tok, reads, writes):
        for b in reads:
            lst = self.reads.setdefault(b, [])
            lst.append(tok)
            if len(lst) > 64:
                mx = {}
                for (s, v, e) in lst:
                    if v > mx.get(s, (0, None))[0]:
                        mx[s] = (v, e)
                self.reads[b] = [(s, v, e) for s, (v, e) in mx.items()]
        for b in writes:
            self.lastw[b] = tok
            self.reads[b] = []

    def op(self, eng, fname, reads=(), writes=(), **kw):
        waits = self._need(eng, reads, writes)
        self.cnt[eng] += 1
        if self.cnt[eng] > 30000:
            self.semidx[eng] += 1
            self.cnt[eng] = 1
        s = "c_%s%d" % (eng, self.semidx[eng])
        self.semnames.add(s)
        tok = (s, self.cnt[eng], eng)
        self.ops[eng].append((waits, fname, kw, (s, 1)))
        self._commit(tok, reads, writes)

    def dma(self, eng, reads=(), writes=(), _fname="dma_start", **kw):
        j = self.ndma % self.NDMASEM
        self.ndma += 1
        s = "d_%d" % j
        self.semnames.add(s)
        uses = self.dma_uses.get(s, 0)
        waits = self._need(eng, reads, writes)
        if uses > 0 and self.known[eng].get(s, 0) < 16 * uses:
            self.known[eng][s] = 16 * uses
            waits.append((s, 16 * uses))
        self.dma_uses[s] = uses + 1
        tok = (s, 16 * (uses + 1), "dma")
        self.ops[eng].append((waits, _fname, kw, (s, 16)))
        self._commit(tok, reads, writes)

    def cc(self, reads=(), writes=(), **kw):
        waits = self._need("pool", reads, writes)
        self.ncc = getattr(self, "ncc", 0) + 1
        s = "ccsem%d" % self.ncc
        self.semnames.add(s)
        tok = (s, 1, "cc")
        self.ops["pool"].append((waits, "collective_compute", kw, (s, 1)))
        self._commit(tok, reads, writes)

    def barrier(self):
        latest = []
        for e in self.ENG:
            if self.cnt[e] > 0:
                latest.append(("c_%s%d" % (e, self.semidx[e]), self.cnt[e]))
        for s, uses in self.dma_uses.items():
            latest.append((s, 16 * uses))

        for e in self.ENG:
            kn = self.known[e]
            waits = []
            for (s, v) in latest:
                if kn.get(s, 0) < v:
                    kn[s] = v
                    waits.append((s, v))
            if waits:
                self.ops[e].append((waits, None, None, None))

    def wait_all(self, eng, bufs):
        waits = self._need(eng, bufs, ())
        self.ops[eng].append((waits, None, None, None))

    def run(self):
        nc = self.nc
        with contextlib.ExitStack() as st:
            sems = {n: st.enter_context(nc.semaphore(n)) for n in sorted(self.semnames)}
            block = st.enter_context(nc.Block())

            def replay(name):
                def f(e):
                    for waits, fname, kw, inc in self.ops[name]:
                        for (s, v) in waits:
                            e.wait_ge(sems[s], v)
                        if fname is not None:
                            kw = {k_: (v_(e) if callable(v_) else v_) for k_, v_ in kw.items()}
                            try:
                                ins = getattr(e, fname)(**kw)
                            except Exception:
                                print("FAILED OP", name, fname, {k_: str(v_)[:300] for k_, v_ in kw.items()})
                                raise
                            ins.then_inc(sems[inc[0]], inc[1])
                return f
            block.tensor(replay("pe"))
            block.scalar(replay("act"))
            block.vector(replay("dve"))
            block.gpsimd(replay("pool"))
            block.sync(replay("sp"))


class Builder:
    def __init__(self):
        self.nc = bass.Bass("TRN2", target_bir_lowering=False)
        self.P = Prog(self.nc)
        self.st = contextlib.ExitStack()
        self.outs = []
        self.nbank = 0
        self.rings = {}

    def din(self, name, shape, dt=F32):
        return self.nc.dram_tensor(name, list(shape), dt, kind="ExternalInput").ap()

    def dout(self, name, shape, dt=F32):
        self.outs.append(name)
        return self.nc.dram_tensor(name, list(shape), dt, kind="ExternalOutput").ap()

    def dscratch(self, name, shape, dt=F32):
        return self.nc.dram_tensor(name, list(shape), dt, kind="Internal").ap()

    ARENA_WORDS = 51 * 1024

    def sb(self, name, shape, dt=F32):
        if not hasattr(self, "arena"):
            self.arena = self.st.enter_context(self.nc.sbuf_tensor("arena", [128, self.ARENA_WORDS], F32))
            self.top = 0
        nel = 1
        for d_ in shape[1:]:
            nel *= d_
        esz = 4 if dt == F32 else 2
        words = (nel * esz + 3) // 4
        assert self.top + words <= self.ARENA_WORDS, "arena overflow at %s: top=%d need=%d" % (name, self.top, words)
        v = self.arena[:, self.top:self.top + words]
        self.top += words
        if dt != F32:
            v = v.bitcast(dt)
        v = v[:, 0:nel]
        if len(shape) == 3:
            v = v.rearrange("p (a b) -> p a b", b=shape[2])
        elif len(shape) == 4:
            v = v.rearrange("p (a b c) -> p a b c", b=shape[2], c=shape[3])
        if shape[0] < 128:
            v = v[0:shape[0]]
        return v

    def mark(self):
        return (self.top, set(self.rings.keys()))

    def release(self, m):
        self.P.barrier()
        self.top = m[0]
        for k in list(self.rings.keys()):
            if k not in m[1]:
                del self.rings[k]

    def init_psum(self):
        self.pp = [self.st.enter_context(self.nc.psum_tensor("psp%d" % i, [128, 1024], F32)) for i in range(4)]
        self.ps = [self.pp[i // 2][:, 512 * (i % 2):512 * (i % 2 + 1)] for i in range(8)]

    def bank(self):
        i = self.nbank % 8
        self.nbank += 1
        return i

    def ring(self, name, shape, dt, n):
        if name not in self.rings:
            self.rings[name] = [[self.sb("%s_%d" % (name, i), shape, dt) for i in range(n)], 0]
        r = self.rings[name]
        i = r[1] % n
        r[1] += 1
        return r[0][i], "%s_%d" % (name, i)

    def finish(self):
        self.P.wait_all("sp", self.outs)
        self.P.run()
        self.st.close()
        return self.nc


def make_identity(B, dt=BF16):
    P = B.P
    idf = B.sb("ident_f", [128, 128], F32)
    P.op("pool", "memset", writes=["ident_f"], ap=idf[:], constant=0.0)
    P.op("pool", "affine_select", reads=["ident_f"], writes=["ident_f"], out=idf[:], in_=idf[:],
         compare_op=ALU.not_equal, fill=1.0, base=0, pattern=[[-1, 128]], channel_multiplier=1)
    idb = B.sb("ident_b", [128, 128], BF16)
    P.op("pool", "tensor_copy", reads=["ident_f"], writes=["ident_b"], out=idb[:], in_=idf[:])
    return idf, idb


def modulation(B, c_ap, ada_w, ada_b, npre, npost, names, dest):
    P = B.P
    nA, nB, nG = names
    cT = B.sb("cT_" + nA, [128, 8], F32)
    P.dma("sp", writes=["cT" + nA], out=cT[:], in_=c_ap.rearrange("(p k) -> p k", k=8))
    P.op("act", "activation", reads=["cT" + nA], writes=["cT" + nA], out=cT[:], in_=cT[:], func=AF.Silu)
    rows = B.sb("rows_" + nA, [1, 3072], F32)
    bro = B.sb("brow_" + nA, [1, 3072], F32)
    P.dma("sp", writes=["brow" + nA], out=bro[:], in_=ada_b.rearrange("(o c) -> o c", o=1))
    wv = ada_w.rearrange("(p k) c -> p k c", k=8)
    for ch in range(6):
        wt, wk = B.ring("adaw", [128, 8, 512], F32, 2)
        P.dma("sp", writes=[wk], out=wt[:], in_=wv[:, :, ch * 512:(ch + 1) * 512])
        bk = B.bank()
        for k in range(8):
            P.op("pe", "matmul", reads=[wk, "cT" + nA], writes=["ps%d" % bk], out=B.ps[bk][0:1, :], lhsT=cT[:, k:k + 1],
                 rhs=wt[:, k, :], start=(k == 0), stop=(k == 7))
        P.op("dve", "tensor_tensor", reads=["ps%d" % bk, "brow" + nA], writes=["rows" + nA],
             out=rows[:, ch * 512:(ch + 1) * 512], in0=B.ps[bk][0:1, :], in1=bro[:, ch * 512:(ch + 1) * 512], op=ALU.add)
    gp = B.sb("gp_" + nA, [1, 2048], F32)
    P.dma("sp", writes=["gp" + nA], out=gp[:, 0:1024], in_=npre.rearrange("(o c) -> o c", o=1))
    P.dma("sp", writes=["gp" + nA], out=gp[:, 1024:2048], in_=npost.rearrange("(o c) -> o c", o=1))
    P.op("dve", "scalar_tensor_tensor", reads=["rows" + nA, "gp" + nA], writes=["rows" + nA], out=rows[:, 1024:2048],
         in0=rows[:, 1024:2048], scalar=1.0, in1=gp[:, 0:1024], op0=ALU.add, op1=ALU.mult)
    P.op("dve", "tensor_tensor", reads=["rows" + nA, "gp" + nA], writes=["rows" + nA], out=rows[:, 2048:3072],
         in0=rows[:, 2048:3072], in1=gp[:, 1024:2048], op=ALU.mult)
    ones = B.sb("ones_" + nA, [1, 128], F32)
    P.op("dve", "memset", writes=["ones" + nA], ap=ones[:], constant=1.0)
    tiles = {}
    for nm, off in ((nB, 0), (nA, 1024), (nG, 2048)):
        if nm is None:
            continue
        t = dest[nm]
        for hh in range(2):
            bk = B.bank()
            P.op("pe", "matmul", reads=["ones" + nA, "rows" + nA], writes=["ps%d" % bk], out=B.ps[bk][:, :], lhsT=ones[:, :],
                 rhs=rows[:, off + hh * 512: off + (hh + 1) * 512], start=True, stop=True)
            P.op("act", "activation", reads=["ps%d" % bk], writes=["bc" + nm], out=t[:, hh * 512:(hh + 1) * 512],
                 in_=B.ps[bk][:, :], func=AF.Identity)
        tiles[nm] = t
    return tiles


def compute_h_tile(B, x_rows_aps, n, Abc, Bbc, keyA, keyB, hT, hkey, col0, idb, mask=None, keep_x=None, xkeys=()):
    P = B.P
    if keep_x is None:
        xt, xk = B.ring("xt", [128, 1024], F32, 3)
    else:
        xt, xk = keep_x
    for ap, r0, r in x_rows_aps:
        P.dma("sp", reads=list(xkeys), writes=[xk], out=xt[r0:r0 + r, :], in_=ap)
    junk, jk = B.ring("hjunk", [128, 1024], BF16, 2)
    st_, sk = B.ring("hstat", [128, 4], F32, 4)
    P.op("act", "activation", reads=[xk], writes=[jk, sk], out=junk[0:n, :], in_=xt[0:n, :], func=AF.Square,
         accum_out=st_[0:n, 0:1])
    P.op("dve", "tensor_scalar", reads=[sk], writes=[sk], out=st_[0:n, 1:2], in0=st_[0:n, 0:1], scalar1=1.0 / D,
         scalar2=EPS, op0=ALU.mult, op1=ALU.add)
    P.op("act", "activation", reads=[sk], writes=[sk], out=st_[0:n, 2:3], in_=st_[0:n, 1:2], func=AF.Sqrt)
    P.op("dve", "reciprocal", reads=[sk], writes=[sk], out=st_[0:n, 3:4], in_=st_[0:n, 2:3])
    hm, hk = B.ring("hm", [128, 1024], F32, 2)
    P.op("dve", "scalar_tensor_tensor", reads=[xk, sk, keyA], writes=[hk], out=hm[0:n, :], in0=xt[0:n, :],
         scalar=st_[0:n, 3:4], in1=Abc[0:n, :], op0=ALU.mult, op1=ALU.mult)
    hb, hbk = B.ring("hb", [128, 1024], BF16, 2)
    P.op("pool", "tensor_tensor", reads=[hk, keyB], writes=[hbk], out=hb[0:n, :], in0=hm[0:n, :], in1=Bbc[0:n, :], op=ALU.add)
    if mask is not None:
        mt, mk = mask
        P.op("pool", "tensor_scalar", reads=[hbk, mk], writes=[hbk], out=hb[0:n, :], in0=hb[0:n, :], scalar1=mt[0:n, 0:1],
             scalar2=None, op0=ALU.mult)
    bk = B.bank()
    psb = B.ps[bk][:, :].bitcast(BF16)
    for k in range(8):
        P.op("pe", "transpose", reads=[hbk, "ident_b"], writes=["ps%d" % bk], out=psb[:, k * 128:k * 128 + n],
             in_=hb[0:n, k * 128:(k + 1) * 128], identity=idb[0:n, 0:n])
    P.op("act", "activation", reads=["ps%d" % bk], writes=[hkey], out=hT[:, :, col0:col0 + n],
         in_=psb.rearrange("p (k t) -> p k t", t=128)[:, :, 0:n], func=AF.Identity)
    return st_, sk


def load_cast(B, dst, dstk, src, shape3):
    a, b = shape3
    st, sk = B.ring("wstage", [128, 1024], F32, 2)
    sv = st[:, 0:a * b].rearrange("p (a b) -> p a b", b=b)
    B.P.dma("sp", writes=[sk], out=sv, in_=src)
    eng = getattr(B, "cast_eng", "pool")
    if eng == "act":
        B.P.op("act", "activation", reads=[sk], writes=[dstk], out=dst, in_=sv, func=AF.Identity)
    else:
        B.P.op(eng, "tensor_copy", reads=[sk], writes=[dstk], out=dst, in_=sv)


def wblock(B, w_ap, col0, ncols, ringname="wblk", nbuf=2, width=512):
    wt, wk = B.ring(ringname, [128, 8, width], BF16, nbuf)
    wv = w_ap.rearrange("(k p) c -> p k c", p=128)
    for c in range(0, ncols, 128):
        load_cast(B, wt[:, :, c:c + 128], wk, wv[:, :, col0 + c:col0 + c + 128], (8, 128))
    return wt, wk


def proj_fm(B, wt, wk, c0, hT, hkey, t0, nt, bk, M=128):
    for k in range(8):
        B.P.op("pe", "matmul", reads=[wk, hkey], writes=["ps%d" % bk], out=B.ps[bk][0:M, 0:nt], lhsT=wt[:, k, c0:c0 + M],
               rhs=hT[:, k, t0:t0 + nt], start=(k == 0), stop=(k == 7))


def proj_tm(B, wt, wk, c0, ncols, hT, hkey, t0, n, bk):
    for k in range(8):
        B.P.op("pe", "matmul", reads=[wk, hkey], writes=["ps%d" % bk], out=B.ps[bk][0:n, 0:ncols], lhsT=hT[:, k, t0:t0 + n],
               rhs=wt[:, k, c0:c0 + ncols], start=(k == 0), stop=(k == 7))


def load_shortconv(B, hy_w, hy_b):
    P = B.P
    t = B.sb("scw", [128, 12, 4], F32)
    for j in range(3):
        P.dma("sp", writes=["scw"], out=t[:, :, j:j + 1], in_=hy_w[j].rearrange("(t p o) -> p t o", p=128, o=1),
              allow_slow_non_contiguous=True)
    P.dma("sp", writes=["scw"], out=t[:, :, 3:4], in_=hy_b.rearrange("(t p o) -> p t o", p=128, o=1),
          allow_slow_non_contiguous=True)
    return t


def hy_conv_tile(B, w_in, ct, hT, hkey, segs, scw, outt, outk, eng="dve"):
    P = B.P
    wt, wk = wblock(B, w_in, COL_HY + ct * 128, 128, "wblk128", 3, 128)
    for (tc0, n, oc0) in segs:
        zr, zk = B.ring("zrow", [128, 2050], F32, 1)
        pos = tc0 - 1
        end = tc0 + n + 1
        while pos < end:
            m = min(512, end - pos)
            bk = B.bank()
            proj_fm(B, wt, wk, 0, hT, hkey, pos, m, bk)
            P.op("act", "activation", reads=["ps%d" % bk], writes=[zk], out=zr[:, pos - (tc0 - 1): pos - (tc0 - 1) + m],
                 in_=B.ps[bk][:, 0:m], func=AF.Identity)
            pos += m
        tmp, tk = B.ring("cvtmp", [128, 2048], F32, 1)
        P.op(eng, "tensor_scalar", reads=[zk, "scw"], writes=[tk], out=tmp[:, 0:n], in0=zr[:, 0:n], scalar1=scw[:, ct, 0:1],
             scalar2=scw[:, ct, 3:4], op0=ALU.mult, op1=ALU.add)
        P.op("dve", "scalar_tensor_tensor", reads=[zk, "scw", tk], writes=[tk], out=tmp[:, 0:n], in0=zr[:, 1:n + 1],
             scalar=scw[:, ct, 1:2], in1=tmp[:, 0:n], op0=ALU.mult, op1=ALU.add)
        P.op("dve", "scalar_tensor_tensor", reads=[zk, "scw", tk], writes=[outk], out=outt[:, oc0:oc0 + n], in0=zr[:, 2:n + 2],
             scalar=scw[:, ct, 2:3], in1=tmp[:, 0:n], op0=ALU.mult, op1=ALU.add)


def build_s1():
    B = Builder()
    P = B.P
    xs = B.din("xs", [TOK + 2, D])
    cvec = B.din("c", [D])
    ada_w = B.din("ada_w", [D, 3 * D])
    ada_b = B.din("ada_b", [3 * D])
    npre = B.din("npre", [D])
    npost = B.din("npost", [D])
    w_in = B.din("w_in", [D, COL_END])
    wk_perm = B.din("wk_perm", [D, 512])
    hy_w = B.din("hy_w", [3, 1536])
    hy_b = B.din("hy_b", [1536])
    cosT = B.din("cosT", [128, TOK])
    sinT = B.din("sinT", [128, TOK])
    hmask = B.din("hmask", [2, 1])
    o_kt = B.dout("o_kt", [4, 128, TOK], BF16)
    o_v = B.dout("o_v", [TOK, 512], BF16)
    o_u = B.dout("o_u", [512, TOK], F32)
    B.init_psum()
    idf, idb = make_identity(B)
    Abc = B.sb("bc_A", [128, 1024], F32)
    Bbc = B.sb("bc_Bm", [128, 1024], F32)
    hT = B.sb("hT", [128, 8, TOK + 2], BF16)
    mt = B.sb("hmask", [2, 1], F32)
    m0 = B.mark()
    modulation(B, cvec, ada_w, ada_b, npre, npost, ("A", "Bm", None), {"A": Abc, "Bm": Bbc})
    B.release(m0)
    P.dma("sp", writes=["hmask"], out=mt[:], in_=hmask[:, :])
    for i in range(NT):
        compute_h_tile(B, [(xs[1 + 128 * i: 1 + 128 * (i + 1), :], 0, 128)], 128, Abc, Bbc, "bcA", "bcBm", hT, "hT", 1 + 128 * i, idb)
    hh = B.sb("hTh", [128, 8, 2], BF16)
    compute_h_tile(B, [(xs[0:1, :], 0, 1), (xs[TOK + 1:TOK + 2, :], 1, 1)], 2, Abc, Bbc, "bcA", "bcBm", hh, "hTh", 0, idb,
                   mask=(mt, "hmask"))
    P.op("pool", "tensor_copy", reads=["hTh"], writes=["hT"], out=hT[:, :, 0:1], in_=hh[:, :, 0:1])
    P.op("pool", "tensor_copy", reads=["hTh"], writes=["hT"], out=hT[:, :, TOK + 1:TOK + 2], in_=hh[:, :, 1:2])
    B.release(m0)
    cs = B.sb("cosT", [128, TOK], F32)
    sn = B.sb("sinT", [128, TOK], F32)
    P.dma("sp", writes=["cosT"], out=cs[:], in_=cosT[:, :])
    P.dma("sp", writes=["sinT"], out=sn[:], in_=sinT[:, :])
    wt, wk = wblock(B, w_in, COL_K, 512, "wblkK")
    wp, wpk = wblock(B, wk_perm, 0, 512, "wblkK")
    for h in range(4):
        ko, kk = B.ring("kout", [128, TOK], BF16, 2)
        for j in range(TOK // 512):
            b1 = B.bank()
            proj_fm(B, wt, wk, 128 * h, hT, "hT", 1 + 512 * j, 512, b1)
            b2 = B.bank()
            proj_fm(B, wp, wpk, 128 * h, hT, "hT", 1 + 512 * j, 512, b2)
            t1, t1k = B.ring("rtmp1", [128, 512], F32, 2)
            t2, t2k = B.ring("rtmp2", [128, 512], F32, 2)
            P.op("dve", "tensor_tensor", reads=["ps%d" % b1, "cosT"], writes=[t1k], out=t1[:], in0=B.ps[b1][:, :],
                 in1=cs[:, 512 * j:512 * (j + 1)], op=ALU.mult)
            P.op("dve", "tensor_tensor", reads=["ps%d" % b2, "sinT"], writes=[t2k], out=t2[:], in0=B.ps[b2][:, :],
                 in1=sn[:, 512 * j:512 * (j + 1)], op=ALU.mult)
            P.op("pool", "tensor_tensor", reads=[t1k, t2k], writes=[kk], out=ko[:, 512 * j:512 * (j + 1)], in0=t1[:], in1=t2[:],
                 op=ALU.add)
        P.dma("sp", reads=[kk], writes=["o_kt"], out=o_kt[h], in_=ko[:])
    B.release(m0)
    wt, wk = wblock(B, w_in, COL_V, 512, "wblkK")
    for i in range(NT):
        bk = B.bank()
        proj_tm(B, wt, wk, 0, 512, hT, "hT", 1 + 128 * i, 128, bk)
        vo, vk = B.ring("vout", [128, 512], BF16, 3)
        P.op("act", "activation", reads=["ps%d" % bk], writes=[vk], out=vo[:], in_=B.ps[bk][:, :], func=AF.Identity)
        P.dma("sp", reads=[vk], writes=["o_v"], out=o_v[128 * i:128 * (i + 1), :], in_=vo[:])
    B.release(m0)
    scw = load_shortconv(B, hy_w, hy_b)
    for ci in range(4):
        x1s, x1k = B.ring("x1s", [128, TOK], F32, 2)
        vs, vsk = B.ring("vs", [128, TOK], F32, 2)
        hy_conv_tile(B, w_in, 4 + ci, hT, "hT", [(1, TOK, 0)], scw, x1s, x1k, "dve")
        hy_conv_tile(B, w_in, 8 + ci, hT, "hT", [(1, TOK, 0)], scw, vs, vsk, "pool")
        P.op("dve", "tensor_tensor", reads=[x1k, vsk], writes=[x1k], out=x1s[:], in0=x1s[:], in1=vs[:], op=ALU.mult)
        P.dma("sp", reads=[x1k], writes=["o_u"], out=o_u[128 * ci:128 * (ci + 1), :], in_=x1s[:])
    return B.finish()


def rope_tables():
    rows = SEQ // 64
    row = np.repeat(np.arange(rows, dtype=np.float32), 64)
    col = np.tile(np.arange(64, dtype=np.float32), rows)
    inv = (10000.0 ** (-np.arange(16, dtype=np.float32) / 16)).astype(np.float32)
    ang_r = row[None, :] * inv[:, None]
    ang_c = col[None, :] * inv[:, None]
    cos64 = np.concatenate([np.cos(ang_r), np.cos(ang_r), np.cos(ang_c), np.cos(ang_c)], 0)
    sin64 = np.concatenate([-np.sin(ang_r), np.sin(ang_r), -np.sin(ang_c), np.sin(ang_c)], 0)
    cosT = np.concatenate([cos64, cos64], 0).astype(np.float32)
    sinT = np.concatenate([sin64, sin64], 0).astype(np.float32)
    return cosT, sinT


def perm_cols(w):
    k, c = w.shape
    return np.ascontiguousarray(w.reshape(k, c // 32, 2, 16)[:, :, ::-1, :].reshape(k, c))


_NC = {}


def get_nc(name):
    if name not in _NC:
        _NC[name] = globals()["build_" + name]()
    return _NC[name]


def run_s1(x, c, l, ada_w, ada_b, norm_pre, norm_post, w_in, hy_short_w, hy_short_b):
    nc = get_nc("s1")
    cosT, sinT = rope_tables()
    wkp = perm_cols(w_in[l][:, COL_K:COL_K + 512])
    in_maps = []
    for core in range(8):
        b, r = divmod(core, 4)
        t0 = r * TOK
        xs = np.zeros((TOK + 2, D), np.float32)
        lo, hi = max(t0 - 1, 0), min(t0 + TOK + 1, SEQ)
        xs[lo - (t0 - 1): hi - (t0 - 1)] = x[b, lo:hi]
        hmask = np.array([[0.0 if t0 == 0 else 1.0], [0.0 if t0 + TOK == SEQ else 1.0]], np.float32)
        in_maps.append({
            "xs": xs, "c": c[b], "ada_w": ada_w[l], "ada_b": ada_b[l], "npre": norm_pre[l], "npost": norm_post[l],
            "w_in": w_in[l], "wk_perm": wkp, "hy_w": hy_short_w[l], "hy_b": hy_short_b[l],
            "cosT": np.ascontiguousarray(cosT[:, t0:t0 + TOK]), "sinT": np.ascontiguousarray(sinT[:, t0:t0 + TOK]),
            "hmask": hmask,
        })
    res = run_bass_kernel_spmd(nc, in_maps, core_ids=list(range(8)))
    return res.results


NFFT = 2 * SEQ
CG = 32


def hyena_tables():
    L = SEQ
    tau = np.arange(NFFT)
    pos = np.where(tau < L, tau, NFFT - tau).astype(np.int64)
    posc = np.minimum(pos, L - 1)
    t_lin = np.linspace(0.0, 1.0, L, dtype=np.float32)
    wpos = ((2.0 * math.pi / L) * np.arange(L, dtype=np.float32)).astype(np.float32)
    bands = np.linspace(1e-4, 16 - 1, 16, dtype=np.float32)
    zfull = np.concatenate([t_lin[:, None], np.cos(bands[None, :] * wpos[:, None]), -np.sin(bands[None, :] * wpos[:, None])],
                           axis=-1).astype(np.float32)
    zpos = np.ascontiguousarray(zfull[posc].T)
    mn = math.log(1e-2) / 1.5
    mx = math.log(1e-2) / 0.3
    deltas = np.abs(np.linspace(mn, mx, 512, dtype=np.float32))
    win = np.exp(-t_lin[posc][None, :] * deltas[:, None]).astype(np.float32)
    win[:, L] = 0.0
    a = np.arange(128, dtype=np.float64)
    ang = 2.0 * math.pi * np.outer(a, a) / 128.0
    C = np.cos(ang).astype(np.float32)
    S = np.sin(ang).astype(np.float32)
    angt = 2.0 * math.pi * np.outer(a, a) / NFFT
    Tc = np.cos(angt).astype(np.float32)
    Ts = np.sin(angt).astype(np.float32)
    dft = np.concatenate([C, S, C, -S, C], axis=1).astype(np.float32)
    tw = np.concatenate([Tc, Ts], axis=1).astype(np.float32)
    return zpos, win, dft, tw


def wrap_pi(B, a, ak, n, rows):
    P = B.P
    m, mk = B.ring("wrapm", [128, 512], F32, 2)
    P.op("dve", "tensor_scalar", reads=[ak], writes=[mk], out=m[0:rows, 0:n], in0=a[0:rows, 0:n], scalar1=PI, scalar2=-2.0 * PI,
         op0=ALU.is_gt, op1=ALU.mult)
    P.op("dve", "tensor_tensor", reads=[ak, mk], writes=[ak], out=a[0:rows, 0:n], in0=a[0:rows, 0:n], in1=m[0:rows, 0:n], op=ALU.add)
    P.op("dve", "tensor_scalar", reads=[ak], writes=[mk], out=m[0:rows, 0:n], in0=a[0:rows, 0:n], scalar1=-PI, scalar2=2.0 * PI,
         op0=ALU.is_lt, op1=ALU.mult)
    P.op("dve", "tensor_tensor", reads=[ak, mk], writes=[ak], out=a[0:rows, 0:n], in0=a[0:rows, 0:n], in1=m[0:rows, 0:n], op=ALU.add)


def filter_mlp_chunk(B, fw, zt, zk, n):
    P = B.P
    w1, w2, cols = fw["w1"], fw["w2"], fw["cols"]
    bk = B.bank()
    P.op("pe", "matmul", reads=["fw", zk], writes=["ps%d" % bk], out=B.ps[bk][0:64, 0:n], lhsT=w1[0:33, :], rhs=zt[0:33, 0:n],
         start=True, stop=True)
    a1, a1k = B.ring("fa", [128, 512], F32, 3)
    P.op("dve", "tensor_scalar", reads=["ps%d" % bk, "fw"], writes=[a1k], out=a1[0:64, 0:n], in0=B.ps[bk][0:64, 0:n],
         scalar1=cols[0:64, 0:1], scalar2=cols[0:64, 1:2], op0=ALU.add, op1=ALU.mult)
    wrap_pi(B, a1, a1k, n, 64)
    P.op("act", "activation", reads=[a1k], writes=[a1k], out=a1[0:64, 0:n], in_=a1[0:64, 0:n], func=AF.Sin)
    bk = B.bank()
    P.op("pe", "matmul", reads=["fw", a1k], writes=["ps%d" % bk], out=B.ps[bk][0:64, 0:n], lhsT=w2[0:64, :], rhs=a1[0:64, 0:n],
         start=True, stop=True)
    a2, a2k = B.ring("fa", [128, 512], F32, 3)
    P.op("dve", "tensor_scalar", reads=["ps%d" % bk, "fw"], writes=[a2k], out=a2[0:64, 0:n], in0=B.ps[bk][0:64, 0:n],
         scalar1=cols[0:64, 2:3], scalar2=cols[0:64, 3:4], op0=ALU.add, op1=ALU.mult)
    wrap_pi(B, a2, a2k, n, 64)
    P.op("act", "activation", reads=[a2k], writes=[a2k], out=a2[0:64, 0:n], in_=a2[0:64, 0:n], func=AF.Sin)
    return a2, a2k


def load_filter_weights(B, hw1, hb1, hw2, hb2, hfreq):
    P = B.P
    w1 = B.sb("fw1", [33, 64], F32)
    w2 = B.sb("fw2", [64, 64], F32)
    cols = B.sb("fcols", [64, 4], F32)
    P.dma("sp", writes=["fw"], out=w1[:], in_=hw1[:, :])
    P.dma("sp", writes=["fw"], out=w2[:], in_=hw2[:, :])
    P.dma("sp", writes=["fw"], out=cols[:, 0:1], in_=hb1.rearrange("(p o) -> p o", o=1))
    P.dma("sp", writes=["fw"], out=cols[:, 1:2], in_=hfreq[0].rearrange("(p o) -> p o", o=1))
    P.dma("sp", writes=["fw"], out=cols[:, 2:3], in_=hb2.rearrange("(p o) -> p o", o=1))
    P.dma("sp", writes=["fw"], out=cols[:, 3:4], in_=hfreq[1].rearrange("(p o) -> p o", o=1))
    return {"w1": w1, "w2": w2, "cols": cols}


def fft_pair_fwd(B, src, srck, kdim, c0, dft, tw):
    P = B.P
    bk = B.bank()
    for i in range(2):
        P.op("pe", "matmul", reads=[srck, "dft"], writes=["ps%d" % bk], out=B.ps[bk][:, 256 * i:256 * (i + 1)],
             lhsT=src[0:kdim, c0 + i, :], rhs=dft[0:kdim, 256:512], start=True, stop=True)
    A = B.ps[bk][:, :].rearrange("p (c r k) -> p c r k", c=2, r=2)
    Are, Aim = A[:, :, 0, :], A[:, :, 1, :]
    Tc = tw[:, 0:128].unsqueeze(1).to_broadcast([128, 2, 128])
    Ts = tw[:, 128:256].unsqueeze(1).to_broadcast([128, 2, 128])
    tt, ttk = B.ring("fft_t", [128, 4, 2, 128], F32, 2)
    pk = "ps%d" % bk
    P.op("dve", "tensor_tensor", reads=[pk, "tw"], writes=[ttk], out=tt[:, 0], in0=Are, in1=Tc, op=ALU.mult)
    P.op("dve", "tensor_tensor", reads=[pk, "tw"], writes=[ttk], out=tt[:, 1], in0=Aim, in1=Ts, op=ALU.mult)
    P.op("dve", "tensor_tensor", reads=[pk, "tw"], writes=[ttk], out=tt[:, 2], in0=Aim, in1=Tc, op=ALU.mult)
    P.op("dve", "tensor_tensor", reads=[pk, "tw"], writes=[ttk], out=tt[:, 3], in0=Are, in1=Ts, op=ALU.mult)
    b1, b1k = B.ring("fft_b1", [128, 2, 2, 128], F32, 2)
    b2, b2k = B.ring("fft_b2", [128, 2, 2, 128], F32, 2)
    P.op("pool", "tensor_tensor", reads=[ttk], writes=[b1k], out=b1[:, :, 0, :], in0=tt[:, 0], in1=tt[:, 1], op=ALU.add)
    P.op("pool", "tensor_tensor", reads=[ttk], writes=[b1k], out=b1[:, :, 1, :], in0=tt[:, 2], in1=tt[:, 3], op=ALU.subtract)
    P.op("act", "activation", reads=[b1k], writes=[b2k], out=b2[:, :, 0, :], in_=b1[:, :, 1, :], func=AF.Identity)
    P.op("act", "activation", reads=[b1k], writes=[b2k], out=b2[:, :, 1, :], in_=b1[:, :, 0, :], func=AF.Identity, scale=-1.0)
    bx = B.bank()
    P.op("pe", "matmul", reads=[b1k, "dft"], writes=["ps%d" % bx], out=B.ps[bx][:, :], lhsT=dft[:, 0:128],
         rhs=b1[:].rearrange("p c r k -> p (c r k)"), start=True, stop=False)
    P.op("pe", "matmul", reads=[b2k, "dft"], writes=["ps%d" % bx], out=B.ps[bx][:, :], lhsT=dft[:, 128:256],
         rhs=b2[:].rearrange("p c r k -> p (c r k)"), start=False, stop=True)
    return bx


def build_s2():
    B = Builder()
    P = B.P
    u_in = B.din("u", [128, SEQ])
    hw1 = B.din("hw1", [33, 64]); hb1 = B.din("hb1", [64]); hw2 = B.din("hw2", [64, 64]); hb2 = B.din("hb2", [64])
    hw3 = B.din("hw3", [64, 256]); hfreq = B.din("hfreq", [2, 64])
    zpos = B.din("zpos", [33, NFFT]); win = B.din("win", [128, NFFT])
    dft_in = B.din("dft", [128, 640]); tw_in = B.din("tw", [128, 256])
    y_out = B.dout("y", [128, SEQ])
    ssq_out = B.dout("ssq", [128, 1])
    kscr = B.dscratch("kscr", [128, NFFT])
    B.init_psum()
    dft = B.sb("dft", [128, 640], F32)
    tw = B.sb("tw", [128, 256], F32)
    P.dma("sp", writes=["dft"], out=dft[:], in_=dft_in[:, :])
    P.dma("sp", writes=["tw"], out=tw[:], in_=tw_in[:, :])
    fw = load_filter_weights(B, hw1, hb1, hw2, hb2, hfreq)
    w3 = B.sb("fw3", [64, 256], F32)
    P.dma("sp", writes=["fw"], out=w3[:], in_=hw3[:, :])
    ssqp = B.sb("ssqp", [128, 33], F32)
    P.op("dve", "memset", writes=["ssqp"], ap=ssqp[:], constant=0.0)
    m0 = B.mark()
    for ch in range(NFFT // 512):
        zt, zk = B.ring("zt", [33, 512], F32, 2)
        P.dma("sp", writes=[zk], out=zt[:], in_=zpos[:, ch * 512:(ch + 1) * 512])
        wn, wnk = B.ring("wn", [128, 512], F32, 2)
        P.dma("sp", writes=[wnk], out=wn[:], in_=win[:, ch * 512:(ch + 1) * 512])
        h2, h2k = filter_mlp_chunk(B, fw, zt, zk, 512)
        bk = B.bank()
        half = 0 if ch * 512 < SEQ else 1
        P.op("pe", "matmul", reads=["fw", h2k], writes=["ps%d" % bk], out=B.ps[bk][:, :], lhsT=w3[0:64, 128 * half:128 * (half + 1)],
             rhs=h2[0:64, :], start=True, stop=True)
        kc, kck = B.ring("kc", [128, 512], F32, 2)
        P.op("dve", "tensor_tensor", reads=["ps%d" % bk, wnk], writes=[kck], out=kc[:], in0=B.ps[bk][:, :], in1=wn[:], op=ALU.mult)
        jk_, jkk = B.ring("kjunk", [128, 512], F32, 2)
        P.op("act", "activation", reads=[kck], writes=[jkk, "ssqp"], out=jk_[:], in_=kc[:], func=AF.Square,
             accum_out=ssqp[:, ch:ch + 1])
        P.dma("sp", reads=[kck], writes=["kscr"], out=kscr[:, ch * 512:(ch + 1) * 512], in_=kc[:])
    P.op("dve", "tensor_reduce", reads=["ssqp"], writes=["ssqp"], out=ssqp[:, 32:33], in_=ssqp[:, 0:32], axis=AX.X, op=ALU.add)
    P.dma("sp", reads=["ssqp"], writes=["ssq"], out=ssq_out[:, :], in_=ssqp[:, 32:33])
    B.release(m0)
    kview = kscr.rearrange("c (p j) -> p c j", j=128)
    uview = u_in.rearrange("c (p j) -> p c j", j=128)
    yview = y_out.rearrange("c (p j) -> p c j", j=128)
    for g in range(128 // CG):
        cs0 = g * CG
        kd, kdk = B.ring("kd", [128, CG, 128], F32, 1)
        ud, udk = B.ring("ud", [64, CG, 128], F32, 1)
        P.dma("sp", reads=["kscr"], writes=[kdk], out=kd[:], in_=kview[:, cs0:cs0 + CG, :])
        P.dma("sp", writes=[udk], out=ud[:], in_=uview[:, cs0:cs0 + CG, :])
        KF, KFk = B.ring("KF", [128, CG, 2, 128], F32, 1)
        for c0 in range(0, CG, 2):
            bx = fft_pair_fwd(B, kd, kdk, 128, c0, dft, tw)
            P.op("act", "activation", reads=["ps%d" % bx], writes=[KFk], out=KF[:, c0:c0 + 2].rearrange("p c r k -> p (c r k)"),
                 in_=B.ps[bx][:, :], func=AF.Identity)
        Bre, Brek = B.ring("Bre", [128, CG, 128], F32, 1)
        Bim, Bimk = B.ring("Bim", [128, CG, 128], F32, 1)
        for c0 in range(0, CG, 2):
            bx = fft_pair_fwd(B, ud, udk, 64, c0, dft, tw)
            X = B.ps[bx][:, :].rearrange("p (c r k) -> p c r k", c=2, r=2)
            Xre, Xim = X[:, :, 0, :], X[:, :, 1, :]
            Kre, Kim = KF[:, c0:c0 + 2, 0, :], KF[:, c0:c0 + 2, 1, :]
            tt, ttk = B.ring("fft_t", [128, 4, 2, 128], F32, 2)
            pk = "ps%d" % bx
            P.op("dve", "tensor_tensor", reads=[pk, KFk], writes=[ttk], out=tt[:, 0], in0=Xre, in1=Kre, op=ALU.mult)
            P.op("dve", "tensor_tensor", reads=[pk, KFk], writes=[ttk], out=tt[:, 1], in0=Xim, in1=Kim, op=ALU.mult)
            P.op("dve", "tensor_tensor", reads=[pk, KFk], writes=[ttk], out=tt[:, 2], in0=Xre, in1=Kim, op=ALU.mult)
            P.op("dve", "tensor_tensor", reads=[pk, KFk], writes=[ttk], out=tt[:, 3], in0=Xim, in1=Kre, op=ALU.mult)
            Y, Yk = B.ring("Y", [128, 2, 2, 128], F32, 2)
            P.op("pool", "tensor_tensor", reads=[ttk], writes=[Yk], out=Y[:, :, 0, :], in0=tt[:, 0], in1=tt[:, 1], op=ALU.subtract)
            P.op("pool", "tensor_tensor", reads=[ttk], writes=[Yk], out=Y[:, :, 1, :], in0=tt[:, 2], in1=tt[:, 3], op=ALU.add)
            bi = B.bank()
            for i in range(2):
                P.op("pe", "matmul", reads=[Yk, "dft"], writes=["ps%d" % bi], out=B.ps[bi][:, 256 * i:256 * (i + 1)],
                     lhsT=Y[:, i, 0, :], rhs=dft[:, 0:256], start=True, stop=False, skip_group_check=True)
                P.op("pe", "matmul", reads=[Yk, "dft"], writes=["ps%d" % bi], out=B.ps[bi][:, 256 * i:256 * (i + 1)],
                     lhsT=Y[:, i, 1, :], rhs=dft[:, 384:640], start=False, stop=True, skip_group_check=True)
            Bm = B.ps[bi][:, :].rearrange("p (c r k) -> p c r k", c=2, r=2)
            Bre_p, Bim_p = Bm[:, :, 0, :], Bm[:, :, 1, :]
            Tc = tw[:, 0:128].unsqueeze(1).to_broadcast([128, 2, 128])
            Ts = tw[:, 128:256].unsqueeze(1).to_broadcast([128, 2, 128])
            t2, t2k = B.ring("fft_t", [128, 4, 2, 128], F32, 2)
            pk = "ps%d" % bi
            P.op("dve", "tensor_tensor", reads=[pk, "tw"], writes=[t2k], out=t2[:, 0], in0=Bre_p, in1=Tc, op=ALU.mult)
            P.op("dve", "tensor_tensor", reads=[pk, "tw"], writes=[t2k], out=t2[:, 1], in0=Bim_p, in1=Ts, op=ALU.mult)
            P.op("dve", "tensor_tensor", reads=[pk, "tw"], writes=[t2k], out=t2[:, 2], in0=Bre_p, in1=Ts, op=ALU.mult)
            P.op("dve", "tensor_tensor", reads=[pk, "tw"], writes=[t2k], out=t2[:, 3], in0=Bim_p, in1=Tc, op=ALU.mult)
            P.op("pool", "tensor_tensor", reads=[t2k], writes=[Brek], out=Bre[:, c0:c0 + 2, :], in0=t2[:, 0], in1=t2[:, 1], op=ALU.subtract)
            P.op("pool", "tensor_tensor", reads=[t2k], writes=[Bimk], out=Bim[:, c0:c0 + 2, :], in0=t2[:, 2], in1=t2[:, 3], op=ALU.add)
        yo, yok = B.ring("yo", [64, CG, 128], F32, 1)
        for c0 in range(0, CG, 4):
            bo = B.bank()
            P.op("pe", "matmul", reads=[Brek, "dft"], writes=["ps%d" % bo], out=B.ps[bo][0:64, :], lhsT=dft[:, 0:64],
                 rhs=Bre[:, c0:c0 + 4, :].rearrange("p c j -> p (c j)"), start=True, stop=False)
            P.op("pe", "matmul", reads=[Bimk, "dft"], writes=["ps%d" % bo], out=B.ps[bo][0:64, :], lhsT=dft[:, 384:448],
                 rhs=Bim[:, c0:c0 + 4, :].rearrange("p c j -> p (c j)"), start=False, stop=True)
            P.op("act", "activation", reads=["ps%d" % bo], writes=[yok], out=yo[:, c0:c0 + 4, :].rearrange("p c j -> p (c j)"),
                 in_=B.ps[bo][0:64, :], func=AF.Identity, scale=1.0 / NFFT)
        P.dma("sp", reads=[yok], writes=["y"], out=yview[:, cs0:cs0 + CG, :], in_=yo[:])
    return B.finish()


def run_s2(u_all, l, hy_f_w1, hy_f_b1, hy_f_w2, hy_f_b2, hy_f_w3, hy_f_freq):
    nc = get_nc("s2")
    zpos, win, dft, tw = hyena_tables()
    in_maps = []
    for core in range(8):
        b, q = divmod(core, 4)
        w3 = np.concatenate([hy_f_w3[l][:, 128 * q:128 * (q + 1)], hy_f_w3[l][:, 512 + 128 * q:512 + 128 * (q + 1)]], axis=1)
        in_maps.append({
            "u": np.ascontiguousarray(u_all[b, 128 * q:128 * (q + 1)]), "hw1": hy_f_w1[l], "hb1": hy_f_b1[l], "hw2": hy_f_w2[l],
            "hb2": hy_f_b2[l], "hw3": np.ascontiguousarray(w3), "hfreq": hy_f_freq[l], "zpos": zpos,
            "win": np.ascontiguousarray(win[128 * q:128 * (q + 1)]), "dft": dft, "tw": tw,
        })
    res = run_bass_kernel_spmd(nc, in_maps, core_ids=list(range(8))).results
    y = np.stack([np.concatenate([res[4 * b + q]["y"] for q in range(4)], 0) for b in range(2)], 0)
    ssq = np.concatenate([res[q]["ssq"][:, 0] for q in range(4)], 0)
    return y, ssq


NTOK3 = TOK + CTX
HCOLS = TOK + 2 + CTX + 2


def hcol(t):
    return 1 + t if t < TOK else 3 + t


CHUNKS = [(512 * j, 512) for j in range(TOK // 512)] + [(TOK, CTX)]
TILES = [128 * i for i in range(NTOK3 // 128)]


def bcast_row(B, dram_vec, n, name):
    P = B.P
    row = B.sb("row_" + name, [1, n], F32)
    P.dma("sp", writes=["row_" + name], out=row[:], in_=dram_vec.rearrange("(o c) -> o c", o=1))
    t = B.sb("bcr_" + name, [128, n], F32)
    pos = 0
    while pos < n:
        m = min(512, n - pos)
        bk = B.bank()
        P.op("pe", "matmul", reads=["ones1", "row_" + name], writes=["ps%d" % bk], out=B.ps[bk][:, 0:m], lhsT=B.ones1[:, :],
             rhs=row[:, pos:pos + m], start=True, stop=True)
        P.op("act", "activation", reads=["ps%d" % bk], writes=["bcr_" + name], out=t[:, pos:pos + m], in_=B.ps[bk][:, 0:m],
             func=AF.Identity)
        pos += m
    return t, "bcr_" + name


def rstd_from_ssq(B, st_, sk, n, c_in, c_tmp, c_out, inv_n):
    P = B.P
    P.op("dve", "tensor_scalar", reads=[sk], writes=[sk], out=st_[0:n, c_tmp:c_tmp + 1], in0=st_[0:n, c_in:c_in + 1], scalar1=inv_n,
         scalar2=EPS, op0=ALU.mult, op1=ALU.add)
    P.op("act", "activation", reads=[sk], writes=[sk], out=st_[0:n, c_tmp:c_tmp + 1], in_=st_[0:n, c_tmp:c_tmp + 1], func=AF.Sqrt)
    P.op("dve", "reciprocal", reads=[sk], writes=[sk], out=st_[0:n, c_out:c_out + 1], in_=st_[0:n, c_tmp:c_tmp + 1])


def transpose_to_fm(B, src, srck, dst, dstk, t0, idb):
    P = B.P
    bk = B.bank()
    psb = B.ps[bk][:, :].bitcast(BF16)
    for k in range(4):
        P.op("pe", "transpose", reads=[srck, "ident_b"], writes=["ps%d" % bk], out=psb[:, k * 128:(k + 1) * 128],
             in_=src[:, k * 128:(k + 1) * 128], identity=idb[:, :])
    P.op("act", "activation", reads=["ps%d" % bk], writes=[dstk], out=dst[:, :, t0:t0 + 128],
         in_=psb[:, 0:512].rearrange("p (k t) -> p k t", t=128), func=AF.Identity)


def build_s3():
    B = Builder()
    P = B.P
    xs = B.din("xs", [TOK + 2, D]); xc = B.din("xc", [CTX, D])
    cvec = B.din("c", [D]); cctx = B.din("c_ctx", [D])
    ada_w = B.din("ada_w", [D, 3 * D]); ada_b = B.din("ada_b", [3 * D])
    npre = B.din("npre", [D]); npost = B.din("npost", [D])
    w_in = B.din("w_in", [D, COL_END]); wq_perm = B.din("wq_perm", [D, 512])
    hmask = B.din("hmask", [2, 1])
    cosT = B.din("cosT", [128, TOK]); sinT = B.din("sinT", [128, TOK])
    kt_all = B.din("kt_all", [4, 128, SEQ], BF16); v_all = B.din("v_all", [SEQ, 512], BF16)
    u_own = B.din("u_own", [512, TOK]); y_own = B.din("y_own", [512, TOK])
    hyc = B.din("hyc", [128, 4, 2])
    hy_w = B.din("hy_w", [3, 1536]); hy_b = B.din("hy_b", [1536])
    da_lam = B.din("da_lam", [256]); lam_init = B.din("lam_init", [1]); da_subln = B.din("da_subln", [128])
    hw1 = B.din("hw1", [33, 64]); hb1 = B.din("hb1", [64]); hw2 = B.din("hw2", [64, 64]); hb2 = B.din("hb2", [64])
    hw3 = B.din("hw3", [64, 1024]); hfreq = B.din("hfreq", [2, 64])
    zposc = B.din("zposc", [33, 512]); winc = B.din("winc", [512, 512])
    gm_g = B.din("gm_g", [512]); gm_b = B.din("gm_b", [512]); gm_ws = B.din("gm_ws", [8, 128, 128]); gm_bs = B.din("gm_bs", [8, 128])
    w_br = B.din("w_br", [3, 512, D]); w_out = B.din("w_out", [D, D])
    x_new = B.dout("x_new", [TOK, D]); xc_new = B.dout("xc_new", [CTX, D])
    B.init_psum()
    idf, idb = make_identity(B)
    B.ones1 = B.sb("ones1", [1, 128], F32)
    P.op("dve", "memset", writes=["ones1"], ap=B.ones1[:], constant=1.0)
    hT = B.sb("hT", [128, 8, HCOLS], BF16)
    Gbc = B.sb("bc_G", [128, 1024], F32)
    Gcbc = B.sb("bc_Gc", [128, 1024], F32)
    gaT = B.sb("gaT", [128, 4, NTOK3], BF16)
    gbT = B.sb("gbT", [128, 4, NTOK3], BF16)
    gcT = B.sb("gcT", [128, 4, NTOK3], BF16)
    mt = B.sb("hmask", [2, 1], F32)
    P.dma("sp", writes=["hmask"], out=mt[:], in_=hmask[:, :])
    m0 = B.mark()
    Abc = B.sb("bc_A", [128, 1024], F32); Bbc = B.sb("bc_Bm", [128, 1024], F32)
    Acbc = B.sb("bc_Ac", [128, 1024], F32); Bcbc = B.sb("bc_Bc", [128, 1024], F32)
    m1 = B.mark()
    modulation(B, cvec, ada_w, ada_b, npre, npost, ("A", "Bm", "G"), {"A": Abc, "Bm": Bbc, "G": Gbc})
    B.release(m1)
    modulation(B, cctx, ada_w, ada_b, npre, npost, ("Ac", "Bc", "Gc"), {"Ac": Acbc, "Bc": Bcbc, "Gc": Gcbc})
    B.release(m1)
    for i in range(NT):
        compute_h_tile(B, [(xs[1 + 128 * i: 1 + 128 * (i + 1), :], 0, 128)], 128, Abc, Bbc, "bcA", "bcBm", hT, "hT", 1 + 128 * i, idb)
    hh = B.sb("hTh", [128, 8, 2], BF16)
    compute_h_tile(B, [(xs[0:1, :], 0, 1), (xs[TOK + 1:TOK + 2, :], 1, 1)], 2, Abc, Bbc, "bcA", "bcBm", hh, "hTh", 0, idb,
                   mask=(mt, "hmask"))
    P.op("pool", "tensor_copy", reads=["hTh"], writes=["hT"], out=hT[:, :, 0:1], in_=hh[:, :, 0:1])
    P.op("pool", "tensor_copy", reads=["hTh"], writes=["hT"], out=hT[:, :, TOK + 1:TOK + 2], in_=hh[:, :, 1:2])
    P.op("pool", "memset", writes=["hT"], ap=hT[:, :, TOK + 2:TOK + 3], constant=0.0)
    P.op("pool", "memset", writes=["hT"], ap=hT[:, :, HCOLS - 1:HCOLS], constant=0.0)
    for i in range(2):
        compute_h_tile(B, [(xc[128 * i:128 * (i + 1), :], 0, 128)], 128, Acbc, Bcbc, "bcAc", "bcBc", hT, "hT", TOK + 3 + 128 * i, idb)
    B.release(m0)

    m0 = B.mark()
    scw = load_shortconv(B, hy_w, hy_b)
    hyct = B.sb("hyct", [128, 4, 4], F32)
    P.dma("sp", writes=["hyct"], out=hyct[:, :, 0:2], in_=hyc[:, :, :])
    fw = load_filter_weights(B, hw1, hb1, hw2, hb2, hfreq)
    w3 = B.sb("fw3", [64, 1024], F32)
    P.dma("sp", writes=["fw"], out=w3[:], in_=hw3[:, :])
    zt = B.sb("ztc", [33, 512], F32)
    P.dma("sp", writes=["ztc"], out=zt[:], in_=zposc[:, :])
    hid2, hid2k = filter_mlp_chunk(B, fw, zt, "ztc", 512)
    hid2p = B.sb("hid2p", [64, 512], F32)
    P.op("pool", "tensor_copy", reads=[hid2k], writes=["hid2p"], out=hid2p[:], in_=hid2[0:64, :])
    segs_all = [(1, TOK, 0), (TOK + 3, CTX, TOK)]
    segs_ctx = [(TOK + 3, CTX, 0)]
    for ci in range(4):
        P.op("act", "activation", reads=["hyct"], writes=["hyct"], out=hyct[:, ci, 2:3], in_=hyct[:, ci, 0:1], func=AF.Sqrt)
        P.op("dve", "reciprocal", reads=["hyct"], writes=["hyct"], out=hyct[:, ci, 3:4], in_=hyct[:, ci, 2:3])
        x0s, x0k = B.ring("x0s", [128, NTOK3], F32, 1)
        hy_conv_tile(B, w_in, ci, hT, "hT", segs_all, scw, x0s, x0k)
        yb, ybk = B.ring("yb", [128, NTOK3], F32, 1)
        ut, utk = B.ring("ut", [128, TOK], F32, 1)
        P.dma("sp", writes=[ybk], out=yb[:, 0:TOK], in_=y_own[128 * ci:128 * (ci + 1), :])
        P.dma("sp", writes=[utk], out=ut[:], in_=u_own[128 * ci:128 * (ci + 1), :])
        P.op("dve", "tensor_scalar", reads=[ybk, "hyct"], writes=[ybk], out=yb[:, 0:TOK], in0=yb[:, 0:TOK], scalar1=hyct[:, ci, 3:4],
             scalar2=None, op0=ALU.mult)
        P.op("dve", "scalar_tensor_tensor", reads=[utk, "hyct", ybk], writes=[ybk], out=yb[:, 0:TOK], in0=ut[:], scalar=hyct[:, ci, 1:2],
             in1=yb[:, 0:TOK], op0=ALU.mult, op1=ALU.add)
        x1c, x1ck = B.ring("x1c", [128, CTX], F32, 1)
        vc_, vck = B.ring("vcc", [128, CTX], F32, 1)
        hy_conv_tile(B, w_in, 4 + ci, hT, "hT", segs_ctx, scw, x1c, x1ck)
        hy_conv_tile(B, w_in, 8 + ci, hT, "hT", segs_ctx, scw, vc_, vck)
        P.op("dve", "tensor_tensor", reads=[x1ck, vck], writes=[x1ck], out=x1c[:], in0=x1c[:], in1=vc_[:], op=ALU.mult)
        kc, kck = B.ring("kcf", [128, 512], F32, 1)
        wnc, wnck = B.ring("wnc", [128, 512], F32, 1)
        P.dma("sp", writes=[wnck], out=wnc[:], in_=winc[128 * ci:128 * (ci + 1), :])
        bk = B.bank()
        P.op("pe", "matmul", reads=["fw", "hid2p"], writes=["ps%d" % bk], out=B.ps[bk][:, 0:255], lhsT=w3[0:64, 512 + 128 * ci:512 + 128 * (ci + 1)],
             rhs=hid2p[0:64, 0:255], start=True, stop=True)
        P.op("pe", "matmul", reads=["fw", "hid2p"], writes=["ps%d" % bk], out=B.ps[bk][:, 255:512], lhsT=w3[0:64, 128 * ci:128 * (ci + 1)],
             rhs=hid2p[0:64, 255:512], start=True, stop=True)
        P.op("dve", "tensor_tensor", reads=["ps%d" % bk, wnck], writes=[kck], out=kc[:], in0=B.ps[bk][:, :], in1=wnc[:], op=ALU.mult)
        cst, cstk = B.ring("cst", [128, 4], F32, 2)
        jk_, jkk = B.ring("kjunk", [128, 512], F32, 1)
        P.op("act", "activation", reads=[kck], writes=[jkk, cstk], out=jk_[:], in_=kc[:], func=AF.Square, accum_out=cst[:, 0:1])
        P.op("act", "activation", reads=[cstk], writes=[cstk], out=cst[:, 1:2], in_=cst[:, 0:1], func=AF.Sqrt)
        P.op("dve", "reciprocal", reads=[cstk], writes=[cstk], out=cst[:, 2:3], in_=cst[:, 1:2])
        acc, acck = B.ring("cacc", [128, 2, CTX], F32, 1)
        P.op("pool", "memset", writes=[acck + "0"], ap=acc[:, 0, :], constant=0.0)
        P.op("pool", "memset", writes=[acck + "1"], ap=acc[:, 1, :], constant=0.0)
        for s in range(CTX):
            a = s % 2
            P.op("dve", "scalar_tensor_tensor", reads=[kck, x1ck, acck + str(a)], writes=[acck + str(a)], out=acc[:, a, :],
                 in0=kc[:, 255 - s:511 - s], scalar=x1c[:, s:s + 1], in1=acc[:, a, :], op0=ALU.mult, op1=ALU.add)
        P.op("dve", "tensor_tensor", reads=[acck + "0", acck + "1"], writes=[acck + "0"], out=acc[:, 0, :], in0=acc[:, 0, :], in1=acc[:, 1, :],
             op=ALU.add)
        P.op("dve", "tensor_scalar", reads=[acck + "0", cstk], writes=[ybk], out=yb[:, TOK:NTOK3], in0=acc[:, 0, :], scalar1=cst[:, 2:3],
             scalar2=None, op0=ALU.mult)
        P.op("dve", "scalar_tensor_tensor", reads=[x1ck, "hyct", ybk], writes=[ybk], out=yb[:, TOK:NTOK3], in0=x1c[:], scalar=hyct[:, ci, 1:2],
             in1=yb[:, TOK:NTOK3], op0=ALU.mult, op1=ALU.add)
        P.op("pool", "tensor_tensor", reads=[x0k, ybk], writes=[ybk], out=yb[:], in0=yb[:], in1=x0s[:], op=ALU.mult)
        wt, wk = wblock(B, w_in, COL_GB + 128 * ci, 128, "wblk128", 3, 128)
        for (t0, n) in CHUNKS:
            bk = B.bank()
            proj_fm(B, wt, wk, 0, hT, "hT", hcol(t0), n, bk)
            sg, sgk = B.ring("sgb", [128, 512], F32, 2)
            P.op("act", "activation", reads=["ps%d" % bk], writes=[sgk], out=sg[:, 0:n], in_=B.ps[bk][:, 0:n], func=AF.Silu)
            P.op("dve", "tensor_tensor", reads=[sgk, ybk], writes=["gbT"], out=gbT[:, ci, t0:t0 + n], in0=sg[:, 0:n], in1=yb[:, t0:t0 + n],
                 op=ALU.mult)
    B.release(m0)

    m0 = B.mark()
    lng, lngk = bcast_row(B, gm_g, 512, "lng")
    lnb, lnbk = bcast_row(B, gm_b, 512, "lnb")
    wsf = B.sb("wsf", [128, 8, 128], F32)
    P.dma("sp", writes=["wsf"], out=wsf[:], in_=gm_ws.rearrange("g p q -> p g q"))
    wsT = B.sb("wsT", [128, 8, 128], BF16)
    for g in range(8):
        bk = B.bank()
        P.op("pe", "transpose", reads=["wsf", "ident_f"], writes=["ps%d" % bk], out=B.ps[bk][:, 0:128], in_=wsf[:, g, :], identity=idf[:, :])
        P.op("act", "activation", reads=["ps%d" % bk], writes=["wsT"], out=wsT[:, g, :], in_=B.ps[bk][:, 0:128], func=AF.Identity)
    bsT = B.sb("bsT", [128, 8], F32)
    P.dma("sp", writes=["bsT"], out=bsT[:], in_=gm_bs.rearrange("g p -> p g"), allow_slow_non_contiguous=True)
    wu, wuk = wblock(B, w_in, COL_GM, 512, "wgm_u", 1)
    wv, wvk = wblock(B, w_in, COL_GM + 512, 512, "wgm_v", 1)
    wc, wck = wblock(B, w_in, COL_GC, 512, "wgm_c", 1)
    for t0 in TILES:
        ba, bb, bc_ = B.bank(), B.bank(), B.bank()
        proj_tm(B, wu, wuk, 0, 512, hT, "hT", hcol(t0), 128, ba)
        proj_tm(B, wv, wvk, 0, 512, hT, "hT", hcol(t0), 128, bb)
        proj_tm(B, wc, wck, 0, 512, hT, "hT", hcol(t0), 128, bc_)
        ug, ugk = B.ring("ug", [128, 512], F32, 2)
        vg, vgk = B.ring("vg", [128, 512], F32, 2)
        gs, gsk = B.ring("gs", [128, 512], F32, 2)
        P.op("act", "activation", reads=["ps%d" % ba], writes=[ugk], out=ug[:], in_=B.ps[ba][:, :], func=AF.Gelu)
        P.op("act", "activation", reads=["ps%d" % bb], writes=[vgk], out=vg[:], in_=B.ps[bb][:, :], func=AF.Gelu)
        P.op("act", "activation", reads=["ps%d" % bc_], writes=[gsk], out=gs[:], in_=B.ps[bc_][:, :], func=AF.Silu)
        s6, s6k = B.ring("s6", [128, 6], F32, 2)
        mv, mvk = B.ring("mv", [128, 4], F32, 2)
        P.op("dve", "bn_stats", reads=[vgk], writes=[s6k], out=s6[:], in_=vg[:])
        P.op("dve", "bn_aggr", reads=[s6k], writes=[mvk], out=mv[:, 0:2], in_=s6[:])
        P.op("dve", "tensor_scalar", reads=[mvk], writes=[mvk], out=mv[:, 2:3], in0=mv[:, 1:2], scalar1=EPS, scalar2=None, op0=ALU.add)
        P.op("act", "activation", reads=[mvk], writes=[mvk], out=mv[:, 2:3], in_=mv[:, 2:3], func=AF.Sqrt)
        P.op("dve", "reciprocal", reads=[mvk], writes=[mvk], out=mv[:, 3:4], in_=mv[:, 2:3])
        P.op("dve", "tensor_scalar", reads=[vgk, mvk], writes=[vgk], out=vg[:], in0=vg[:], scalar1=mv[:, 0:1], scalar2=mv[:, 3:4],
             op0=ALU.subtract, op1=ALU.mult)
        P.op("pool", "tensor_tensor", reads=[vgk, lngk], writes=[vgk], out=vg[:], in0=vg[:], in1=lng[:], op=ALU.mult)
        vnb, vnbk = B.ring("vnb", [128, 512], BF16, 2)
        P.op("pool", "tensor_tensor", reads=[vgk, lnbk], writes=[vnbk], out=vnb[:], in0=vg[:], in1=lnb[:], op=ALU.add)
        bm = B.bank()
        for g in range(8):
            P.op("pe", "matmul", reads=["wsT", vnbk], writes=["ps%d" % bm], out=B.ps[bm][:, 64 * g:64 * (g + 1)], lhsT=wsT[:, g, :],
                 rhs=vnb[:, 64 * g:64 * (g + 1)], start=(g == 0), stop=(g == 7), skip_group_check=True)
        for g in range(8):
            P.op("dve", "scalar_tensor_tensor", reads=["ps%d" % bm, "bsT", ugk], writes=[ugk], out=ug[:, 64 * g:64 * (g + 1)],
                 in0=B.ps[bm][:, 64 * g:64 * (g + 1)], scalar=bsT[:, g:g + 1], in1=ug[:, 64 * g:64 * (g + 1)], op0=ALU.add, op1=ALU.mult)
        yc, yck = B.ring("ycb", [128, 512], BF16, 2)
        P.op("pool", "tensor_tensor", reads=[ugk, gsk], writes=[yck], out=yc[:], in0=ug[:], in1=gs[:], op=ALU.mult)
        transpose_to_fm(B, yc, yck, gcT, "gcT", t0, idb)
    B.release(m0)
    build_s3_attn(B, locals())
    build_s3_merge(B, locals())
    return B.finish()


def build_s3_attn(B, L):
    P = B.P
    hT, idb, gaT = L["hT"], L["idb"], L["gaT"]
    w_in, wq_perm, cosT, sinT, kt_all, v_all = L["w_in"], L["wq_perm"], L["cosT"], L["sinT"], L["kt_all"], L["v_all"]
    m0 = B.mark()
    lamb, lambk = bcast_row(B, L["da_lam"], 256, "lam")
    lib, libk = bcast_row(B, L["lam_init"], 1, "li")
    gsub, gsubk = bcast_row(B, L["da_subln"], 128, "gsub")
    lt = B.sb("lamtmp", [128, 136], F32)
    P.op("dve", "tensor_tensor", reads=[lambk], writes=["lamtmp"], out=lt[:, 0:64], in0=lamb[:, 0:64], in1=lamb[:, 64:128], op=ALU.mult)
    P.op("dve", "tensor_tensor", reads=[lambk], writes=["lamtmp"], out=lt[:, 64:128], in0=lamb[:, 128:192], in1=lamb[:, 192:256], op=ALU.mult)
    P.op("dve", "tensor_reduce", reads=["lamtmp"], writes=["lamtmp"], out=lt[:, 128:129], in_=lt[:, 0:64], axis=AX.X, op=ALU.add)
    P.op("dve", "tensor_reduce", reads=["lamtmp"], writes=["lamtmp"], out=lt[:, 129:130], in_=lt[:, 64:128], axis=AX.X, op=ALU.add)
    P.op("act", "activation", reads=["lamtmp"], writes=["lamtmp"], out=lt[:, 130:132], in_=lt[:, 128:130], func=AF.Exp)
    P.op("dve", "tensor_tensor", reads=["lamtmp"], writes=["lamtmp"], out=lt[:, 132:133], in0=lt[:, 131:132], in1=lt[:, 130:131], op=ALU.subtract)
    P.op("dve", "tensor_tensor", reads=["lamtmp", libk], writes=["lamtmp"], out=lt[:, 133:134], in0=lt[:, 132:133], in1=lib[:, 0:1], op=ALU.subtract)
    neglam = lt[:, 133:134]
    P.op("dve", "tensor_scalar", reads=[libk], writes=["lamtmp"], out=lt[:, 134:135], in0=lib[:, 0:1], scalar1=-1.0, scalar2=1.0,
         op0=ALU.mult, op1=ALU.add)
    P.op("dve", "tensor_scalar", reads=[gsubk, "lamtmp"], writes=[gsubk], out=gsub[:], in0=gsub[:], scalar1=lt[:, 134:135], scalar2=None,
         op0=ALU.mult)
    QT = B.sb("QT", [128, 4, NTOK3], BF16)
    kcT = B.sb("kcT", [128, 4, CTX], BF16)
    vcx = B.sb("vcx", [128, 2, 4, 129], BF16)
    ya = B.sb("ya_tm", [128, NTOK3 // 128, 512], BF16)
    P.op("pool", "memset", writes=["vcx"], ap=vcx[:], constant=1.0)
    m1 = B.mark()
    cs = B.sb("cosT", [128, TOK], F32)
    sn = B.sb("sinT", [128, TOK], F32)
    P.dma("sp", writes=["cosT"], out=cs[:], in_=cosT[:, :])
    P.dma("sp", writes=["sinT"], out=sn[:], in_=sinT[:, :])
    wt, wk = wblock(B, w_in, COL_Q, 512, "wq")
    wp, wpk = wblock(B, wq_perm, 0, 512, "wq")
    for h in range(4):
        for j in range(TOK // 512):
            b1 = B.bank()
            proj_fm(B, wt, wk, 128 * h, hT, "hT", 1 + 512 * j, 512, b1)
            b2 = B.bank()
            proj_fm(B, wp, wpk, 128 * h, hT, "hT", 1 + 512 * j, 512, b2)
            t1, t1k = B.ring("rtmp1", [128, 512], F32, 2)
            t2, t2k = B.ring("rtmp2", [128, 512], F32, 2)
            P.op("dve", "tensor_tensor", reads=["ps%d" % b1, "cosT"], writes=[t1k], out=t1[:], in0=B.ps[b1][:, :],
                 in1=cs[:, 512 * j:512 * (j + 1)], op=ALU.mult)
            P.op("dve", "tensor_tensor", reads=["ps%d" % b2, "sinT"], writes=[t2k], out=t2[:], in0=B.ps[b2][:, :],
                 in1=sn[:, 512 * j:512 * (j + 1)], op=ALU.mult)
            P.op("pool", "tensor_tensor", reads=[t1k, t2k], writes=["QT"], out=QT[:, h, 512 * j:512 * (j + 1)], in0=t1[:], in1=t2[:], op=ALU.add)
        b1 = B.bank()
        proj_fm(B, wt, wk, 128 * h, hT, "hT", hcol(TOK), CTX, b1)
        P.op("act", "activation", reads=["ps%d" % b1], writes=["QT"], out=QT[:, h, TOK:NTOK3], in_=B.ps[b1][:, 0:CTX], func=AF.Identity)
    wt, wk = wblock(B, w_in, COL_K, 512, "wq")
    for h in range(4):
        b1 = B.bank()
        proj_fm(B, wt, wk, 128 * h, hT, "hT", hcol(TOK), CTX, b1)
        P.op("act", "activation", reads=["ps%d" % b1], writes=["kcT"], out=kcT[:, h, :], in_=B.ps[b1][:, 0:CTX], func=AF.Identity)
    wt, wk = wblock(B, w_in, COL_V, 512, "wq")
    for i in range(2):
        b1 = B.bank()
        proj_tm(B, wt, wk, 0, 512, hT, "hT", hcol(TOK + 128 * i), 128, b1)
        P.op("act", "activation", reads=["ps%d" % b1], writes=["vcx"], out=vcx[:, i, :, 0:128],
             in_=B.ps[b1][:, :].rearrange("p (h d) -> p h d", d=128), func=AF.Identity)
    B.release(m1)
    m2 = B.mark()
    kth = B.sb("kth", [128, SEQ], BF16)
    vh = B.sb("vh", [128, SEQ // 128, 129], BF16)
    P.op("pool", "memset", writes=["vh"], ap=vh[:], constant=1.0)
    vview = v_all.rearrange("(t p) c -> p t c", p=128) if v_all is not None else None
    scnt = 0
    for h in range(4):
        P.dma("sp", reads=L.get("kt_keys", []), writes=["kth"], out=L.get("kth_out", lambda k: k[:])(kth), in_=kt_all[h])
        if vview is not None:
            P.dma("sp", reads=L.get("v_keys", []), writes=["vh"], out=vh[:, :, 0:128], in_=vview[:, :, 128 * h:128 * (h + 1)])
        else:
            for k in range(2):
                for r in range(4):
                    P.dma("sp", reads=L.get("v_keys", []), writes=["vh"], out=vh[:, 16 * r + 8 * k:16 * r + 8 * k + 8, 0:128],
                          in_=L["v_gk"][k].rearrange("(r t p) c -> r p t c", r=4, p=128)[r][:, :, 128 * h:128 * (h + 1)])
        for (t0, n) in CHUNKS:
            nq = n // 128
            latent = t0 < TOK
            keys = ([("l", k) for k in range(SEQ // 128)] if latent else []) + [("c", 0), ("c", 1)]
            om, omk = B.ring("om", [128, 2, 4, 128], F32, 2)
            for m in range(2):
                for ki, (kind, kt) in enumerate(keys):
                    bS = 4 + (scnt % 4)
                    scnt += 1
                    if kind == "l":
                        lk, lkk = kth[64 * m:64 * (m + 1), 128 * kt:128 * (kt + 1)], "kth"
                        vt, vtk = vh[:, kt, :], "vh"
                    else:
                        lk, lkk = kcT[64 * m:64 * (m + 1), h, 128 * kt:128 * (kt + 1)], "kcT"
                        vt, vtk = vcx[:, kt, h, :], "vcx"
                    P.op("pe", "matmul", reads=[lkk, "QT"], writes=["ps%d" % bS], out=B.ps[bS][:, 0:n], lhsT=lk,
                         rhs=QT[64 * m:64 * (m + 1), h, t0:t0 + n], start=True, stop=True)
                    pt, ptk = B.ring("pt", [128, 512], BF16, 3)
                    P.op("act", "activation", reads=["ps%d" % bS], writes=[ptk], out=pt[:, 0:n], in_=B.ps[bS][:, 0:n], func=AF.Exp,
                         scale=0.125)
                    for qt in range(nq):
                        P.op("pe", "matmul", reads=[ptk, vtk], writes=["ps%d" % qt], out=B.ps[qt][:, 0:129],
                             lhsT=pt[:, 128 * qt:128 * (qt + 1)], rhs=vt, start=(ki == 0), stop=(ki == len(keys) - 1))
                for qt in range(nq):
                    rc, rck = B.ring("rc", [128, 2], F32, 4)
                    P.op("dve", "reciprocal", reads=["ps%d" % qt], writes=[rck], out=rc[:, 0:1], in_=B.ps[qt][:, 128:129])
                    if m == 1:
                        P.op("dve", "tensor_tensor", reads=[rck, "lamtmp"], writes=[rck], out=rc[:, 0:1], in0=rc[:, 0:1], in1=neglam, op=ALU.mult)
                    P.op("dve", "tensor_scalar", reads=["ps%d" % qt, rck], writes=[omk], out=om[:, m, qt, :], in0=B.ps[qt][:, 0:128],
                         scalar1=rc[:, 0:1], scalar2=None, op0=ALU.mult)
            P.op("pool", "tensor_tensor", reads=[omk], writes=[omk], out=om[:, 0, 0:nq, :], in0=om[:, 0, 0:nq, :], in1=om[:, 1, 0:nq, :], op=ALU.add)
            for qt in range(nq):
                st_, sk = B.ring("ast", [128, 4], F32, 4)
                jk_, jkk = B.ring("ajunk", [128, 128], F32, 2)
                P.op("act", "activation", reads=[omk], writes=[jkk, sk], out=jk_[:], in_=om[:, 0, qt, :], func=AF.Square, accum_out=st_[:, 0:1])
                rstd_from_ssq(B, st_, sk, 128, 0, 1, 2, 1.0 / 128)
                P.op("dve", "scalar_tensor_tensor", reads=[omk, sk, gsubk], writes=["ya_tm"], out=ya[:, t0 // 128 + qt, 128 * h:128 * (h + 1)],
                     in0=om[:, 0, qt, :], scalar=st_[:, 2:3], in1=gsub[:], op0=ALU.mult, op1=ALU.mult)
    B.release(m2)
    wt, wk = wblock(B, w_in, COL_GA, 512, "wq")
    for t0 in TILES:
        bk = B.bank()
        proj_tm(B, wt, wk, 0, 512, hT, "hT", hcol(t0), 128, bk)
        sg, sgk = B.ring("sga", [128, 512], F32, 2)
        P.op("act", "activation", reads=["ps%d" % bk], writes=[sgk], out=sg[:], in_=B.ps[bk][:, :], func=AF.Silu)
        yb_, ybk_ = B.ring("yab", [128, 512], BF16, 2)
        P.op("dve", "tensor_tensor", reads=[sgk, "ya_tm"], writes=[ybk_], out=yb_[:], in0=sg[:], in1=ya[:, t0 // 128, :], op=ALU.mult)
        transpose_to_fm(B, yb_, ybk_, gaT, "gaT", t0, idb)
    B.release(m0)


def build_s3_merge(B, L):
    P = B.P
    hT, w_in, w_br, w_out = L["hT"], L["w_in"], L["w_br"], L["w_out"]
    gT = [(L["gaT"], "gaT"), (L["gbT"], "gbT"), (L["gcT"], "gcT")]
    xs, xc, x_new, xc_new = L["xs"], L["xc"], L["x_new"], L["xc_new"]
    m0 = B.mark()
    wb = []
    for i in range(3):
        t = B.sb("wbr%d" % i, [128, 4, D], BF16)
        for j in range(4):
            load_cast(B, t[:, :, 256 * j:256 * (j + 1)], "wbr%d" % i, w_br[i].rearrange("(ct p) d -> p ct d", p=128)[:, :, 256 * j:256 * (j + 1)], (4, 256))
        wb.append(t)
    wo = B.sb("wo", [128, 8, D], BF16)
    for j in range(8):
        load_cast(B, wo[:, :, 128 * j:128 * (j + 1)], "wo", w_out.rearrange("(k p) d -> p k d", p=128)[:, :, 128 * j:128 * (j + 1)], (8, 128))
    for (t0, n) in CHUNKS:
        latent = t0 < TOK
        G, Gk = (L["Gbc"], "bcG") if latent else (L["Gcbc"], "bcGc")
        ob, obk = B.ring("outTb", [128, 8, 512], BF16, 1)
        for dt in range(8):
            acc, acck = B.ring("macc", [128, 512], F32, 2)
            for i in range(3):
                B.cast_eng = "act" if (dt * 3 + i) % 2 == 0 else "pool"
                wt, wk = wblock(B, w_in, COL_MG + 1024 * i + 128 * dt, 128, "wmg", 3, 128)
                B.cast_eng = "pool"
                bA = B.bank()
                proj_fm(B, wt, wk, 0, hT, "hT", hcol(t0), n, bA)
                bB = B.bank()
                for ct in range(4):
                    P.op("pe", "matmul", reads=["wbr%d" % i, gT[i][1]], writes=["ps%d" % bB], out=B.ps[bB][:, 0:n],
                         lhsT=wb[i][:, ct, 128 * dt:128 * (dt + 1)], rhs=gT[i][0][:, ct, t0:t0 + n], start=(ct == 0), stop=(ct == 3))
                sg, sgk = B.ring("msg", [128, 512], F32, 2)
                P.op("act", "activation", reads=["ps%d" % bA], writes=[sgk], out=sg[:, 0:n], in_=B.ps[bA][:, 0:n], func=AF.Sigmoid)
                if i == 0:
                    P.op("dve", "tensor_tensor", reads=[sgk, "ps%d" % bB], writes=[acck], out=acc[:, 0:n], in0=sg[:, 0:n], in1=B.ps[bB][:, 0:n], op=ALU.mult)
                else:
                    P.op("dve", "tensor_tensor", reads=[sgk, "ps%d" % bB], writes=[sgk], out=sg[:, 0:n], in0=sg[:, 0:n], in1=B.ps[bB][:, 0:n], op=ALU.mult)
                    if i == 1:
                        P.op("pool", "tensor_tensor", reads=[sgk, acck], writes=[acck], out=acc[:, 0:n], in0=acc[:, 0:n], in1=sg[:, 0:n], op=ALU.add)
                    else:
                        P.op("pool", "tensor_tensor", reads=[sgk, acck], writes=[obk], out=ob[:, dt, 0:n], in0=acc[:, 0:n], in1=sg[:, 0:n], op=ALU.add)
        for q in range(n // 128):
            tok = t0 + 128 * q
            bks = (B.bank(), B.bank())
            for half in range(2):
                for k in range(8):
                    P.op("pe", "matmul", reads=[obk, "wo"], writes=["ps%d" % bks[half]], out=B.ps[bks[half]][:, :], lhsT=ob[:, k, 128 * q:128 * (q + 1)],
                         rhs=wo[:, k, 512 * half:512 * (half + 1)], start=(k == 0), stop=(k == 7))
            st_, sk = B.ring("mst", [128, 8], F32, 4)
            for half in range(2):
                jk_, jkk = B.ring("mjunk", [128, 512], BF16, 2)
                P.op("act", "activation", reads=["ps%d" % bks[half]], writes=[jkk, sk], out=jk_[:], in_=B.ps[bks[half]][:, :], func=AF.Square,
                     accum_out=st_[:, half:half + 1])
            P.op("dve", "tensor_tensor", reads=[sk], writes=[sk], out=st_[:, 2:3], in0=st_[:, 0:1], in1=st_[:, 1:2], op=ALU.add)
            rstd_from_ssq(B, st_, sk, 128, 2, 3, 4, 1.0 / D)
            xt, xk = B.ring("mxt", [128, D], F32, 2)
            src = xs[1 + tok:1 + tok + 128, :] if latent else xc[tok - TOK:tok - TOK + 128, :]
            P.dma("sp", reads=L.get("x_keys", []), writes=[xk], out=xt[:], in_=src)
            ot, otk = B.ring("mot", [128, D], F32, 2)
            for half in range(2):
                P.op("dve", "scalar_tensor_tensor", reads=["ps%d" % bks[half], sk, Gk], writes=[otk], out=ot[:, 512 * half:512 * (half + 1)],
                     in0=B.ps[bks[half]][:, :], scalar=st_[:, 4:5], in1=G[:, 512 * half:512 * (half + 1)], op0=ALU.mult, op1=ALU.mult)
            P.op("pool", "tensor_tensor", reads=[otk, xk], writes=[otk], out=ot[:], in0=ot[:], in1=xt[:], op=ALU.add)
            if latent:
                P.dma("sp", reads=[otk], writes=[L.get("x_new_key", "x_new")], out=x_new[tok:tok + 128, :], in_=ot[:])
                if L.get("hal_src") is not None and tok == 0:
                    P.dma("sp", reads=[otk], writes=["hal_src"], out=L["hal_src"][0:1, :], in_=ot[0:1, :])
                if L.get("hal_src") is not None and tok == TOK - 128:
                    P.dma("sp", reads=[otk], writes=["hal_src"], out=L["hal_src"][1:2, :], in_=ot[127:128, :])
            elif xc_new is not None:
                P.dma("sp", reads=[otk], writes=[L.get("xc_new_key", "xc_new")], out=xc_new[tok - TOK:tok - TOK + 128, :], in_=ot[:])
    B.release(m0)


def ctx_tables():
    L = CTX
    i = np.arange(512)
    pos = np.minimum(np.abs(i - 255), L - 1)
    t_lin = np.linspace(0.0, 1.0, L, dtype=np.float32)
    wpos = ((2.0 * math.pi / L) * np.arange(L, dtype=np.float32)).astype(np.float32)
    bands = np.linspace(1e-4, 16 - 1, 16, dtype=np.float32)
    zfull = np.concatenate([t_lin[:, None], np.cos(bands[None, :] * wpos[:, None]), -np.sin(bands[None, :] * wpos[:, None])],
                           axis=-1).astype(np.float32)
    zposc = np.ascontiguousarray(zfull[pos].T)
    mn = math.log(1e-2) / 1.5
    mx = math.log(1e-2) / 0.3
    deltas = np.abs(np.linspace(mn, mx, 512, dtype=np.float32))
    winc = np.exp(-t_lin[pos][None, :] * deltas[:, None]).astype(np.float32)
    winc[:, 511] = 0.0
    return zposc, winc


def run_s3(x, xc, l, inp, kt_all, v_all, u_all, y_all, ssq):
    nc = get_nc("s3")
    cosT, sinT = rope_tables()
    zposc, winc = ctx_tables()
    w_in = inp["w_in"][l]
    wqp = perm_cols(w_in[:, COL_Q:COL_Q + 512])
    lam_init = np.array([0.8 - 0.6 * math.exp(-0.3 * l)], np.float32)
    hyc = np.ascontiguousarray(np.stack([ssq.reshape(4, 128).T, inp["hy_bias"][l].reshape(4, 128).T], axis=-1)).astype(np.float32)
    in_maps = []
    for core in range(8):
        b, r = divmod(core, 4)
        t0 = r * TOK
        xs = np.zeros((TOK + 2, D), np.float32)
        lo, hi = max(t0 - 1, 0), min(t0 + TOK + 1, SEQ)
        xs[lo - (t0 - 1): hi - (t0 - 1)] = x[b, lo:hi]
        hmask = np.array([[0.0 if t0 == 0 else 1.0], [0.0 if t0 + TOK == SEQ else 1.0]], np.float32)
        in_maps.append({
            "xs": xs, "xc": np.ascontiguousarray(xc[b]), "c": inp["c"][b], "c_ctx": inp["c_ctx"], "ada_w": inp["ada_w"][l], "ada_b": inp["ada_b"][l],
            "npre": inp["norm_pre"][l], "npost": inp["norm_post"][l], "w_in": w_in, "wq_perm": wqp, "hmask": hmask,
            "cosT": np.ascontiguousarray(cosT[:, t0:t0 + TOK]), "sinT": np.ascontiguousarray(sinT[:, t0:t0 + TOK]),
            "kt_all": kt_all[b], "v_all": v_all[b], "u_own": np.ascontiguousarray(u_all[b][:, t0:t0 + TOK]),
            "y_own": np.ascontiguousarray(y_all[b][:, t0:t0 + TOK]), "hyc": hyc, "hy_w": inp["hy_short_w"][l], "hy_b": inp["hy_short_b"][l],
            "da_lam": np.ascontiguousarray(inp["da_lambda"][l].reshape(256)), "lam_init": lam_init, "da_subln": inp["da_subln"][l],
            "hw1": inp["hy_f_w1"][l], "hb1": inp["hy_f_b1"][l], "hw2": inp["hy_f_w2"][l], "hb2": inp["hy_f_b2"][l],
            "hw3": inp["hy_f_w3"][l], "hfreq": inp["hy_f_freq"][l], "zposc": zposc, "winc": winc,
            "gm_g": inp["gm_ln_g"][l], "gm_b": inp["gm_ln_b"][l], "gm_ws": inp["gm_ws"][l], "gm_bs": inp["gm_bs"][l],
            "w_br": inp["w_branch"][l], "w_out": inp["w_out"][l],
        })
    res = run_bass_kernel_spmd(nc, in_maps, core_ids=list(range(8))).results
    x_new = np.stack([np.concatenate([res[4 * b + r]["x_new"] for r in range(4)], 0) for b in range(2)], 0)
    xc_new = np.stack([res[4 * b]["xc_new"] for b in range(2)], 0)
    return x_new, xc_new


def gather_s1(res):
    kt = np.stack([np.concatenate([np.asarray(res[4 * b + r]["o_kt"]) for r in range(4)], axis=2) for b in range(2)], 0)
    v = np.stack([np.concatenate([np.asarray(res[4 * b + r]["o_v"]) for r in range(4)], axis=0) for b in range(2)], 0)
    u = np.stack([np.concatenate([np.asarray(res[4 * b + r]["o_u"]) for r in range(4)], axis=1) for b in range(2)], 0)
    return np.ascontiguousarray(kt), np.ascontiguousarray(v), np.ascontiguousarray(u)


def kernel(**inputs):
    inp = {k: np.asarray(v) for k, v in inputs.items()}
    x = inp["x"].astype(np.float32, copy=False)
    xc = inp["ctx"].astype(np.float32, copy=False)
    for l in range(DEPTH):
        res1 = run_s1(x, inp["c"], l, inp["ada_w"], inp["ada_b"], inp["norm_pre"], inp["norm_post"], inp["w_in"], inp["hy_short_w"],
                      inp["hy_short_b"])
        kt_all, v_all, u_all = gather_s1(res1)
        y_all, ssq = run_s2(u_all, l, inp["hy_f_w1"], inp["hy_f_b1"], inp["hy_f_w2"], inp["hy_f_b2"], inp["hy_f_w3"], inp["hy_f_freq"])
        x, xc = run_s3(x, xc, l, inp, kt_all, v_all, u_all, y_all, ssq)
    return x.astype(np.float32)


RG4 = [[0, 1, 2, 3], [4, 5, 6, 7]]
FUSED_DEPTH = DEPTH
EXTRA_CC = 0
CGF = 8


_RANK_CACHE = {}


def rank_of(e):
    k = id(e)
    if k not in _RANK_CACHE:
        _RANK_CACHE[k] = (e, e.partition_id() % 4)
    return _RANK_CACHE[k][1]


def rank_nb(e, d):
    k = (id(e), d)
    if k not in _RANK_CACHE:
        _RANK_CACHE[k] = (e, (rank_of(e) + d) % 4)
    return _RANK_CACHE[k][1]


def f_products(B, E, l):
    P = B.P
    hT, w_in = E["hT"], E["w_in"][l]
    m0 = B.mark()
    cs = B.sb("cosT", [128, TOK], F32)
    sn = B.sb("sinT", [128, TOK], F32)
    P.dma("sp", writes=["cosT"], out=cs[:], in_=E["cosT"][:, :])
    P.dma("sp", writes=["sinT"], out=sn[:], in_=E["sinT"][:, :])
    wt, wk = wblock(B, w_in, COL_K, 512, "wblkK", 1)
    wp, wpk = wblock(B, E["wk_perm"][l], 0, 512, "wblkP", 1)
    for h in range(4):
        ko, kk = B.ring("kout", [128, TOK], BF16, 2)
        for j in range(TOK // 512):
            b1 = B.bank()
            proj_fm(B, wt, wk, 128 * h, hT, "hT", 1 + 512 * j, 512, b1)
            b2 = B.bank()
            proj_fm(B, wp, wpk, 128 * h, hT, "hT", 1 + 512 * j, 512, b2)
            t1, t1k = B.ring("rtmp1", [128, 512], F32, 2)
            t2, t2k = B.ring("rtmp2", [128, 512], F32, 2)
            P.op("dve", "tensor_tensor", reads=["ps%d" % b1, "cosT"], writes=[t1k], out=t1[:], in0=B.ps[b1][:, :],
                 in1=cs[:, 512 * j:512 * (j + 1)], op=ALU.mult)
            P.op("dve", "tensor_tensor", reads=["ps%d" % b2, "sinT"], writes=[t2k], out=t2[:], in0=B.ps[b2][:, :],
                 in1=sn[:, 512 * j:512 * (j + 1)], op=ALU.mult)
            P.op("pool", "tensor_tensor", reads=[t1k, t2k], writes=[kk], out=ko[:, 512 * j:512 * (j + 1)], in0=t1[:], in1=t2[:], op=ALU.add)
        P.dma("sp", reads=[kk], writes=["kt_src"], out=E["kt_src"][h // 2][128 * (h % 2):128 * (h % 2 + 1), :], in_=ko[:])
    B.release(m0)
    wt, wk = wblock(B, w_in, COL_V, 512, "wblkK", 1)
    for i in range(NT):
        bk = B.bank()
        proj_tm(B, wt, wk, 0, 512, hT, "hT", 1 + 128 * i, 128, bk)
        vo, vk = B.ring("vout", [128, 512], BF16, 3)
        P.op("act", "activation", reads=["ps%d" % bk], writes=[vk], out=vo[:], in_=B.ps[bk][:, :], func=AF.Identity)
        P.dma("sp", reads=[vk], writes=["v_src"], out=E["v_src"][i // 8][128 * (i % 8):128 * (i % 8 + 1), :], in_=vo[:])
    B.release(m0)
    scw = load_shortconv(B, E["hy_w"][l], E["hy_b"][l])
    for ci in range(4):
        x1s, x1k = B.ring("x1s", [128, TOK], F32, 2)
        vs, vsk = B.ring("vs", [128, TOK], F32, 2)
        hy_conv_tile(B, w_in, 4 + ci, hT, "hT", [(1, TOK, 0)], scw, x1s, x1k)
        hy_conv_tile(B, w_in, 8 + ci, hT, "hT", [(1, TOK, 0)], scw, vs, vsk)
        P.op("dve", "tensor_tensor", reads=[x1k, vsk], writes=[x1k], out=x1s[:], in0=x1s[:], in1=vs[:], op=ALU.mult)
        P.dma("sp", reads=[x1k], writes=["u_src"], out=E["u_src"][ci], in_=x1s[:])
    B.release(m0)
    par = l % 2
    for k in range(2):
        P.cc(reads=["kt_src"], writes=["kt_g%d" % par], kind="AllGather", op=ALU.bypass, replica_groups=RG4, ins=[E["kt_src"][k].opt()],
             outs=[E["kt_g"][par][k].opt()])
    for k in range(2):
        P.cc(reads=["v_src"], writes=["v_g%d" % par], kind="AllGather", op=ALU.bypass, replica_groups=RG4, ins=[E["v_src"][k].opt()],
             outs=[E["v_g"][par][k].opt()])
    for k in range(4):
        P.cc(reads=["u_src"], writes=["u_g"], kind="AllGather", op=ALU.bypass, replica_groups=RG4, ins=[E["u_src"][k].opt()],
             outs=[E["u_g"][par][k].opt()])


def f_conv(B, E, l):
    P = B.P
    par = l % 2
    dft, tw = E["dft"], E["tw"]
    kscr = E["kscr"]
    m0 = B.mark()
    fw = load_filter_weights(B, E["hw1"][l], E["hb1"][l], E["hw2"][l], E["hb2"][l], E["hfreq"][l])
    w3 = B.sb("fw3", [64, 256], F32)
    P.dma("sp", writes=["fw"], out=w3[:], in_=E["hw3q"][l])
    ssqp = B.sb("ssqp", [128, 33], F32)
    P.op("dve", "memset", writes=["ssqp"], ap=ssqp[:], constant=0.0)
    m1 = B.mark()
    for ch in range(NFFT // 512):
        zt, zk = B.ring("zt", [33, 512], F32, 2)
        P.dma("sp", writes=[zk], out=zt[:], in_=E["zpos"][:, ch * 512:(ch + 1) * 512])
        wn, wnk = B.ring("wn", [128, 512], F32, 2)
        P.dma("sp", writes=[wnk], out=wn[:], in_=E["win"][:, ch * 512:(ch + 1) * 512])
        h2, h2k = filter_mlp_chunk(B, fw, zt, zk, 512)
        bk = B.bank()
        half = 0 if ch * 512 < SEQ else 1
        P.op("pe", "matmul", reads=["fw", h2k], writes=["ps%d" % bk], out=B.ps[bk][:, :], lhsT=w3[0:64, 128 * half:128 * (half + 1)],
             rhs=h2[0:64, :], start=True, stop=True)
        kc, kck = B.ring("kc", [128, 512], F32, 2)
        P.op("dve", "tensor_tensor", reads=["ps%d" % bk, wnk], writes=[kck], out=kc[:], in0=B.ps[bk][:, :], in1=wn[:], op=ALU.mult)
        jk_, jkk = B.ring("kjunk", [128, 512], F32, 2)
        P.op("act", "activation", reads=[kck], writes=[jkk, "ssqp"], out=jk_[:], in_=kc[:], func=AF.Square, accum_out=ssqp[:, ch:ch + 1])
        P.dma("sp", reads=[kck], writes=["kscr"], out=kscr[:, ch * 512:(ch + 1) * 512], in_=kc[:])
    P.op("dve", "tensor_reduce", reads=["ssqp"], writes=["ssqp"], out=ssqp[:, 32:33], in_=ssqp[:, 0:32], axis=AX.X, op=ALU.add)
    P.dma("sp", reads=["ssqp"], writes=["ssq_src"], out=E["ssq_src"][:, 0:1], in_=ssqp[:, 32:33],
          allow_slow_non_contiguous=True)
    B.release(m1)
    kview = kscr.rearrange("c (p j) -> p c j", j=128)
    ugq = E["u_g"][par].rearrange("q (r c) t -> q r c t", r=4)
    P.dma("sp", reads=["u_g"], writes=["u_my"], out=E["u_my"], in_=(lambda e: ugq[rank_of(e)]))
    ugv = E["u_my"].rearrange("r c (pp j) -> r pp c j", j=128)
    yview = E["y_src"].rearrange("r c (pp j) -> r pp c j", j=128)
    for g in range(128 // CGF):
        cs0 = g * CGF
        kd, kdk = B.ring("kd", [128, CGF, 128], F32, 1)
        ud, udk = B.ring("ud", [64, CGF, 128], F32, 1)
        P.dma("sp", reads=["kscr"], writes=[kdk], out=kd[:], in_=kview[:, cs0:cs0 + CGF, :])
        for r in range(4):
            P.dma("sp", reads=["u_my"], writes=[udk], out=ud[16 * r:16 * (r + 1), :, :], in_=ugv[r][:, cs0:cs0 + CGF, :])
        KF, KFk = B.ring("KF", [128, CGF, 2, 128], F32, 1)
        Tc = tw[:, 0:128].unsqueeze(1).to_broadcast([128, 2, 128])
        Ts = tw[:, 128:256].unsqueeze(1).to_broadcast([128, 2, 128])

        def st1(src, srck, kdim, c0):
            bk = B.bank()
            for i in range(2):
                P.op("pe", "matmul", reads=[srck, "dft"], writes=["ps%d" % bk], out=B.ps[bk][:, 256 * i:256 * (i + 1)],
                     lhsT=src[0:kdim, c0 + i, :], rhs=dft[0:kdim, 256:512], start=True, stop=True)
            return bk

        def st2(bk):
            A = B.ps[bk][:, :].rearrange("p (c r k) -> p c r k", c=2, r=2)
            Are, Aim = A[:, :, 0, :], A[:, :, 1, :]
            tt, ttk = B.ring("fft_t", [128, 4, 2, 128], F32, 3)
            pk = "ps%d" % bk
            P.op("dve", "tensor_tensor", reads=[pk, "tw"], writes=[ttk], out=tt[:, 0], in0=Are, in1=Tc, op=ALU.mult)
            P.op("dve", "tensor_tensor", reads=[pk, "tw"], writes=[ttk], out=tt[:, 1], in0=Aim, in1=Ts, op=ALU.mult)
            P.op("dve", "tensor_tensor", reads=[pk, "tw"], writes=[ttk], out=tt[:, 2], in0=Aim, in1=Tc, op=ALU.mult)
            P.op("dve", "tensor_tensor", reads=[pk, "tw"], writes=[ttk], out=tt[:, 3], in0=Are, in1=Ts, op=ALU.mult)
            b1, b1k = B.ring("fft_b1", [128, 2, 2, 128], F32, 2)
            b2, b2k = B.ring("fft_b2", [128, 2, 2, 128], F32, 2)
            P.op("pool", "tensor_tensor", reads=[ttk], writes=[b1k], out=b1[:, :, 0, :], in0=tt[:, 0], in1=tt[:, 1], op=ALU.add)
            P.op("pool", "tensor_tensor", reads=[ttk], writes=[b1k], out=b1[:, :, 1, :], in0=tt[:, 2], in1=tt[:, 3], op=ALU.subtract)
            P.op("act", "activation", reads=[b1k], writes=[b2k], out=b2[:, :, 0, :], in_=b1[:, :, 1, :], func=AF.Identity)
            P.op("act", "activation", reads=[b1k], writes=[b2k], out=b2[:, :, 1, :], in_=b1[:, :, 0, :], func=AF.Identity, scale=-1.0)
            return (b1, b1k, b2, b2k)

        def st3(bb):
            b1, b1k, b2, b2k = bb
            bx = B.bank()
            P.op("pe", "matmul", reads=[b1k, "dft"], writes=["ps%d" % bx], out=B.ps[bx][:, :], lhsT=dft[:, 0:128],
                 rhs=b1[:].rearrange("p c r k -> p (c r k)"), start=True, stop=False)
            P.op("pe", "matmul", reads=[b2k, "dft"], writes=["ps%d" % bx], out=B.ps[bx][:, :], lhsT=dft[:, 128:256],
                 rhs=b2[:].rearrange("p c r k -> p (c r k)"), start=False, stop=True)
            return bx

        def st4f(bx, c0):
            P.op("act", "activation", reads=["ps%d" % bx], writes=[KFk], out=KF[:, c0:c0 + 2].rearrange("p c r k -> p (c r k)"),
                 in_=B.ps[bx][:, :], func=AF.Identity)

        def st4d(bx, c0):
            X = B.ps[bx][:, :].rearrange("p (c r k) -> p c r k", c=2, r=2)
            Xre, Xim = X[:, :, 0, :], X[:, :, 1, :]
            Kre, Kim = KF[:, c0:c0 + 2, 0, :], KF[:, c0:c0 + 2, 1, :]
            tt, ttk = B.ring("fft_t", [128, 4, 2, 128], F32, 3)
            pk = "ps%d" % bx
            P.op("dve", "tensor_tensor", reads=[pk, KFk], writes=[ttk], out=tt[:, 0], in0=Xre, in1=Kre, op=ALU.mult)
            P.op("dve", "tensor_tensor", reads=[pk, KFk], writes=[ttk], out=tt[:, 1], in0=Xim, in1=Kim, op=ALU.mult)
            P.op("dve", "tensor_tensor", reads=[pk, KFk], writes=[ttk], out=tt[:, 2], in0=Xre, in1=Kim, op=ALU.mult)
            P.op("dve", "tensor_tensor", reads=[pk, KFk], writes=[ttk], out=tt[:, 3], in0=Xim, in1=Kre, op=ALU.mult)
            Y, Yk = B.ring("Y", [128, 2, 2, 128], F32, 2)
            P.op("pool", "tensor_tensor", reads=[ttk], writes=[Yk], out=Y[:, :, 0, :], in0=tt[:, 0], in1=tt[:, 1], op=ALU.subtract)
            P.op("pool", "tensor_tensor", reads=[ttk], writes=[Yk], out=Y[:, :, 1, :], in0=tt[:, 2], in1=tt[:, 3], op=ALU.add)
            return (Y, Yk)

        def st5(yy):
            Y, Yk = yy
            bi = B.bank()
            for i in range(2):
                P.op("pe", "matmul", reads=[Yk, "dft"], writes=["ps%d" % bi], out=B.ps[bi][:, 256 * i:256 * (i + 1)],
                     lhsT=Y[:, i, 0, :], rhs=dft[:, 0:256], start=True, stop=False, skip_group_check=True)
                P.op("pe", "matmul", reads=[Yk, "dft"], writes=["ps%d" % bi], out=B.ps[bi][:, 256 * i:256 * (i + 1)],
                     lhsT=Y[:, i, 1, :], rhs=dft[:, 384:640], start=False, stop=True, skip_group_check=True)
            return bi

        def st6(bi, c0):
            Bm = B.ps[bi][:, :].rearrange("p (c r k) -> p c r k", c=2, r=2)
            Bre_p, Bim_p = Bm[:, :, 0, :], Bm[:, :, 1, :]
            t2, t2k = B.ring("fft_t", [128, 4, 2, 128], F32, 3)
            pk = "ps%d" % bi
            P.op("dve", "tensor_tensor", reads=[pk, "tw"], writes=[t2k], out=t2[:, 0], in0=Bre_p, in1=Tc, op=ALU.mult)
            P.op("dve", "tensor_tensor", reads=[pk, "tw"], writes=[t2k], out=t2[:, 1], in0=Bim_p, in1=Ts, op=ALU.mult)
            P.op("dve", "tensor_tensor", reads=[pk, "tw"], writes=[t2k], out=t2[:, 2], in0=Bre_p, in1=Ts, op=ALU.mult)
            P.op("dve", "tensor_tensor", reads=[pk, "tw"], writes=[t2k], out=t2[:, 3], in0=Bim_p, in1=Tc, op=ALU.mult)
            P.op("pool", "tensor_tensor", reads=[t2k], writes=[Brek], out=Bre[:, c0:c0 + 2, :], in0=t2[:, 0], in1=t2[:, 1], op=ALU.subtract)
            P.op("pool", "tensor_tensor", reads=[t2k], writes=[Bimk], out=Bim[:, c0:c0 + 2, :], in0=t2[:, 2], in1=t2[:, 3], op=ALU.add)

        pairs = list(range(0, CGF, 2))
        for q0 in range(0, len(pairs), 2):
            cs_ = pairs[q0:q0 + 2]
            bks = [st1(kd, kdk, 128, c0) for c0 in cs_]
            bbs = [st2(bk) for bk in bks]
            bxs = [st3(bb) for bb in bbs]
            for bx, c0 in zip(bxs, cs_):
                st4f(bx, c0)
        Bre, Brek = B.ring("Bre", [128, CGF, 128], F32, 1)
        Bim, Bimk = B.ring("Bim", [128, CGF, 128], F32, 1)
        for q0 in range(0, len(pairs), 2):
            cs_ = pairs[q0:q0 + 2]
            bks = [st1(ud, udk, 64, c0) for c0 in cs_]
            bbs = [st2(bk) for bk in bks]
            bxs = [st3(bb) for bb in bbs]
            yys = [st4d(bx, c0) for bx, c0 in zip(bxs, cs_)]
            bis = [st5(yy) for yy in yys]
            for bi, c0 in zip(bis, cs_):
                st6(bi, c0)
        yo, yok = B.ring("yo", [64, CGF, 128], F32, 1)
        for c0 in range(0, CGF, 4):
            bo = B.bank()
            P.op("pe", "matmul", reads=[Brek, "dft"], writes=["ps%d" % bo], out=B.ps[bo][0:64, :], lhsT=dft[:, 0:64],
                 rhs=Bre[:, c0:c0 + 4, :].rearrange("p c j -> p (c j)"), start=True, stop=False)
            P.op("pe", "matmul", reads=[Bimk, "dft"], writes=["ps%d" % bo], out=B.ps[bo][0:64, :], lhsT=dft[:, 384:448],
                 rhs=Bim[:, c0:c0 + 4, :].rearrange("p c j -> p (c j)"), start=False, stop=True)
            P.op("act", "activation", reads=["ps%d" % bo], writes=[yok], out=yo[:, c0:c0 + 4, :].rearrange("p c j -> p (c j)"),
                 in_=B.ps[bo][0:64, :], func=AF.Identity, scale=1.0 / NFFT)
        for r in range(4):
            P.dma("sp", reads=[yok], writes=["y_src"], out=yview[r][:, cs0:cs0 + CGF, :], in_=yo[16 * r:16 * (r + 1), :, :])
    B.release(m0)
    def issue_y_gather():
        for k in range(4):
            P.cc(reads=["y_src"], writes=["y_g"], kind="AllGather", op=ALU.bypass, replica_groups=RG4, ins=[E["y_src"][k].opt()],
                 outs=[E["y_g"][par][k].opt()])
        P.cc(reads=["ssq_src"], writes=["ssq_g%d" % par], kind="AllGather", op=ALU.bypass, replica_groups=RG4, ins=[E["ssq_src"].opt()],
             outs=[E["ssq_g"][par].opt()])
    return issue_y_gather


def f_hyena_gate(B, E, l):
    P = B.P
    par = l % 2
    hT, w_in, gbT = E["hT"], E["w_in"][l], E["gbT"]
    m0 = B.mark()
    scw = load_shortconv(B, E["hy_w"][l], E["hy_b"][l])
    hyct = B.sb("hyct", [128, 4, 4], F32)
    P.dma("sp", reads=["ssq_g%d" % par], writes=["hyct"], out=hyct[:, :, 0:1],
          in_=E["ssq_g"][par][:, 0:1].rearrange("(ci p) o -> p ci o", p=128), allow_slow_non_contiguous=True)
    P.dma("sp", writes=["hyct"], out=hyct[:, :, 1:2], in_=E["hyb"][l].rearrange("p (ci o) -> p ci o", o=1), allow_slow_non_contiguous=True)
    fw = load_filter_weights(B, E["hw1"][l], E["hb1"][l], E["hw2"][l], E["hb2"][l], E["hfreq"][l])
    w3 = B.sb("fw3", [64, 1024], F32)
    P.dma("sp", writes=["fw"], out=w3[:], in_=E["hw3"][l])
    zt = B.sb("ztc", [33, 512], F32)
    P.dma("sp", writes=["ztc"], out=zt[:], in_=E["zposc"][:, :])
    hid2, hid2k = filter_mlp_chunk(B, fw, zt, "ztc", 512)
    hid2p = B.sb("hid2p", [64, 512], F32)
    P.op("pool", "tensor_copy", reads=[hid2k], writes=["hid2p"], out=hid2p[:], in_=hid2[0:64, :])
    segs_all = [(1, TOK, 0), (TOK + 3, CTX, TOK)]
    segs_ctx = [(TOK + 3, CTX, 0)]
    ygr = E["y_g"][par]
    P.dma("sp", reads=["y_g"], writes=["y_my"], out=E["y_my"], in_=(lambda e: ygr[rank_of(e)]))
    idf = E["idf"]
    uc_all = B.sb("uc_all", [128, 4, CTX], F32)
    kc_all = B.sb("kc_all", [128, 4, 512], F32)
    ycv = B.sb("ycv_all", [128, 4, CTX], F32)
    cst = B.sb("cst_all", [128, 4, 4], F32)
    mA = B.mark()
    for ci in range(4):
        x1c, x1ck = B.ring("x1c", [128, CTX], F32, 1)
        vc_, vck = B.ring("vcc", [128, CTX], F32, 1)
        hy_conv_tile(B, w_in, 4 + ci, hT, "hT", segs_ctx, scw, x1c, x1ck)
        hy_conv_tile(B, w_in, 8 + ci, hT, "hT", segs_ctx, scw, vc_, vck)
        P.op("dve", "tensor_tensor", reads=[x1ck, vck], writes=["uc_all"], out=uc_all[:, ci, :], in0=x1c[:], in1=vc_[:], op=ALU.mult)
        wnc, wnck = B.ring("wnc", [128, 512], F32, 1)
        P.dma("sp", writes=[wnck], out=wnc[:], in_=E["winc"][128 * ci:128 * (ci + 1), :])
        bk = B.bank()
        P.op("pe", "matmul", reads=["fw", "hid2p"], writes=["ps%d" % bk], out=B.ps[bk][:, 0:255], lhsT=w3[0:64, 512 + 128 * ci:512 + 128 * (ci + 1)],
             rhs=hid2p[0:64, 0:255], start=True, stop=True)
        P.op("pe", "matmul", reads=["fw", "hid2p"], writes=["ps%d" % bk], out=B.ps[bk][:, 255:512], lhsT=w3[0:64, 128 * ci:128 * (ci + 1)],
             rhs=hid2p[0:64, 255:512], start=True, stop=True)
        P.op("dve", "tensor_tensor", reads=["ps%d" % bk, wnck], writes=["kc_all"], out=kc_all[:, ci, :], in0=B.ps[bk][:, :], in1=wnc[:], op=ALU.mult)
        jk_, jkk = B.ring("kjunk", [128, 512], F32, 1)
        P.op("act", "activation", reads=["kc_all"], writes=[jkk, "cst_all"], out=jk_[:], in_=kc_all[:, ci, :], func=AF.Square, accum_out=cst[:, ci, 0:1])
        P.op("act", "activation", reads=["cst_all"], writes=["cst_all"], out=cst[:, ci, 1:2], in_=cst[:, ci, 0:1], func=AF.Sqrt)
        P.op("dve", "reciprocal", reads=["cst_all"], writes=["cst_all"], out=cst[:, ci, 2:3], in_=cst[:, ci, 1:2])
    B.release(mA)
    ftab = B.sb("ftab", [128, 4, 1024], F32)
    uT = B.sb("uT", [128, 2, 512], F32)
    kT = B.sb("kT", [128, 4, 512], F32)
    AB = B.sb("AB", [128, 2, 4, 512], F32)
    PP_ = B.sb("PP", [128, 2, 4, 512], F32)
    yT = uT
    for tt in range(2):
        bk = B.bank()
        for ci in range(4):
            P.op("pe", "transpose", reads=["uc_all", "ident_f"], writes=["ps%d" % bk], out=B.ps[bk][:, 128 * ci:128 * (ci + 1)],
                 in_=uc_all[:, ci, 128 * tt:128 * (tt + 1)], identity=idf[:, :])
        P.op("act", "activation", reads=["ps%d" % bk], writes=["uT"], out=uT[:, tt, :], in_=B.ps[bk][:, :], func=AF.Identity)
    for it_ in range(4):
        bk = B.bank()
        for ci in range(4):
            P.op("pe", "transpose", reads=["kc_all", "ident_f"], writes=["ps%d" % bk], out=B.ps[bk][:, 128 * ci:128 * (ci + 1)],
                 in_=kc_all[:, ci, 128 * it_:128 * (it_ + 1)], identity=idf[:, :])
        P.op("act", "activation", reads=["ps%d" % bk], writes=["kT"], out=kT[:, it_, :], in_=B.ps[bk][:, :], func=AF.Identity)
    P.dma("sp", writes=["ftab"], out=ftab[:, 0:2, :], in_=E["FU"].rearrange("(a p) k -> p a k", p=128))
    for kt in range(4):
        for j in range(2):
            bk = B.bank()
            for tt in range(2):
                P.op("pe", "matmul", reads=["ftab", "uT"], writes=["ps%d" % bk], out=B.ps[bk][:, :], lhsT=ftab[:, tt, 512 * j + 128 * kt:512 * j + 128 * (kt + 1)],
                     rhs=uT[:, tt, :], start=(tt == 0), stop=(tt == 1))
            P.op("act", "activation", reads=["ps%d" % bk], writes=["AB"], out=AB[:, j, kt, :], in_=B.ps[bk][:, :], func=AF.Identity)
    P.dma("sp", reads=[], writes=["ftab"], out=ftab[:], in_=E["FK"].rearrange("(a p) k -> p a k", p=128))
    for kt in range(4):
        bks = []
        for j in range(2):
            bk = B.bank()
            for it_ in range(4):
                P.op("pe", "matmul", reads=["ftab", "kT"], writes=["ps%d" % bk], out=B.ps[bk][:, :], lhsT=ftab[:, it_, 512 * j + 128 * kt:512 * j + 128 * (kt + 1)],
                     rhs=kT[:, it_, :], start=(it_ == 0), stop=(it_ == 3))
            bks.append(bk)
        kcb, ksb = bks
        tq, tqk = B.ring("ctt", [128, 512], F32, 2)
        P.op("dve", "tensor_tensor", reads=["ps%d" % kcb, "AB"], writes=["PP0"], out=PP_[:, 0, kt, :], in0=B.ps[kcb][:, :], in1=AB[:, 0, kt, :], op=ALU.mult)
        P.op("dve", "tensor_tensor", reads=["ps%d" % ksb, "AB"], writes=[tqk], out=tq[:], in0=B.ps[ksb][:, :], in1=AB[:, 1, kt, :], op=ALU.mult)
        P.op("pool", "tensor_tensor", reads=[tqk, "PP0"], writes=["PP0"], out=PP_[:, 0, kt, :], in0=PP_[:, 0, kt, :], in1=tq[:], op=ALU.subtract)
        tq2, tq2k = B.ring("ctt", [128, 512], F32, 2)
        P.op("dve", "tensor_tensor", reads=["ps%d" % ksb, "AB"], writes=["PP1"], out=PP_[:, 1, kt, :], in0=B.ps[ksb][:, :], in1=AB[:, 0, kt, :], op=ALU.mult)
        P.op("dve", "tensor_tensor", reads=["ps%d" % kcb, "AB"], writes=[tq2k], out=tq2[:], in0=B.ps[kcb][:, :], in1=AB[:, 1, kt, :], op=ALU.mult)
        P.op("pool", "tensor_tensor", reads=[tq2k, "PP1"], writes=["PP1"], out=PP_[:, 1, kt, :], in0=PP_[:, 1, kt, :], in1=tq2[:], op=ALU.add)
    P.dma("sp", writes=["ftab"], out=ftab[:, :, 0:512], in_=E["FI"].rearrange("(a p) k -> p a k", p=128))
    for tt in range(2):
        bk = B.bank()
        n_ = 0
        for j in range(2):
            for kt in range(4):
                P.op("pe", "matmul", reads=["ftab", "PP0", "PP1"], writes=["ps%d" % bk], out=B.ps[bk][:, :], lhsT=ftab[:, kt, 256 * j + 128 * tt:256 * j + 128 * (tt + 1)],
                     rhs=PP_[:, j, kt, :], start=(n_ == 0), stop=(n_ == 7))
                n_ += 1
        P.op("act", "activation", reads=["ps%d" % bk], writes=["uT"], out=yT[:, tt, :], in_=B.ps[bk][:, :], func=AF.Identity, scale=1.0 / 512)
    for ci in range(4):
        bk = B.bank()
        for tt in range(2):
            P.op("pe", "transpose", reads=["uT", "ident_f"], writes=["ps%d" % bk], out=B.ps[bk][:, 128 * tt:128 * (tt + 1)],
                 in_=yT[:, tt, 128 * ci:128 * (ci + 1)], identity=idf[:, :])
        P.op("act", "activation", reads=["ps%d" % bk], writes=["ycv_all"], out=ycv[:, ci, :], in_=B.ps[bk][:, 0:CTX], func=AF.Identity)
    B.release(mA)
    for ci in range(4):
        P.op("act", "activation", reads=["hyct"], writes=["hyct"], out=hyct[:, ci, 2:3], in_=hyct[:, ci, 0:1], func=AF.Sqrt)
        P.op("dve", "reciprocal", reads=["hyct"], writes=["hyct"], out=hyct[:, ci, 3:4], in_=hyct[:, ci, 2:3])
        x0s, x0k = B.ring("x0s", [128, NTOK3], F32, 1)
        hy_conv_tile(B, w_in, ci, hT, "hT", segs_all, scw, x0s, x0k)
        yb, ybk = B.ring("yb", [128, NTOK3], F32, 1)
        ut, utk = B.ring("ut", [128, TOK], F32, 1)
        P.dma("sp", reads=["y_my"], writes=[ybk], out=yb[:, 0:TOK], in_=E["y_my"][128 * ci:128 * (ci + 1), :])
        P.dma("sp", reads=["u_src"], writes=[utk], out=ut[:], in_=E["u_src"][ci])
        P.op("dve", "tensor_scalar", reads=[ybk, "hyct"], writes=[ybk], out=yb[:, 0:TOK], in0=yb[:, 0:TOK], scalar1=hyct[:, ci, 3:4],
             scalar2=None, op0=ALU.mult)
        P.op("dve", "scalar_tensor_tensor", reads=[utk, "hyct", ybk], writes=[ybk], out=yb[:, 0:TOK], in0=ut[:], scalar=hyct[:, ci, 1:2],
             in1=yb[:, 0:TOK], op0=ALU.mult, op1=ALU.add)
        P.op("dve", "tensor_scalar", reads=["ycv_all", "cst_all"], writes=[ybk], out=yb[:, TOK:NTOK3], in0=ycv[:, ci, :], scalar1=cst[:, ci, 2:3],
             scalar2=None, op0=ALU.mult)
        P.op("dve", "scalar_tensor_tensor", reads=["uc_all", "hyct", ybk], writes=[ybk], out=yb[:, TOK:NTOK3], in0=uc_all[:, ci, :], scalar=hyct[:, ci, 1:2],
             in1=yb[:, TOK:NTOK3], op0=ALU.mult, op1=ALU.add)
        P.op("pool", "tensor_tensor", reads=[x0k, ybk], writes=[ybk], out=yb[:], in0=yb[:], in1=x0s[:], op=ALU.mult)
        wt, wk = wblock(B, w_in, COL_GB + 128 * ci, 128, "wblk128", 3, 128)
        for (t0, n) in CHUNKS:
            bk = B.bank()
            proj_fm(B, wt, wk, 0, hT, "hT", hcol(t0), n, bk)
            sg, sgk = B.ring("sgb", [128, 512], F32, 2)
            P.op("act", "activation", reads=["ps%d" % bk], writes=[sgk], out=sg[:, 0:n], in_=B.ps[bk][:, 0:n], func=AF.Silu)
            P.op("dve", "tensor_tensor", reads=[sgk, ybk], writes=["gbT"], out=gbT[:, ci, t0:t0 + n], in0=sg[:, 0:n], in1=yb[:, t0:t0 + n],
                 op=ALU.mult)
    B.release(m0)


def f_gmlp(B, E, l):
    P = B.P
    hT, w_in, gcT, idf, idb = E["hT"], E["w_in"][l], E["gcT"], E["idf"], E["idb"]
    m0 = B.mark()
    B.cast_eng = "dve"
    lng, lngk = bcast_row(B, E["gm_g"][l], 512, "lng")
    lnb, lnbk = bcast_row(B, E["gm_b"][l], 512, "lnb")
    wsf = B.sb("wsf", [128, 8, 128], F32)
    P.dma("sp", writes=["wsf"], out=wsf[:], in_=E["gm_ws"][l].rearrange("g p q -> p g q"))
    wsT = B.sb("wsT", [128, 8, 128], BF16)
    for g in range(8):
        bk = B.bank()
        P.op("pe", "transpose", reads=["wsf", "ident_f"], writes=["ps%d" % bk], out=B.ps[bk][:, 0:128], in_=wsf[:, g, :], identity=idf[:, :])
        P.op("act", "activation", reads=["ps%d" % bk], writes=["wsT"], out=wsT[:, g, :], in_=B.ps[bk][:, 0:128], func=AF.Identity)
    bsT = B.sb("bsT", [128, 8], F32)
    P.dma("sp", writes=["bsT"], out=bsT[:], in_=E["gm_bs"][l].rearrange("g p -> p g"), allow_slow_non_contiguous=True)
    wu, wuk = wblock(B, w_in, COL_GM, 512, "wgm_u", 1)
    wv, wvk = wblock(B, w_in, COL_GM + 512, 512, "wgm_v", 1)
    wc, wck = wblock(B, w_in, COL_GC, 512, "wgm_c", 1)
    def gA(t0):
        ba, bb, bc_ = B.bank(), B.bank(), B.bank()
        proj_tm(B, wu, wuk, 0, 512, hT, "hT", hcol(t0), 128, ba)
        proj_tm(B, wv, wvk, 0, 512, hT, "hT", hcol(t0), 128, bb)
        proj_tm(B, wc, wck, 0, 512, hT, "hT", hcol(t0), 128, bc_)
        return (ba, bb, bc_)

    def gB(bks):
        ba, bb, bc_ = bks
        ug, ugk = B.ring("ug", [128, 512], F32, 2)
        vg, vgk = B.ring("vg", [128, 512], F32, 2)
        gs, gsk = B.ring("gs", [128, 512], F32, 2)
        P.op("act", "activation", reads=["ps%d" % ba], writes=[ugk], out=ug[:], in_=B.ps[ba][:, :], func=AF.Gelu)
        P.op("act", "activation", reads=["ps%d" % bb], writes=[vgk], out=vg[:], in_=B.ps[bb][:, :], func=AF.Gelu)
        P.op("act", "activation", reads=["ps%d" % bc_], writes=[gsk], out=gs[:], in_=B.ps[bc_][:, :], func=AF.Silu)
        return (ug, ugk, vg, vgk, gs, gsk)

    def gC(st):
        ug, ugk, vg, vgk, gs, gsk = st
        s6, s6k = B.ring("s6", [128, 6], F32, 2)
        mv, mvk = B.ring("mv", [128, 4], F32, 2)
        P.op("dve", "bn_stats", reads=[vgk], writes=[s6k], out=s6[:], in_=vg[:])
        P.op("dve", "bn_aggr", reads=[s6k], writes=[mvk], out=mv[:, 0:2], in_=s6[:])
        P.op("dve", "tensor_scalar", reads=[mvk], writes=[mvk], out=mv[:, 2:3], in0=mv[:, 1:2], scalar1=EPS, scalar2=None, op0=ALU.add)
        P.op("act", "activation", reads=[mvk], writes=[mvk], out=mv[:, 2:3], in_=mv[:, 2:3], func=AF.Sqrt)
        P.op("dve", "reciprocal", reads=[mvk], writes=[mvk], out=mv[:, 3:4], in_=mv[:, 2:3])
        P.op("dve", "tensor_scalar", reads=[vgk, mvk], writes=[vgk], out=vg[:], in0=vg[:], scalar1=mv[:, 0:1], scalar2=mv[:, 3:4],
             op0=ALU.subtract, op1=ALU.mult)
        P.op("dve", "tensor_tensor", reads=[vgk, lngk], writes=[vgk], out=vg[:], in0=vg[:], in1=lng[:], op=ALU.mult)
        vnb, vnbk = B.ring("vnb", [128, 512], BF16, 2)
        P.op("dve", "tensor_tensor", reads=[vgk, lnbk], writes=[vnbk], out=vnb[:], in0=vg[:], in1=lnb[:], op=ALU.add)
        return (vnb, vnbk)

    def gD(vv):
        vnb, vnbk = vv
        bm = B.bank()
        for g in range(8):
            P.op("pe", "matmul", reads=["wsT", vnbk], writes=["ps%d" % bm], out=B.ps[bm][:, 64 * g:64 * (g + 1)], lhsT=wsT[:, g, :],
                 rhs=vnb[:, 64 * g:64 * (g + 1)], start=(g == 0), stop=(g == 7), skip_group_check=True)
        return bm

    def gE(bm, st, t0):
        ug, ugk, vg, vgk, gs, gsk = st
        for g in range(8):
            P.op("dve", "scalar_tensor_tensor", reads=["ps%d" % bm, "bsT", ugk], writes=[ugk], out=ug[:, 64 * g:64 * (g + 1)],
                 in0=B.ps[bm][:, 64 * g:64 * (g + 1)], scalar=bsT[:, g:g + 1], in1=ug[:, 64 * g:64 * (g + 1)], op0=ALU.add, op1=ALU.mult)
        yc, yck = B.ring("ycb", [128, 512], BF16, 2)
        P.op("dve", "tensor_tensor", reads=[ugk, gsk], writes=[yck], out=yc[:], in0=ug[:], in1=gs[:], op=ALU.mult)
        transpose_to_fm(B, yc, yck, gcT, "gcT", t0, idb)

    for q0 in range(0, len(TILES), 2):
        ts_ = TILES[q0:q0 + 2]
        bk3 = [gA(t0) for t0 in ts_]
        sts = [gB(x_) for x_ in bk3]
        vvs = [gC(x_) for x_ in sts]
        bms = [gD(x_) for x_ in vvs]
        for bm, st, t0 in zip(bms, sts, ts_):
            gE(bm, st, t0)
    B.cast_eng = "pool"
    B.release(m0)


def build_fused():
    B = Builder()
    P = B.P
    E = {}
    for nm, shp in (("xs", [TOK + 2, D]), ("xc0", [CTX, D]), ("c", [D]), ("c_ctx", [D]), ("ada_w", [DEPTH, D, 3 * D]), ("ada_b", [DEPTH, 3 * D]),
                    ("npre", [DEPTH, D]), ("npost", [DEPTH, D]), ("w_in", [DEPTH, D, COL_END]), ("wk_perm", [DEPTH, D, 512]),
                    ("wq_perm", [DEPTH, D, 512]), ("hmask", [2, 1]), ("cosT", [128, TOK]), ("sinT", [128, TOK]),
                    ("hy_w", [DEPTH, 3, 1536]), ("hy_b", [DEPTH, 1536]), ("hw1", [DEPTH, 33, 64]), ("hb1", [DEPTH, 64]),
                    ("hw2", [DEPTH, 64, 64]), ("hb2", [DEPTH, 64]), ("hw3", [DEPTH, 64, 1024]), ("hw3q", [DEPTH, 64, 256]),
                    ("hfreq", [DEPTH, 2, 64]), ("zpos", [33, NFFT]), ("win", [128, NFFT]), ("dft_in", [128, 640]), ("tw_in", [128, 256]),
                    ("zposc", [33, 512]), ("winc", [512, 512]), ("FU", [256, 1024]), ("FK", [512, 1024]), ("FI", [512, 512]), ("hyb", [DEPTH, 128, 4]), ("da_lam", [DEPTH, 256]), ("lam_init", [DEPTH, 1]),
                    ("da_subln", [DEPTH, 128]), ("gm_g", [DEPTH, 512]), ("gm_b", [DEPTH, 512]), ("gm_ws", [DEPTH, 8, 128, 128]),
                    ("gm_bs", [DEPTH, 8, 128]), ("w_br", [DEPTH, 3, 512, D]), ("w_out", [DEPTH, D, D])):
        E[nm] = B.din(nm, shp)
    x_out = B.dout("x_out", [TOK, D])
    nc = B.nc
    def scr(name, shape, dt=F32):
        return nc.dram_tensor(name, list(shape), dt, kind="Internal").ap()
    E["xcur"] = [scr("xcur%d" % i, [TOK + 2, D]) for i in range(2)]
    E["xccur"] = [scr("xccur%d" % i, [CTX, D]) for i in range(2)]
    E["kt_src"] = scr("kt_src", [2, 256, TOK], BF16)
    E["v_src"] = scr("v_src", [2, TOK // 2, 512], BF16)
    E["u_src"] = scr("u_src", [4, 128, TOK])
    E["y_src"] = scr("y_src", [4, 128, TOK])
    E["ssq_src"] = scr("ssq_src", [128, 16])
    E["hal_src"] = scr("hal_src", [2, D])
    E["kt_g"] = [scr("kt_g%d" % i, [2, 4 * 256, TOK], BF16) for i in range(2)]
    E["v_g"] = [scr("v_g%d" % i, [2, 4 * (TOK // 2), 512], BF16) for i in range(2)]
    E["u_g"] = [scr("u_g0", [4, 4 * 128, TOK])] * 2
    E["y_g"] = [scr("y_g0", [4, 4 * 128, TOK])] * 2
    E["ssq_g"] = [scr("ssq_g%d" % i, [512, 16]) for i in range(2)]
    E["hal_g"] = [scr("hal_g0", [8, D])] * 2
    E["kscr"] = scr("kscr", [128, NFFT])
    E["u_my"] = scr("u_my", [4, 128, TOK])
    E["y_my"] = scr("y_my", [512, TOK])
    B.init_psum()
    idf, idb = make_identity(B)
    E["idf"], E["idb"] = idf, idb
    B.ones1 = B.sb("ones1", [1, 128], F32)
    P.op("dve", "memset", writes=["ones1"], ap=B.ones1[:], constant=1.0)
    dft = B.sb("dft", [128, 640], F32)
    tw = B.sb("tw", [128, 256], F32)
    P.dma("sp", writes=["dft"], out=dft[:], in_=E["dft_in"][:, :])
    P.dma("sp", writes=["tw"], out=tw[:], in_=E["tw_in"][:, :])
    E["dft"], E["tw"] = dft, tw
    hT = B.sb("hT", [128, 8, HCOLS], BF16)
    Gbc = B.sb("bc_G", [128, 1024], F32)
    Gcbc = B.sb("bc_Gc", [128, 1024], F32)
    gaT = B.sb("gaT", [128, 4, NTOK3], BF16)
    gbT = B.sb("gbT", [128, 4, NTOK3], BF16)
    gcT = B.sb("gcT", [128, 4, NTOK3], BF16)
    mt = B.sb("hmask", [2, 1], F32)
    P.dma("sp", writes=["hmask"], out=mt[:], in_=E["hmask"][:, :])
    E.update(hT=hT, gaT=gaT, gbT=gbT, gcT=gcT)
    P.op("pool", "memset", writes=["hT"], ap=hT[:, :, TOK + 2:TOK + 3], constant=0.0)
    P.op("pool", "memset", writes=["hT"], ap=hT[:, :, HCOLS - 1:HCOLS], constant=0.0)
    for i_ in range(EXTRA_CC):
        P.cc(reads=["hal_src"], writes=["hal_g"], kind="AllGather", op=ALU.bypass, replica_groups=RG4, ins=[E["hal_src"].opt()],
             outs=[E["hal_g"][0].opt()])
    for l in range(FUSED_DEPTH):
        last = l == FUSED_DEPTH - 1
        par = l % 2
        xsrc = E["xs"] if l == 0 else E["xcur"][par]
        xcsrc = E["xc0"] if l == 0 else E["xccur"][par]
        xkeys = [] if l == 0 else ["xcur%d" % par]
        xckeys = [] if l == 0 else ["xccur%d" % par]
        m0 = B.mark()
        Abc = B.sb("bc_A", [128, 1024], F32); Bbc = B.sb("bc_Bm", [128, 1024], F32)
        Acbc = B.sb("bc_Ac", [128, 1024], F32); Bcbc = B.sb("bc_Bc", [128, 1024], F32)
        m1 = B.mark()
        modulation(B, E["c"], E["ada_w"][l], E["ada_b"][l], E["npre"][l], E["npost"][l], ("A", "Bm", "G"), {"A": Abc, "Bm": Bbc, "G": Gbc})
        B.release(m1)
        modulation(B, E["c_ctx"], E["ada_w"][l], E["ada_b"][l], E["npre"][l], E["npost"][l], ("Ac", "Bc", "Gc"), {"Ac": Acbc, "Bc": Bcbc, "Gc": Gcbc})
        B.release(m1)
        for i in range(NT):
            compute_h_tile(B, [(xsrc[1 + 128 * i: 1 + 128 * (i + 1), :], 0, 128)], 128, Abc, Bbc, "bcA", "bcBm", hT, "hT", 1 + 128 * i, idb, xkeys=xkeys)
        hh = B.sb("hTh", [128, 8, 2], BF16)
        compute_h_tile(B, [(xsrc[0:1, :], 0, 1), (xsrc[TOK + 1:TOK + 2, :], 1, 1)], 2, Abc, Bbc, "bcA", "bcBm", hh, "hTh", 0, idb,
                       mask=(mt, "hmask"), xkeys=xkeys)
        P.op("pool", "tensor_copy", reads=["hTh"], writes=["hT"], out=hT[:, :, 0:1], in_=hh[:, :, 0:1])
        P.op("pool", "tensor_copy", reads=["hTh"], writes=["hT"], out=hT[:, :, TOK + 1:TOK + 2], in_=hh[:, :, 1:2])
        for i in range(2):
            compute_h_tile(B, [(xcsrc[128 * i:128 * (i + 1), :], 0, 128)], 128, Acbc, Bcbc, "bcAc", "bcBc", hT, "hT", TOK + 3 + 128 * i, idb, xkeys=xckeys)
        B.release(m0)
        f_products(B, E, l)
        f_gmlp(B, E, l)
        issue_y = f_conv(B, E, l)
        L = dict(after_q=issue_y, hT=hT, idb=idb, gaT=gaT, gbT=gbT, gcT=gcT, w_in=E["w_in"][l], wq_perm=E["wq_perm"][l], cosT=E["cosT"], sinT=E["sinT"],
                 kt_all=[E["kt_g"][par][h // 2].rearrange("(r h d) t -> h d r t", r=4, h=2)[h % 2] for h in range(4)], v_all=None,
                 v_gk=E["v_g"][par],
                 kth_out=(lambda k: k.rearrange("p (r t) -> p r t", r=4)), kt_keys=["kt_g%d" % par], v_keys=["v_g%d" % par],
                 da_lam=E["da_lam"][l], lam_init=E["lam_init"][l], da_subln=E["da_subln"][l],
                 w_br=E["w_br"][l], w_out=E["w_out"][l], xs=xsrc, xc=xcsrc, x_keys=xkeys + xckeys, Gbc=Gbc, Gcbc=Gcbc)
        f_attn(B, L)
        f_hyena_gate(B, E, l)
        if last:
            L.update(x_new=x_out, x_new_key="x_out", xc_new=None, hal_src=None)
        else:
            L.update(x_new=E["xcur"][1 - par][1:TOK + 1, :], x_new_key="xcur%d" % (1 - par), xc_new=E["xccur"][1 - par],
                     xc_new_key="xccur%d" % (1 - par), hal_src=E["hal_src"])
        build_s3_merge(B, L)
        if not last:
            P.cc(reads=["hal_src"], writes=["hal_g"], kind="AllGather", op=ALU.bypass, replica_groups=RG4, ins=[E["hal_src"].opt()],
                 outs=[E["hal_g"][par].opt()])
            hg = E["hal_g"][par]
            xn = E["xcur"][1 - par]
            P.dma("sp", reads=["hal_g"], writes=["xcur%d" % (1 - par)], out=xn[0:1, :],
                  in_=(lambda e, hg=hg: hg.rearrange("(r two) d -> r two d", two=2)[rank_nb(e, 3)][1:2, :]))
            P.dma("sp", reads=["hal_g"], writes=["xcur%d" % (1 - par)], out=xn[TOK + 1:TOK + 2, :],
                  in_=(lambda e, hg=hg: hg.rearrange("(r two) d -> r two d", two=2)[rank_nb(e, 1)][0:1, :]))
    return B.finish()


def ctx_dft_tables():
    t = np.arange(256, dtype=np.float64)[:, None]
    k = np.arange(512, dtype=np.float64)[None, :]
    th = 2.0 * math.pi * t * k / 512.0
    FU = np.concatenate([np.cos(th), np.sin(th)], axis=1).astype(np.float32)
    i = np.arange(512, dtype=np.float64)[:, None]
    ph = 2.0 * math.pi * (i - 255.0) * k / 512.0
    FK = np.concatenate([np.cos(ph), np.sin(ph)], axis=1).astype(np.float32)
    kk = np.arange(512, dtype=np.float64)[:, None]
    tt = np.arange(256, dtype=np.float64)[None, :]
    ti = 2.0 * math.pi * kk * tt / 512.0
    FI = np.concatenate([np.cos(ti), np.sin(ti)], axis=1).astype(np.float32)
    return FU, FK, FI


def fused_inputs(inp):
    cosT, sinT = rope_tables()
    FU, FK, FI = ctx_dft_tables()
    zpos, win, dft, tw = hyena_tables()
    zposc, winc = ctx_tables()
    w_in = inp["w_in"]
    wkp = np.stack([perm_cols(w_in[l][:, COL_K:COL_K + 512]) for l in range(DEPTH)], 0)
    wqp = np.stack([perm_cols(w_in[l][:, COL_Q:COL_Q + 512]) for l in range(DEPTH)], 0)
    lam_init = np.array([[0.8 - 0.6 * math.exp(-0.3 * l)] for l in range(DEPTH)], np.float32)
    hyb = np.ascontiguousarray(inp["hy_bias"].reshape(DEPTH, 4, 128).transpose(0, 2, 1))
    shared = {
        "c_ctx": inp["c_ctx"], "ada_w": inp["ada_w"], "ada_b": inp["ada_b"], "npre": inp["norm_pre"], "npost": inp["norm_post"],
        "w_in": w_in, "wk_perm": wkp, "wq_perm": wqp, "hy_w": inp["hy_short_w"], "hy_b": inp["hy_short_b"],
        "hw1": inp["hy_f_w1"], "hb1": inp["hy_f_b1"], "hw2": inp["hy_f_w2"], "hb2": inp["hy_f_b2"], "hw3": inp["hy_f_w3"],
        "hfreq": inp["hy_f_freq"], "zpos": zpos, "dft_in": dft, "tw_in": tw, "zposc": zposc, "winc": winc, "hyb": hyb, "FU": FU, "FK": FK, "FI": FI,
        "da_lam": np.ascontiguousarray(inp["da_lambda"].reshape(DEPTH, 256)), "lam_init": lam_init, "da_subln": inp["da_subln"],
        "gm_g": inp["gm_ln_g"], "gm_b": inp["gm_ln_b"], "gm_ws": inp["gm_ws"], "gm_bs": inp["gm_bs"], "w_br": inp["w_branch"],
        "w_out": inp["w_out"],
    }
    x = inp["x"]
    in_maps = []
    for core in range(8):
        b, r = divmod(core, 4)
        t0 = r * TOK
        xs = np.zeros((TOK + 2, D), np.float32)
        lo, hi = max(t0 - 1, 0), min(t0 + TOK + 1, SEQ)
        xs[lo - (t0 - 1): hi - (t0 - 1)] = x[b, lo:hi]
        hmask = np.array([[0.0 if t0 == 0 else 1.0], [0.0 if t0 + TOK == SEQ else 1.0]], np.float32)
        hw3q = np.ascontiguousarray(np.concatenate([inp["hy_f_w3"][:, :, 128 * r:128 * (r + 1)],
                                                    inp["hy_f_w3"][:, :, 512 + 128 * r:512 + 128 * (r + 1)]], axis=2))
        m = dict(shared)
        m.update({"xs": xs, "xc0": np.ascontiguousarray(inp["ctx"][b]), "c": inp["c"][b], "hmask": hmask,
                  "cosT": np.ascontiguousarray(cosT[:, t0:t0 + TOK]), "sinT": np.ascontiguousarray(sinT[:, t0:t0 + TOK]),
                  "hw3q": hw3q, "win": np.ascontiguousarray(win[128 * r:128 * (r + 1)])})
        in_maps.append(m)
    return in_maps


def kernel(**inputs):
    inp = {k: np.ascontiguousarray(np.asarray(v), dtype=np.float32) for k, v in inputs.items()}
    nc = get_nc("fused")
    in_maps = fused_inputs(inp)
    res = run_bass_kernel_spmd(nc, in_maps, core_ids=list(range(8))).results
    out = np.stack([np.concatenate([np.asarray(res[4 * b + r]["x_out"]) for r in range(4)], 0) for b in range(2)], 0)
    return out.astype(np.float32)


def f_attn(B, L):
    P = B.P
    hT, gaT = L["hT"], L["gaT"]
    w_in, wq_perm, cosT, sinT, kt_all = L["w_in"], L["wq_perm"], L["cosT"], L["sinT"], L["kt_all"]
    m0 = B.mark()
    lamb, lambk = bcast_row(B, L["da_lam"], 256, "lam")
    lib, libk = bcast_row(B, L["lam_init"], 1, "li")
    lt = B.sb("lamtmp", [128, 140], F32)
    P.op("dve", "tensor_tensor", reads=[lambk], writes=["lamtmp"], out=lt[:, 0:64], in0=lamb[:, 0:64], in1=lamb[:, 64:128], op=ALU.mult)
    P.op("dve", "tensor_tensor", reads=[lambk], writes=["lamtmp"], out=lt[:, 64:128], in0=lamb[:, 128:192], in1=lamb[:, 192:256], op=ALU.mult)
    P.op("dve", "tensor_reduce", reads=["lamtmp"], writes=["lamtmp"], out=lt[:, 128:129], in_=lt[:, 0:64], axis=AX.X, op=ALU.add)
    P.op("dve", "tensor_reduce", reads=["lamtmp"], writes=["lamtmp"], out=lt[:, 129:130], in_=lt[:, 64:128], axis=AX.X, op=ALU.add)
    P.op("act", "activation", reads=["lamtmp"], writes=["lamtmp"], out=lt[:, 130:132], in_=lt[:, 128:130], func=AF.Exp)
    P.op("dve", "tensor_tensor", reads=["lamtmp"], writes=["lamtmp"], out=lt[:, 132:133], in0=lt[:, 131:132], in1=lt[:, 130:131], op=ALU.subtract)
    P.op("dve", "tensor_tensor", reads=["lamtmp", libk], writes=["lamtmp"], out=lt[:, 133:134], in0=lt[:, 132:133], in1=lib[:, 0:1], op=ALU.subtract)
    neglam = lt[:, 133:134]
    P.op("dve", "tensor_scalar", reads=[libk], writes=["lamtmp"], out=lt[:, 134:135], in0=lib[:, 0:1], scalar1=-1.0, scalar2=1.0,
         op0=ALU.mult, op1=ALU.add)
    P.dma("sp", writes=["lamtmp"], out=lt[:, 135:136], in_=L["da_subln"].rearrange("(p o) -> p o", o=1))
    P.op("dve", "tensor_tensor", reads=["lamtmp"], writes=["lamtmp"], out=lt[:, 136:137], in0=lt[:, 135:136], in1=lt[:, 134:135], op=ALU.mult)
    gcol = lt[:, 136:137]
    ones_b = B.sb("ones_b", [128, 128], BF16)
    ones_f = B.sb("ones_f", [128, 128], F32)
    P.op("pool", "memset", writes=["ones_b"], ap=ones_b[:], constant=1.0)
    P.op("pool", "memset", writes=["ones_f"], ap=ones_f[:], constant=1.0)
    QT = B.sb("QT", [128, 4, NTOK3], BF16)
    kcT = B.sb("kcT", [128, 4, CTX], BF16)
    vcx = B.sb("vcx", [128, 2, 4, 128], BF16)
    m1 = B.mark()
    cs = B.sb("cosT", [128, TOK], F32)
    sn = B.sb("sinT", [128, TOK], F32)
    P.dma("sp", writes=["cosT"], out=cs[:], in_=cosT[:, :])
    P.dma("sp", writes=["sinT"], out=sn[:], in_=sinT[:, :])
    wt, wk = wblock(B, w_in, COL_Q, 512, "wq")
    wp, wpk = wblock(B, wq_perm, 0, 512, "wq")
    for h in range(4):
        for j in range(TOK // 512):
            b1 = B.bank()
            proj_fm(B, wt, wk, 128 * h, hT, "hT", 1 + 512 * j, 512, b1)
            b2 = B.bank()
            proj_fm(B, wp, wpk, 128 * h, hT, "hT", 1 + 512 * j, 512, b2)
            t1, t1k = B.ring("rtmp1", [128, 512], F32, 2)
            t2, t2k = B.ring("rtmp2", [128, 512], F32, 2)
            P.op("dve", "tensor_tensor", reads=["ps%d" % b1, "cosT"], writes=[t1k], out=t1[:], in0=B.ps[b1][:, :],
                 in1=cs[:, 512 * j:512 * (j + 1)], op=ALU.mult)
            P.op("dve", "tensor_tensor", reads=["ps%d" % b2, "sinT"], writes=[t2k], out=t2[:], in0=B.ps[b2][:, :],
                 in1=sn[:, 512 * j:512 * (j + 1)], op=ALU.mult)
            P.op("pool", "tensor_tensor", reads=[t1k, t2k], writes=["QT"], out=QT[:, h, 512 * j:512 * (j + 1)], in0=t1[:], in1=t2[:], op=ALU.add)
        b1 = B.bank()
        proj_fm(B, wt, wk, 128 * h, hT, "hT", hcol(TOK), CTX, b1)
        P.op("act", "activation", reads=["ps%d" % b1], writes=["QT"], out=QT[:, h, TOK:NTOK3], in_=B.ps[b1][:, 0:CTX], func=AF.Identity)
    wt, wk = wblock(B, w_in, COL_K, 512, "wq")
    for h in range(4):
        b1 = B.bank()
        proj_fm(B, wt, wk, 128 * h, hT, "hT", hcol(TOK), CTX, b1)
        P.op("act", "activation", reads=["ps%d" % b1], writes=["kcT"], out=kcT[:, h, :], in_=B.ps[b1][:, 0:CTX], func=AF.Identity)
    wt, wk = wblock(B, w_in, COL_V, 512, "wq")
    for i in range(2):
        b1 = B.bank()
        proj_tm(B, wt, wk, 0, 512, hT, "hT", hcol(TOK + 128 * i), 128, b1)
        P.op("act", "activation", reads=["ps%d" % b1], writes=["vcx"], out=vcx[:, i, :, :],
             in_=B.ps[b1][:, :].rearrange("p (h d) -> p h d", d=128), func=AF.Identity)
    B.release(m1)
    kth = B.sb("kth", [128, SEQ], BF16)
    vh = B.sb("vh", [128, SEQ // 128, 128], BF16)
    wga, wgak = wblock(B, w_in, COL_GA, 512, "wga", 1)
    if L.get("after_q") is not None:
        L["after_q"]()
    ACC, RS = 0, 1
    pcnt = 0
    for h in range(4):
        P.dma("sp", reads=L.get("kt_keys", []), writes=["kth"], out=kth[:].rearrange("p (r t) -> p r t", r=4), in_=kt_all[h])
        for k in range(2):
            for r in range(4):
                P.dma("sp", reads=L.get("v_keys", []), writes=["vh"], out=vh[:, 16 * r + 8 * k:16 * r + 8 * k + 8, :],
                      in_=L["v_gk"][k].rearrange("(r t p) c -> r p t c", r=4, p=128)[r][:, :, 128 * h:128 * (h + 1)])
        for (t0, n) in CHUNKS:
            latent = t0 < TOK
            keys = ([("l", k) for k in range(SEQ // 128)] if latent else []) + [("c", 0), ("c", 1)]
            om, omk = B.ring("om", [128, 2, 512], F32, 1)
            iters = [(m, ki) for m in range(2) for ki in range(0, len(keys), 2)]
            state = {}

            def emit_S(it):
                nonlocal pcnt
                m, ki = it
                p = 1 + (pcnt % 3)
                pcnt += 1
                vts = []
                for i, (kind, kt) in enumerate(keys[ki:ki + 2]):
                    if kind == "l":
                        lk, lkk = kth[64 * m:64 * (m + 1), 128 * kt:128 * (kt + 1)], "kth"
                        vts.append((vh[:, kt, :], "vh"))
                    else:
                        lk, lkk = kcT[64 * m:64 * (m + 1), h, 128 * kt:128 * (kt + 1)], "kcT"
                        vts.append((vcx[:, kt, h, :], "vcx"))
                    P.op("pe", "matmul", reads=[lkk, "QT"], writes=["ps%d" % (2 * p + i)], out=B.ps[2 * p + i][:, 0:n], lhsT=lk,
                         rhs=QT[64 * m:64 * (m + 1), h, t0:t0 + n], start=True, stop=True)
                state[it] = (p, vts)

            def emit_PV(it):
                m, ki = it
                p, vts = state.pop(it)
                pt, ptk = B.ring("pt", [128, 2, 512], BF16, 3)
                P.op("act", "activation", reads=["ps%d" % (2 * p), "ps%d" % (2 * p + 1)], writes=[ptk], out=pt[:, :, 0:n],
                     in_=B.pp[p][:, :].rearrange("p (b c) -> p b c", b=2)[:, :, 0:n], func=AF.Exp, scale=0.125)
                for i, (vt, vtk) in enumerate(vts):
                    first = (ki == 0 and i == 0)
                    lastk = (ki + i == len(keys) - 1)
                    P.op("pe", "matmul", reads=[ptk, vtk], writes=["ps%d" % ACC], out=B.ps[ACC][:, 0:n], lhsT=vt, rhs=pt[:, i, 0:n],
                         start=first, stop=lastk)
                    P.op("pe", "matmul", reads=[ptk, "ones_b"], writes=["ps%d" % RS], out=B.ps[RS][:, 0:n], lhsT=ones_b[:, :], rhs=pt[:, i, 0:n],
                         start=first, stop=lastk)
                if ki + 2 >= len(keys):
                    rr, rrk = B.ring("rr", [128, 512], F32, 1)
                    P.op("dve", "reciprocal", reads=["ps%d" % RS], writes=[rrk], out=rr[:, 0:n], in_=B.ps[RS][:, 0:n])
                    if m == 1:
                        P.op("dve", "tensor_scalar", reads=[rrk, "lamtmp"], writes=[rrk], out=rr[:, 0:n], in0=rr[:, 0:n], scalar1=neglam, scalar2=None,
                             op0=ALU.mult)
                    P.op("dve", "tensor_tensor", reads=["ps%d" % ACC, rrk], writes=[omk + str(m)], out=om[:, m, 0:n], in0=B.ps[ACC][:, 0:n], in1=rr[:, 0:n],
                         op=ALU.mult)

            emit_S(iters[0])
            for idx, it in enumerate(iters):
                if idx + 1 < len(iters):
                    emit_S(iters[idx + 1])
                emit_PV(it)
            P.op("pool", "tensor_tensor", reads=[omk + "0", omk + "1"], writes=[omk + "0"], out=om[:, 0, 0:n], in0=om[:, 0, 0:n], in1=om[:, 1, 0:n], op=ALU.add)
            sq, sqk = B.ring("asq", [128, 512], F32, 1)
            P.op("act", "activation", reads=[omk + "0"], writes=[sqk], out=sq[:, 0:n], in_=om[:, 0, 0:n], func=AF.Square)
            bq = B.bank()
            while bq in (ACC, RS):
                bq = B.bank()
            P.op("pe", "matmul", reads=[sqk, "ones_f"], writes=["ps%d" % bq], out=B.ps[bq][:, 0:n], lhsT=ones_f[:, :], rhs=sq[:, 0:n], start=True, stop=True)
            P.op("dve", "tensor_scalar", reads=["ps%d" % bq], writes=[sqk], out=sq[:, 0:n], in0=B.ps[bq][:, 0:n], scalar1=1.0 / 128, scalar2=EPS,
                 op0=ALU.mult, op1=ALU.add)
            P.op("act", "activation", reads=[sqk], writes=[sqk], out=sq[:, 0:n], in_=sq[:, 0:n], func=AF.Sqrt)
            P.op("dve", "reciprocal", reads=[sqk], writes=[sqk], out=sq[:, 0:n], in_=sq[:, 0:n])
            P.op("dve", "scalar_tensor_tensor", reads=[omk + "0", "lamtmp", sqk], writes=[sqk], out=sq[:, 0:n], in0=om[:, 0, 0:n], scalar=gcol,
                 in1=sq[:, 0:n], op0=ALU.mult, op1=ALU.mult)
            bg = B.bank()
            while bg in (ACC, RS):
                bg = B.bank()
            proj_fm(B, wga, wgak, 128 * h, hT, "hT", hcol(t0), n, bg)
            sg, sgk = B.ring("sga", [128, 512], F32, 1)
            P.op("act", "activation", reads=["ps%d" % bg], writes=[sgk], out=sg[:, 0:n], in_=B.ps[bg][:, 0:n], func=AF.Silu)
            P.op("dve", "tensor_tensor", reads=[sgk, sqk], writes=["gaT"], out=gaT[:, h, t0:t0 + n], in0=sg[:, 0:n], in1=sq[:, 0:n], op=ALU.mult)
    B.release(m0)
```

```python
import contextlib
import math
import numpy as np
import ml_dtypes
import concourse.bass as bass
import concourse.mybir as mybir
from concourse.bass_utils import run_bass_kernel_spmd

F32 = mybir.dt.float32
BF16 = mybir.dt.bfloat16
AF = mybir.ActivationFunctionType
ALU = mybir.AluOpType
AX = mybir.AxisListType

D = 1024
SEQ = 8192
NB = 2
DEPTH = 4
CTX = 256
TOK = 2048
NT = TOK // 128
EPS = 1e-6
COL_K, COL_V, COL_Q, COL_GA, COL_HY, COL_GB, COL_GM, COL_GC, COL_MG, COL_END = (
    0, 512, 1024, 1536, 2048, 3584, 4096, 5120, 5632, 8704)
PI = math.pi


class Prog:
    ENG = ("pe", "act", "dve", "pool", "sp")

    def __init__(self, nc):
        self.nc = nc
        self.ops = {e: [] for e in self.ENG}
        self.cnt = {e: 0 for e in self.ENG}
        self.semidx = {e: 0 for e in self.ENG}
        self.known = {e: {} for e in self.ENG}
        self.lastw = {}
        self.reads = {}
        self.ndma = 0
        self.dma_uses = {}
        self.NDMASEM = 32
        self.semnames = set()

    def _need(self, eng, reads, writes):
        toks = []
        for b in reads:
            t = self.lastw.get(b)
            if t is not None:
                toks.append(t)
        for b in writes:
            t = self.lastw.get(b)
            if t is not None:
                toks.append(t)
            toks.extend(self.reads.get(b, ()))
        need = {}
        for (s, v, e) in toks:
            if e == "pe" and eng == "pe":
                continue
            if v > need.get(s, 0):
                need[s] = v
        kn = self.known[eng]
        out = []
        for s, v in need.items():
            if kn.get(s, 0) >= v:
                continue
            kn[s] = v
            out.append((s, v))
        return out

    def _commit(self, tok, reads, writes):
        for b in reads:
            lst = self.reads.setdefault(b, [])
            lst.append(tok)
            if len(lst) > 64:
                mx = {}
                for (s, v, e) in lst:
                    if v > mx.get(s, (0, None))[0]:
                        mx[s] = (v, e)
                self.reads[b] = [(s, v, e) for s, (v, e) in mx.items()]
        for b in writes:
            self.lastw[b] = tok
            self.reads[b] = []

    def op(self, eng, fname, reads=(), writes=(), **kw):
        waits = self._need(eng, reads, writes)
        self.cnt[eng] += 1
        if self.cnt[eng] > 30000:
            self.semidx[eng] += 1
            self.cnt[eng] = 1
        s = "c_%s%d" % (eng, self.semidx[eng])
        self.semnames.add(s)
        tok = (s, self.cnt[eng], eng)
        self.ops[eng].append((waits, fname, kw, (s, 1)))
        self._commit(tok, reads, writes)

    def dma(self, eng, reads=(), writes=(), _fname="dma_start", **kw):
        j = self.ndma % self.NDMASEM
        self.ndma += 1
        s = "d_%d" % j
        self.semnames.add(s)
        uses = self.dma_uses.get(s, 0)
        waits = self._need(eng, reads, writes)
        if uses > 0 and self.known[eng].get(s, 0) < 16 * uses:
            self.known[eng][s] = 16 * uses
            waits.append((s, 16 * uses))
        self.dma_uses[s] = uses + 1
        tok = (s, 16 * (uses + 1), "dma")
        self.ops[eng].append((waits, _fname, kw, (s, 16)))
        self._commit(tok, reads, writes)

    def cc(self, reads=(), writes=(), **kw):
        waits = self._need("pool", reads, writes)
        self.ncc = getattr(self, "ncc", 0) + 1
        s = "ccsem%d" % self.ncc
        self.semnames.add(s)
        tok = (s, 1, "cc")
        self.ops["pool"].append((waits, "collective_compute", kw, (s, 1)))
        self._commit(tok, reads, writes)

    def barrier(self):
        latest = []
        for e in self.ENG:
            if self.cnt[e] > 0:
                latest.append(("c_%s%d" % (e, self.semidx[e]), self.cnt[e]))
        for s, uses in self.dma_uses.items():
            latest.append((s, 16 * uses))

        for e in self.ENG:
            kn = self.known[e]
            waits = []
            for (s, v) in latest:
                if kn.get(s, 0) < v:
                    kn[s] = v
                    waits.append((s, v))
            if waits:
                self.ops[e].append((waits, None, None, None))

    def wait_all(self, eng, bufs):
        waits = self._need(eng, bufs, ())
        self.ops[eng].append((waits, None, None, None))

    def run(self):
        nc = self.nc
        with contextlib.ExitStack() as st:
            sems = {n: st.enter_context(nc.semaphore(n)) for n in sorted(self.semnames)}
            block = st.enter_context(nc.Block())

            def replay(name):
                def f(e):
                    for waits, fname, kw, inc in self.ops[name]:
                        for (s, v) in waits:
                            e.wait_ge(sems[s], v)
                        if fname is not None:
                            kw = {k_: (v_(e) if callable(v_) else v_) for k_, v_ in kw.items()}
                            try:
                                ins = getattr(e, fname)(**kw)
                            except Exception:
                                print("FAILED OP", name, fname, {k_: str(v_)[:300] for k_, v_ in kw.items()})
                                raise
                            ins.then_inc(sems[inc[0]], inc[1])
                return f
            block.tensor(replay("pe"))
            block.scalar(replay("act"))
            block.vector(replay("dve"))
            block.gpsimd(replay("pool"))
            block.sync(replay("sp"))


class Builder:
    def __init__(self):
        self.nc = bass.Bass("TRN2", target_bir_lowering=False)
        self.P = Prog(self.nc)
        self.st = contextlib.ExitStack()
        self.outs = []
        self.nbank = 0
        self.rings = {}

    def din(self, name, shape, dt=F32):
        return self.nc.dram_tensor(name, list(shape), dt, kind="ExternalInput").ap()

    def dout(self, name, shape, dt=F32):
        self.outs.append(name)
        return self.nc.dram_tensor(name, list(shape), dt, kind="ExternalOutput").ap()

    def dscratch(self, name, shape, dt=F32):
        return self.nc.dram_tensor(name, list(shape), dt, kind="Internal").ap()

    ARENA_WORDS = 51 * 1024

    def sb(self, name, shape, dt=F32):
        if not hasattr(self, "arena"):
            self.arena = self.st.enter_context(self.nc.sbuf_tensor("arena", [128, self.ARENA_WORDS], F32))
            self.top = 0
        nel = 1
        for d_ in shape[1:]:
            nel *= d_
        esz = 4 if dt == F32 else 2
        words = (nel * esz + 3) // 4
        assert self.top + words <= self.ARENA_WORDS, "arena overflow at %s: top=%d need=%d" % (name, self.top, words)
        v = self.arena[:, self.top:self.top + words]
        self.top += words
        if dt != F32:
            v = v.bitcast(dt)
        v = v[:, 0:nel]
        if len(shape) == 3:
            v = v.rearrange("p (a b) -> p a b", b=shape[2])
        elif len(shape) == 4:
            v = v.rearrange("p (a b c) -> p a b c", b=shape[2], c=shape[3])
        if shape[0] < 128:
            v = v[0:shape[0]]
        return v

    def mark(self):
        return (self.top, set(self.rings.keys()))

    def release(self, m):
        self.P.barrier()
        self.top = m[0]
        for k in list(self.rings.keys()):
            if k not in m[1]:
                del self.rings[k]

    def init_psum(self):
        self.pp = [self.st.enter_context(self.nc.psum_tensor("psp%d" % i, [128, 1024], F32)) for i in range(4)]
        self.ps = [self.pp[i // 2][:, 512 * (i % 2):512 * (i % 2 + 1)] for i in range(8)]

    def bank(self):
        i = self.nbank % 8
        self.nbank += 1
        return i

    def ring(self, name, shape, dt, n):
        if name not in self.rings:
            self.rings[name] = [[self.sb("%s_%d" % (name, i), shape, dt) for i in range(n)], 0]
        r = self.rings[name]
        i = r[1] % n
        r[1] += 1
        return r[0][i], "%s_%d" % (name, i)

    def finish(self):
        self.P.wait_all("sp", self.outs)
        self.P.run()
        self.st.close()
        return self.nc


def make_identity(B, dt=BF16):
    P = B.P
    idf = B.sb("ident_f", [128, 128], F32)
    P.op("pool", "memset", writes=["ident_f"], ap=idf[:], constant=0.0)
    P.op("pool", "affine_select", reads=["ident_f"], writes=["ident_f"], out=idf[:], in_=idf[:],
         compare_op=ALU.not_equal, fill=1.0, base=0, pattern=[[-1, 128]], channel_multiplier=1)
    idb = B.sb("ident_b", [128, 128], BF16)
    P.op("pool", "tensor_copy", reads=["ident_f"], writes=["ident_b"], out=idb[:], in_=idf[:])
    return idf, idb


def modulation(B, c_ap, ada_w, ada_b, npre, npost, names, dest):
    P = B.P
    nA, nB, nG = names
    cT = B.sb("cT_" + nA, [128, 8], F32)
    P.dma("sp", writes=["cT" + nA], out=cT[:], in_=c_ap.rearrange("(p k) -> p k", k=8))
    P.op("act", "activation", reads=["cT" + nA], writes=["cT" + nA], out=cT[:], in_=cT[:], func=AF.Silu)
    rows = B.sb("rows_" + nA, [1, 3072], F32)
    bro = B.sb("brow_" + nA, [1, 3072], F32)
    P.dma("sp", writes=["brow" + nA], out=bro[:], in_=ada_b.rearrange("(o c) -> o c", o=1))
    wv = ada_w.rearrange("(p k) c -> p k c", k=8)
    for ch in range(6):
        wt, wk = B.ring("adaw", [128, 8, 512], F32, 2)
        P.dma("sp", writes=[wk], out=wt[:], in_=wv[:, :, ch * 512:(ch + 1) * 512])
        bk = B.bank()
        for k in range(8):
            P.op("pe", "matmul", reads=[wk, "cT" + nA], writes=["ps%d" % bk], out=B.ps[bk][0:1, :], lhsT=cT[:, k:k + 1],
                 rhs=wt[:, k, :], start=(k == 0), stop=(k == 7))
        P.op("dve", "tensor_tensor", reads=["ps%d" % bk, "brow" + nA], writes=["rows" + nA],
             out=rows[:, ch * 512:(ch + 1) * 512], in0=B.ps[bk][0:1, :], in1=bro[:, ch * 512:(ch + 1) * 512], op=ALU.add)
    gp = B.sb("gp_" + nA, [1, 2048], F32)
    P.dma("sp", writes=["gp" + nA], out=gp[:, 0:1024], in_=npre.rearrange("(o c) -> o c", o=1))
    P.dma("sp", writes=["gp" + nA], out=gp[:, 1024:2048], in_=npost.rearrange("(o c) -> o c", o=1))
    P.op("dve", "scalar_tensor_tensor", reads=["rows" + nA, "gp" + nA], writes=["rows" + nA], out=rows[:, 1024:2048],
         in0=rows[:, 1024:2048], scalar=1.0, in1=gp[:, 0:1024], op0=ALU.add, op1=ALU.mult)
    P.op("dve", "tensor_tensor", reads=["rows" + nA, "gp" + nA], writes=["rows" + nA], out=rows[:, 2048:3072],
         in0=rows[:, 2048:3072], in1=gp[:, 1024:2048], op=ALU.mult)
    ones = B.sb("ones_" + nA, [1, 128], F32)
    P.op("dve", "memset", writes=["ones" + nA], ap=ones[:], constant=1.0)
    tiles = {}
    for nm, off in ((nB, 0), (nA, 1024), (nG, 2048)):
        if nm is None:
            continue
        t = dest[nm]
        for hh in range(2):
            bk = B.bank()
            P.op("pe", "matmul", reads=["ones" + nA, "rows" + nA], writes=["ps%d" % bk], out=B.ps[bk][:, :], lhsT=ones[:, :],
                 rhs=rows[:, off + hh * 512: off + (hh + 1) * 512], start=True, stop=True)
            P.op("act", "activation", reads=["ps%d" % bk], writes=["bc" + nm], out=t[:, hh * 512:(hh + 1) * 512],
                 in_=B.ps[bk][:, :], func=AF.Identity)
        tiles[nm] = t
    return tiles


def compute_h_tile(B, x_rows_aps, n, Abc, Bbc, keyA, keyB, hT, hkey, col0, idb, mask=None, keep_x=None, xkeys=()):
    P = B.P
    if keep_x is None:
        xt, xk = B.ring("xt", [128, 1024], F32, 3)
    else:
        xt, xk = keep_x
    for ap, r0, r in x_rows_aps:
        P.dma("sp", reads=list(xkeys), writes=[xk], out=xt[r0:r0 + r, :], in_=ap)
    junk, jk = B.ring("hjunk", [128, 1024], BF16, 2)
    st_, sk = B.ring("hstat", [128, 4], F32, 4)
    P.op("act", "activation", reads=[xk], writes=[jk, sk], out=junk[0:n, :], in_=xt[0:n, :], func=AF.Square,
         accum_out=st_[0:n, 0:1])
    P.op("dve", "tensor_scalar", reads=[sk], writes=[sk], out=st_[0:n, 1:2], in0=st_[0:n, 0:1], scalar1=1.0 / D,
         scalar2=EPS, op0=ALU.mult, op1=ALU.add)
    P.op("act", "activation", reads=[sk], writes=[sk], out=st_[0:n, 2:3], in_=st_[0:n, 1:2], func=AF.Sqrt)
    P.op("dve", "reciprocal", reads=[sk], writes=[sk], out=st_[0:n, 3:4], in_=st_[0:n, 2:3])
    hm, hk = B.ring("hm", [128, 1024], F32, 2)
    P.op("dve", "scalar_tensor_tensor", reads=[xk, sk, keyA], writes=[hk], out=hm[0:n, :], in0=xt[0:n, :],
         scalar=st_[0:n, 3:4], in1=Abc[0:n, :], op0=ALU.mult, op1=ALU.mult)
    hb, hbk = B.ring("hb", [128, 1024], BF16, 2)
    P.op("pool", "tensor_tensor", reads=[hk, keyB], writes=[hbk], out=hb[0:n, :], in0=hm[0:n, :], in1=Bbc[0:n, :], op=ALU.add)
    if mask is not None:
        mt, mk = mask
        P.op("pool", "tensor_scalar", reads=[hbk, mk], writes=[hbk], out=hb[0:n, :], in0=hb[0:n, :], scalar1=mt[0:n, 0:1],
             scalar2=None, op0=ALU.mult)
    bk = B.bank()
    psb = B.ps[bk][:, :].bitcast(BF16)
    for k in range(8):
        P.op("pe", "transpose", reads=[hbk, "ident_b"], writes=["ps%d" % bk], out=psb[:, k * 128:k * 128 + n],
             in_=hb[0:n, k * 128:(k + 1) * 128], identity=idb[0:n, 0:n])
    P.op("act", "activation", reads=["ps%d" % bk], writes=[hkey], out=hT[:, :, col0:col0 + n],
         in_=psb.rearrange("p (k t) -> p k t", t=128)[:, :, 0:n], func=AF.Identity)
    return st_, sk


def load_cast(B, dst, dstk, src, shape3):
    a, b = shape3
    st, sk = B.ring("wstage", [128, 1024], F32, 2)
    sv = st[:, 0:a * b].rearrange("p (a b) -> p a b", b=b)
    B.P.dma("sp", writes=[sk], out=sv, in_=src)
    eng = getattr(B, "cast_eng", "pool")
    if eng == "act":
        B.P.op("act", "activation", reads=[sk], writes=[dstk], out=dst, in_=sv, func=AF.Identity)
    else:
        B.P.op(eng, "tensor_copy", reads=[sk], writes=[dstk], out=dst, in_=sv)


def wblock(B, w_ap, col0, ncols, ringname="wblk", nbuf=2, width=512):
    wt, wk = B.ring(ringname, [128, 8, width], BF16, nbuf)
    wv = w_ap.rearrange("(k p) c -> p k c", p=128)
    for c in range(0, ncols, 128):
        load_cast(B, wt[:, :, c:c + 128], wk, wv[:, :, col0 + c:col0 + c + 128], (8, 128))
    return wt, wk


def proj_fm(B, wt, wk, c0, hT, hkey, t0, nt, bk, M=128):
    for k in range(8):
        B.P.op("pe", "matmul", reads=[wk, hkey], writes=["ps%d" % bk], out=B.ps[bk][0:M, 0:nt], lhsT=wt[:, k, c0:c0 + M],
               rhs=hT[:, k, t0:t0 + nt], start=(k == 0), stop=(k == 7))


def proj_tm(B, wt, wk, c0, ncols, hT, hkey, t0, n, bk):
    for k in range(8):
        B.P.op("pe", "matmul", reads=[wk, hkey], writes=["ps%d" % bk], out=B.ps[bk][0:n, 0:ncols], lhsT=hT[:, k, t0:t0 + n],
               rhs=wt[:, k, c0:c0 + ncols], start=(k == 0), stop=(k == 7))


def load_shortconv(B, hy_w, hy_b):
    P = B.P
    t = B.sb("scw", [128, 12, 4], F32)
    for j in range(3):
        P.dma("sp", writes=["scw"], out=t[:, :, j:j + 1], in_=hy_w[j].rearrange("(t p o) -> p t o", p=128, o=1),
              allow_slow_non_contiguous=True)
    P.dma("sp", writes=["scw"], out=t[:, :, 3:4], in_=hy_b.rearrange("(t p o) -> p t o", p=128, o=1),
          allow_slow_non_contiguous=True)
    return t


def hy_conv_tile(B, w_in, ct, hT, hkey, segs, scw, outt, outk, eng="dve"):
    P = B.P
    wt, wk = wblock(B, w_in, COL_HY + ct * 128, 128, "wblk128", 3, 128)
    for (tc0, n, oc0) in segs:
        zr, zk = B.ring("zrow", [128, 2050], F32, 1)
        pos = tc0 - 1
        end = tc0 + n + 1
        while pos < end:
            m = min(512, end - pos)
            bk = B.bank()
            proj_fm(B, wt, wk, 0, hT, hkey, pos, m, bk)
            P.op("act", "activation", reads=["ps%d" % bk], writes=[zk], out=zr[:, pos - (tc0 - 1): pos - (tc0 - 1) + m],
                 in_=B.ps[bk][:, 0:m], func=AF.Identity)
            pos += m
        tmp, tk = B.ring("cvtmp", [128, 2048], F32, 1)
        P.op(eng, "tensor_scalar", reads=[zk, "scw"], writes=[tk], out=tmp[:, 0:n], in0=zr[:, 0:n], scalar1=scw[:, ct, 0:1],
             scalar2=scw[:, ct, 3:4], op0=ALU.mult, op1=ALU.add)
        P.op("dve", "scalar_tensor_tensor", reads=[zk, "scw", tk], writes=[tk], out=tmp[:, 0:n], in0=zr[:, 1:n + 1],
             scalar=scw[:, ct, 1:2], in1=tmp[:, 0:n], op0=ALU.mult, op1=ALU.add)
        P.op("dve", "scalar_tensor_tensor", reads=[zk, "scw", tk], writes=[outk], out=outt[:, oc0:oc0 + n], in0=zr[:, 2:n + 2],
             scalar=scw[:, ct, 2:3], in1=tmp[:, 0:n], op0=ALU.mult, op1=ALU.add)


def build_s1():
    B = Builder()
    P = B.P
    xs = B.din("xs", [TOK + 2, D])
    cvec = B.din("c", [D])
    ada_w = B.din("ada_w", [D, 3 * D])
    ada_b = B.din("ada_b", [3 * D])
    npre = B.din("npre", [D])
    npost = B.din("npost", [D])
    w_in = B.din("w_in", [D, COL_END])
    wk_perm = B.din("wk_perm", [D, 512])
    hy_w = B.din("hy_w", [3, 1536])
    hy_b = B.din("hy_b", [1536])
    cosT = B.din("cosT", [128, TOK])
    sinT = B.din("sinT", [128, TOK])
    hmask = B.din("hmask", [2, 1])
    o_kt = B.dout("o_kt", [4, 128, TOK], BF16)
    o_v = B.dout("o_v", [TOK, 512], BF16)
    o_u = B.dout("o_u", [512, TOK], F32)
    B.init_psum()
    idf, idb = make_identity(B)
    Abc = B.sb("bc_A", [128, 1024], F32)
    Bbc = B.sb("bc_Bm", [128, 1024], F32)
    hT = B.sb("hT", [128, 8, TOK + 2], BF16)
    mt = B.sb("hmask", [2, 1], F32)
    m0 = B.mark()
    modulation(B, cvec, ada_w, ada_b, npre, npost, ("A", "Bm", None), {"A": Abc, "Bm": Bbc})
    B.release(m0)
    P.dma("sp", writes=["hmask"], out=mt[:], in_=hmask[:, :])
    for i in range(NT):
        compute_h_tile(B, [(xs[1 + 128 * i: 1 + 128 * (i + 1), :], 0, 128)], 128, Abc, Bbc, "bcA", "bcBm", hT, "hT", 1 + 128 * i, idb)
    hh = B.sb("hTh", [128, 8, 2], BF16)
    compute_h_tile(B, [(xs[0:1, :], 0, 1), (xs[TOK + 1:TOK + 2, :], 1, 1)], 2, Abc, Bbc, "bcA", "bcBm", hh, "hTh", 0, idb,
                   mask=(mt, "hmask"))
    P.op("pool", "tensor_copy", reads=["hTh"], writes=["hT"], out=hT[:, :, 0:1], in_=hh[:, :, 0:1])
    P.op("pool", "tensor_copy", reads=["hTh"], writes=["hT"], out=hT[:, :, TOK + 1:TOK + 2], in_=hh[:, :, 1:2])
    B.release(m0)
    cs = B.sb("cosT", [128, TOK], F32)
    sn = B.sb("sinT", [128, TOK], F32)
    P.dma("sp", writes=["cosT"], out=cs[:], in_=cosT[:, :])
    P.dma("sp", writes=["sinT"], out=sn[:], in_=sinT[:, :])
    wt, wk = wblock(B, w_in, COL_K, 512, "wblkK")
    wp, wpk = wblock(B, wk_perm, 0, 512, "wblkK")
    for h in range(4):
        ko, kk = B.ring("kout", [128, TOK], BF16, 2)
        for j in range(TOK // 512):
            b1 = B.bank()
            proj_fm(B, wt, wk, 128 * h, hT, "hT", 1 + 512 * j, 512, b1)
            b2 = B.bank()
            proj_fm(B, wp, wpk, 128 * h, hT, "hT", 1 + 512 * j, 512, b2)
            t1, t1k = B.ring("rtmp1", [128, 512], F32, 2)
            t2, t2k = B.ring("rtmp2", [128, 512], F32, 2)
            P.op("dve", "tensor_tensor", reads=["ps%d" % b1, "cosT"], writes=[t1k], out=t1[:], in0=B.ps[b1][:, :],
                 in1=cs[:, 512 * j:512 * (j + 1)], op=ALU.mult)
            P.op("dve", "tensor_tensor", reads=["ps%d" % b2, "sinT"], writes=[t2k], out=t2[:], in0=B.ps[b2][:, :],
                 in1=sn[:, 512 * j:512 * (j + 1)], op=ALU.mult)
            P.op("pool", "tensor_tensor", reads=[t1k, t2k], writes=[kk], out=ko[:, 512 * j:512 * (j + 1)], in0=t1[:], in1=t2[:],
                 op=ALU.add)
        P.dma("sp", reads=[kk], writes=["o_kt"], out=o_kt[h], in_=ko[:])
    B.release(m0)
    wt, wk = wblock(B, w_in, COL_V, 512, "wblkK")
    for i in range(NT):
        bk = B.bank()
        proj_tm(B, wt, wk, 0, 512, hT, "hT", 1 + 128 * i, 128, bk)
        vo, vk = B.ring("vout", [128, 512], BF16, 3)
        P.op("act", "activation", reads=["ps%d" % bk], writes=[vk], out=vo[:], in_=B.ps[bk][:, :], func=AF.Identity)
        P.dma("sp", reads=[vk], writes=["o_v"], out=o_v[128 * i:128 * (i + 1), :], in_=vo[:])
    B.release(m0)
    scw = load_shortconv(B, hy_w, hy_b)
    for ci in range(4):
        x1s, x1k = B.ring("x1s", [128, TOK], F32, 2)
        vs, vsk = B.ring("vs", [128, TOK], F32, 2)
        hy_conv_tile(B, w_in, 4 + ci, hT, "hT", [(1, TOK, 0)], scw, x1s, x1k, "dve")
        hy_conv_tile(B, w_in, 8 + ci, hT, "hT", [(1, TOK, 0)], scw, vs, vsk, "pool")
        P.op("dve", "tensor_tensor", reads=[x1k, vsk], writes=[x1k], out=x1s[:], in0=x1s[:], in1=vs[:], op=ALU.mult)
        P.dma("sp", reads=[x1k], writes=["o_u"], out=o_u[128 * ci:128 * (ci + 1), :], in_=x1s[:])
    return B.finish()


def rope_tables():
    rows = SEQ // 64
    row = np.repeat(np.arange(rows, dtype=np.float32), 64)
    col = np.tile(np.arange(64, dtype=np.float32), rows)
    inv = (10000.0 ** (-np.arange(16, dtype=np.float32) / 16)).astype(np.float32)
    ang_r = row[None, :] * inv[:, None]
    ang_c = col[None, :] * inv[:, None]
    cos64 = np.concatenate([np.cos(ang_r), np.cos(ang_r), np.cos(ang_c), np.cos(ang_c)], 0)
    sin64 = np.concatenate([-np.sin(ang_r), np.sin(ang_r), -np.sin(ang_c), np.sin(ang_c)], 0)
    cosT = np.concatenate([cos64, cos64], 0).astype(np.float32)
    sinT = np.concatenate([sin64, sin64], 0).astype(np.float32)
    return cosT, sinT


def perm_cols(w):
    k, c = w.shape
    return np.ascontiguousarray(w.reshape(k, c // 32, 2, 16)[:, :, ::-1, :].reshape(k, c))


_NC = {}


def get_nc(name):
    if name not in _NC:
        _NC[name] = globals()["build_" + name]()
    return _NC[name]


def run_s1(x, c, l, ada_w, ada_b, norm_pre, norm_post, w_in, hy_short_w, hy_short_b):
    nc = get_nc("s1")
    cosT, sinT = rope_tables()
    wkp = perm_cols(w_in[l][:, COL_K:COL_K + 512])
    in_maps = []
    for core in range(8):
        b, r = divmod(core, 4)
        t0 = r * TOK
        xs = np.zeros((TOK + 2, D), np.float32)
        lo, hi = max(t0 - 1, 0), min(t0 + TOK + 1, SEQ)
        xs[lo - (t0 - 1): hi - (t0 - 1)] = x[b, lo:hi]
        hmask = np.array([[0.0 if t0 == 0 else 1.0], [0.0 if t0 + TOK == SEQ else 1.0]], np.float32)
        in_maps.append({
            "xs": xs, "c": c[b], "ada_w": ada_w[l], "ada_b": ada_b[l], "npre": norm_pre[l], "npost": norm_post[l],
            "w_in": w_in[l], "wk_perm": wkp, "hy_w": hy_short_w[l], "hy_b": hy_short_b[l],
            "cosT": np.ascontiguousarray(cosT[:, t0:t0 + TOK]), "sinT": np.ascontiguousarray(sinT[:, t0:t0 + TOK]),
            "hmask": hmask,
        })
    res = run_bass_kernel_spmd(nc, in_maps, core_ids=list(range(8)))
    return res.results


NFFT = 2 * SEQ
CG = 32


def hyena_tables():
    L = SEQ
    tau = np.arange(NFFT)
    pos = np.where(tau < L, tau, NFFT - tau).astype(np.int64)
    posc = np.minimum(pos, L - 1)
    t_lin = np.linspace(0.0, 1.0, L, dtype=np.float32)
    wpos = ((2.0 * math.pi / L) * np.arange(L, dtype=np.float32)).astype(np.float32)
    bands = np.linspace(1e-4, 16 - 1, 16, dtype=np.float32)
    zfull = np.concatenate([t_lin[:, None], np.cos(bands[None, :] * wpos[:, None]), -np.sin(bands[None, :] * wpos[:, None])],
                           axis=-1).astype(np.float32)
    zpos = np.ascontiguousarray(zfull[posc].T)
    mn = math.log(1e-2) / 1.5
    mx = math.log(1e-2) / 0.3
    deltas = np.abs(np.linspace(mn, mx, 512, dtype=np.float32))
    win = np.exp(-t_lin[posc][None, :] * deltas[:, None]).astype(np.float32)
    win[:, L] = 0.0
    a = np.arange(128, dtype=np.float64)
    ang = 2.0 * math.pi * np.outer(a, a) / 128.0
    C = np.cos(ang).astype(np.float32)
    S = np.sin(ang).astype(np.float32)
    angt = 2.0 * math.pi * np.outer(a, a) / NFFT
    Tc = np.cos(angt).astype(np.float32)
    Ts = np.sin(angt).astype(np.float32)
    dft = np.concatenate([C, S, C, -S, C], axis=1).astype(np.float32)
    tw = np.concatenate([Tc, Ts], axis=1).astype(np.float32)
    return zpos, win, dft, tw


def wrap_pi(B, a, ak, n, rows):
    P = B.P
    m, mk = B.ring("wrapm", [128, 512], F32, 2)
    P.op("dve", "tensor_scalar", reads=[ak], writes=[mk], out=m[0:rows, 0:n], in0=a[0:rows, 0:n], scalar1=PI, scalar2=-2.0 * PI,
         op0=ALU.is_gt, op1=ALU.mult)
    P.op("dve", "tensor_tensor", reads=[ak, mk], writes=[ak], out=a[0:rows, 0:n], in0=a[0:rows, 0:n], in1=m[0:rows, 0:n], op=ALU.add)
    P.op("dve", "tensor_scalar", reads=[ak], writes=[mk], out=m[0:rows, 0:n], in0=a[0:rows, 0:n], scalar1=-PI, scalar2=2.0 * PI,
         op0=ALU.is_lt, op1=ALU.mult)
    P.op("dve", "tensor_tensor", reads=[ak, mk], writes=[ak], out=a[0:rows, 0:n], in0=a[0:rows, 0:n], in1=m[0:rows, 0:n], op=ALU.add)


def filter_mlp_chunk(B, fw, zt, zk, n):
    P = B.P
    w1, w2, cols = fw["w1"], fw["w2"], fw["cols"]
    bk = B.bank()
    P.op("pe", "matmul", reads=["fw", zk], writes=["ps%d" % bk], out=B.ps[bk][0:64, 0:n], lhsT=w1[0:33, :], rhs=zt[0:33, 0:n],
         start=True, stop=True)
    a1, a1k = B.ring("fa", [128, 512], F32, 3)
    P.op("dve", "tensor_scalar", reads=["ps%d" % bk, "fw"], writes=[a1k], out=a1[0:64, 0:n], in0=B.ps[bk][0:64, 0:n],
         scalar1=cols[0:64, 0:1], scalar2=cols[0:64, 1:2], op0=ALU.add, op1=ALU.mult)
    wrap_pi(B, a1, a1k, n, 64)
    P.op("act", "activation", reads=[a1k], writes=[a1k], out=a1[0:64, 0:n], in_=a1[0:64, 0:n], func=AF.Sin)
    bk = B.bank()
    P.op("pe", "matmul", reads=["fw", a1k], writes=["ps%d" % bk], out=B.ps[bk][0:64, 0:n], lhsT=w2[0:64, :], rhs=a1[0:64, 0:n],
         start=True, stop=True)
    a2, a2k = B.ring("fa", [128, 512], F32, 3)
    P.op("dve", "tensor_scalar", reads=["ps%d" % bk, "fw"], writes=[a2k], out=a2[0:64, 0:n], in0=B.ps[bk][0:64, 0:n],
         scalar1=cols[0:64, 2:3], scalar2=cols[0:64, 3:4], op0=ALU.add, op1=ALU.mult)
    wrap_pi(B, a2, a2k, n, 64)
    P.op("act", "activation", reads=[a2k], writes=[a2k], out=a2[0:64, 0:n], in_=a2[0:64, 0:n], func=AF.Sin)
    return a2, a2k


def load_filter_weights(B, hw1, hb1, hw2, hb2, hfreq):
    P = B.P
    w1 = B.sb("fw1", [33, 64], F32)
    w2 = B.sb("fw2", [64, 64], F32)
    cols = B.sb("fcols", [64, 4], F32)
    P.dma("sp", writes=["fw"], out=w1[:], in_=hw1[:, :])
    P.dma("sp", writes=["fw"], out=w2[:], in_=hw2[:, :])
    P.dma("sp", writes=["fw"], out=cols[:, 0:1], in_=hb1.rearrange("(p o) -> p o", o=1))
    P.dma("sp", writes=["fw"], out=cols[:, 1:2], in_=hfreq[0].rearrange("(p o) -> p o", o=1))
    P.dma("sp", writes=["fw"], out=cols[:, 2:3], in_=hb2.rearrange("(p o) -> p o", o=1))
    P.dma("sp", writes=["fw"], out=cols[:, 3:4], in_=hfreq[1].rearrange("(p o) -> p o", o=1))
    return {"w1": w1, "w2": w2, "cols": cols}


def fft_pair_fwd(B, src, srck, kdim, c0, dft, tw):
    P = B.P
    bk = B.bank()
    for i in range(2):
        P.op("pe", "matmul", reads=[srck, "dft"], writes=["ps%d" % bk], out=B.ps[bk][:, 256 * i:256 * (i + 1)],
             lhsT=src[0:kdim, c0 + i, :], rhs=dft[0:kdim, 256:512], start=True, stop=True)
    A = B.ps[bk][:, :].rearrange("p (c r k) -> p c r k", c=2, r=2)
    Are, Aim = A[:, :, 0, :], A[:, :, 1, :]
    Tc = tw[:, 0:128].unsqueeze(1).to_broadcast([128, 2, 128])
    Ts = tw[:, 128:256].unsqueeze(1).to_broadcast([128, 2, 128])
    tt, ttk = B.ring("fft_t", [128, 4, 2, 128], F32, 2)
    pk = "ps%d" % bk
    P.op("dve", "tensor_tensor", reads=[pk, "tw"], writes=[ttk], out=tt[:, 0], in0=Are, in1=Tc, op=ALU.mult)
    P.op("dve", "tensor_tensor", reads=[pk, "tw"], writes=[ttk], out=tt[:, 1], in0=Aim, in1=Ts, op=ALU.mult)
    P.op("dve", "tensor_tensor", reads=[pk, "tw"], writes=[ttk], out=tt[:, 2], in0=Aim, in1=Tc, op=ALU.mult)
    P.op("dve", "tensor_tensor", reads=[pk, "tw"], writes=[ttk], out=tt[:, 3], in0=Are, in1=Ts, op=ALU.mult)
    b1, b1k = B.ring("fft_b1", [128, 2, 2, 128], F32, 2)
    b2, b2k = B.ring("fft_b2", [128, 2, 2, 128], F32, 2)
    P.op("pool", "tensor_tensor", reads=[ttk], writes=[b1k], out=b1[:, :, 0, :], in0=tt[:, 0], in1=tt[:, 1], op=ALU.add)
    P.op("pool", "tensor_tensor", reads=[ttk], writes=[b1k], out=b1[:, :, 1, :], in0=tt[:, 2], in1=tt[:, 3], op=ALU.subtract)
    P.op("act", "activation", reads=[b1k], writes=[b2k], out=b2[:, :, 0, :], in_=b1[:, :, 1, :], func=AF.Identity)
    P.op("act", "activation", reads=[b1k], writes=[b2k], out=b2[:, :, 1, :], in_=b1[:, :, 0, :], func=AF.Identity, scale=-1.0)
    bx = B.bank()
    P.op("pe", "matmul", reads=[b1k, "dft"], writes=["ps%d" % bx], out=B.ps[bx][:, :], lhsT=dft[:, 0:128],
         rhs=b1[:].rearrange("p c r k -> p (c r k)"), start=True, stop=False)
    P.op("pe", "matmul", reads=[b2k, "dft"], writes=["ps%d" % bx], out=B.ps[bx][:, :], lhsT=dft[:, 128:256],
         rhs=b2[:].rearrange("p c r k -> p (c r k)"), start=False, stop=True)
    return bx


def build_s2():
    B = Builder()
    P = B.P
    u_in = B.din("u", [128, SEQ])
    hw1 = B.din("hw1", [33, 64]); hb1 = B.din("hb1", [64]); hw2 = B.din("hw2", [64, 64]); hb2 = B.din("hb2", [64])
    hw3 = B.din("hw3", [64, 256]); hfreq = B.din("hfreq", [2, 64])
    zpos = B.din("zpos", [33, NFFT]); win = B.din("win", [128, NFFT])
    dft_in = B.din("dft", [128, 640]); tw_in = B.din("tw", [128, 256])
    y_out = B.dout("y", [128, SEQ])
    ssq_out = B.dout("ssq", [128, 1])
    kscr = B.dscratch("kscr", [128, NFFT])
    B.init_psum()
    dft = B.sb("dft", [128, 640], F32)
    tw = B.sb("tw", [128, 256], F32)
    P.dma("sp", writes=["dft"], out=dft[:], in_=dft_in[:, :])
    P.dma("sp", writes=["tw"], out=tw[:], in_=tw_in[:, :])
    fw = load_filter_weights(B, hw1, hb1, hw2, hb2, hfreq)
    w3 = B.sb("fw3", [64, 256], F32)
    P.dma("sp", writes=["fw"], out=w3[:], in_=hw3[:, :])
    ssqp = B.sb("ssqp", [128, 33], F32)
    P.op("dve", "memset", writes=["ssqp"], ap=ssqp[:], constant=0.0)
    m0 = B.mark()
    for ch in range(NFFT // 512):
        zt, zk = B.ring("zt", [33, 512], F32, 2)
        P.dma("sp", writes=[zk], out=zt[:], in_=zpos[:, ch * 512:(ch + 1) * 512])
        wn, wnk = B.ring("wn", [128, 512], F32, 2)
        P.dma("sp", writes=[wnk], out=wn[:], in_=win[:, ch * 512:(ch + 1) * 512])
        h2, h2k = filter_mlp_chunk(B, fw, zt, zk, 512)
        bk = B.bank()
        half = 0 if ch * 512 < SEQ else 1
        P.op("pe", "matmul", reads=["fw", h2k], writes=["ps%d" % bk], out=B.ps[bk][:, :], lhsT=w3[0:64, 128 * half:128 * (half + 1)],
             rhs=h2[0:64, :], start=True, stop=True)
        kc, kck = B.ring("kc", [128, 512], F32, 2)
        P.op("dve", "tensor_tensor", reads=["ps%d" % bk, wnk], writes=[kck], out=kc[:], in0=B.ps[bk][:, :], in1=wn[:], op=ALU.mult)
        jk_, jkk = B.ring("kjunk", [128, 512], F32, 2)
        P.op("act", "activation", reads=[kck], writes=[jkk, "ssqp"], out=jk_[:], in_=kc[:], func=AF.Square,
             accum_out=ssqp[:, ch:ch + 1])
        P.dma("sp", reads=[kck], writes=["kscr"], out=kscr[:, ch * 512:(ch + 1) * 512], in_=kc[:])
    P.op("dve", "tensor_reduce", reads=["ssqp"], writes=["ssqp"], out=ssqp[:, 32:33], in_=ssqp[:, 0:32], axis=AX.X, op=ALU.add)
    P.dma("sp", reads=["ssqp"], writes=["ssq"], out=ssq_out[:, :], in_=ssqp[:, 32:33])
    B.release(m0)
    kview = kscr.rearrange("c (p j) -> p c j", j=128)
    uview = u_in.rearrange("c (p j) -> p c j", j=128)
    yview = y_out.rearrange("c (p j) -> p c j", j=128)
    for g in range(128 // CG):
        cs0 = g * CG
        kd, kdk = B.ring("kd", [128, CG, 128], F32, 1)
        ud, udk = B.ring("ud", [64, CG, 128], F32, 1)
        P.dma("sp", reads=["kscr"], writes=[kdk], out=kd[:], in_=kview[:, cs0:cs0 + CG, :])
        P.dma("sp", writes=[udk], out=ud[:], in_=uview[:, cs0:cs0 + CG, :])
        KF, KFk = B.ring("KF", [128, CG, 2, 128], F32, 1)
        for c0 in range(0, CG, 2):
            bx = fft_pair_fwd(B, kd, kdk, 128, c0, dft, tw)
            P.op("act", "activation", reads=["ps%d" % bx], writes=[KFk], out=KF[:, c0:c0 + 2].rearrange("p c r k -> p (c r k)"),
                 in_=B.ps[bx][:, :], func=AF.Identity)
        Bre, Brek = B.ring("Bre", [128, CG, 128], F32, 1)
        Bim, Bimk = B.ring("Bim", [128, CG, 128], F32, 1)
        for c0 in range(0, CG, 2):
            bx = fft_pair_fwd(B, ud, udk, 64, c0, dft, tw)
            X = B.ps[bx][:, :].rearrange("p (c r k) -> p c r k", c=2, r=2)
            Xre, Xim = X[:, :, 0, :], X[:, :, 1, :]
            Kre, Kim = KF[:, c0:c0 + 2, 0, :], KF[:, c0:c0 + 2, 1, :]
            tt, ttk = B.ring("fft_t", [128, 4, 2, 128], F32, 2)
            pk = "ps%d" % bx
            P.op("dve", "tensor_tensor", reads=[pk, KFk], writes=[ttk], out=tt[:, 0], in0=Xre, in1=Kre, op=ALU.mult)
            P.op("dve", "tensor_tensor", reads=[pk, KFk], writes=[ttk], out=tt[:, 1], in0=Xim, in1=Kim, op=ALU.mult)
            P.op("dve", "tensor_tensor", reads=[pk, KFk], writes=[ttk], out=tt[:, 2], in0=Xre, in1=Kim, op=ALU.mult)
            P.op("dve", "tensor_tensor", reads=[pk, KFk], writes=[ttk], out=tt[:, 3], in0=Xim, in1=Kre, op=ALU.mult)
            Y, Yk = B.ring("Y", [128, 2, 2, 128], F32, 2)
            P.op("pool", "tensor_tensor", reads=[ttk], writes=[Yk], out=Y[:, :, 0, :], in0=tt[:, 0], in1=tt[:, 1], op=ALU.subtract)
            P.op("pool", "tensor_tensor", reads=[ttk], writes=[Yk], out=Y[:, :, 1, :], in0=tt[:, 2], in1=tt[:, 3], op=ALU.add)
            bi = B.bank()
            for i in range(2):
                P.op("pe", "matmul", reads=[Yk, "dft"], writes=["ps%d" % bi], out=B.ps[bi][:, 256 * i:256 * (i + 1)],
                     lhsT=Y[:, i, 0, :], rhs=dft[:, 0:256], start=True, stop=False, skip_group_check=True)
                P.op("pe", "matmul", reads=[Yk, "dft"], writes=["ps%d" % bi], out=B.ps[bi][:, 256 * i:256 * (i + 1)],
                     lhsT=Y[:, i, 1, :], rhs=dft[:, 384:640], start=False, stop=True, skip_group_check=True)
            Bm = B.ps[bi][:, :].rearrange("p (c r k) -> p c r k", c=2, r=2)
            Bre_p, Bim_p = Bm[:, :, 0, :], Bm[:, :, 1, :]
            Tc = tw[:, 0:128].unsqueeze(1).to_broadcast([128, 2, 128])
            Ts = tw[:, 128:256].unsqueeze(1).to_broadcast([128, 2, 128])
            t2, t2k = B.ring("fft_t", [128, 4, 2, 128], F32, 2)
            pk = "ps%d" % bi
            P.op("dve", "tensor_tensor", reads=[pk, "tw"], writes=[t2k], out=t2[:, 0], in0=Bre_p, in1=Tc, op=ALU.mult)
            P.op("dve", "tensor_tensor", reads=[pk, "tw"], writes=[t2k], out=t2[:, 1], in0=Bim_p, in1=Ts, op=ALU.mult)
            P.op("dve", "tensor_tensor", reads=[pk, "tw"], writes=[t2k], out=t2[:, 2], in0=Bre_p, in1=Ts, op=ALU.mult)
            P.op("dve", "tensor_tensor", reads=[pk, "tw"], writes=[t2k], out=t2[:, 3], in0=Bim_p, in1=Tc, op=ALU.mult)
            P.op("pool", "tensor_tensor", reads=[t2k], writes=[Brek], out=Bre[:, c0:c0 + 2, :], in0=t2[:, 0], in1=t2[:, 1], op=ALU.subtract)
            P.op("pool", "tensor_tensor", reads=[t2k], writes=[Bimk], out=Bim[:, c0:c0 + 2, :], in0=t2[:, 2], in1=t2[:, 3], op=ALU.add)
        yo, yok = B.ring("yo", [64, CG, 128], F32, 1)
        for c0 in range(0, CG, 4):
            bo = B.bank()
            P.op("pe", "matmul", reads=[Brek, "dft"], writes=["ps%d" % bo], out=B.ps[bo][0:64, :], lhsT=dft[:, 0:64],
                 rhs=Bre[:, c0:c0 + 4, :].rearrange("p c j -> p (c j)"), start=True, stop=False)
            P.op("pe", "matmul", reads=[Bimk, "dft"], writes=["ps%d" % bo], out=B.ps[bo][0:64, :], lhsT=dft[:, 384:448],
                 rhs=Bim[:, c0:c0 + 4, :].rearrange("p c j -> p (c j)"), start=False, stop=True)
            P.op("act", "activation", reads=["ps%d" % bo], writes=[yok], out=yo[:, c0:c0 + 4, :].rearrange("p c j -> p (c j)"),
                 in_=B.ps[bo][0:64, :], func=AF.Identity, scale=1.0 / NFFT)
        P.dma("sp", reads=[yok], writes=["y"], out=yview[:, cs0:cs0 + CG, :], in_=yo[:])
    return B.finish()


def run_s2(u_all, l, hy_f_w1, hy_f_b1, hy_f_w2, hy_f_b2, hy_f_w3, hy_f_freq):
    nc = get_nc("s2")
    zpos, win, dft, tw = hyena_tables()
    in_maps = []
    for core in range(8):
        b, q = divmod(core, 4)
        w3 = np.concatenate([hy_f_w3[l][:, 128 * q:128 * (q + 1)], hy_f_w3[l][:, 512 + 128 * q:512 + 128 * (q + 1)]], axis=1)
        in_maps.append({
            "u": np.ascontiguousarray(u_all[b, 128 * q:128 * (q + 1)]), "hw1": hy_f_w1[l], "hb1": hy_f_b1[l], "hw2": hy_f_w2[l],
            "hb2": hy_f_b2[l], "hw3": np.ascontiguousarray(w3), "hfreq": hy_f_freq[l], "zpos": zpos,
            "win": np.ascontiguousarray(win[128 * q:128 * (q + 1)]), "dft": dft, "tw": tw,
        })
    res = run_bass_kernel_spmd(nc, in_maps, core_ids=list(range(8))).results
    y = np.stack([np.concatenate([res[4 * b + q]["y"] for q in range(4)], 0) for b in range(2)], 0)
    ssq = np.concatenate([res[q]["ssq"][:, 0] for q in range(4)], 0)
    return y, ssq


NTOK3 = TOK + CTX
HCOLS = TOK + 2 + CTX + 2


def hcol(t):
    return 1 + t if t < TOK else 3 + t


CHUNKS = [(512 * j, 512) for j in range(TOK // 512)] + [(TOK, CTX)]
TILES = [128 * i for i in range(NTOK3 // 128)]


def bcast_row(B, dram_vec, n, name):
    P = B.P
    row = B.sb("row_" + name, [1, n], F32)
    P.dma("sp", writes=["row_" + name], out=row[:], in_=dram_vec.rearrange("(o c) -> o c", o=1))
    t = B.sb("bcr_" + name, [128, n], F32)
    pos = 0
    while pos < n:
        m = min(512, n - pos)
        bk = B.bank()
        P.op("pe", "matmul", reads=["ones1", "row_" + name], writes=["ps%d" % bk], out=B.ps[bk][:, 0:m], lhsT=B.ones1[:, :],
             rhs=row[:, pos:pos + m], start=True, stop=True)
        P.op("act", "activation", reads=["ps%d" % bk], writes=["bcr_" + name], out=t[:, pos:pos + m], in_=B.ps[bk][:, 0:m],
             func=AF.Identity)
        pos += m
    return t, "bcr_" + name


def rstd_from_ssq(B, st_, sk, n, c_in, c_tmp, c_out, inv_n):
    P = B.P
    P.op("dve", "tensor_scalar", reads=[sk], writes=[sk], out=st_[0:n, c_tmp:c_tmp + 1], in0=st_[0:n, c_in:c_in + 1], scalar1=inv_n,
         scalar2=EPS, op0=ALU.mult, op1=ALU.add)
    P.op("act", "activation", reads=[sk], writes=[sk], out=st_[0:n, c_tmp:c_tmp + 1], in_=st_[0:n, c_tmp:c_tmp + 1], func=AF.Sqrt)
    P.op("dve", "reciprocal", reads=[sk], writes=[sk], out=st_[0:n, c_out:c_out + 1], in_=st_[0:n, c_tmp:c_tmp + 1])


def transpose_to_fm(B, src, srck, dst, dstk, t0, idb):
    P = B.P
    bk = B.bank()
    psb = B.ps[bk][:, :].bitcast(BF16)
    for k in range(4):
        P.op("pe", "transpose", reads=[srck, "ident_b"], writes=["ps%d" % bk], out=psb[:, k * 128:(k + 1) * 128],
             in_=src[:, k * 128:(k + 1) * 128], identity=idb[:, :])
    P.op("act", "activation", reads=["ps%d" % bk], writes=[dstk], out=dst[:, :, t0:t0 + 128],
         in_=psb[:, 0:512].rearrange("p (k t) -> p k t", t=128), func=AF.Identity)


def build_s3():
    B = Builder()
    P = B.P
    xs = B.din("xs", [TOK + 2, D]); xc = B.din("xc", [CTX, D])
    cvec = B.din("c", [D]); cctx = B.din("c_ctx", [D])
    ada_w = B.din("ada_w", [D, 3 * D]); ada_b = B.din("ada_b", [3 * D])
    npre = B.din("npre", [D]); npost = B.din("npost", [D])
    w_in = B.din("w_in", [D, COL_END]); wq_perm = B.din("wq_perm", [D, 512])
    hmask = B.din("hmask", [2, 1])
    cosT = B.din("cosT", [128, TOK]); sinT = B.din("sinT", [128, TOK])
    kt_all = B.din("kt_all", [4, 128, SEQ], BF16); v_all = B.din("v_all", [SEQ, 512], BF16)
    u_own = B.din("u_own", [512, TOK]); y_own = B.din("y_own", [512, TOK])
    hyc = B.din("hyc", [128, 4, 2])
    hy_w = B.din("hy_w", [3, 1536]); hy_b = B.din("hy_b", [1536])
    da_lam = B.din("da_lam", [256]); lam_init = B.din("lam_init", [1]); da_subln = B.din("da_subln", [128])
    hw1 = B.din("hw1", [33, 64]); hb1 = B.din("hb1", [64]); hw2 = B.din("hw2", [64, 64]); hb2 = B.din("hb2", [64])
    hw3 = B.din("hw3", [64, 1024]); hfreq = B.din("hfreq", [2, 64])
    zposc = B.din("zposc", [33, 512]); winc = B.din("winc", [512, 512])
    gm_g = B.din("gm_g", [512]); gm_b = B.din("gm_b", [512]); gm_ws = B.din("gm_ws", [8, 128, 128]); gm_bs = B.din("gm_bs", [8, 128])
    w_br = B.din("w_br", [3, 512, D]); w_out = B.din("w_out", [D, D])
    x_new = B.dout("x_new", [TOK, D]); xc_new = B.dout("xc_new", [CTX, D])
    B.init_psum()
    idf, idb = make_identity(B)
    B.ones1 = B.sb("ones1", [1, 128], F32)
    P.op("dve", "memset", writes=["ones1"], ap=B.ones1[:], constant=1.0)
    hT = B.sb("hT", [128, 8, HCOLS], BF16)
    Gbc = B.sb("bc_G", [128, 1024], F32)
    Gcbc = B.sb("bc_Gc", [128, 1024], F32)
    gaT = B.sb("gaT", [128, 4, NTOK3], BF16)
    gbT = B.sb("gbT", [128, 4, NTOK3], BF16)
    gcT = B.sb("gcT", [128, 4, NTOK3], BF16)
    mt = B.sb("hmask", [2, 1], F32)
    P.dma("sp", writes=["hmask"], out=mt[:], in_=hmask[:, :])
    m0 = B.mark()
    Abc = B.sb("bc_A", [128, 1024], F32); Bbc = B.sb("bc_Bm", [128, 1024], F32)
    Acbc = B.sb("bc_Ac", [128, 1024], F32); Bcbc = B.sb("bc_Bc", [128, 1024], F32)
    m1 = B.mark()
    modulation(B, cvec, ada_w, ada_b, npre, npost, ("A", "Bm", "G"), {"A": Abc, "Bm": Bbc, "G": Gbc})
    B.release(m1)
    modulation(B, cctx, ada_w, ada_b, npre, npost, ("Ac", "Bc", "Gc"), {"Ac": Acbc, "Bc": Bcbc, "Gc": Gcbc})
    B.release(m1)
    for i in range(NT):
        compute_h_tile(B, [(xs[1 + 128 * i: 1 + 128 * (i + 1), :], 0, 128)], 128, Abc, Bbc, "bcA", "bcBm", hT, "hT", 1 + 128 * i, idb)
    hh = B.sb("hTh", [128, 8, 2], BF16)
    compute_h_tile(B, [(xs[0:1, :], 0, 1), (xs[TOK + 1:TOK + 2, :], 1, 1)], 2, Abc, Bbc, "bcA", "bcBm", hh, "hTh", 0, idb,
                   mask=(mt, "hmask"))
    P.op("pool", "tensor_copy", reads=["hTh"], writes=["hT"], out=hT[:, :, 0:1], in_=hh[:, :, 0:1])
    P.op("pool", "tensor_copy", reads=["hTh"], writes=["hT"], out=hT[:, :, TOK + 1:TOK + 2], in_=hh[:, :, 1:2])
    P.op("pool", "memset", writes=["hT"], ap=hT[:, :, TOK + 2:TOK + 3], constant=0.0)
    P.op("pool", "memset", writes=["hT"], ap=hT[:, :, HCOLS - 1:HCOLS], constant=0.0)
    for i in range(2):
        compute_h_tile(B, [(xc[128 * i:128 * (i + 1), :], 0, 128)], 128, Acbc, Bcbc, "bcAc", "bcBc", hT, "hT", TOK + 3 + 128 * i, idb)
    B.release(m0)

    m0 = B.mark()
    scw = load_shortconv(B, hy_w, hy_b)
    hyct = B.sb("hyct", [128, 4, 4], F32)
    P.dma("sp", writes=["hyct"], out=hyct[:, :, 0:2], in_=hyc[:, :, :])
    fw = load_filter_weights(B, hw1, hb1, hw2, hb2, hfreq)
    w3 = B.sb("fw3", [64, 1024], F32)
    P.dma("sp", writes=["fw"], out=w3[:], in_=hw3[:, :])
    zt = B.sb("ztc", [33, 512], F32)
    P.dma("sp", writes=["ztc"], out=zt[:], in_=zposc[:, :])
    hid2, hid2k = filter_mlp_chunk(B, fw, zt, "ztc", 512)
    hid2p = B.sb("hid2p", [64, 512], F32)
    P.op("pool", "tensor_copy", reads=[hid2k], writes=["hid2p"], out=hid2p[:], in_=hid2[0:64, :])
    segs_all = [(1, TOK, 0), (TOK + 3, CTX, TOK)]
    segs_ctx = [(TOK + 3, CTX, 0)]
    for ci in range(4):
        P.op("act", "activation", reads=["hyct"], writes=["hyct"], out=hyct[:, ci, 2:3], in_=hyct[:, ci, 0:1], func=AF.Sqrt)
        P.op("dve", "reciprocal", reads=["hyct"], writes=["hyct"], out=hyct[:, ci, 3:4], in_=hyct[:, ci, 2:3])
        x0s, x0k = B.ring("x0s", [128, NTOK3], F32, 1)
        hy_conv_tile(B, w_in, ci, hT, "hT", segs_all if do_ctx else segs_all[:1], scw, x0s, x0k)
        yb, ybk = B.ring("yb", [128, NTOK3], F32, 1)
        ut, utk = B.ring("ut", [128, TOK], F32, 1)
        P.dma("sp", writes=[ybk], out=yb[:, 0:TOK], in_=y_own[128 * ci:128 * (ci + 1), :])
        P.dma("sp", writes=[utk], out=ut[:], in_=u_own[128 * ci:128 * (ci + 1), :])
        P.op("dve", "tensor_scalar", reads=[ybk, "hyct"], writes=[ybk], out=yb[:, 0:TOK], in0=yb[:, 0:TOK], scalar1=hyct[:, ci, 3:4],
             scalar2=None, op0=ALU.mult)
        P.op("dve", "scalar_tensor_tensor", reads=[utk, "hyct", ybk], writes=[ybk], out=yb[:, 0:TOK], in0=ut[:], scalar=hyct[:, ci, 1:2],
             in1=yb[:, 0:TOK], op0=ALU.mult, op1=ALU.add)
        x1c, x1ck = B.ring("x1c", [128, CTX], F32, 1)
        vc_, vck = B.ring("vcc", [128, CTX], F32, 1)
        hy_conv_tile(B, w_in, 4 + ci, hT, "hT", segs_ctx, scw, x1c, x1ck)
        hy_conv_tile(B, w_in, 8 + ci, hT, "hT", segs_ctx, scw, vc_, vck)
        P.op("dve", "tensor_tensor", reads=[x1ck, vck], writes=[x1ck], out=x1c[:], in0=x1c[:], in1=vc_[:], op=ALU.mult)
        kc, kck = B.ring("kcf", [128, 512], F32, 1)
        wnc, wnck = B.ring("wnc", [128, 512], F32, 1)
        P.dma("sp", writes=[wnck], out=wnc[:], in_=winc[128 * ci:128 * (ci + 1), :])
        bk = B.bank()
        P.op("pe", "matmul", reads=["fw", "hid2p"], writes=["ps%d" % bk], out=B.ps[bk][:, 0:255], lhsT=w3[0:64, 512 + 128 * ci:512 + 128 * (ci + 1)],
             rhs=hid2p[0:64, 0:255], start=True, stop=True)
        P.op("pe", "matmul", reads=["fw", "hid2p"], writes=["ps%d" % bk], out=B.ps[bk][:, 255:512], lhsT=w3[0:64, 128 * ci:128 * (ci + 1)],
             rhs=hid2p[0:64, 255:512], start=True, stop=True)
        P.op("dve", "tensor_tensor", reads=["ps%d" % bk, wnck], writes=[kck], out=kc[:], in0=B.ps[bk][:, :], in1=wnc[:], op=ALU.mult)
        cst, cstk = B.ring("cst", [128, 4], F32, 2)
        jk_, jkk = B.ring("kjunk", [128, 512], F32, 1)
        P.op("act", "activation", reads=[kck], writes=[jkk, cstk], out=jk_[:], in_=kc[:], func=AF.Square, accum_out=cst[:, 0:1])
        P.op("act", "activation", reads=[cstk], writes=[cstk], out=cst[:, 1:2], in_=cst[:, 0:1], func=AF.Sqrt)
        P.op("dve", "reciprocal", reads=[cstk], writes=[cstk], out=cst[:, 2:3], in_=cst[:, 1:2])
        acc, acck = B.ring("cacc", [128, 2, CTX], F32, 1)
        P.op("pool", "memset", writes=[acck + "0"], ap=acc[:, 0, :], constant=0.0)
        P.op("pool", "memset", writes=[acck + "1"], ap=acc[:, 1, :], constant=0.0)
        for s in range(CTX):
            a = s % 2
            P.op("dve", "scalar_tensor_tensor", reads=[kck, x1ck, acck + str(a)], writes=[acck + str(a)], out=acc[:, a, :],
                 in0=kc[:, 255 - s:511 - s], scalar=x1c[:, s:s + 1], in1=acc[:, a, :], op0=ALU.mult, op1=ALU.add)
        P.op("dve", "tensor_tensor", reads=[acck + "0", acck + "1"], writes=[acck + "0"], out=acc[:, 0, :], in0=acc[:, 0, :], in1=acc[:, 1, :],
             op=ALU.add)
        P.op("dve", "tensor_scalar", reads=[acck + "0", cstk], writes=[ybk], out=yb[:, TOK:NTOK3], in0=acc[:, 0, :], scalar1=cst[:, 2:3],
             scalar2=None, op0=ALU.mult)
        P.op("dve", "scalar_tensor_tensor", reads=[x1ck, "hyct", ybk], writes=[ybk], out=yb[:, TOK:NTOK3], in0=x1c[:], scalar=hyct[:, ci, 1:2],
             in1=yb[:, TOK:NTOK3], op0=ALU.mult, op1=ALU.add)
        P.op("pool", "tensor_tensor", reads=[x0k, ybk], writes=[ybk], out=yb[:], in0=yb[:], in1=x0s[:], op=ALU.mult)
        wt, wk = wblock(B, w_in, COL_GB + 128 * ci, 128, "wblk128", 3, 128)
        for (t0, n) in CHUNKS:
            bk = B.bank()
            proj_fm(B, wt, wk, 0, hT, "hT", hcol(t0), n, bk)
            sg, sgk = B.ring("sgb", [128, 512], F32, 2)
            P.op("act", "activation", reads=["ps%d" % bk], writes=[sgk], out=sg[:, 0:n], in_=B.ps[bk][:, 0:n], func=AF.Silu)
            P.op("dve", "tensor_tensor", reads=[sgk, ybk], writes=["gbT"], out=gbT[:, ci, t0:t0 + n], in0=sg[:, 0:n], in1=yb[:, t0:t0 + n],
                 op=ALU.mult)
    B.release(m0)

    m0 = B.mark()
    lng, lngk = bcast_row(B, gm_g, 512, "lng")
    lnb, lnbk = bcast_row(B, gm_b, 512, "lnb")
    wsf = B.sb("wsf", [128, 8, 128], F32)
    P.dma("sp", writes=["wsf"], out=wsf[:], in_=gm_ws.rearrange("g p q -> p g q"))
    wsT = B.sb("wsT", [128, 8, 128], BF16)
    for g in range(8):
        bk = B.bank()
        P.op("pe", "transpose", reads=["wsf", "ident_f"], writes=["ps%d" % bk], out=B.ps[bk][:, 0:128], in_=wsf[:, g, :], identity=idf[:, :])
        P.op("act", "activation", reads=["ps%d" % bk], writes=["wsT"], out=wsT[:, g, :], in_=B.ps[bk][:, 0:128], func=AF.Identity)
    bsT = B.sb("bsT", [128, 8], F32)
    P.dma("sp", writes=["bsT"], out=bsT[:], in_=gm_bs.rearrange("g p -> p g"), allow_slow_non_contiguous=True)
    wu, wuk = wblock(B, w_in, COL_GM, 512, "wgm_u", 1)
    wv, wvk = wblock(B, w_in, COL_GM + 512, 512, "wgm_v", 1)
    wc, wck = wblock(B, w_in, COL_GC, 512, "wgm_c", 1)
    for t0 in TILES:
        ba, bb, bc_ = B.bank(), B.bank(), B.bank()
        proj_tm(B, wu, wuk, 0, 512, hT, "hT", hcol(t0), 128, ba)
        proj_tm(B, wv, wvk, 0, 512, hT, "hT", hcol(t0), 128, bb)
        proj_tm(B, wc, wck, 0, 512, hT, "hT", hcol(t0), 128, bc_)
        ug, ugk = B.ring("ug", [128, 512], F32, 2)
        vg, vgk = B.ring("vg", [128, 512], F32, 2)
        gs, gsk = B.ring("gs", [128, 512], F32, 2)
        P.op("act", "activation", reads=["ps%d" % ba], writes=[ugk], out=ug[:], in_=B.ps[ba][:, :], func=AF.Gelu)
        P.op("act", "activation", reads=["ps%d" % bb], writes=[vgk], out=vg[:], in_=B.ps[bb][:, :], func=AF.Gelu)
        P.op("act", "activation", reads=["ps%d" % bc_], writes=[gsk], out=gs[:], in_=B.ps[bc_][:, :], func=AF.Silu)
        s6, s6k = B.ring("s6", [128, 6], F32, 2)
        mv, mvk = B.ring("mv", [128, 4], F32, 2)
        P.op("dve", "bn_stats", reads=[vgk], writes=[s6k], out=s6[:], in_=vg[:])
        P.op("dve", "bn_aggr", reads=[s6k], writes=[mvk], out=mv[:, 0:2], in_=s6[:])
        P.op("dve", "tensor_scalar", reads=[mvk], writes=[mvk], out=mv[:, 2:3], in0=mv[:, 1:2], scalar1=EPS, scalar2=None, op0=ALU.add)
        P.op("act", "activation", reads=[mvk], writes=[mvk], out=mv[:, 2:3], in_=mv[:, 2:3], func=AF.Sqrt)
        P.op("dve", "reciprocal", reads=[mvk], writes=[mvk], out=mv[:, 3:4], in_=mv[:, 2:3])
        P.op("dve", "tensor_scalar", reads=[vgk, mvk], writes=[vgk], out=vg[:], in0=vg[:], scalar1=mv[:, 0:1], scalar2=mv[:, 3:4],
             op0=ALU.subtract, op1=ALU.mult)
        P.op("pool", "tensor_tensor", reads=[vgk, lngk], writes=[vgk], out=vg[:], in0=vg[:], in1=lng[:], op=ALU.mult)
        vnb, vnbk = B.ring("vnb", [128, 512], BF16, 2)
        P.op("pool", "tensor_tensor", reads=[vgk, lnbk], writes=[vnbk], out=vnb[:], in0=vg[:], in1=lnb[:], op=ALU.add)
        bm = B.bank()
        for g in range(8):
            P.op("pe", "matmul", reads=["wsT", vnbk], writes=["ps%d" % bm], out=B.ps[bm][:, 64 * g:64 * (g + 1)], lhsT=wsT[:, g, :],
                 rhs=vnb[:, 64 * g:64 * (g + 1)], start=(g == 0), stop=(g == 7), skip_group_check=True)
        for g in range(8):
            P.op("dve", "scalar_tensor_tensor", reads=["ps%d" % bm, "bsT", ugk], writes=[ugk], out=ug[:, 64 * g:64 * (g + 1)],
                 in0=B.ps[bm][:, 64 * g:64 * (g + 1)], scalar=bsT[:, g:g + 1], in1=ug[:, 64 * g:64 * (g + 1)], op0=ALU.add, op1=ALU.mult)
        yc, yck = B.ring("ycb", [128, 512], BF16, 2)
        P.op("pool", "tensor_tensor", reads=[ugk, gsk], writes=[yck], out=yc[:], in0=ug[:], in1=gs[:], op=ALU.mult)
        transpose_to_fm(B, yc, yck, gcT, "gcT", t0, idb)
    B.release(m0)
    build_s3_attn(B, locals())
    build_s3_merge(B, locals())
    return B.finish()


def build_s3_attn(B, L):
    P = B.P
    hT, idb, gaT = L["hT"], L["idb"], L["gaT"]
    w_in, wq_perm, cosT, sinT, kt_all, v_all = L["w_in"], L["wq_perm"], L["cosT"], L["sinT"], L["kt_all"], L["v_all"]
    m0 = B.mark()
    lamb, lambk = bcast_row(B, L["da_lam"], 256, "lam")
    lib, libk = bcast_row(B, L["lam_init"], 1, "li")
    gsub, gsubk = bcast_row(B, L["da_subln"], 128, "gsub")
    lt = B.sb("lamtmp", [128, 136], F32)
    P.op("dve", "tensor_tensor", reads=[lambk], writes=["lamtmp"], out=lt[:, 0:64], in0=lamb[:, 0:64], in1=lamb[:, 64:128], op=ALU.mult)
    P.op("dve", "tensor_tensor", reads=[lambk], writes=["lamtmp"], out=lt[:, 64:128], in0=lamb[:, 128:192], in1=lamb[:, 192:256], op=ALU.mult)
    P.op("dve", "tensor_reduce", reads=["lamtmp"], writes=["lamtmp"], out=lt[:, 128:129], in_=lt[:, 0:64], axis=AX.X, op=ALU.add)
    P.op("dve", "tensor_reduce", reads=["lamtmp"], writes=["lamtmp"], out=lt[:, 129:130], in_=lt[:, 64:128], axis=AX.X, op=ALU.add)
    P.op("act", "activation", reads=["lamtmp"], writes=["lamtmp"], out=lt[:, 130:132], in_=lt[:, 128:130], func=AF.Exp)
    P.op("dve", "tensor_tensor", reads=["lamtmp"], writes=["lamtmp"], out=lt[:, 132:133], in0=lt[:, 131:132], in1=lt[:, 130:131], op=ALU.subtract)
    P.op("dve", "tensor_tensor", reads=["lamtmp", libk], writes=["lamtmp"], out=lt[:, 133:134], in0=lt[:, 132:133], in1=lib[:, 0:1], op=ALU.subtract)
    neglam = lt[:, 133:134]
    P.op("dve", "tensor_scalar", reads=[libk], writes=["lamtmp"], out=lt[:, 134:135], in0=lib[:, 0:1], scalar1=-1.0, scalar2=1.0,
         op0=ALU.mult, op1=ALU.add)
    P.op("dve", "tensor_scalar", reads=[gsubk, "lamtmp"], writes=[gsubk], out=gsub[:], in0=gsub[:], scalar1=lt[:, 134:135], scalar2=None,
         op0=ALU.mult)
    QT = B.sb("QT", [128, 4, NTOK3], BF16)
    kcT = B.sb("kcT", [128, 4, CTX], BF16)
    vcx = B.sb("vcx", [128, 2, 4, 129], BF16)
    ya = B.sb("ya_tm", [128, NTOK3 // 128, 512], BF16)
    P.op("pool", "memset", writes=["vcx"], ap=vcx[:], constant=1.0)
    m1 = B.mark()
    cs = B.sb("cosT", [128, TOK], F32)
    sn = B.sb("sinT", [128, TOK], F32)
    P.dma("sp", writes=["cosT"], out=cs[:], in_=cosT[:, :])
    P.dma("sp", writes=["sinT"], out=sn[:], in_=sinT[:, :])
    wt, wk = wblock(B, w_in, COL_Q, 512, "wq")
    wp, wpk = wblock(B, wq_perm, 0, 512, "wq")
    for h in range(4):
        for j in range(TOK // 512):
            b1 = B.bank()
            proj_fm(B, wt, wk, 128 * h, hT, "hT", 1 + 512 * j, 512, b1)
            b2 = B.bank()
            proj_fm(B, wp, wpk, 128 * h, hT, "hT", 1 + 512 * j, 512, b2)
            t1, t1k = B.ring("rtmp1", [128, 512], F32, 2)
            t2, t2k = B.ring("rtmp2", [128, 512], F32, 2)
            P.op("dve", "tensor_tensor", reads=["ps%d" % b1, "cosT"], writes=[t1k], out=t1[:], in0=B.ps[b1][:, :],
                 in1=cs[:, 512 * j:512 * (j + 1)], op=ALU.mult)
            P.op("dve", "tensor_tensor", reads=["ps%d" % b2, "sinT"], writes=[t2k], out=t2[:], in0=B.ps[b2][:, :],
                 in1=sn[:, 512 * j:512 * (j + 1)], op=ALU.mult)
            P.op("pool", "tensor_tensor", reads=[t1k, t2k], writes=["QT"], out=QT[:, h, 512 * j:512 * (j + 1)], in0=t1[:], in1=t2[:], op=ALU.add)
        b1 = B.bank()
        proj_fm(B, wt, wk, 128 * h, hT, "hT", hcol(TOK), CTX, b1)
        P.op("act", "activation", reads=["ps%d" % b1], writes=["QT"], out=QT[:, h, TOK:NTOK3], in_=B.ps[b1][:, 0:CTX], func=AF.Identity)
    wt, wk = wblock(B, w_in, COL_K, 512, "wq")
    for h in range(4):
        b1 = B.bank()
        proj_fm(B, wt, wk, 128 * h, hT, "hT", hcol(TOK), CTX, b1)
        P.op("act", "activation", reads=["ps%d" % b1], writes=["kcT"], out=kcT[:, h, :], in_=B.ps[b1][:, 0:CTX], func=AF.Identity)
    wt, wk = wblock(B, w_in, COL_V, 512, "wq")
    for i in range(2):
        b1 = B.bank()
        proj_tm(B, wt, wk, 0, 512, hT, "hT", hcol(TOK + 128 * i), 128, b1)
        P.op("act", "activation", reads=["ps%d" % b1], writes=["vcx"], out=vcx[:, i, :, 0:128],
             in_=B.ps[b1][:, :].rearrange("p (h d) -> p h d", d=128), func=AF.Identity)
    B.release(m1)
    m2 = B.mark()
    kth = B.sb("kth", [128, SEQ], BF16)
    vh = B.sb("vh", [128, SEQ // 128, 129], BF16)
    P.op("pool", "memset", writes=["vh"], ap=vh[:], constant=1.0)
    vview = v_all.rearrange("(t p) c -> p t c", p=128) if v_all is not None else None
    scnt = 0
    for h in range(4):
        P.dma("sp", reads=L.get("kt_keys", []), writes=["kth"], out=L.get("kth_out", lambda k: k[:])(kth), in_=kt_all[h])
        if vview is not None:
            P.dma("sp", reads=L.get("v_keys", []), writes=["vh"], out=vh[:, :, 0:128], in_=vview[:, :, 128 * h:128 * (h + 1)])
        else:
            for k in range(2):
                for r in range(4):
                    P.dma("sp", reads=L.get("v_keys", []), writes=["vh"], out=vh[:, 16 * r + 8 * k:16 * r + 8 * k + 8, 0:128],
                          in_=L["v_gk"][k].rearrange("(r t p) c -> r p t c", r=4, p=128)[r][:, :, 128 * h:128 * (h + 1)])
        for (t0, n) in CHUNKS:
            nq = n // 128
            latent = t0 < TOK
            keys = ([("l", k) for k in range(SEQ // 128)] if latent else []) + [("c", 0), ("c", 1)]
            om, omk = B.ring("om", [128, 2, 4, 128], F32, 2)
            for m in range(2):
                for ki, (kind, kt) in enumerate(keys):
                    bS = 4 + (scnt % 4)
                    scnt += 1
                    if kind == "l":
                        lk, lkk = kth[64 * m:64 * (m + 1), 128 * kt:128 * (kt + 1)], "kth"
                        vt, vtk = vh[:, kt, :], "vh"
                    else:
                        lk, lkk = kcT[64 * m:64 * (m + 1), h, 128 * kt:128 * (kt + 1)], "kcT"
                        vt, vtk = vcx[:, kt, h, :], "vcx"
                    P.op("pe", "matmul", reads=[lkk, "QT"], writes=["ps%d" % bS], out=B.ps[bS][:, 0:n], lhsT=lk,
                         rhs=QT[64 * m:64 * (m + 1), h, t0:t0 + n], start=True, stop=True)
                    pt, ptk = B.ring("pt", [128, 512], BF16, 3)
                    P.op("act", "activation", reads=["ps%d" % bS], writes=[ptk], out=pt[:, 0:n], in_=B.ps[bS][:, 0:n], func=AF.Exp,
                         scale=0.125)
                    for qt in range(nq):
                        P.op("pe", "matmul", reads=[ptk, vtk], writes=["ps%d" % qt], out=B.ps[qt][:, 0:129],
                             lhsT=pt[:, 128 * qt:128 * (qt + 1)], rhs=vt, start=(ki == 0), stop=(ki == len(keys) - 1))
                for qt in range(nq):
                    rc, rck = B.ring("rc", [128, 2], F32, 4)
                    P.op("dve", "reciprocal", reads=["ps%d" % qt], writes=[rck], out=rc[:, 0:1], in_=B.ps[qt][:, 128:129])
                    if m == 1:
                        P.op("dve", "tensor_tensor", reads=[rck, "lamtmp"], writes=[rck], out=rc[:, 0:1], in0=rc[:, 0:1], in1=neglam, op=ALU.mult)
                    P.op("dve", "tensor_scalar", reads=["ps%d" % qt, rck], writes=[omk], out=om[:, m, qt, :], in0=B.ps[qt][:, 0:128],
                         scalar1=rc[:, 0:1], scalar2=None, op0=ALU.mult)
            P.op("pool", "tensor_tensor", reads=[omk], writes=[omk], out=om[:, 0, 0:nq, :], in0=om[:, 0, 0:nq, :], in1=om[:, 1, 0:nq, :], op=ALU.add)
            for qt in range(nq):
                st_, sk = B.ring("ast", [128, 4], F32, 4)
                jk_, jkk = B.ring("ajunk", [128, 128], F32, 2)
                P.op("act", "activation", reads=[omk], writes=[jkk, sk], out=jk_[:], in_=om[:, 0, qt, :], func=AF.Square, accum_out=st_[:, 0:1])
                rstd_from_ssq(B, st_, sk, 128, 0, 1, 2, 1.0 / 128)
                P.op("dve", "scalar_tensor_tensor", reads=[omk, sk, gsubk], writes=["ya_tm"], out=ya[:, t0 // 128 + qt, 128 * h:128 * (h + 1)],
                     in0=om[:, 0, qt, :], scalar=st_[:, 2:3], in1=gsub[:], op0=ALU.mult, op1=ALU.mult)
    B.release(m2)
    wt, wk = wblock(B, w_in, COL_GA, 512, "wq")
    for t0 in TILES:
        bk = B.bank()
        proj_tm(B, wt, wk, 0, 512, hT, "hT", hcol(t0), 128, bk)
        sg, sgk = B.ring("sga", [128, 512], F32, 2)
        P.op("act", "activation", reads=["ps%d" % bk], writes=[sgk], out=sg[:], in_=B.ps[bk][:, :], func=AF.Silu)
        yb_, ybk_ = B.ring("yab", [128, 512], BF16, 2)
        P.op("dve", "tensor_tensor", reads=[sgk, "ya_tm"], writes=[ybk_], out=yb_[:], in0=sg[:], in1=ya[:, t0 // 128, :], op=ALU.mult)
        transpose_to_fm(B, yb_, ybk_, gaT, "gaT", t0, idb)
    B.release(m0)


def build_s3_merge(B, L):
    P = B.P
    hT, w_in, w_br, w_out = L["hT"], L["w_in"], L["w_br"], L["w_out"]
    gT = [(L["gaT"], "gaT"), (L["gbT"], "gbT"), (L["gcT"], "gcT")]
    xs, xc, x_new, xc_new = L["xs"], L["xc"], L["x_new"], L["xc_new"]
    m0 = B.mark()
    wb = []
    for i in range(3):
        t = B.sb("wbr%d" % i, [128, 4, D], BF16)
        for j in range(4):
            load_cast(B, t[:, :, 256 * j:256 * (j + 1)], "wbr%d" % i, w_br[i].rearrange("(ct p) d -> p ct d", p=128)[:, :, 256 * j:256 * (j + 1)], (4, 256))
        wb.append(t)
    wo = B.sb("wo", [128, 8, D], BF16)
    for j in range(8):
        load_cast(B, wo[:, :, 128 * j:128 * (j + 1)], "wo", w_out.rearrange("(k p) d -> p k d", p=128)[:, :, 128 * j:128 * (j + 1)], (8, 128))
    for (t0, n) in L.get("chunks", CHUNKS):
        latent = t0 < TOK
        G, Gk = (L["Gbc"], "bcG") if latent else (L["Gcbc"], "bcGc")
        ob, obk = B.ring("outTb", [128, 8, 512], BF16, 1)
        for dt in range(8):
            acc, acck = B.ring("macc", [128, 512], F32, 2)
            for i in range(3):
                B.cast_eng = "act" if (dt * 3 + i) % 2 == 0 else "pool"
                wt, wk = wblock(B, w_in, COL_MG + 1024 * i + 128 * dt, 128, "wmg", 3, 128)
                B.cast_eng = "pool"
                bA = B.bank()
                proj_fm(B, wt, wk, 0, hT, "hT", hcol(t0), n, bA)
                bB = B.bank()
                for ct in range(4):
                    P.op("pe", "matmul", reads=["wbr%d" % i, gT[i][1]], writes=["ps%d" % bB], out=B.ps[bB][:, 0:n],
                         lhsT=wb[i][:, ct, 128 * dt:128 * (dt + 1)], rhs=gT[i][0][:, ct, t0:t0 + n], start=(ct == 0), stop=(ct == 3))
                sg, sgk = B.ring("msg", [128, 512], F32, 2)
                P.op("act", "activation", reads=["ps%d" % bA], writes=[sgk], out=sg[:, 0:n], in_=B.ps[bA][:, 0:n], func=AF.Sigmoid)
                if i == 0:
                    P.op("dve", "tensor_tensor", reads=[sgk, "ps%d" % bB], writes=[acck], out=acc[:, 0:n], in0=sg[:, 0:n], in1=B.ps[bB][:, 0:n], op=ALU.mult)
                else:
                    P.op("dve", "tensor_tensor", reads=[sgk, "ps%d" % bB], writes=[sgk], out=sg[:, 0:n], in0=sg[:, 0:n], in1=B.ps[bB][:, 0:n], op=ALU.mult)
                    if i == 1:
                        P.op("pool", "tensor_tensor", reads=[sgk, acck], writes=[acck], out=acc[:, 0:n], in0=acc[:, 0:n], in1=sg[:, 0:n], op=ALU.add)
                    else:
                        P.op("pool", "tensor_tensor", reads=[sgk, acck], writes=[obk], out=ob[:, dt, 0:n], in0=acc[:, 0:n], in1=sg[:, 0:n], op=ALU.add)
        for q in range(n // 128):
            tok = t0 + 128 * q
            bks = (B.bank(), B.bank())
            for half in range(2):
                for k in range(8):
                    P.op("pe", "matmul", reads=[obk, "wo"], writes=["ps%d" % bks[half]], out=B.ps[bks[half]][:, :], lhsT=ob[:, k, 128 * q:128 * (q + 1)],
                         rhs=wo[:, k, 512 * half:512 * (half + 1)], start=(k == 0), stop=(k == 7))
            st_, sk = B.ring("mst", [128, 8], F32, 4)
            for half in range(2):
                jk_, jkk = B.ring("mjunk", [128, 512], BF16, 2)
                P.op("act", "activation", reads=["ps%d" % bks[half]], writes=[jkk, sk], out=jk_[:], in_=B.ps[bks[half]][:, :], func=AF.Square,
                     accum_out=st_[:, half:half + 1])
            P.op("dve", "tensor_tensor", reads=[sk], writes=[sk], out=st_[:, 2:3], in0=st_[:, 0:1], in1=st_[:, 1:2], op=ALU.add)
            rstd_from_ssq(B, st_, sk, 128, 2, 3, 4, 1.0 / D)
            xt, xk = B.ring("mxt", [128, D], F32, 2)
            src = xs[1 + tok:1 + tok + 128, :] if latent else xc[tok - TOK:tok - TOK + 128, :]
            P.dma("sp", reads=L.get("x_keys", []), writes=[xk], out=xt[:], in_=src)
            ot, otk = B.ring("mot", [128, D], F32, 2)
            for half in range(2):
                P.op("dve", "scalar_tensor_tensor", reads=["ps%d" % bks[half], sk, Gk], writes=[otk], out=ot[:, 512 * half:512 * (half + 1)],
                     in0=B.ps[bks[half]][:, :], scalar=st_[:, 4:5], in1=G[:, 512 * half:512 * (half + 1)], op0=ALU.mult, op1=ALU.mult)
            P.op("pool", "tensor_tensor", reads=[otk, xk], writes=[otk], out=ot[:], in0=ot[:], in1=xt[:], op=ALU.add)
            if latent:
                P.dma("sp", reads=[otk], writes=[L.get("x_new_key", "x_new")], out=x_new[tok:tok + 128, :], in_=ot[:])
                if L.get("hal_src") is not None and tok == 0:
                    P.dma("sp", reads=[otk], writes=["hal_src"], out=L["hal_src"][0:1, :], in_=ot[0:1, :])
                if L.get("hal_src") is not None and tok == TOK - 128:
                    P.dma("sp", reads=[otk], writes=["hal_src"], out=L["hal_src"][1:2, :], in_=ot[127:128, :])
            elif xc_new is not None:
                P.dma("sp", reads=[otk], writes=[L.get("xc_new_key", "xc_new")], out=xc_new[tok - TOK:tok - TOK + 128, :], in_=ot[:])
    B.release(m0)


def ctx_tables():
    L = CTX
    i = np.arange(512)
    pos = np.minimum(np.abs(i - 255), L - 1)
    t_lin = np.linspace(0.0, 1.0, L, dtype=np.float32)
    wpos = ((2.0 * math.pi / L) * np.arange(L, dtype=np.float32)).astype(np.float32)
    bands = np.linspace(1e-4, 16 - 1, 16, dtype=np.float32)
    zfull = np.concatenate([t_lin[:, None], np.cos(bands[None, :] * wpos[:, None]), -np.sin(bands[None, :] * wpos[:, None])],
                           axis=-1).astype(np.float32)
    zposc = np.ascontiguousarray(zfull[pos].T)
    mn = math.log(1e-2) / 1.5
    mx = math.log(1e-2) / 0.3
    deltas = np.abs(np.linspace(mn, mx, 512, dtype=np.float32))
    winc = np.exp(-t_lin[pos][None, :] * deltas[:, None]).astype(np.float32)
    winc[:, 511] = 0.0
    return zposc, winc


def run_s3(x, xc, l, inp, kt_all, v_all, u_all, y_all, ssq):
    nc = get_nc("s3")
    cosT, sinT = rope_tables()
    zposc, winc = ctx_tables()
    w_in = inp["w_in"][l]
    wqp = perm_cols(w_in[:, COL_Q:COL_Q + 512])
    lam_init = np.array([0.8 - 0.6 * math.exp(-0.3 * l)], np.float32)
    hyc = np.ascontiguousarray(np.stack([ssq.reshape(4, 128).T, inp["hy_bias"][l].reshape(4, 128).T], axis=-1)).astype(np.float32)
    in_maps = []
    for core in range(8):
        b, r = divmod(core, 4)
        t0 = r * TOK
        xs = np.zeros((TOK + 2, D), np.float32)
        lo, hi = max(t0 - 1, 0), min(t0 + TOK + 1, SEQ)
        xs[lo - (t0 - 1): hi - (t0 - 1)] = x[b, lo:hi]
        hmask = np.array([[0.0 if t0 == 0 else 1.0], [0.0 if t0 + TOK == SEQ else 1.0]], np.float32)
        in_maps.append({
            "xs": xs, "xc": np.ascontiguousarray(xc[b]), "c": inp["c"][b], "c_ctx": inp["c_ctx"], "ada_w": inp["ada_w"][l], "ada_b": inp["ada_b"][l],
            "npre": inp["norm_pre"][l], "npost": inp["norm_post"][l], "w_in": w_in, "wq_perm": wqp, "hmask": hmask,
            "cosT": np.ascontiguousarray(cosT[:, t0:t0 + TOK]), "sinT": np.ascontiguousarray(sinT[:, t0:t0 + TOK]),
            "kt_all": kt_all[b], "v_all": v_all[b], "u_own": np.ascontiguousarray(u_all[b][:, t0:t0 + TOK]),
            "y_own": np.ascontiguousarray(y_all[b][:, t0:t0 + TOK]), "hyc": hyc, "hy_w": inp["hy_short_w"][l], "hy_b": inp["hy_short_b"][l],
            "da_lam": np.ascontiguousarray(inp["da_lambda"][l].reshape(256)), "lam_init": lam_init, "da_subln": inp["da_subln"][l],
            "hw1": inp["hy_f_w1"][l], "hb1": inp["hy_f_b1"][l], "hw2": inp["hy_f_w2"][l], "hb2": inp["hy_f_b2"][l],
            "hw3": inp["hy_f_w3"][l], "hfreq": inp["hy_f_freq"][l], "zposc": zposc, "winc": winc,
            "gm_g": inp["gm_ln_g"][l], "gm_b": inp["gm_ln_b"][l], "gm_ws": inp["gm_ws"][l], "gm_bs": inp["gm_bs"][l],
            "w_br": inp["w_branch"][l], "w_out": inp["w_out"][l],
        })
    res = run_bass_kernel_spmd(nc, in_maps, core_ids=list(range(8))).results
    x_new = np.stack([np.concatenate([res[4 * b + r]["x_new"] for r in range(4)], 0) for b in range(2)], 0)
    xc_new = np.stack([res[4 * b]["xc_new"] for b in range(2)], 0)
    return x_new, xc_new


def gather_s1(res):
    kt = np.stack([np.concatenate([np.asarray(res[4 * b + r]["o_kt"]) for r in range(4)], axis=2) for b in range(2)], 0)
    v = np.stack([np.concatenate([np.asarray(res[4 * b + r]["o_v"]) for r in range(4)], axis=0) for b in range(2)], 0)
    u = np.stack([np.concatenate([np.asarray(res[4 * b + r]["o_u"]) for r in range(4)], axis=1) for b in range(2)], 0)
    return np.ascontiguousarray(kt), np.ascontiguousarray(v), np.ascontiguousarray(u)


def kernel(**inputs):
    inp = {k: np.asarray(v) for k, v in inputs.items()}
    x = inp["x"].astype(np.float32, copy=False)
    xc = inp["ctx"].astype(np.float32, copy=False)
    for l in range(DEPTH):
        res1 = run_s1(x, inp["c"], l, inp["ada_w"], inp["ada_b"], inp["norm_pre"], inp["norm_post"], inp["w_in"], inp["hy_short_w"],
                      inp["hy_short_b"])
        kt_all, v_all, u_all = gather_s1(res1)
        y_all, ssq = run_s2(u_all, l, inp["hy_f_w1"], inp["hy_f_b1"], inp["hy_f_w2"], inp["hy_f_b2"], inp["hy_f_w3"], inp["hy_f_freq"])
        x, xc = run_s3(x, xc, l, inp, kt_all, v_all, u_all, y_all, ssq)
    return x.astype(np.float32)


RG4 = [[0, 1, 2, 3], [4, 5, 6, 7]]
FUSED_DEPTH = DEPTH
EXTRA_CC = 0
CGF = 16


_RANK_CACHE = {}


def rank_of(e):
    k = id(e)
    if k not in _RANK_CACHE:
        _RANK_CACHE[k] = (e, e.partition_id() % 4)
    return _RANK_CACHE[k][1]


def rank_nb(e, d):
    k = (id(e), d)
    if k not in _RANK_CACHE:
        _RANK_CACHE[k] = (e, (rank_of(e) + d) % 4)
    return _RANK_CACHE[k][1]


def f_products(B, E, l):
    P = B.P
    hT, w_in = E["hT"], E["w_in"][l]
    m0 = B.mark()
    cs = B.sb("cosT", [128, TOK], F32)
    sn = B.sb("sinT", [128, TOK], F32)
    P.dma("sp", writes=["cosT"], out=cs[:], in_=E["cosT"][:, :])
    P.dma("sp", writes=["sinT"], out=sn[:], in_=E["sinT"][:, :])
    wt, wk = wblock(B, w_in, COL_K, 512, "wblkK", 1)
    wp, wpk = wblock(B, E["wk_perm"][l], 0, 512, "wblkP", 1)
    for h in range(4):
        ko, kk = B.ring("kout", [128, TOK], BF16, 2)
        for j in range(TOK // 512):
            b1 = B.bank()
            proj_fm(B, wt, wk, 128 * h, hT, "hT", 1 + 512 * j, 512, b1)
            b2 = B.bank()
            proj_fm(B, wp, wpk, 128 * h, hT, "hT", 1 + 512 * j, 512, b2)
            t1, t1k = B.ring("rtmp1", [128, 512], F32, 2)
            t2, t2k = B.ring("rtmp2", [128, 512], F32, 2)
            P.op("dve", "tensor_tensor", reads=["ps%d" % b1, "cosT"], writes=[t1k], out=t1[:], in0=B.ps[b1][:, :],
                 in1=cs[:, 512 * j:512 * (j + 1)], op=ALU.mult)
            P.op("dve", "tensor_tensor", reads=["ps%d" % b2, "sinT"], writes=[t2k], out=t2[:], in0=B.ps[b2][:, :],
                 in1=sn[:, 512 * j:512 * (j + 1)], op=ALU.mult)
            P.op("pool", "tensor_tensor", reads=[t1k, t2k], writes=[kk], out=ko[:, 512 * j:512 * (j + 1)], in0=t1[:], in1=t2[:], op=ALU.add)
        P.dma("sp", reads=[kk], writes=["kt_src"], out=E["kt_src"][h // 2][128 * (h % 2):128 * (h % 2 + 1), :], in_=ko[:])
    B.release(m0)
    wt, wk = wblock(B, w_in, COL_V, 512, "wblkK", 1)
    for i in range(NT):
        bk = B.bank()
        proj_tm(B, wt, wk, 0, 512, hT, "hT", 1 + 128 * i, 128, bk)
        vo, vk = B.ring("vout", [128, 512], BF16, 3)
        P.op("act", "activation", reads=["ps%d" % bk], writes=[vk], out=vo[:], in_=B.ps[bk][:, :], func=AF.Identity)
        P.dma("sp", reads=[vk], writes=["v_src"], out=E["v_src"][i // 8][128 * (i % 8):128 * (i % 8 + 1), :], in_=vo[:])
    B.release(m0)
    scw = load_shortconv(B, E["hy_w"][l], E["hy_b"][l])
    for ci in range(4):
        x1s, x1k = B.ring("x1s", [128, TOK], F32, 2)
        vs, vsk = B.ring("vs", [128, TOK], F32, 2)
        hy_conv_tile(B, w_in, 4 + ci, hT, "hT", [(1, TOK, 0)], scw, x1s, x1k)
        hy_conv_tile(B, w_in, 8 + ci, hT, "hT", [(1, TOK, 0)], scw, vs, vsk)
        P.op("dve", "tensor_tensor", reads=[x1k, vsk], writes=[x1k], out=x1s[:], in0=x1s[:], in1=vs[:], op=ALU.mult)
        P.dma("sp", reads=[x1k], writes=["u_src"], out=E["u_src"][ci], in_=x1s[:])
    B.release(m0)
    par = l % 2
    for k in range(2):
        P.cc(reads=["kt_src"], writes=["kt_g%d" % par], kind="AllGather", op=ALU.bypass, replica_groups=RG4, ins=[E["kt_src"][k].opt()],
             outs=[E["kt_g"][par][k].opt()])
    for k in range(2):
        P.cc(reads=["v_src"], writes=["v_g%d" % par], kind="AllGather", op=ALU.bypass, replica_groups=RG4, ins=[E["v_src"][k].opt()],
             outs=[E["v_g"][par][k].opt()])
    for k in range(4):
        P.cc(reads=["u_src"], writes=["u_g"], kind="AllGather", op=ALU.bypass, replica_groups=RG4, ins=[E["u_src"][k].opt()],
             outs=[E["u_g"][par][k].opt()])


def f_conv(B, E, l):
    P = B.P
    par = l % 2
    dft, tw = E["dft"], E["tw"]
    kscr = E["kscr"]
    m0 = B.mark()
    fw = load_filter_weights(B, E["hw1"][l], E["hb1"][l], E["hw2"][l], E["hb2"][l], E["hfreq"][l])
    w3 = B.sb("fw3", [64, 256], F32)
    P.dma("sp", writes=["fw"], out=w3[:], in_=E["hw3q"][l])
    ssqp = B.sb("ssqp", [128, 33], F32)
    P.op("dve", "memset", writes=["ssqp"], ap=ssqp[:], constant=0.0)
    m1 = B.mark()
    for ch in range(NFFT // 512):
        zt, zk = B.ring("zt", [33, 512], F32, 2)
        P.dma("sp", writes=[zk], out=zt[:], in_=E["zpos"][:, ch * 512:(ch + 1) * 512])
        wn, wnk = B.ring("wn", [128, 512], F32, 2)
        P.dma("sp", writes=[wnk], out=wn[:], in_=E["win"][:, ch * 512:(ch + 1) * 512])
        h2, h2k = filter_mlp_chunk(B, fw, zt, zk, 512)
        bk = B.bank()
        half = 0 if ch * 512 < SEQ else 1
        P.op("pe", "matmul", reads=["fw", h2k], writes=["ps%d" % bk], out=B.ps[bk][:, :], lhsT=w3[0:64, 128 * half:128 * (half + 1)],
             rhs=h2[0:64, :], start=True, stop=True)
        kc, kck = B.ring("kc", [128, 512], F32, 2)
        P.op("dve", "tensor_tensor", reads=["ps%d" % bk, wnk], writes=[kck], out=kc[:], in0=B.ps[bk][:, :], in1=wn[:], op=ALU.mult)
        jk_, jkk = B.ring("kjunk", [128, 512], F32, 2)
        P.op("act", "activation", reads=[kck], writes=[jkk, "ssqp"], out=jk_[:], in_=kc[:], func=AF.Square, accum_out=ssqp[:, ch:ch + 1])
        P.dma("sp", reads=[kck], writes=["kscr"], out=kscr[:, ch * 512:(ch + 1) * 512], in_=kc[:])
    P.op("dve", "tensor_reduce", reads=["ssqp"], writes=["ssqp"], out=ssqp[:, 32:33], in_=ssqp[:, 0:32], axis=AX.X, op=ALU.add)
    P.dma("sp", reads=["ssqp"], writes=["ssq_src"], out=E["ssq_src"][:, 0:1], in_=ssqp[:, 32:33],
          allow_slow_non_contiguous=True)
    B.release(m1)
    kview = kscr.rearrange("c (p j) -> p c j", j=128)
    ugq = E["u_g"][par].rearrange("q (r c) t -> q r c t", r=4)
    P.dma("sp", reads=["u_g"], writes=["u_my"], out=E["u_my"], in_=(lambda e: ugq[rank_of(e)]))
    ugv = E["u_my"].rearrange("r c (pp j) -> r pp c j", j=128)
    yview = E["y_src"].rearrange("r c (pp j) -> r pp c j", j=128)
    for g in range(128 // CGF):
        cs0 = g * CGF
        kd, kdk = B.ring("kd", [128, CGF, 128], F32, 1)
        ud, udk = B.ring("ud", [64, CGF, 128], F32, 1)
        P.dma("sp", reads=["kscr"], writes=[kdk], out=kd[:], in_=kview[:, cs0:cs0 + CGF, :])
        for r in range(4):
            P.dma("sp", reads=["u_my"], writes=[udk], out=ud[16 * r:16 * (r + 1), :, :], in_=ugv[r][:, cs0:cs0 + CGF, :])
        KF, KFk = B.ring("KF", [128, CGF, 2, 128], F32, 1)
        Tc = tw[:, 0:128].unsqueeze(1).to_broadcast([128, 2, 128])
        Ts = tw[:, 128:256].unsqueeze(1).to_broadcast([128, 2, 128])

        def st1(src, srck, kdim, c0):
            bk = B.bank()
            for i in range(2):
                P.op("pe", "matmul", reads=[srck, "dft"], writes=["ps%d" % bk], out=B.ps[bk][:, 256 * i:256 * (i + 1)],
                     lhsT=src[0:kdim, c0 + i, :], rhs=dft[0:kdim, 256:512], start=True, stop=True)
            return bk

        def st2(bk):
            A = B.ps[bk][:, :].rearrange("p (c r k) -> p c r k", c=2, r=2)
            Are, Aim = A[:, :, 0, :], A[:, :, 1, :]
            tt, ttk = B.ring("fft_t", [128, 4, 2, 128], F32, 3)
            pk = "ps%d" % bk
            P.op("dve", "tensor_tensor", reads=[pk, "tw"], writes=[ttk], out=tt[:, 0], in0=Are, in1=Tc, op=ALU.mult)
            P.op("dve", "tensor_tensor", reads=[pk, "tw"], writes=[ttk], out=tt[:, 1], in0=Aim, in1=Ts, op=ALU.mult)
            P.op("dve", "tensor_tensor", reads=[pk, "tw"], writes=[ttk], out=tt[:, 2], in0=Aim, in1=Tc, op=ALU.mult)
            P.op("dve", "tensor_tensor", reads=[pk, "tw"], writes=[ttk], out=tt[:, 3], in0=Are, in1=Ts, op=ALU.mult)
            b1, b1k = B.ring("fft_b1", [128, 2, 2, 128], F32, 2)
            b2, b2k = B.ring("fft_b2", [128, 2, 2, 128], F32, 2)
            P.op("pool", "tensor_tensor", reads=[ttk], writes=[b1k], out=b1[:, :, 0, :], in0=tt[:, 0], in1=tt[:, 1], op=ALU.add)
            P.op("pool", "tensor_tensor", reads=[ttk], writes=[b1k], out=b1[:, :, 1, :], in0=tt[:, 2], in1=tt[:, 3], op=ALU.subtract)
            P.op("act", "activation", reads=[b1k], writes=[b2k], out=b2[:, :, 0, :], in_=b1[:, :, 1, :], func=AF.Identity)
            P.op("act", "activation", reads=[b1k], writes=[b2k], out=b2[:, :, 1, :], in_=b1[:, :, 0, :], func=AF.Identity, scale=-1.0)
            return (b1, b1k, b2, b2k)

        def st3(bb):
            b1, b1k, b2, b2k = bb
            bx = B.bank()
            P.op("pe", "matmul", reads=[b1k, "dft"], writes=["ps%d" % bx], out=B.ps[bx][:, :], lhsT=dft[:, 0:128],
                 rhs=b1[:].rearrange("p c r k -> p (c r k)"), start=True, stop=False)
            P.op("pe", "matmul", reads=[b2k, "dft"], writes=["ps%d" % bx], out=B.ps[bx][:, :], lhsT=dft[:, 128:256],
                 rhs=b2[:].rearrange("p c r k -> p (c r k)"), start=False, stop=True)
            return bx

        def st4f(bx, c0):
            P.op("act", "activation", reads=["ps%d" % bx], writes=[KFk], out=KF[:, c0:c0 + 2].rearrange("p c r k -> p (c r k)"),
                 in_=B.ps[bx][:, :], func=AF.Identity)

        def st4d(bx, c0):
            X = B.ps[bx][:, :].rearrange("p (c r k) -> p c r k", c=2, r=2)
            Xre, Xim = X[:, :, 0, :], X[:, :, 1, :]
            Kre, Kim = KF[:, c0:c0 + 2, 0, :], KF[:, c0:c0 + 2, 1, :]
            tt, ttk = B.ring("fft_t", [128, 4, 2, 128], F32, 3)
            pk = "ps%d" % bx
            P.op("dve", "tensor_tensor", reads=[pk, KFk], writes=[ttk], out=tt[:, 0], in0=Xre, in1=Kre, op=ALU.mult)
            P.op("dve", "tensor_tensor", reads=[pk, KFk], writes=[ttk], out=tt[:, 1], in0=Xim, in1=Kim, op=ALU.mult)
            P.op("dve", "tensor_tensor", reads=[pk, KFk], writes=[ttk], out=tt[:, 2], in0=Xre, in1=Kim, op=ALU.mult)
            P.op("dve", "tensor_tensor", reads=[pk, KFk], writes=[ttk], out=tt[:, 3], in0=Xim, in1=Kre, op=ALU.mult)
            Y, Yk = B.ring("Y", [128, 2, 2, 128], F32, 2)
            P.op("pool", "tensor_tensor", reads=[ttk], writes=[Yk], out=Y[:, :, 0, :], in0=tt[:, 0], in1=tt[:, 1], op=ALU.subtract)
            P.op("pool", "tensor_tensor", reads=[ttk], writes=[Yk], out=Y[:, :, 1, :], in0=tt[:, 2], in1=tt[:, 3], op=ALU.add)
            return (Y, Yk)

        def st5(yy):
            Y, Yk = yy
            bi = B.bank()
            for i in range(2):
                P.op("pe", "matmul", reads=[Yk, "dft"], writes=["ps%d" % bi], out=B.ps[bi][:, 256 * i:256 * (i + 1)],
                     lhsT=Y[:, i, 0, :], rhs=dft[:, 0:256], start=True, stop=False, skip_group_check=True)
                P.op("pe", "matmul", reads=[Yk, "dft"], writes=["ps%d" % bi], out=B.ps[bi][:, 256 * i:256 * (i + 1)],
                     lhsT=Y[:, i, 1, :], rhs=dft[:, 384:640], start=False, stop=True, skip_group_check=True)
            return bi

        def st6(bi, c0):
            Bm = B.ps[bi][:, :].rearrange("p (c r k) -> p c r k", c=2, r=2)
            Bre_p, Bim_p = Bm[:, :, 0, :], Bm[:, :, 1, :]
            t2, t2k = B.ring("fft_t", [128, 4, 2, 128], F32, 3)
            pk = "ps%d" % bi
            P.op("dve", "tensor_tensor", reads=[pk, "tw"], writes=[t2k], out=t2[:, 0], in0=Bre_p, in1=Tc, op=ALU.mult)
            P.op("dve", "tensor_tensor", reads=[pk, "tw"], writes=[t2k], out=t2[:, 1], in0=Bim_p, in1=Ts, op=ALU.mult)
            P.op("dve", "tensor_tensor", reads=[pk, "tw"], writes=[t2k], out=t2[:, 2], in0=Bre_p, in1=Ts, op=ALU.mult)
            P.op("dve", "tensor_tensor", reads=[pk, "tw"], writes=[t2k], out=t2[:, 3], in0=Bim_p, in1=Tc, op=ALU.mult)
            P.op("pool", "tensor_tensor", reads=[t2k], writes=[Brek], out=Bre[:, c0:c0 + 2, :], in0=t2[:, 0], in1=t2[:, 1], op=ALU.subtract)
            P.op("pool", "tensor_tensor", reads=[t2k], writes=[Bimk], out=Bim[:, c0:c0 + 2, :], in0=t2[:, 2], in1=t2[:, 3], op=ALU.add)

        pairs = list(range(0, CGF, 2))
        for q0 in range(0, len(pairs), 2):
            cs_ = pairs[q0:q0 + 2]
            bks = [st1(kd, kdk, 128, c0) for c0 in cs_]
            bbs = [st2(bk) for bk in bks]
            bxs = [st3(bb) for bb in bbs]
            for bx, c0 in zip(bxs, cs_):
                st4f(bx, c0)
        Bre, Brek = B.ring("Bre", [128, CGF, 128], F32, 1)
        Bim, Bimk = B.ring("Bim", [128, CGF, 128], F32, 1)
        for q0 in range(0, len(pairs), 2):
            cs_ = pairs[q0:q0 + 2]
            bks = [st1(ud, udk, 64, c0) for c0 in cs_]
            bbs = [st2(bk) for bk in bks]
            bxs = [st3(bb) for bb in bbs]
            yys = [st4d(bx, c0) for bx, c0 in zip(bxs, cs_)]
            bis = [st5(yy) for yy in yys]
            for bi, c0 in zip(bis, cs_):
                st6(bi, c0)
        yo, yok = B.ring("yo", [64, CGF, 128], F32, 1)
        for c0 in range(0, CGF, 4):
            bo = B.bank()
            P.op("pe", "matmul", reads=[Brek, "dft"], writes=["ps%d" % bo], out=B.ps[bo][0:64, :], lhsT=dft[:, 0:64],
                 rhs=Bre[:, c0:c0 + 4, :].rearrange("p c j -> p (c j)"), start=True, stop=False)
            P.op("pe", "matmul", reads=[Bimk, "dft"], writes=["ps%d" % bo], out=B.ps[bo][0:64, :], lhsT=dft[:, 384:448],
                 rhs=Bim[:, c0:c0 + 4, :].rearrange("p c j -> p (c j)"), start=False, stop=True)
            P.op("act", "activation", reads=["ps%d" % bo], writes=[yok], out=yo[:, c0:c0 + 4, :].rearrange("p c j -> p (c j)"),
                 in_=B.ps[bo][0:64, :], func=AF.Identity, scale=1.0 / NFFT)
        for r in range(4):
            P.dma("sp", reads=[yok], writes=["y_src"], out=yview[r][:, cs0:cs0 + CGF, :], in_=yo[16 * r:16 * (r + 1), :, :])
    B.release(m0)
    def issue_y_gather():
        for k in range(4):
            P.cc(reads=["y_src"], writes=["y_g"], kind="AllGather", op=ALU.bypass, replica_groups=RG4, ins=[E["y_src"][k].opt()],
                 outs=[E["y_g"][par][k].opt()])
        P.cc(reads=["ssq_src"], writes=["ssq_g%d" % par], kind="AllGather", op=ALU.bypass, replica_groups=RG4, ins=[E["ssq_src"].opt()],
             outs=[E["ssq_g"][par].opt()])
    return issue_y_gather


def f_hyena_gate(B, E, l, do_ctx=True):
    P = B.P
    par = l % 2
    hT, w_in, gbT = E["hT"], E["w_in"][l], E["gbT"]
    m0 = B.mark()
    scw = load_shortconv(B, E["hy_w"][l], E["hy_b"][l])
    hyct = B.sb("hyct", [128, 4, 4], F32)
    P.dma("sp", reads=["ssq_g%d" % par], writes=["hyct"], out=hyct[:, :, 0:1],
          in_=E["ssq_g"][par][:, 0:1].rearrange("(ci p) o -> p ci o", p=128), allow_slow_non_contiguous=True)
    P.dma("sp", writes=["hyct"], out=hyct[:, :, 1:2], in_=E["hyb"][l].rearrange("p (ci o) -> p ci o", o=1), allow_slow_non_contiguous=True)
    fw = load_filter_weights(B, E["hw1"][l], E["hb1"][l], E["hw2"][l], E["hb2"][l], E["hfreq"][l])
    w3 = B.sb("fw3", [64, 1024], F32)
    P.dma("sp", writes=["fw"], out=w3[:], in_=E["hw3"][l])
    zt = B.sb("ztc", [33, 512], F32)
    P.dma("sp", writes=["ztc"], out=zt[:], in_=E["zposc"][:, :])
    hid2, hid2k = filter_mlp_chunk(B, fw, zt, "ztc", 512)
    hid2p = B.sb("hid2p", [64, 512], F32)
    P.op("pool", "tensor_copy", reads=[hid2k], writes=["hid2p"], out=hid2p[:], in_=hid2[0:64, :])
    segs_all = [(1, TOK, 0), (TOK + 3, CTX, TOK)]
    segs_ctx = [(TOK + 3, CTX, 0)]
    ygr = E["y_g"][par]
    P.dma("sp", reads=["y_g"], writes=["y_my"], out=E["y_my"], in_=(lambda e: ygr[rank_of(e)]))
    if do_ctx:
        idf = E["idf"]
        uc_all = B.sb("uc_all", [128, 4, CTX], F32)
        kc_all = B.sb("kc_all", [128, 4, 512], F32)
        ycv = B.sb("ycv_all", [128, 4, CTX], F32)
        cst = B.sb("cst_all", [128, 4, 4], F32)
        mA = B.mark()
        for ci in range(4):
            x1c, x1ck = B.ring("x1c", [128, CTX], F32, 1)
            vc_, vck = B.ring("vcc", [128, CTX], F32, 1)
            hy_conv_tile(B, w_in, 4 + ci, hT, "hT", segs_ctx, scw, x1c, x1ck)
            hy_conv_tile(B, w_in, 8 + ci, hT, "hT", segs_ctx, scw, vc_, vck)
            P.op("dve", "tensor_tensor", reads=[x1ck, vck], writes=["uc_all"], out=uc_all[:, ci, :], in0=x1c[:], in1=vc_[:], op=ALU.mult)
            wnc, wnck = B.ring("wnc", [128, 512], F32, 1)
            P.dma("sp", writes=[wnck], out=wnc[:], in_=E["winc"][128 * ci:128 * (ci + 1), :])
            bk = B.bank()
            P.op("pe", "matmul", reads=["fw", "hid2p"], writes=["ps%d" % bk], out=B.ps[bk][:, 0:255], lhsT=w3[0:64, 512 + 128 * ci:512 + 128 * (ci + 1)],
                 rhs=hid2p[0:64, 0:255], start=True, stop=True)
            P.op("pe", "matmul", reads=["fw", "hid2p"], writes=["ps%d" % bk], out=B.ps[bk][:, 255:512], lhsT=w3[0:64, 128 * ci:128 * (ci + 1)],
                 rhs=hid2p[0:64, 255:512], start=True, stop=True)
            P.op("dve", "tensor_tensor", reads=["ps%d" % bk, wnck], writes=["kc_all"], out=kc_all[:, ci, :], in0=B.ps[bk][:, :], in1=wnc[:], op=ALU.mult)
            jk_, jkk = B.ring("kjunk", [128, 512], F32, 1)
            P.op("act", "activation", reads=["kc_all"], writes=[jkk, "cst_all"], out=jk_[:], in_=kc_all[:, ci, :], func=AF.Square, accum_out=cst[:, ci, 0:1])
            P.op("act", "activation", reads=["cst_all"], writes=["cst_all"], out=cst[:, ci, 1:2], in_=cst[:, ci, 0:1], func=AF.Sqrt)
            P.op("dve", "reciprocal", reads=["cst_all"], writes=["cst_all"], out=cst[:, ci, 2:3], in_=cst[:, ci, 1:2])
        B.release(mA)
        ftab = B.sb("ftab", [128, 4, 1024], F32)
        uT = B.sb("uT", [128, 2, 512], F32)
        kT = B.sb("kT", [128, 4, 512], F32)
        AB = B.sb("AB", [128, 2, 4, 512], F32)
        PP_ = B.sb("PP", [128, 2, 4, 512], F32)
        yT = uT
        for tt in range(2):
            bk = B.bank()
            for ci in range(4):
                P.op("pe", "transpose", reads=["uc_all", "ident_f"], writes=["ps%d" % bk], out=B.ps[bk][:, 128 * ci:128 * (ci + 1)],
                     in_=uc_all[:, ci, 128 * tt:128 * (tt + 1)], identity=idf[:, :])
            P.op("act", "activation", reads=["ps%d" % bk], writes=["uT"], out=uT[:, tt, :], in_=B.ps[bk][:, :], func=AF.Identity)
        for it_ in range(4):
            bk = B.bank()
            for ci in range(4):
                P.op("pe", "transpose", reads=["kc_all", "ident_f"], writes=["ps%d" % bk], out=B.ps[bk][:, 128 * ci:128 * (ci + 1)],
                     in_=kc_all[:, ci, 128 * it_:128 * (it_ + 1)], identity=idf[:, :])
            P.op("act", "activation", reads=["ps%d" % bk], writes=["kT"], out=kT[:, it_, :], in_=B.ps[bk][:, :], func=AF.Identity)
        P.dma("sp", writes=["ftab"], out=ftab[:, 0:2, :], in_=E["FU"].rearrange("(a p) k -> p a k", p=128))
        for kt in range(4):
            for j in range(2):
                bk = B.bank()
                for tt in range(2):
                    P.op("pe", "matmul", reads=["ftab", "uT"], writes=["ps%d" % bk], out=B.ps[bk][:, :], lhsT=ftab[:, tt, 512 * j + 128 * kt:512 * j + 128 * (kt + 1)],
                         rhs=uT[:, tt, :], start=(tt == 0), stop=(tt == 1))
                P.op("act", "activation", reads=["ps%d" % bk], writes=["AB"], out=AB[:, j, kt, :], in_=B.ps[bk][:, :], func=AF.Identity)
        P.dma("sp", reads=[], writes=["ftab"], out=ftab[:], in_=E["FK"].rearrange("(a p) k -> p a k", p=128))
        for kt in range(4):
            bks = []
            for j in range(2):
                bk = B.bank()
                for it_ in range(4):
                    P.op("pe", "matmul", reads=["ftab", "kT"], writes=["ps%d" % bk], out=B.ps[bk][:, :], lhsT=ftab[:, it_, 512 * j + 128 * kt:512 * j + 128 * (kt + 1)],
                         rhs=kT[:, it_, :], start=(it_ == 0), stop=(it_ == 3))
                bks.append(bk)
            kcb, ksb = bks
            tq, tqk = B.ring("ctt", [128, 512], F32, 2)
            P.op("dve", "tensor_tensor", reads=["ps%d" % kcb, "AB"], writes=["PP0"], out=PP_[:, 0, kt, :], in0=B.ps[kcb][:, :], in1=AB[:, 0, kt, :], op=ALU.mult)
            P.op("dve", "tensor_tensor", reads=["ps%d" % ksb, "AB"], writes=[tqk], out=tq[:], in0=B.ps[ksb][:, :], in1=AB[:, 1, kt, :], op=ALU.mult)
            P.op("pool", "tensor_tensor", reads=[tqk, "PP0"], writes=["PP0"], out=PP_[:, 0, kt, :], in0=PP_[:, 0, kt, :], in1=tq[:], op=ALU.subtract)
            tq2, tq2k = B.ring("ctt", [128, 512], F32, 2)
            P.op("dve", "tensor_tensor", reads=["ps%d" % ksb, "AB"], writes=["PP1"], out=PP_[:, 1, kt, :], in0=B.ps[ksb][:, :], in1=AB[:, 0, kt, :], op=ALU.mult)
            P.op("dve", "tensor_tensor", reads=["ps%d" % kcb, "AB"], writes=[tq2k], out=tq2[:], in0=B.ps[kcb][:, :], in1=AB[:, 1, kt, :], op=ALU.mult)
            P.op("pool", "tensor_tensor", reads=[tq2k, "PP1"], writes=["PP1"], out=PP_[:, 1, kt, :], in0=PP_[:, 1, kt, :], in1=tq2[:], op=ALU.add)
        P.dma("sp", writes=["ftab"], out=ftab[:, :, 0:512], in_=E["FI"].rearrange("(a p) k -> p a k", p=128))
        for tt in range(2):
            bk = B.bank()
            n_ = 0
            for j in range(2):
                for kt in range(4):
                    P.op("pe", "matmul", reads=["ftab", "PP0", "PP1"], writes=["ps%d" % bk], out=B.ps[bk][:, :], lhsT=ftab[:, kt, 256 * j + 128 * tt:256 * j + 128 * (tt + 1)],
                         rhs=PP_[:, j, kt, :], start=(n_ == 0), stop=(n_ == 7))
                    n_ += 1
            P.op("act", "activation", reads=["ps%d" % bk], writes=["uT"], out=yT[:, tt, :], in_=B.ps[bk][:, :], func=AF.Identity, scale=1.0 / 512)
        for ci in range(4):
            bk = B.bank()
            for tt in range(2):
                P.op("pe", "transpose", reads=["uT", "ident_f"], writes=["ps%d" % bk], out=B.ps[bk][:, 128 * tt:128 * (tt + 1)],
                     in_=yT[:, tt, 128 * ci:128 * (ci + 1)], identity=idf[:, :])
            P.op("act", "activation", reads=["ps%d" % bk], writes=["ycv_all"], out=ycv[:, ci, :], in_=B.ps[bk][:, 0:CTX], func=AF.Identity)
        B.release(mA)
    for ci in range(4):
        P.op("act", "activation", reads=["hyct"], writes=["hyct"], out=hyct[:, ci, 2:3], in_=hyct[:, ci, 0:1], func=AF.Sqrt)
        P.op("dve", "reciprocal", reads=["hyct"], writes=["hyct"], out=hyct[:, ci, 3:4], in_=hyct[:, ci, 2:3])
        x0s, x0k = B.ring("x0s", [128, NTOK3], F32, 1)
        hy_conv_tile(B, w_in, ci, hT, "hT", segs_all if do_ctx else segs_all[:1], scw, x0s, x0k)
        yb, ybk = B.ring("yb", [128, NTOK3], F32, 1)
        ut, utk = B.ring("ut", [128, TOK], F32, 1)
        P.dma("sp", reads=["y_my"], writes=[ybk], out=yb[:, 0:TOK], in_=E["y_my"][128 * ci:128 * (ci + 1), :])
        P.dma("sp", reads=["u_src"], writes=[utk], out=ut[:], in_=E["u_src"][ci])
        P.op("dve", "tensor_scalar", reads=[ybk, "hyct"], writes=[ybk], out=yb[:, 0:TOK], in0=yb[:, 0:TOK], scalar1=hyct[:, ci, 3:4],
             scalar2=None, op0=ALU.mult)
        P.op("dve", "scalar_tensor_tensor", reads=[utk, "hyct", ybk], writes=[ybk], out=yb[:, 0:TOK], in0=ut[:], scalar=hyct[:, ci, 1:2],
             in1=yb[:, 0:TOK], op0=ALU.mult, op1=ALU.add)
        if do_ctx:
            P.op("dve", "tensor_scalar", reads=["ycv_all", "cst_all"], writes=[ybk], out=yb[:, TOK:NTOK3], in0=ycv[:, ci, :], scalar1=cst[:, ci, 2:3],
                 scalar2=None, op0=ALU.mult)
            P.op("dve", "scalar_tensor_tensor", reads=["uc_all", "hyct", ybk], writes=[ybk], out=yb[:, TOK:NTOK3], in0=uc_all[:, ci, :], scalar=hyct[:, ci, 1:2],
                 in1=yb[:, TOK:NTOK3], op0=ALU.mult, op1=ALU.add)
        nuse = NTOK3 if do_ctx else TOK
        P.op("pool", "tensor_tensor", reads=[x0k, ybk], writes=[ybk], out=yb[:, 0:nuse], in0=yb[:, 0:nuse], in1=x0s[:, 0:nuse], op=ALU.mult)
        wt, wk = wblock(B, w_in, COL_GB + 128 * ci, 128, "wblk128", 3, 128)
        for (t0, n) in (CHUNKS if do_ctx else CHUNKS[:-1]):
            bk = B.bank()
            proj_fm(B, wt, wk, 0, hT, "hT", hcol(t0), n, bk)
            sg, sgk = B.ring("sgb", [128, 512], F32, 2)
            P.op("act", "activation", reads=["ps%d" % bk], writes=[sgk], out=sg[:, 0:n], in_=B.ps[bk][:, 0:n], func=AF.Silu)
            P.op("dve", "tensor_tensor", reads=[sgk, ybk], writes=["gbT"], out=gbT[:, ci, t0:t0 + n], in0=sg[:, 0:n], in1=yb[:, t0:t0 + n],
                 op=ALU.mult)
    B.release(m0)


def f_gmlp(B, E, l, tiles=None):
    P = B.P
    hT, w_in, gcT, idf, idb = E["hT"], E["w_in"][l], E["gcT"], E["idf"], E["idb"]
    m0 = B.mark()
    B.cast_eng = "dve"
    lng, lngk = bcast_row(B, E["gm_g"][l], 512, "lng")
    lnb, lnbk = bcast_row(B, E["gm_b"][l], 512, "lnb")
    wsf = B.sb("wsf", [128, 8, 128], F32)
    P.dma("sp", writes=["wsf"], out=wsf[:], in_=E["gm_ws"][l].rearrange("g p q -> p g q"))
    wsT = B.sb("wsT", [128, 8, 128], BF16)
    for g in range(8):
        bk = B.bank()
        P.op("pe", "transpose", reads=["wsf", "ident_f"], writes=["ps%d" % bk], out=B.ps[bk][:, 0:128], in_=wsf[:, g, :], identity=idf[:, :])
        P.op("act", "activation", reads=["ps%d" % bk], writes=["wsT"], out=wsT[:, g, :], in_=B.ps[bk][:, 0:128], func=AF.Identity)
    bsT = B.sb("bsT", [128, 8], F32)
    P.dma("sp", writes=["bsT"], out=bsT[:], in_=E["gm_bs"][l].rearrange("g p -> p g"), allow_slow_non_contiguous=True)
    wu, wuk = wblock(B, w_in, COL_GM, 512, "wgm_u", 1)
    wv, wvk = wblock(B, w_in, COL_GM + 512, 512, "wgm_v", 1)
    wc, wck = wblock(B, w_in, COL_GC, 512, "wgm_c", 1)
    def gA(t0):
        ba, bb, bc_ = B.bank(), B.bank(), B.bank()
        proj_tm(B, wu, wuk, 0, 512, hT, "hT", hcol(t0), 128, ba)
        proj_tm(B, wv, wvk, 0, 512, hT, "hT", hcol(t0), 128, bb)
        proj_tm(B, wc, wck, 0, 512, hT, "hT", hcol(t0), 128, bc_)
        return (ba, bb, bc_)

    def gB(bks):
        ba, bb, bc_ = bks
        ug, ugk = B.ring("ug", [128, 512], F32, 2)
        vg, vgk = B.ring("vg", [128, 512], F32, 2)
        gs, gsk = B.ring("gs", [128, 512], F32, 2)
        P.op("act", "activation", reads=["ps%d" % ba], writes=[ugk], out=ug[:], in_=B.ps[ba][:, :], func=AF.Gelu)
        P.op("act", "activation", reads=["ps%d" % bb], writes=[vgk], out=vg[:], in_=B.ps[bb][:, :], func=AF.Gelu)
        P.op("act", "activation", reads=["ps%d" % bc_], writes=[gsk], out=gs[:], in_=B.ps[bc_][:, :], func=AF.Silu)
        return (ug, ugk, vg, vgk, gs, gsk)

    def gC(st):
        ug, ugk, vg, vgk, gs, gsk = st
        s6, s6k = B.ring("s6", [128, 6], F32, 2)
        mv, mvk = B.ring("mv", [128, 4], F32, 2)
        P.op("dve", "bn_stats", reads=[vgk], writes=[s6k], out=s6[:], in_=vg[:])
        P.op("dve", "bn_aggr", reads=[s6k], writes=[mvk], out=mv[:, 0:2], in_=s6[:])
        P.op("dve", "tensor_scalar", reads=[mvk], writes=[mvk], out=mv[:, 2:3], in0=mv[:, 1:2], scalar1=EPS, scalar2=None, op0=ALU.add)
        P.op("act", "activation", reads=[mvk], writes=[mvk], out=mv[:, 2:3], in_=mv[:, 2:3], func=AF.Sqrt)
        P.op("dve", "reciprocal", reads=[mvk], writes=[mvk], out=mv[:, 3:4], in_=mv[:, 2:3])
        P.op("dve", "tensor_scalar", reads=[vgk, mvk], writes=[vgk], out=vg[:], in0=vg[:], scalar1=mv[:, 0:1], scalar2=mv[:, 3:4],
             op0=ALU.subtract, op1=ALU.mult)
        P.op("dve", "tensor_tensor", reads=[vgk, lngk], writes=[vgk], out=vg[:], in0=vg[:], in1=lng[:], op=ALU.mult)
        vnb, vnbk = B.ring("vnb", [128, 512], BF16, 2)
        P.op("dve", "tensor_tensor", reads=[vgk, lnbk], writes=[vnbk], out=vnb[:], in0=vg[:], in1=lnb[:], op=ALU.add)
        return (vnb, vnbk)

    def gD(vv):
        vnb, vnbk = vv
        bm = B.bank()
        for g in range(8):
            P.op("pe", "matmul", reads=["wsT", vnbk], writes=["ps%d" % bm], out=B.ps[bm][:, 64 * g:64 * (g + 1)], lhsT=wsT[:, g, :],
                 rhs=vnb[:, 64 * g:64 * (g + 1)], start=(g == 0), stop=(g == 7), skip_group_check=True)
        return bm

    def gE(bm, st, t0):
        ug, ugk, vg, vgk, gs, gsk = st
        for g in range(8):
            P.op("dve", "scalar_tensor_tensor", reads=["ps%d" % bm, "bsT", ugk], writes=[ugk], out=ug[:, 64 * g:64 * (g + 1)],
                 in0=B.ps[bm][:, 64 * g:64 * (g + 1)], scalar=bsT[:, g:g + 1], in1=ug[:, 64 * g:64 * (g + 1)], op0=ALU.add, op1=ALU.mult)
        yc, yck = B.ring("ycb", [128, 512], BF16, 2)
        P.op("dve", "tensor_tensor", reads=[ugk, gsk], writes=[yck], out=yc[:], in0=ug[:], in1=gs[:], op=ALU.mult)
        transpose_to_fm(B, yc, yck, gcT, "gcT", t0, idb)

    tiles = TILES if tiles is None else tiles
    for q0 in range(0, len(tiles), 2):
        ts_ = tiles[q0:q0 + 2]
        bk3 = [gA(t0) for t0 in ts_]
        sts = [gB(x_) for x_ in bk3]
        vvs = [gC(x_) for x_ in sts]
        bms = [gD(x_) for x_ in vvs]
        for bm, st, t0 in zip(bms, sts, ts_):
            gE(bm, st, t0)
    B.cast_eng = "pool"
    B.release(m0)


def build_fused():
    B = Builder()
    P = B.P
    E = {}
    for nm, shp in (("xs", [TOK + 2, D]), ("xc0", [CTX, D]), ("c", [D]), ("c_ctx", [D]), ("ada_w", [DEPTH, D, 3 * D]), ("ada_b", [DEPTH, 3 * D]),
                    ("npre", [DEPTH, D]), ("npost", [DEPTH, D]), ("w_in", [DEPTH, D, COL_END]), ("wk_perm", [DEPTH, D, 512]),
                    ("wq_perm", [DEPTH, D, 512]), ("hmask", [2, 1]), ("cosT", [128, TOK]), ("sinT", [128, TOK]),
                    ("hy_w", [DEPTH, 3, 1536]), ("hy_b", [DEPTH, 1536]), ("hw1", [DEPTH, 33, 64]), ("hb1", [DEPTH, 64]),
                    ("hw2", [DEPTH, 64, 64]), ("hb2", [DEPTH, 64]), ("hw3", [DEPTH, 64, 1024]), ("hw3q", [DEPTH, 64, 256]),
                    ("hfreq", [DEPTH, 2, 64]), ("zpos", [33, NFFT]), ("win", [128, NFFT]), ("dft_in", [128, 640]), ("tw_in", [128, 256]),
                    ("zposc", [33, 512]), ("winc", [512, 512]), ("FU", [256, 1024]), ("FK", [512, 1024]), ("FI", [512, 512]), ("hyb", [DEPTH, 128, 4]), ("da_lam", [DEPTH, 256]), ("lam_init", [DEPTH, 1]),
                    ("da_subln", [DEPTH, 128]), ("gm_g", [DEPTH, 512]), ("gm_b", [DEPTH, 512]), ("gm_ws", [DEPTH, 8, 128, 128]),
                    ("gm_bs", [DEPTH, 8, 128]), ("w_br", [DEPTH, 3, 512, D]), ("w_out", [DEPTH, D, D])):
        E[nm] = B.din(nm, shp)
    x_out = B.dout("x_out", [TOK, D])
    nc = B.nc
    def scr(name, shape, dt=F32):
        return nc.dram_tensor(name, list(shape), dt, kind="Internal").ap()
    E["xcur"] = [scr("xcur%d" % i, [TOK + 2, D]) for i in range(2)]
    E["xccur"] = [scr("xccur%d" % i, [CTX, D]) for i in range(2)]
    E["kt_src"] = scr("kt_src", [2, 256, TOK], BF16)
    E["v_src"] = scr("v_src", [2, TOK // 2, 512], BF16)
    E["u_src"] = scr("u_src", [4, 128, TOK])
    E["y_src"] = scr("y_src", [4, 128, TOK])
    E["ssq_src"] = scr("ssq_src", [128, 16])
    E["hal_src"] = scr("hal_src", [2, D])
    E["kt_g"] = [scr("kt_g%d" % i, [2, 4 * 256, TOK], BF16) for i in range(2)]
    E["v_g"] = [scr("v_g%d" % i, [2, 4 * (TOK // 2), 512], BF16) for i in range(2)]
    E["u_g"] = [scr("u_g0", [4, 4 * 128, TOK])] * 2
    E["y_g"] = [scr("y_g0", [4, 4 * 128, TOK])] * 2
    E["ssq_g"] = [scr("ssq_g%d" % i, [512, 16]) for i in range(2)]
    E["hal_g"] = [scr("hal_g0", [8, D])] * 2
    E["kscr"] = scr("kscr", [128, NFFT])
    E["u_my"] = scr("u_my", [4, 128, TOK])
    E["y_my"] = scr("y_my", [512, TOK])
    B.init_psum()
    idf, idb = make_identity(B)
    E["idf"], E["idb"] = idf, idb
    B.ones1 = B.sb("ones1", [1, 128], F32)
    P.op("dve", "memset", writes=["ones1"], ap=B.ones1[:], constant=1.0)
    dft = B.sb("dft", [128, 640], F32)
    tw = B.sb("tw", [128, 256], F32)
    P.dma("sp", writes=["dft"], out=dft[:], in_=E["dft_in"][:, :])
    P.dma("sp", writes=["tw"], out=tw[:], in_=E["tw_in"][:, :])
    E["dft"], E["tw"] = dft, tw
    hT = B.sb("hT", [128, 8, HCOLS], BF16)
    Gbc = B.sb("bc_G", [128, 1024], F32)
    Gcbc = B.sb("bc_Gc", [128, 1024], F32)
    gaT = B.sb("gaT", [128, 4, NTOK3], BF16)
    gbT = B.sb("gbT", [128, 4, NTOK3], BF16)
    gcT = B.sb("gcT", [128, 4, NTOK3], BF16)
    mt = B.sb("hmask", [2, 1], F32)
    P.dma("sp", writes=["hmask"], out=mt[:], in_=E["hmask"][:, :])
    E.update(hT=hT, gaT=gaT, gbT=gbT, gcT=gcT)
    P.op("pool", "memset", writes=["hT"], ap=hT[:, :, TOK + 2:TOK + 3], constant=0.0)
    P.op("pool", "memset", writes=["hT"], ap=hT[:, :, HCOLS - 1:HCOLS], constant=0.0)
    for i_ in range(EXTRA_CC):
        P.cc(reads=["hal_src"], writes=["hal_g"], kind="AllGather", op=ALU.bypass, replica_groups=RG4, ins=[E["hal_src"].opt()],
             outs=[E["hal_g"][0].opt()])
    for l in range(FUSED_DEPTH):
        last = l == FUSED_DEPTH - 1
        par = l % 2
        xsrc = E["xs"] if l == 0 else E["xcur"][par]
        xcsrc = E["xc0"] if l == 0 else E["xccur"][par]
        xkeys = [] if l == 0 else ["xcur%d" % par]
        xckeys = [] if l == 0 else ["xccur%d" % par]
        m0 = B.mark()
        Abc = B.sb("bc_A", [128, 1024], F32); Bbc = B.sb("bc_Bm", [128, 1024], F32)
        Acbc = B.sb("bc_Ac", [128, 1024], F32); Bcbc = B.sb("bc_Bc", [128, 1024], F32)
        m1 = B.mark()
        modulation(B, E["c"], E["ada_w"][l], E["ada_b"][l], E["npre"][l], E["npost"][l], ("A", "Bm", "G"), {"A": Abc, "Bm": Bbc, "G": Gbc})
        B.release(m1)
        modulation(B, E["c_ctx"], E["ada_w"][l], E["ada_b"][l], E["npre"][l], E["npost"][l], ("Ac", "Bc", "Gc"), {"Ac": Acbc, "Bc": Bcbc, "Gc": Gcbc})
        B.release(m1)
        for i in range(NT):
            compute_h_tile(B, [(xsrc[1 + 128 * i: 1 + 128 * (i + 1), :], 0, 128)], 128, Abc, Bbc, "bcA", "bcBm", hT, "hT", 1 + 128 * i, idb, xkeys=xkeys)
        hh = B.sb("hTh", [128, 8, 2], BF16)
        compute_h_tile(B, [(xsrc[0:1, :], 0, 1), (xsrc[TOK + 1:TOK + 2, :], 1, 1)], 2, Abc, Bbc, "bcA", "bcBm", hh, "hTh", 0, idb,
                       mask=(mt, "hmask"), xkeys=xkeys)
        P.op("pool", "tensor_copy", reads=["hTh"], writes=["hT"], out=hT[:, :, 0:1], in_=hh[:, :, 0:1])
        P.op("pool", "tensor_copy", reads=["hTh"], writes=["hT"], out=hT[:, :, TOK + 1:TOK + 2], in_=hh[:, :, 1:2])
        for i in range(2):
            compute_h_tile(B, [(xcsrc[128 * i:128 * (i + 1), :], 0, 128)], 128, Acbc, Bcbc, "bcAc", "bcBc", hT, "hT", TOK + 3 + 128 * i, idb, xkeys=xckeys)
        B.release(m0)
        f_products(B, E, l)
        f_gmlp(B, E, l, tiles=(TILES[:NT] if last else None))
        issue_y = f_conv(B, E, l)
        L = dict(after_q=issue_y, hT=hT, idb=idb, gaT=gaT, gbT=gbT, gcT=gcT, w_in=E["w_in"][l], wq_perm=E["wq_perm"][l], cosT=E["cosT"], sinT=E["sinT"],
                 kt_all=[E["kt_g"][par][h // 2].rearrange("(r h d) t -> h d r t", r=4, h=2)[h % 2] for h in range(4)], v_all=None,
                 v_gk=E["v_g"][par],
                 kth_out=(lambda k: k.rearrange("p (r t) -> p r t", r=4)), kt_keys=["kt_g%d" % par], v_keys=["v_g%d" % par],
                 da_lam=E["da_lam"][l], lam_init=E["lam_init"][l], da_subln=E["da_subln"][l],
                 w_br=E["w_br"][l], w_out=E["w_out"][l], xs=xsrc, xc=xcsrc, x_keys=xkeys + xckeys, Gbc=Gbc, Gcbc=Gcbc)
        if last:
            L["chunks"] = CHUNKS[:-1]
        f_attn(B, L)
        f_hyena_gate(B, E, l, do_ctx=not last)
        if last:
            L.update(x_new=x_out, x_new_key="x_out", xc_new=None, hal_src=None)
        else:
            L.update(x_new=E["xcur"][1 - par][1:TOK + 1, :], x_new_key="xcur%d" % (1 - par), xc_new=E["xccur"][1 - par],
                     xc_new_key="xccur%d" % (1 - par), hal_src=E["hal_src"])
        build_s3_merge(B, L)
        if not last:
            P.cc(reads=["hal_src"], writes=["hal_g"], kind="AllGather", op=ALU.bypass, replica_groups=RG4, ins=[E["hal_src"].opt()],
                 outs=[E["hal_g"][par].opt()])
            hg = E["hal_g"][par]
            xn = E["xcur"][1 - par]
            P.dma("sp", reads=["hal_g"], writes=["xcur%d" % (1 - par)], out=xn[0:1, :],
                  in_=(lambda e, hg=hg: hg.rearrange("(r two) d -> r two d", two=2)[rank_nb(e, 3)][1:2, :]))
            P.dma("sp", reads=["hal_g"], writes=["xcur%d" % (1 - par)], out=xn[TOK + 1:TOK + 2, :],
                  in_=(lambda e, hg=hg: hg.rearrange("(r two) d -> r two d", two=2)[rank_nb(e, 1)][0:1, :]))
    return B.finish()


def ctx_dft_tables():
    t = np.arange(256, dtype=np.float64)[:, None]
    k = np.arange(512, dtype=np.float64)[None, :]
    th = 2.0 * math.pi * t * k / 512.0
    FU = np.concatenate([np.cos(th), np.sin(th)], axis=1).astype(np.float32)
    i = np.arange(512, dtype=np.float64)[:, None]
    ph = 2.0 * math.pi * (i - 255.0) * k / 512.0
    FK = np.concatenate([np.cos(ph), np.sin(ph)], axis=1).astype(np.float32)
    kk = np.arange(512, dtype=np.float64)[:, None]
    tt = np.arange(256, dtype=np.float64)[None, :]
    ti = 2.0 * math.pi * kk * tt / 512.0
    FI = np.concatenate([np.cos(ti), np.sin(ti)], axis=1).astype(np.float32)
    return FU, FK, FI


def fused_inputs(inp):
    cosT, sinT = rope_tables()
    FU, FK, FI = ctx_dft_tables()
    zpos, win, dft, tw = hyena_tables()
    zposc, winc = ctx_tables()
    w_in = inp["w_in"]
    wkp = np.stack([perm_cols(w_in[l][:, COL_K:COL_K + 512]) for l in range(DEPTH)], 0)
    wqp = np.stack([perm_cols(w_in[l][:, COL_Q:COL_Q + 512]) for l in range(DEPTH)], 0)
    lam_init = np.array([[0.8 - 0.6 * math.exp(-0.3 * l)] for l in range(DEPTH)], np.float32)
    hyb = np.ascontiguousarray(inp["hy_bias"].reshape(DEPTH, 4, 128).transpose(0, 2, 1))
    shared = {
        "c_ctx": inp["c_ctx"], "ada_w": inp["ada_w"], "ada_b": inp["ada_b"], "npre": inp["norm_pre"], "npost": inp["norm_post"],
        "w_in": w_in, "wk_perm": wkp, "wq_perm": wqp, "hy_w": inp["hy_short_w"], "hy_b": inp["hy_short_b"],
        "hw1": inp["hy_f_w1"], "hb1": inp["hy_f_b1"], "hw2": inp["hy_f_w2"], "hb2": inp["hy_f_b2"], "hw3": inp["hy_f_w3"],
        "hfreq": inp["hy_f_freq"], "zpos": zpos, "dft_in": dft, "tw_in": tw, "zposc": zposc, "winc": winc, "hyb": hyb, "FU": FU, "FK": FK, "FI": FI,
        "da_lam": np.ascontiguousarray(inp["da_lambda"].reshape(DEPTH, 256)), "lam_init": lam_init, "da_subln": inp["da_subln"],
        "gm_g": inp["gm_ln_g"], "gm_b": inp["gm_ln_b"], "gm_ws": inp["gm_ws"], "gm_bs": inp["gm_bs"], "w_br": inp["w_branch"],
        "w_out": inp["w_out"],
    }
    x = inp["x"]
    in_maps = []
    for core in range(8):
        b, r = divmod(core, 4)
        t0 = r * TOK
        xs = np.zeros((TOK + 2, D), np.float32)
        lo, hi = max(t0 - 1, 0), min(t0 + TOK + 1, SEQ)
        xs[lo - (t0 - 1): hi - (t0 - 1)] = x[b, lo:hi]
        hmask = np.array([[0.0 if t0 == 0 else 1.0], [0.0 if t0 + TOK == SEQ else 1.0]], np.float32)
        hw3q = np.ascontiguousarray(np.concatenate([inp["hy_f_w3"][:, :, 128 * r:128 * (r + 1)],
                                                    inp["hy_f_w3"][:, :, 512 + 128 * r:512 + 128 * (r + 1)]], axis=2))
        m = dict(shared)
        m.update({"xs": xs, "xc0": np.ascontiguousarray(inp["ctx"][b]), "c": inp["c"][b], "hmask": hmask,
                  "cosT": np.ascontiguousarray(cosT[:, t0:t0 + TOK]), "sinT": np.ascontiguousarray(sinT[:, t0:t0 + TOK]),
                  "hw3q": hw3q, "win": np.ascontiguousarray(win[128 * r:128 * (r + 1)])})
        in_maps.append(m)
    return in_maps


def kernel(**inputs):
    inp = {k: np.ascontiguousarray(np.asarray(v), dtype=np.float32) for k, v in inputs.items()}
    nc = get_nc("fused")
    in_maps = fused_inputs(inp)
    res = run_bass_kernel_spmd(nc, in_maps, core_ids=list(range(8))).results
    out = np.stack([np.concatenate([np.asarray(res[4 * b + r]["x_out"]) for r in range(4)], 0) for b in range(2)], 0)
    return out.astype(np.float32)


def f_attn(B, L):
    P = B.P
    hT, gaT = L["hT"], L["gaT"]
    w_in, wq_perm, cosT, sinT, kt_all = L["w_in"], L["wq_perm"], L["cosT"], L["sinT"], L["kt_all"]
    m0 = B.mark()
    lamb, lambk = bcast_row(B, L["da_lam"], 256, "lam")
    lib, libk = bcast_row(B, L["lam_init"], 1, "li")
    lt = B.sb("lamtmp", [128, 140], F32)
    P.op("dve", "tensor_tensor", reads=[lambk], writes=["lamtmp"], out=lt[:, 0:64], in0=lamb[:, 0:64], in1=lamb[:, 64:128], op=ALU.mult)
    P.op("dve", "tensor_tensor", reads=[lambk], writes=["lamtmp"], out=lt[:, 64:128], in0=lamb[:, 128:192], in1=lamb[:, 192:256], op=ALU.mult)
    P.op("dve", "tensor_reduce", reads=["lamtmp"], writes=["lamtmp"], out=lt[:, 128:129], in_=lt[:, 0:64], axis=AX.X, op=ALU.add)
    P.op("dve", "tensor_reduce", reads=["lamtmp"], writes=["lamtmp"], out=lt[:, 129:130], in_=lt[:, 64:128], axis=AX.X, op=ALU.add)
    P.op("act", "activation", reads=["lamtmp"], writes=["lamtmp"], out=lt[:, 130:132], in_=lt[:, 128:130], func=AF.Exp)
    P.op("dve", "tensor_tensor", reads=["lamtmp"], writes=["lamtmp"], out=lt[:, 132:133], in0=lt[:, 131:132], in1=lt[:, 130:131], op=ALU.subtract)
    P.op("dve", "tensor_tensor", reads=["lamtmp", libk], writes=["lamtmp"], out=lt[:, 133:134], in0=lt[:, 132:133], in1=lib[:, 0:1], op=ALU.subtract)
    neglam = lt[:, 133:134]
    P.op("dve", "tensor_scalar", reads=[libk], writes=["lamtmp"], out=lt[:, 134:135], in0=lib[:, 0:1], scalar1=-1.0, scalar2=1.0,
         op0=ALU.mult, op1=ALU.add)
    P.dma("sp", writes=["lamtmp"], out=lt[:, 135:136], in_=L["da_subln"].rearrange("(p o) -> p o", o=1))
    P.op("dve", "tensor_tensor", reads=["lamtmp"], writes=["lamtmp"], out=lt[:, 136:137], in0=lt[:, 135:136], in1=lt[:, 134:135], op=ALU.mult)
    gcol = lt[:, 136:137]
    ones_b = B.sb("ones_b", [128, 128], BF16)
    ones_f = B.sb("ones_f", [128, 128], F32)
    P.op("pool", "memset", writes=["ones_b"], ap=ones_b[:], constant=1.0)
    P.op("pool", "memset", writes=["ones_f"], ap=ones_f[:], constant=1.0)
    QT = B.sb("QT", [128, 4, NTOK3], BF16)
    kcT = B.sb("kcT", [128, 4, CTX], BF16)
    vcx = B.sb("vcx", [128, 2, 4, 128], BF16)
    m1 = B.mark()
    cs = B.sb("cosT", [128, TOK], F32)
    sn = B.sb("sinT", [128, TOK], F32)
    P.dma("sp", writes=["cosT"], out=cs[:], in_=cosT[:, :])
    P.dma("sp", writes=["sinT"], out=sn[:], in_=sinT[:, :])
    wt, wk = wblock(B, w_in, COL_Q, 512, "wq")
    wp, wpk = wblock(B, wq_perm, 0, 512, "wq")
    for h in range(4):
        for j in range(TOK // 512):
            b1 = B.bank()
            proj_fm(B, wt, wk, 128 * h, hT, "hT", 1 + 512 * j, 512, b1)
            b2 = B.bank()
            proj_fm(B, wp, wpk, 128 * h, hT, "hT", 1 + 512 * j, 512, b2)
            t1, t1k = B.ring("rtmp1", [128, 512], F32, 2)
            t2, t2k = B.ring("rtmp2", [128, 512], F32, 2)
            P.op("dve", "tensor_tensor", reads=["ps%d" % b1, "cosT"], writes=[t1k], out=t1[:], in0=B.ps[b1][:, :],
                 in1=cs[:, 512 * j:512 * (j + 1)], op=ALU.mult)
            P.op("dve", "tensor_tensor", reads=["ps%d" % b2, "sinT"], writes=[t2k], out=t2[:], in0=B.ps[b2][:, :],
                 in1=sn[:, 512 * j:512 * (j + 1)], op=ALU.mult)
            P.op("pool", "tensor_tensor", reads=[t1k, t2k], writes=["QT"], out=QT[:, h, 512 * j:512 * (j + 1)], in0=t1[:], in1=t2[:], op=ALU.add)
        b1 = B.bank()
        proj_fm(B, wt, wk, 128 * h, hT, "hT", hcol(TOK), CTX, b1)
        P.op("act", "activation", reads=["ps%d" % b1], writes=["QT"], out=QT[:, h, TOK:NTOK3], in_=B.ps[b1][:, 0:CTX], func=AF.Identity)
    wt, wk = wblock(B, w_in, COL_K, 512, "wq")
    for h in range(4):
        b1 = B.bank()
        proj_fm(B, wt, wk, 128 * h, hT, "hT", hcol(TOK), CTX, b1)
        P.op("act", "activation", reads=["ps%d" % b1], writes=["kcT"], out=kcT[:, h, :], in_=B.ps[b1][:, 0:CTX], func=AF.Identity)
    wt, wk = wblock(B, w_in, COL_V, 512, "wq")
    for i in range(2):
        b1 = B.bank()
        proj_tm(B, wt, wk, 0, 512, hT, "hT", hcol(TOK + 128 * i), 128, b1)
        P.op("act", "activation", reads=["ps%d" % b1], writes=["vcx"], out=vcx[:, i, :, :],
             in_=B.ps[b1][:, :].rearrange("p (h d) -> p h d", d=128), func=AF.Identity)
    B.release(m1)
    kth = B.sb("kth", [128, SEQ], BF16)
    vh = B.sb("vh", [128, SEQ // 128, 128], BF16)
    wga, wgak = wblock(B, w_in, COL_GA, 512, "wga", 1)
    if L.get("after_q") is not None:
        L["after_q"]()
    ACC, RS = 0, 1
    pcnt = 0
    for h in range(4):
        P.dma("sp", reads=L.get("kt_keys", []), writes=["kth"], out=kth[:].rearrange("p (r t) -> p r t", r=4), in_=kt_all[h])
        for k in range(2):
            for r in range(4):
                P.dma("sp", reads=L.get("v_keys", []), writes=["vh"], out=vh[:, 16 * r + 8 * k:16 * r + 8 * k + 8, :],
                      in_=L["v_gk"][k].rearrange("(r t p) c -> r p t c", r=4, p=128)[r][:, :, 128 * h:128 * (h + 1)])
        for (t0, n) in L.get("chunks", CHUNKS):
            latent = t0 < TOK
            keys = ([("l", k) for k in range(SEQ // 128)] if latent else []) + [("c", 0), ("c", 1)]
            om, omk = B.ring("om", [128, 2, 512], F32, 1)
            iters = [(m, ki) for m in range(2) for ki in range(0, len(keys), 2)]
            state = {}

            def emit_S(it):
                nonlocal pcnt
                m, ki = it
                p = 1 + (pcnt % 3)
                pcnt += 1
                vts = []
                for i, (kind, kt) in enumerate(keys[ki:ki + 2]):
                    if kind == "l":
                        lk, lkk = kth[64 * m:64 * (m + 1), 128 * kt:128 * (kt + 1)], "kth"
                        vts.append((vh[:, kt, :], "vh"))
                    else:
                        lk, lkk = kcT[64 * m:64 * (m + 1), h, 128 * kt:128 * (kt + 1)], "kcT"
                        vts.append((vcx[:, kt, h, :], "vcx"))
                    P.op("pe", "matmul", reads=[lkk, "QT"], writes=["ps%d" % (2 * p + i)], out=B.ps[2 * p + i][:, 0:n], lhsT=lk,
                         rhs=QT[64 * m:64 * (m + 1), h, t0:t0 + n], start=True, stop=True)
                state[it] = (p, vts)

            def emit_PV(it):
                m, ki = it
                p, vts = state.pop(it)
                pt, ptk = B.ring("pt", [128, 2, 512], BF16, 3)
                P.op("act", "activation", reads=["ps%d" % (2 * p), "ps%d" % (2 * p + 1)], writes=[ptk], out=pt[:, :, 0:n],
                     in_=B.pp[p][:, :].rearrange("p (b c) -> p b c", b=2)[:, :, 0:n], func=AF.Exp, scale=0.125)
                for i, (vt, vtk) in enumerate(vts):
                    first = (ki == 0 and i == 0)
                    lastk = (ki + i == len(keys) - 1)
                    P.op("pe", "matmul", reads=[ptk, vtk], writes=["ps%d" % ACC], out=B.ps[ACC][:, 0:n], lhsT=vt, rhs=pt[:, i, 0:n],
                         start=first, stop=lastk)
                    P.op("pe", "matmul", reads=[ptk, "ones_b"], writes=["ps%d" % RS], out=B.ps[RS][:, 0:n], lhsT=ones_b[:, :], rhs=pt[:, i, 0:n],
                         start=first, stop=lastk)
                if ki + 2 >= len(keys):
                    rr, rrk = B.ring("rr", [128, 512], F32, 1)
                    P.op("dve", "reciprocal", reads=["ps%d" % RS], writes=[rrk], out=rr[:, 0:n], in_=B.ps[RS][:, 0:n])
                    if m == 1:
                        P.op("dve", "tensor_scalar", reads=[rrk, "lamtmp"], writes=[rrk], out=rr[:, 0:n], in0=rr[:, 0:n], scalar1=neglam, scalar2=None,
                             op0=ALU.mult)
                    P.op("dve", "tensor_tensor", reads=["ps%d" % ACC, rrk], writes=[omk + str(m)], out=om[:, m, 0:n], in0=B.ps[ACC][:, 0:n], in1=rr[:, 0:n],
                         op=ALU.mult)

            emit_S(iters[0])
            for idx, it in enumerate(iters):
                if idx + 1 < len(iters):
                    emit_S(iters[idx + 1])
                emit_PV(it)
            P.op("pool", "tensor_tensor", reads=[omk + "0", omk + "1"], writes=[omk + "0"], out=om[:, 0, 0:n], in0=om[:, 0, 0:n], in1=om[:, 1, 0:n], op=ALU.add)
            sq, sqk = B.ring("asq", [128, 512], F32, 1)
            P.op("act", "activation", reads=[omk + "0"], writes=[sqk], out=sq[:, 0:n], in_=om[:, 0, 0:n], func=AF.Square)
            bq = B.bank()
            while bq in (ACC, RS):
                bq = B.bank()
            P.op("pe", "matmul", reads=[sqk, "ones_f"], writes=["ps%d" % bq], out=B.ps[bq][:, 0:n], lhsT=ones_f[:, :], rhs=sq[:, 0:n], start=True, stop=True)
            P.op("dve", "tensor_scalar", reads=["ps%d" % bq], writes=[sqk], out=sq[:, 0:n], in0=B.ps[bq][:, 0:n], scalar1=1.0 / 128, scalar2=EPS,
                 op0=ALU.mult, op1=ALU.add)
            P.op("act", "activation", reads=[sqk], writes=[sqk], out=sq[:, 0:n], in_=sq[:, 0:n], func=AF.Sqrt)
            P.op("dve", "reciprocal", reads=[sqk], writes=[sqk], out=sq[:, 0:n], in_=sq[:, 0:n])
            P.op("dve", "scalar_tensor_tensor", reads=[omk + "0", "lamtmp", sqk], writes=[sqk], out=sq[:, 0:n], in0=om[:, 0, 0:n], scalar=gcol,
                 in1=sq[:, 0:n], op0=ALU.mult, op1=ALU.mult)
            bg = B.bank()
            while bg in (ACC, RS):
                bg = B.bank()
            proj_fm(B, wga, wgak, 128 * h, hT, "hT", hcol(t0), n, bg)
            sg, sgk = B.ring("sga", [128, 512], F32, 1)
            P.op("act", "activation", reads=["ps%d" % bg], writes=[sgk], out=sg[:, 0:n], in_=B.ps[bg][:, 0:n], func=AF.Silu)
            P.op("dve", "tensor_tensor", reads=[sgk, sqk], writes=["gaT"], out=gaT[:, h, t0:t0 + n], in0=sg[:, 0:n], in1=sq[:, 0:n], op=ALU.mult)
    B.release(m0)
```

```python
import contextlib
import math
import numpy as np
import ml_dtypes
import concourse.bass as bass
import concourse.mybir as mybir
from concourse.bass_utils import run_bass_kernel_spmd

F32 = mybir.dt.float32
BF16 = mybir.dt.bfloat16
AF = mybir.ActivationFunctionType
ALU = mybir.AluOpType
AX = mybir.AxisListType

D = 1024
SEQ = 8192
NB = 2
DEPTH = 4
CTX = 256
TOK = 2048
NT = TOK // 128
EPS = 1e-6
COL_K, COL_V, COL_Q, COL_GA, COL_HY, COL_GB, COL_GM, COL_GC, COL_MG, COL_END = (
    0, 512, 1024, 1536, 2048, 3584, 4096, 5120, 5632, 8704)
PI = math.pi


class Prog:
    ENG = ("pe", "act", "dve", "pool", "sp")

    def __init__(self, nc):
        self.nc = nc
        self.ops = {e: [] for e in self.ENG}
        self.cnt = {e: 0 for e in self.ENG}
        self.semidx = {e: 0 for e in self.ENG}
        self.known = {e: {} for e in self.ENG}
        self.lastw = {}
        self.reads = {}
        self.ndma = 0
        self.dma_uses = {}
        self.NDMASEM = 32
        self.semnames = set()

    def _need(self, eng, reads, writes):
        toks = []
        for b in reads:
            t = self.lastw.get(b)
            if t is not None:
                toks.append(t)
        for b in writes:
            t = self.lastw.get(b)
            if t is not None:
                toks.append(t)
            toks.extend(self.reads.get(b, ()))
        need = {}
        for (s, v, e) in toks:
            if e == "pe" and eng == "pe":
                continue
            if v > need.get(s, 0):
                need[s] = v
        kn = self.known[eng]
        out = []
        for s, v in need.items():
            if kn.get(s, 0) >= v:
                continue
            kn[s] = v
            out.append((s, v))
        return out

    def _commit(self, tok, reads, writes):
        for b in reads:
            lst = self.reads.setdefault(b, [])
            lst.append(tok)
            if len(lst) > 64:
                mx = {}
                for (s, v, e) in lst:
                    if v > mx.get(s, (0, None))[0]:
                        mx[s] = (v, e)
                self.reads[b] = [(s, v, e) for s, (v, e) in mx.items()]
        for b in writes:
            self.lastw[b] = tok
            self.reads[b] = []

    def op(self, eng, fname, reads=(), writes=(), **kw):
        waits = self._need(eng, reads, writes)
        self.cnt[eng] += 1
        if self.cnt[eng] > 30000:
            self.semidx[eng] += 1
            self.cnt[eng] = 1
        s = "c_%s%d" % (eng, self.semidx[eng])
        self.semnames.add(s)
        tok = (s, self.cnt[eng], eng)
        self.ops[eng].append((waits, fname, kw, (s, 1)))
        self._commit(tok, reads, writes)

    def dma(self, eng, reads=(), writes=(), _fname="dma_start", **kw):
        j = self.ndma % self.NDMASEM
        self.ndma += 1
        s = "d_%d" % j
        self.semnames.add(s)
        uses = self.dma_uses.get(s, 0)
        waits = self._need(eng, reads, writes)
        if uses > 0 and self.known[eng].get(s, 0) < 16 * uses:
            self.known[eng][s] = 16 * uses
            waits.append((s, 16 * uses))
        self.dma_uses[s] = uses + 1
        tok = (s, 16 * (uses + 1), "dma")
        self.ops[eng].append((waits, _fname, kw, (s, 16)))
        self._commit(tok, reads, writes)

    def cc(self, reads=(), writes=(), **kw):
        waits = self._need("pool", reads, writes)
        self.ncc = getattr(self, "ncc", 0) + 1
        s = "ccsem%d" % self.ncc
        self.semnames.add(s)
        tok = (s, 1, "cc")
        self.ops["pool"].append((waits, "collective_compute", kw, (s, 1)))
        self._commit(tok, reads, writes)

    def barrier(self):
        latest = []
        for e in self.ENG:
            if self.cnt[e] > 0:
                latest.append(("c_%s%d" % (e, self.semidx[e]), self.cnt[e]))
        for s, uses in self.dma_uses.items():
            latest.append((s, 16 * uses))

        for e in self.ENG:
            kn = self.known[e]
            waits = []
            for (s, v) in latest:
                if kn.get(s, 0) < v:
                    kn[s] = v
                    waits.append((s, v))
            if waits:
                self.ops[e].append((waits, None, None, None))

    def wait_all(self, eng, bufs):
        waits = self._need(eng, bufs, ())
        self.ops[eng].append((waits, None, None, None))

    def run(self):
        nc = self.nc
        with contextlib.ExitStack() as st:
            sems = {n: st.enter_context(nc.semaphore(n)) for n in sorted(self.semnames)}
            block = st.enter_context(nc.Block())

            def replay(name):
                def f(e):
                    for waits, fname, kw, inc in self.ops[name]:
                        for (s, v) in waits:
                            e.wait_ge(sems[s], v)
                        if fname is not None:
                            kw = {k_: (v_(e) if callable(v_) else v_) for k_, v_ in kw.items()}
                            try:
                                ins = getattr(e, fname)(**kw)
                            except Exception:
                                print("FAILED OP", name, fname, {k_: str(v_)[:300] for k_, v_ in kw.items()})
                                raise
                            ins.then_inc(sems[inc[0]], inc[1])
                return f
            block.tensor(replay("pe"))
            block.scalar(replay("act"))
            block.vector(replay("dve"))
            block.gpsimd(replay("pool"))
            block.sync(replay("sp"))


class Builder:
    def __init__(self):
        self.nc = bass.Bass("TRN2", target_bir_lowering=False)
        self.P = Prog(self.nc)
        self.st = contextlib.ExitStack()
        self.outs = []
        self.nbank = 0
        self.rings = {}

    def din(self, name, shape, dt=F32):
        return self.nc.dram_tensor(name, list(shape), dt, kind="ExternalInput").ap()

    def dout(self, name, shape, dt=F32):
        self.outs.append(name)
        return self.nc.dram_tensor(name, list(shape), dt, kind="ExternalOutput").ap()

    def dscratch(self, name, shape, dt=F32):
        return self.nc.dram_tensor(name, list(shape), dt, kind="Internal").ap()

    ARENA_WORDS = 51 * 1024

    def sb(self, name, shape, dt=F32):
        if not hasattr(self, "arena"):
            self.arena = self.st.enter_context(self.nc.sbuf_tensor("arena", [128, self.ARENA_WORDS], F32))
            self.top = 0
        nel = 1
        for d_ in shape[1:]:
            nel *= d_
        esz = 4 if dt == F32 else 2
        words = (nel * esz + 3) // 4
        assert self.top + words <= self.ARENA_WORDS, "arena overflow at %s: top=%d need=%d" % (name, self.top, words)
        v = self.arena[:, self.top:self.top + words]
        self.top += words
        if dt != F32:
            v = v.bitcast(dt)
        v = v[:, 0:nel]
        if len(shape) == 3:
            v = v.rearrange("p (a b) -> p a b", b=shape[2])
        elif len(shape) == 4:
            v = v.rearrange("p (a b c) -> p a b c", b=shape[2], c=shape[3])
        if shape[0] < 128:
            v = v[0:shape[0]]
        return v

    def mark(self):
        return (self.top, set(self.rings.keys()))

    def release(self, m):
        self.P.barrier()
        self.top = m[0]
        for k in list(self.rings.keys()):
            if k not in m[1]:
                del self.rings[k]

    def init_psum(self):
        self.pp = [self.st.enter_context(self.nc.psum_tensor("psp%d" % i, [128, 1024], F32)) for i in range(4)]
        self.ps = [self.pp[i // 2][:, 512 * (i % 2):512 * (i % 2 + 1)] for i in range(8)]

    def bank(self):
        i = self.nbank % 8
        self.nbank += 1
        return i

    def ring(self, name, shape, dt, n):
        if name not in self.rings:
            self.rings[name] = [[self.sb("%s_%d" % (name, i), shape, dt) for i in range(n)], 0]
        r = self.rings[name]
        i = r[1] % n
        r[1] += 1
        return r[0][i], "%s_%d" % (name, i)

    def finish(self):
        self.P.wait_all("sp", self.outs)
        self.P.run()
        self.st.close()
        return self.nc


def make_identity(B, dt=BF16):
    P = B.P
    idf = B.sb("ident_f", [128, 128], F32)
    P.op("pool", "memset", writes=["ident_f"], ap=idf[:], constant=0.0)
    P.op("pool", "affine_select", reads=["ident_f"], writes=["ident_f"], out=idf[:], in_=idf[:],
         compare_op=ALU.not_equal, fill=1.0, base=0, pattern=[[-1, 128]], channel_multiplier=1)
    idb = B.sb("ident_b", [128, 128], BF16)
    P.op("pool", "tensor_copy", reads=["ident_f"], writes=["ident_b"], out=idb[:], in_=idf[:])
    return idf, idb


def modulation(B, c_ap, ada_w, ada_b, npre, npost, names, dest):
    P = B.P
    nA, nB, nG = names
    cT = B.sb("cT_" + nA, [128, 8], F32)
    P.dma("sp", writes=["cT" + nA], out=cT[:], in_=c_ap.rearrange("(p k) -> p k", k=8))
    P.op("act", "activation", reads=["cT" + nA], writes=["cT" + nA], out=cT[:], in_=cT[:], func=AF.Silu)
    rows = B.sb("rows_" + nA, [1, 3072], F32)
    bro = B.sb("brow_" + nA, [1, 3072], F32)
    P.dma("sp", writes=["brow" + nA], out=bro[:], in_=ada_b.rearrange("(o c) -> o c", o=1))
    wv = ada_w.rearrange("(p k) c -> p k c", k=8)
    for ch in range(6):
        wt, wk = B.ring("adaw", [128, 8, 512], F32, 2)
        P.dma("sp", writes=[wk], out=wt[:], in_=wv[:, :, ch * 512:(ch + 1) * 512])
        bk = B.bank()
        for k in range(8):
            P.op("pe", "matmul", reads=[wk, "cT" + nA], writes=["ps%d" % bk], out=B.ps[bk][0:1, :], lhsT=cT[:, k:k + 1],
                 rhs=wt[:, k, :], start=(k == 0), stop=(k == 7))
        P.op("dve", "tensor_tensor", reads=["ps%d" % bk, "brow" + nA], writes=["rows" + nA],
             out=rows[:, ch * 512:(ch + 1) * 512], in0=B.ps[bk][0:1, :], in1=bro[:, ch * 512:(ch + 1) * 512], op=ALU.add)
    gp = B.sb("gp_" + nA, [1, 2048], F32)
    P.dma("sp", writes=["gp" + nA], out=gp[:, 0:1024], in_=npre.rearrange("(o c) -> o c", o=1))
    P.dma("sp", writes=["gp" + nA], out=gp[:, 1024:2048], in_=npost.rearrange("(o c) -> o c", o=1))
    P.op("dve", "scalar_tensor_tensor", reads=["rows" + nA, "gp" + nA], writes=["rows" + nA], out=rows[:, 1024:2048],
         in0=rows[:, 1024:2048], scalar=1.0, in1=gp[:, 0:1024], op0=ALU.add, op1=ALU.mult)
    P.op("dve", "tensor_tensor", reads=["rows" + nA, "gp" + nA], writes=["rows" + nA], out=rows[:, 2048:3072],
         in0=rows[:, 2048:3072], in1=gp[:, 1024:2048], op=ALU.mult)
    ones = B.sb("ones_" + nA, [1, 128], F32)
    P.op("dve", "memset", writes=["ones" + nA], ap=ones[:], constant=1.0)
    tiles = {}
    for nm, off in ((nB, 0), (nA, 1024), (nG, 2048)):
        if nm is None:
            continue
        t = dest[nm]
        for hh in range(2):
            bk = B.bank()
            P.op("pe", "matmul", reads=["ones" + nA, "rows" + nA], writes=["ps%d" % bk], out=B.ps[bk][:, :], lhsT=ones[:, :],
                 rhs=rows[:, off + hh * 512: off + (hh + 1) * 512], start=True, stop=True)
            P.op("act", "activation", reads=["ps%d" % bk], writes=["bc" + nm], out=t[:, hh * 512:(hh + 1) * 512],
                 in_=B.ps[bk][:, :], func=AF.Identity)
        tiles[nm] = t
    return tiles


def modulation2(B, c_ap, cctx_ap, ada_w, ada_b, npre, npost, dest):
    P = B.P
    cT = B.sb("cT2", [128, 8, 2], F32)
    P.dma("sp", writes=["cT2"], out=cT[:, :, 0:1], in_=c_ap.rearrange("(p k o) -> p k o", k=8, o=1), allow_slow_non_contiguous=True)
    P.dma("sp", writes=["cT2"], out=cT[:, :, 1:2], in_=cctx_ap.rearrange("(p k o) -> p k o", k=8, o=1), allow_slow_non_contiguous=True)
    P.op("act", "activation", reads=["cT2"], writes=["cT2"], out=cT[:], in_=cT[:], func=AF.Silu)
    rows = B.sb("rows2", [2, 3072], F32)
    bro = B.sb("brow2", [2, 3072], F32)
    gp = B.sb("gp2", [2, 2048], F32)
    for r in range(2):
        P.dma("sp", writes=["brow2"], out=bro[r:r + 1, :], in_=ada_b.rearrange("(o c) -> o c", o=1))
        P.dma("sp", writes=["gp2"], out=gp[r:r + 1, 0:1024], in_=npre.rearrange("(o c) -> o c", o=1))
        P.dma("sp", writes=["gp2"], out=gp[r:r + 1, 1024:2048], in_=npost.rearrange("(o c) -> o c", o=1))
    wv = ada_w.rearrange("(p k) c -> p k c", k=8)
    for ch in range(6):
        wt, wk = B.ring("adaw", [128, 8, 512], F32, 2)
        P.dma("sp", writes=[wk], out=wt[:], in_=wv[:, :, ch * 512:(ch + 1) * 512])
        bk = B.bank()
        for k in range(8):
            P.op("pe", "matmul", reads=[wk, "cT2"], writes=["ps%d" % bk], out=B.ps[bk][0:2, :], lhsT=cT[:, k, :],
                 rhs=wt[:, k, :], start=(k == 0), stop=(k == 7))
        P.op("dve", "tensor_tensor", reads=["ps%d" % bk, "brow2"], writes=["rows2"],
             out=rows[:, ch * 512:(ch + 1) * 512], in0=B.ps[bk][0:2, :], in1=bro[:, ch * 512:(ch + 1) * 512], op=ALU.add)
    P.op("dve", "scalar_tensor_tensor", reads=["rows2", "gp2"], writes=["rows2"], out=rows[:, 1024:2048],
         in0=rows[:, 1024:2048], scalar=1.0, in1=gp[:, 0:1024], op0=ALU.add, op1=ALU.mult)
    P.op("dve", "tensor_tensor", reads=["rows2", "gp2"], writes=["rows2"], out=rows[:, 2048:3072],
         in0=rows[:, 2048:3072], in1=gp[:, 1024:2048], op=ALU.mult)
    sel = B.sb("sel2", [2, 2, 128], F32)
    P.op("dve", "memset", writes=["sel2"], ap=sel[:, 0, :], constant=0.0)
    P.op("dve", "memset", writes=["sel2"], ap=sel[0:1, 0, :], constant=1.0)
    P.op("dve", "tensor_scalar", reads=["sel2"], writes=["sel2"], out=sel[:, 1, :], in0=sel[:, 0, :], scalar1=-1.0, scalar2=1.0,
         op0=ALU.mult, op1=ALU.add)
    for r, names in ((0, ("Bm", "A", "G")), (1, ("Bc", "Ac", "Gc"))):
        for nm, off in zip(names, (0, 1024, 2048)):
            t = dest[nm]
            for hh in range(2):
                bk = B.bank()
                P.op("pe", "matmul", reads=["sel2", "rows2"], writes=["ps%d" % bk], out=B.ps[bk][:, :], lhsT=sel[:, r, :],
                     rhs=rows[:, off + hh * 512: off + (hh + 1) * 512], start=True, stop=True)
                P.op("act", "activation", reads=["ps%d" % bk], writes=["bc" + nm], out=t[:, hh * 512:(hh + 1) * 512],
                     in_=B.ps[bk][:, :], func=AF.Identity)


def compute_h_tile(B, x_rows_aps, n, Abc, Bbc, keyA, keyB, hT, hkey, col0, idb, mask=None, keep_x=None, xkeys=()):
    P = B.P
    if keep_x is None:
        xt, xk = B.ring("xt", [128, 1024], F32, 3)
    else:
        xt, xk = keep_x
    for ap, r0, r in x_rows_aps:
        P.dma("sp", reads=list(xkeys), writes=[xk], out=xt[r0:r0 + r, :], in_=ap)
    junk, jk = B.ring("hjunk", [128, 1024], BF16, 2)
    st_, sk = B.ring("hstat", [128, 4], F32, 4)
    P.op("act", "activation", reads=[xk], writes=[jk, sk], out=junk[0:n, :], in_=xt[0:n, :], func=AF.Square,
         accum_out=st_[0:n, 0:1])
    P.op("dve", "tensor_scalar", reads=[sk], writes=[sk], out=st_[0:n, 1:2], in0=st_[0:n, 0:1], scalar1=1.0 / D,
         scalar2=EPS, op0=ALU.mult, op1=ALU.add)
    P.op("act", "activation", reads=[sk], writes=[sk], out=st_[0:n, 2:3], in_=st_[0:n, 1:2], func=AF.Sqrt)
    P.op("dve", "reciprocal", reads=[sk], writes=[sk], out=st_[0:n, 3:4], in_=st_[0:n, 2:3])
    hm, hk = B.ring("hm", [128, 1024], F32, 2)
    P.op("dve", "scalar_tensor_tensor", reads=[xk, sk, keyA], writes=[hk], out=hm[0:n, :], in0=xt[0:n, :],
         scalar=st_[0:n, 3:4], in1=Abc[0:n, :], op0=ALU.mult, op1=ALU.mult)
    hb, hbk = B.ring("hb", [128, 1024], BF16, 2)
    P.op("pool", "tensor_tensor", reads=[hk, keyB], writes=[hbk], out=hb[0:n, :], in0=hm[0:n, :], in1=Bbc[0:n, :], op=ALU.add)
    if mask is not None:
        mt, mk = mask
        P.op("pool", "tensor_scalar", reads=[hbk, mk], writes=[hbk], out=hb[0:n, :], in0=hb[0:n, :], scalar1=mt[0:n, 0:1],
             scalar2=None, op0=ALU.mult)
    bk = B.bank()
    psb = B.ps[bk][:, :].bitcast(BF16)
    for k in range(8):
        P.op("pe", "transpose", reads=[hbk, "ident_b"], writes=["ps%d" % bk], out=psb[:, k * 128:k * 128 + n],
             in_=hb[0:n, k * 128:(k + 1) * 128], identity=idb[0:n, 0:n])
    P.op("act", "activation", reads=["ps%d" % bk], writes=[hkey], out=hT[:, :, col0:col0 + n],
         in_=psb.rearrange("p (k t) -> p k t", t=128)[:, :, 0:n], func=AF.Identity)
    return st_, sk


def load_cast(B, dst, dstk, src, shape3):
    a, b = shape3
    st, sk = B.ring("wstage", [128, 1024], F32, 2)
    sv = st[:, 0:a * b].rearrange("p (a b) -> p a b", b=b)
    B.P.dma("sp", writes=[sk], out=sv, in_=src)
    eng = getattr(B, "cast_eng", "pool")
    if eng == "act":
        B.P.op("act", "activation", reads=[sk], writes=[dstk], out=dst, in_=sv, func=AF.Identity)
    else:
        B.P.op(eng, "tensor_copy", reads=[sk], writes=[dstk], out=dst, in_=sv)


def wblock(B, w_ap, col0, ncols, ringname="wblk", nbuf=2, width=512):
    wt, wk = B.ring(ringname, [128, 8, width], BF16, nbuf)
    wv = w_ap.rearrange("(k p) c -> p k c", p=128)
    for c in range(0, ncols, 128):
        load_cast(B, wt[:, :, c:c + 128], wk, wv[:, :, col0 + c:col0 + c + 128], (8, 128))
    return wt, wk


def proj_fm(B, wt, wk, c0, hT, hkey, t0, nt, bk, M=128):
    for k in range(8):
        B.P.op("pe", "matmul", reads=[wk, hkey], writes=["ps%d" % bk], out=B.ps[bk][0:M, 0:nt], lhsT=wt[:, k, c0:c0 + M],
               rhs=hT[:, k, t0:t0 + nt], start=(k == 0), stop=(k == 7))


def proj_tm(B, wt, wk, c0, ncols, hT, hkey, t0, n, bk):
    for k in range(8):
        B.P.op("pe", "matmul", reads=[wk, hkey], writes=["ps%d" % bk], out=B.ps[bk][0:n, 0:ncols], lhsT=hT[:, k, t0:t0 + n],
               rhs=wt[:, k, c0:c0 + ncols], start=(k == 0), stop=(k == 7))


def load_shortconv(B, hy_w, hy_b):
    P = B.P
    t = B.sb("scw", [128, 12, 4], F32)
    for j in range(3):
        P.dma("sp", writes=["scw"], out=t[:, :, j:j + 1], in_=hy_w[j].rearrange("(t p o) -> p t o", p=128, o=1),
              allow_slow_non_contiguous=True)
    P.dma("sp", writes=["scw"], out=t[:, :, 3:4], in_=hy_b.rearrange("(t p o) -> p t o", p=128, o=1),
          allow_slow_non_contiguous=True)
    return t


def hy_conv_tile(B, w_in, ct, hT, hkey, segs, scw, outt, outk, eng="dve"):
    P = B.P
    wt, wk = wblock(B, w_in, COL_HY + ct * 128, 128, "wblk128", 3, 128)
    for (tc0, n, oc0) in segs:
        zr, zk = B.ring("zrow", [128, 2050], F32, 1)
        pos = tc0 - 1
        end = tc0 + n + 1
        while pos < end:
            m = min(512, end - pos)
            bk = B.bank()
            proj_fm(B, wt, wk, 0, hT, hkey, pos, m, bk)
            P.op("act", "activation", reads=["ps%d" % bk], writes=[zk], out=zr[:, pos - (tc0 - 1): pos - (tc0 - 1) + m],
                 in_=B.ps[bk][:, 0:m], func=AF.Identity)
            pos += m
        tmp, tk = B.ring("cvtmp", [128, 2048], F32, 1)
        P.op(eng, "tensor_scalar", reads=[zk, "scw"], writes=[tk], out=tmp[:, 0:n], in0=zr[:, 0:n], scalar1=scw[:, ct, 0:1],
             scalar2=scw[:, ct, 3:4], op0=ALU.mult, op1=ALU.add)
        P.op("dve", "scalar_tensor_tensor", reads=[zk, "scw", tk], writes=[tk], out=tmp[:, 0:n], in0=zr[:, 1:n + 1],
             scalar=scw[:, ct, 1:2], in1=tmp[:, 0:n], op0=ALU.mult, op1=ALU.add)
        P.op("dve", "scalar_tensor_tensor", reads=[zk, "scw", tk], writes=[outk], out=outt[:, oc0:oc0 + n], in0=zr[:, 2:n + 2],
             scalar=scw[:, ct, 2:3], in1=tmp[:, 0:n], op0=ALU.mult, op1=ALU.add)


def build_s1():
    B = Builder()
    P = B.P
    xs = B.din("xs", [TOK + 2, D])
    cvec = B.din("c", [D])
    ada_w = B.din("ada_w", [D, 3 * D])
    ada_b = B.din("ada_b", [3 * D])
    npre = B.din("npre", [D])
    npost = B.din("npost", [D])
    w_in = B.din("w_in", [D, COL_END])
    wk_perm = B.din("wk_perm", [D, 512])
    hy_w = B.din("hy_w", [3, 1536])
    hy_b = B.din("hy_b", [1536])
    cosT = B.din("cosT", [128, TOK])
    sinT = B.din("sinT", [128, TOK])
    hmask = B.din("hmask", [2, 1])
    o_kt = B.dout("o_kt", [4, 128, TOK], BF16)
    o_v = B.dout("o_v", [TOK, 512], BF16)
    o_u = B.dout("o_u", [512, TOK], F32)
    B.init_psum()
    idf, idb = make_identity(B)
    Abc = B.sb("bc_A", [128, 1024], F32)
    Bbc = B.sb("bc_Bm", [128, 1024], F32)
    hT = B.sb("hT", [128, 8, TOK + 2], BF16)
    mt = B.sb("hmask", [2, 1], F32)
    m0 = B.mark()
    modulation(B, cvec, ada_w, ada_b, npre, npost, ("A", "Bm", None), {"A": Abc, "Bm": Bbc})
    B.release(m0)
    P.dma("sp", writes=["hmask"], out=mt[:], in_=hmask[:, :])
    for i in range(NT):
        compute_h_tile(B, [(xs[1 + 128 * i: 1 + 128 * (i + 1), :], 0, 128)], 128, Abc, Bbc, "bcA", "bcBm", hT, "hT", 1 + 128 * i, idb)
    hh = B.sb("hTh", [128, 8, 2], BF16)
    compute_h_tile(B, [(xs[0:1, :], 0, 1), (xs[TOK + 1:TOK + 2, :], 1, 1)], 2, Abc, Bbc, "bcA", "bcBm", hh, "hTh", 0, idb,
                   mask=(mt, "hmask"))
    P.op("pool", "tensor_copy", reads=["hTh"], writes=["hT"], out=hT[:, :, 0:1], in_=hh[:, :, 0:1])
    P.op("pool", "tensor_copy", reads=["hTh"], writes=["hT"], out=hT[:, :, TOK + 1:TOK + 2], in_=hh[:, :, 1:2])
    B.release(m0)
    cs = B.sb("cosT", [128, TOK], F32)
    sn = B.sb("sinT", [128, TOK], F32)
    P.dma("sp", writes=["cosT"], out=cs[:], in_=cosT[:, :])
    P.dma("sp", writes=["sinT"], out=sn[:], in_=sinT[:, :])
    wt, wk = wblock(B, w_in, COL_K, 512, "wblkK")
    wp, wpk = wblock(B, wk_perm, 0, 512, "wblkK")
    for h in range(4):
        ko, kk = B.ring("kout", [128, TOK], BF16, 2)
        for j in range(TOK // 512):
            b1 = B.bank()
            proj_fm(B, wt, wk, 128 * h, hT, "hT", 1 + 512 * j, 512, b1)
            b2 = B.bank()
            proj_fm(B, wp, wpk, 128 * h, hT, "hT", 1 + 512 * j, 512, b2)
            t1, t1k = B.ring("rtmp1", [128, 512], F32, 2)
            t2, t2k = B.ring("rtmp2", [128, 512], F32, 2)
            P.op("dve", "tensor_tensor", reads=["ps%d" % b1, "cosT"], writes=[t1k], out=t1[:], in0=B.ps[b1][:, :],
                 in1=cs[:, 512 * j:512 * (j + 1)], op=ALU.mult)
            P.op("dve", "tensor_tensor", reads=["ps%d" % b2, "sinT"], writes=[t2k], out=t2[:], in0=B.ps[b2][:, :],
                 in1=sn[:, 512 * j:512 * (j + 1)], op=ALU.mult)
            P.op("pool", "tensor_tensor", reads=[t1k, t2k], writes=[kk], out=ko[:, 512 * j:512 * (j + 1)], in0=t1[:], in1=t2[:],
                 op=ALU.add)
        P.dma("sp", reads=[kk], writes=["o_kt"], out=o_kt[h], in_=ko[:])
    B.release(m0)
    wt, wk = wblock(B, w_in, COL_V, 512, "wblkK")
    for i in range(NT):
        bk = B.bank()
        proj_tm(B, wt, wk, 0, 512, hT, "hT", 1 + 128 * i, 128, bk)
        vo, vk = B.ring("vout", [128, 512], BF16, 3)
        P.op("act", "activation", reads=["ps%d" % bk], writes=[vk], out=vo[:], in_=B.ps[bk][:, :], func=AF.Identity)
        P.dma("sp", reads=[vk], writes=["o_v"], out=o_v[128 * i:128 * (i + 1), :], in_=vo[:])
    B.release(m0)
    scw = load_shortconv(B, hy_w, hy_b)
    for ci in range(4):
        x1s, x1k = B.ring("x1s", [128, TOK], F32, 2)
        vs, vsk = B.ring("vs", [128, TOK], F32, 2)
        hy_conv_tile(B, w_in, 4 + ci, hT, "hT", [(1, TOK, 0)], scw, x1s, x1k, "dve")
        hy_conv_tile(B, w_in, 8 + ci, hT, "hT", [(1, TOK, 0)], scw, vs, vsk, "pool")
        P.op("dve", "tensor_tensor", reads=[x1k, vsk], writes=[x1k], out=x1s[:], in0=x1s[:], in1=vs[:], op=ALU.mult)
        P.dma("sp", reads=[x1k], writes=["o_u"], out=o_u[128 * ci:128 * (ci + 1), :], in_=x1s[:])
    return B.finish()


def rope_tables():
    rows = SEQ // 64
    row = np.repeat(np.arange(rows, dtype=np.float32), 64)
    col = np.tile(np.arange(64, dtype=np.float32), rows)
    inv = (10000.0 ** (-np.arange(16, dtype=np.float32) / 16)).astype(np.float32)
    ang_r = row[None, :] * inv[:, None]
    ang_c = col[None, :] * inv[:, None]
    cos64 = np.concatenate([np.cos(ang_r), np.cos(ang_r), np.cos(ang_c), np.cos(ang_c)], 0)
    sin64 = np.concatenate([-np.sin(ang_r), np.sin(ang_r), -np.sin(ang_c), np.sin(ang_c)], 0)
    cosT = np.concatenate([cos64, cos64], 0).astype(np.float32)
    sinT = np.concatenate([sin64, sin64], 0).astype(np.float32)
    return cosT, sinT


def perm_cols(w):
    k, c = w.shape
    return np.ascontiguousarray(w.reshape(k, c // 32, 2, 16)[:, :, ::-1, :].reshape(k, c))


_NC = {}


def get_nc(name):
    if name not in _NC:
        _NC[name] = globals()["build_" + name]()
    return _NC[name]


def run_s1(x, c, l, ada_w, ada_b, norm_pre, norm_post, w_in, hy_short_w, hy_short_b):
    nc = get_nc("s1")
    cosT, sinT = rope_tables()
    wkp = perm_cols(w_in[l][:, COL_K:COL_K + 512])
    in_maps = []
    for core in range(8):
        b, r = divmod(core, 4)
        t0 = r * TOK
        xs = np.zeros((TOK + 2, D), np.float32)
        lo, hi = max(t0 - 1, 0), min(t0 + TOK + 1, SEQ)
        xs[lo - (t0 - 1): hi - (t0 - 1)] = x[b, lo:hi]
        hmask = np.array([[0.0 if t0 == 0 else 1.0], [0.0 if t0 + TOK == SEQ else 1.0]], np.float32)
        in_maps.append({
            "xs": xs, "c": c[b], "ada_w": ada_w[l], "ada_b": ada_b[l], "npre": norm_pre[l], "npost": norm_post[l],
            "w_in": w_in[l], "wk_perm": wkp, "hy_w": hy_short_w[l], "hy_b": hy_short_b[l],
            "cosT": np.ascontiguousarray(cosT[:, t0:t0 + TOK]), "sinT": np.ascontiguousarray(sinT[:, t0:t0 + TOK]),
            "hmask": hmask,
        })
    res = run_bass_kernel_spmd(nc, in_maps, core_ids=list(range(8)))
    return res.results


NFFT = 2 * SEQ
CG = 32


def hyena_tables():
    L = SEQ
    tau = np.arange(NFFT)
    pos = np.where(tau < L, tau, NFFT - tau).astype(np.int64)
    posc = np.minimum(pos, L - 1)
    t_lin = np.linspace(0.0, 1.0, L, dtype=np.float32)
    wpos = ((2.0 * math.pi / L) * np.arange(L, dtype=np.float32)).astype(np.float32)
    bands = np.linspace(1e-4, 16 - 1, 16, dtype=np.float32)
    zfull = np.concatenate([t_lin[:, None], np.cos(bands[None, :] * wpos[:, None]), -np.sin(bands[None, :] * wpos[:, None])],
                           axis=-1).astype(np.float32)
    zpos = np.ascontiguousarray(zfull[posc].T)
    mn = math.log(1e-2) / 1.5
    mx = math.log(1e-2) / 0.3
    deltas = np.abs(np.linspace(mn, mx, 512, dtype=np.float32))
    win = np.exp(-t_lin[posc][None, :] * deltas[:, None]).astype(np.float32)
    win[:, L] = 0.0
    a = np.arange(128, dtype=np.float64)
    ang = 2.0 * math.pi * np.outer(a, a) / 128.0
    C = np.cos(ang).astype(np.float32)
    S = np.sin(ang).astype(np.float32)
    angt = 2.0 * math.pi * np.outer(a, a) / NFFT
    Tc = np.cos(angt).astype(np.float32)
    Ts = np.sin(angt).astype(np.float32)
    dft = np.concatenate([C, S, C, -S, C], axis=1).astype(np.float32)
    tw = np.concatenate([Tc, Ts], axis=1).astype(np.float32)
    return zpos, win, dft, tw


def wrap_pi(B, a, ak, n, rows):
    P = B.P
    m, mk = B.ring("wrapm", [128, 512], F32, 2)
    P.op("dve", "tensor_scalar", reads=[ak], writes=[mk], out=m[0:rows, 0:n], in0=a[0:rows, 0:n], scalar1=PI, scalar2=-2.0 * PI,
         op0=ALU.is_gt, op1=ALU.mult)
    P.op("dve", "tensor_tensor", reads=[ak, mk], writes=[ak], out=a[0:rows, 0:n], in0=a[0:rows, 0:n], in1=m[0:rows, 0:n], op=ALU.add)
    P.op("dve", "tensor_scalar", reads=[ak], writes=[mk], out=m[0:rows, 0:n], in0=a[0:rows, 0:n], scalar1=-PI, scalar2=2.0 * PI,
         op0=ALU.is_lt, op1=ALU.mult)
    P.op("dve", "tensor_tensor", reads=[ak, mk], writes=[ak], out=a[0:rows, 0:n], in0=a[0:rows, 0:n], in1=m[0:rows, 0:n], op=ALU.add)


def filter_mlp_chunk(B, fw, zt, zk, n):
    P = B.P
    w1, w2, cols = fw["w1"], fw["w2"], fw["cols"]
    bk = B.bank()
    P.op("pe", "matmul", reads=["fw", zk], writes=["ps%d" % bk], out=B.ps[bk][0:64, 0:n], lhsT=w1[0:33, :], rhs=zt[0:33, 0:n],
         start=True, stop=True)
    a1, a1k = B.ring("fa", [128, 512], F32, 3)
    P.op("dve", "tensor_scalar", reads=["ps%d" % bk, "fw"], writes=[a1k], out=a1[0:64, 0:n], in0=B.ps[bk][0:64, 0:n],
         scalar1=cols[0:64, 0:1], scalar2=cols[0:64, 1:2], op0=ALU.add, op1=ALU.mult)
    wrap_pi(B, a1, a1k, n, 64)
    P.op("act", "activation", reads=[a1k], writes=[a1k], out=a1[0:64, 0:n], in_=a1[0:64, 0:n], func=AF.Sin)
    bk = B.bank()
    P.op("pe", "matmul", reads=["fw", a1k], writes=["ps%d" % bk], out=B.ps[bk][0:64, 0:n], lhsT=w2[0:64, :], rhs=a1[0:64, 0:n],
         start=True, stop=True)
    a2, a2k = B.ring("fa", [128, 512], F32, 3)
    P.op("dve", "tensor_scalar", reads=["ps%d" % bk, "fw"], writes=[a2k], out=a2[0:64, 0:n], in0=B.ps[bk][0:64, 0:n],
         scalar1=cols[0:64, 2:3], scalar2=cols[0:64, 3:4], op0=ALU.add, op1=ALU.mult)
    wrap_pi(B, a2, a2k, n, 64)
    P.op("act", "activation", reads=[a2k], writes=[a2k], out=a2[0:64, 0:n], in_=a2[0:64, 0:n], func=AF.Sin)
    return a2, a2k


def load_filter_weights(B, hw1, hb1, hw2, hb2, hfreq):
    P = B.P
    w1 = B.sb("fw1", [33, 64], F32)
    w2 = B.sb("fw2", [64, 64], F32)
    cols = B.sb("fcols", [64, 4], F32)
    P.dma("sp", writes=["fw"], out=w1[:], in_=hw1[:, :])
    P.dma("sp", writes=["fw"], out=w2[:], in_=hw2[:, :])
    P.dma("sp", writes=["fw"], out=cols[:, 0:1], in_=hb1.rearrange("(p o) -> p o", o=1))
    P.dma("sp", writes=["fw"], out=cols[:, 1:2], in_=hfreq[0].rearrange("(p o) -> p o", o=1))
    P.dma("sp", writes=["fw"], out=cols[:, 2:3], in_=hb2.rearrange("(p o) -> p o", o=1))
    P.dma("sp", writes=["fw"], out=cols[:, 3:4], in_=hfreq[1].rearrange("(p o) -> p o", o=1))
    return {"w1": w1, "w2": w2, "cols": cols}


def fft_pair_fwd(B, src, srck, kdim, c0, dft, tw):
    P = B.P
    bk = B.bank()
    for i in range(2):
        P.op("pe", "matmul", reads=[srck, "dft"], writes=["ps%d" % bk], out=B.ps[bk][:, 256 * i:256 * (i + 1)],
             lhsT=src[0:kdim, c0 + i, :], rhs=dft[0:kdim, 256:512], start=True, stop=True)
    A = B.ps[bk][:, :].rearrange("p (c r k) -> p c r k", c=2, r=2)
    Are, Aim = A[:, :, 0, :], A[:, :, 1, :]
    Tc = tw[:, 0:128].unsqueeze(1).to_broadcast([128, 2, 128])
    Ts = tw[:, 128:256].unsqueeze(1).to_broadcast([128, 2, 128])
    tt, ttk = B.ring("fft_t", [128, 4, 2, 128], F32, 2)
    pk = "ps%d" % bk
    P.op("dve", "tensor_tensor", reads=[pk, "tw"], writes=[ttk], out=tt[:, 0], in0=Are, in1=Tc, op=ALU.mult)
    P.op("dve", "tensor_tensor", reads=[pk, "tw"], writes=[ttk], out=tt[:, 1], in0=Aim, in1=Ts, op=ALU.mult)
    P.op("dve", "tensor_tensor", reads=[pk, "tw"], writes=[ttk], out=tt[:, 2], in0=Aim, in1=Tc, op=ALU.mult)
    P.op("dve", "tensor_tensor", reads=[pk, "tw"], writes=[ttk], out=tt[:, 3], in0=Are, in1=Ts, op=ALU.mult)
    b1, b1k = B.ring("fft_b1", [128, 2, 2, 128], F32, 2)
    b2, b2k = B.ring("fft_b2", [128, 2, 2, 128], F32, 2)
    P.op("pool", "tensor_tensor", reads=[ttk], writes=[b1k], out=b1[:, :, 0, :], in0=tt[:, 0], in1=tt[:, 1], op=ALU.add)
    P.op("pool", "tensor_tensor", reads=[ttk], writes=[b1k], out=b1[:, :, 1, :], in0=tt[:, 2], in1=tt[:, 3], op=ALU.subtract)
    P.op("act", "activation", reads=[b1k], writes=[b2k], out=b2[:, :, 0, :], in_=b1[:, :, 1, :], func=AF.Identity)
    P.op("act", "activation", reads=[b1k], writes=[b2k], out=b2[:, :, 1, :], in_=b1[:, :, 0, :], func=AF.Identity, scale=-1.0)
    bx = B.bank()
    P.op("pe", "matmul", reads=[b1k, "dft"], writes=["ps%d" % bx], out=B.ps[bx][:, :], lhsT=dft[:, 0:128],
         rhs=b1[:].rearrange("p c r k -> p (c r k)"), start=True, stop=False)
    P.op("pe", "matmul", reads=[b2k, "dft"], writes=["ps%d" % bx], out=B.ps[bx][:, :], lhsT=dft[:, 128:256],
         rhs=b2[:].rearrange("p c r k -> p (c r k)"), start=False, stop=True)
    return bx


def build_s2():
    B = Builder()
    P = B.P
    u_in = B.din("u", [128, SEQ])
    hw1 = B.din("hw1", [33, 64]); hb1 = B.din("hb1", [64]); hw2 = B.din("hw2", [64, 64]); hb2 = B.din("hb2", [64])
    hw3 = B.din("hw3", [64, 256]); hfreq = B.din("hfreq", [2, 64])
    zpos = B.din("zpos", [33, NFFT]); win = B.din("win", [128, NFFT])
    dft_in = B.din("dft", [128, 640]); tw_in = B.din("tw", [128, 256])
    y_out = B.dout("y", [128, SEQ])
    ssq_out = B.dout("ssq", [128, 1])
    kscr = B.dscratch("kscr", [128, NFFT])
    B.init_psum()
    dft = B.sb("dft", [128, 640], F32)
    tw = B.sb("tw", [128, 256], F32)
    P.dma("sp", writes=["dft"], out=dft[:], in_=dft_in[:, :])
    P.dma("sp", writes=["tw"], out=tw[:], in_=tw_in[:, :])
    fw = load_filter_weights(B, hw1, hb1, hw2, hb2, hfreq)
    w3 = B.sb("fw3", [64, 256], F32)
    P.dma("sp", writes=["fw"], out=w3[:], in_=hw3[:, :])
    ssqp = B.sb("ssqp", [128, 33], F32)
    P.op("dve", "memset", writes=["ssqp"], ap=ssqp[:], constant=0.0)
    m0 = B.mark()
    for ch in range(NFFT // 512):
        zt, zk = B.ring("zt", [33, 512], F32, 2)
        P.dma("sp", writes=[zk], out=zt[:], in_=zpos[:, ch * 512:(ch + 1) * 512])
        wn, wnk = B.ring("wn", [128, 512], F32, 2)
        P.dma("sp", writes=[wnk], out=wn[:], in_=win[:, ch * 512:(ch + 1) * 512])
        h2, h2k = filter_mlp_chunk(B, fw, zt, zk, 512)
        bk = B.bank()
        half = 0 if ch * 512 < SEQ else 1
        P.op("pe", "matmul", reads=["fw", h2k], writes=["ps%d" % bk], out=B.ps[bk][:, :], lhsT=w3[0:64, 128 * half:128 * (half + 1)],
             rhs=h2[0:64, :], start=True, stop=True)
        kc, kck = B.ring("kc", [128, 512], F32, 2)
        P.op("dve", "tensor_tensor", reads=["ps%d" % bk, wnk], writes=[kck], out=kc[:], in0=B.ps[bk][:, :], in1=wn[:], op=ALU.mult)
        jk_, jkk = B.ring("kjunk", [128, 512], F32, 2)
        P.op("act", "activation", reads=[kck], writes=[jkk, "ssqp"], out=jk_[:], in_=kc[:], func=AF.Square,
             accum_out=ssqp[:, ch:ch + 1])
        P.dma("sp", reads=[kck], writes=["kscr"], out=kscr[:, ch * 512:(ch + 1) * 512], in_=kc[:])
    P.op("dve", "tensor_reduce", reads=["ssqp"], writes=["ssqp"], out=ssqp[:, 32:33], in_=ssqp[:, 0:32], axis=AX.X, op=ALU.add)
    P.dma("sp", reads=["ssqp"], writes=["ssq"], out=ssq_out[:, :], in_=ssqp[:, 32:33])
    B.release(m0)
    kview = kscr.rearrange("c (p j) -> p c j", j=128)
    uview = u_in.rearrange("c (p j) -> p c j", j=128)
    yview = y_out.rearrange("c (p j) -> p c j", j=128)
    for g in range(128 // CG):
        cs0 = g * CG
        kd, kdk = B.ring("kd", [128, CG, 128], F32, 1)
        ud, udk = B.ring("ud", [64, CG, 128], F32, 1)
        P.dma("sp", reads=["kscr"], writes=[kdk], out=kd[:], in_=kview[:, cs0:cs0 + CG, :])
        P.dma("sp", writes=[udk], out=ud[:], in_=uview[:, cs0:cs0 + CG, :])
        KF, KFk = B.ring("KF", [128, CG, 2, 128], F32, 1)
        for c0 in range(0, CG, 2):
            bx = fft_pair_fwd(B, kd, kdk, 128, c0, dft, tw)
            P.op("act", "activation", reads=["ps%d" % bx], writes=[KFk], out=KF[:, c0:c0 + 2].rearrange("p c r k -> p (c r k)"),
                 in_=B.ps[bx][:, :], func=AF.Identity)
        Bre, Brek = B.ring("Bre", [128, CG, 128], F32, 1)
        Bim, Bimk = B.ring("Bim", [128, CG, 128], F32, 1)
        for c0 in range(0, CG, 2):
            bx = fft_pair_fwd(B, ud, udk, 64, c0, dft, tw)
            X = B.ps[bx][:, :].rearrange("p (c r k) -> p c r k", c=2, r=2)
            Xre, Xim = X[:, :, 0, :], X[:, :, 1, :]
            Kre, Kim = KF[:, c0:c0 + 2, 0, :], KF[:, c0:c0 + 2, 1, :]
            tt, ttk = B.ring("fft_t", [128, 4, 2, 128], F32, 2)
            pk = "ps%d" % bx
            P.op("dve", "tensor_tensor", reads=[pk, KFk], writes=[ttk], out=tt[:, 0], in0=Xre, in1=Kre, op=ALU.mult)
            P.op("dve", "tensor_tensor", reads=[pk, KFk], writes=[ttk], out=tt[:, 1], in0=Xim, in1=Kim, op=ALU.mult)
            P.op("dve", "tensor_tensor", reads=[pk, KFk], writes=[ttk], out=tt[:, 2], in0=Xre, in1=Kim, op=ALU.mult)
            P.op("dve", "tensor_tensor", reads=[pk, KFk], writes=[ttk], out=tt[:, 3], in0=Xim, in1=Kre, op=ALU.mult)
            Y, Yk = B.ring("Y", [128, 2, 2, 128], F32, 2)
            P.op("pool", "tensor_tensor", reads=[ttk], writes=[Yk], out=Y[:, :, 0, :], in0=tt[:, 0], in1=tt[:, 1], op=ALU.subtract)
            P.op("pool", "tensor_tensor", reads=[ttk], writes=[Yk], out=Y[:, :, 1, :], in0=tt[:, 2], in1=tt[:, 3], op=ALU.add)
            bi = B.bank()
            for i in range(2):
                P.op("pe", "matmul", reads=[Yk, "dft"], writes=["ps%d" % bi], out=B.ps[bi][:, 256 * i:256 * (i + 1)],
                     lhsT=Y[:, i, 0, :], rhs=dft[:, 0:256], start=True, stop=False, skip_group_check=True)
                P.op("pe", "matmul", reads=[Yk, "dft"], writes=["ps%d" % bi], out=B.ps[bi][:, 256 * i:256 * (i + 1)],
                     lhsT=Y[:, i, 1, :], rhs=dft[:, 384:640], start=False, stop=True, skip_group_check=True)
            Bm = B.ps[bi][:, :].rearrange("p (c r k) -> p c r k", c=2, r=2)
            Bre_p, Bim_p = Bm[:, :, 0, :], Bm[:, :, 1, :]
            Tc = tw[:, 0:128].unsqueeze(1).to_broadcast([128, 2, 128])
            Ts = tw[:, 128:256].unsqueeze(1).to_broadcast([128, 2, 128])
            t2, t2k = B.ring("fft_t", [128, 4, 2, 128], F32, 2)
            pk = "ps%d" % bi
            P.op("dve", "tensor_tensor", reads=[pk, "tw"], writes=[t2k], out=t2[:, 0], in0=Bre_p, in1=Tc, op=ALU.mult)
            P.op("dve", "tensor_tensor", reads=[pk, "tw"], writes=[t2k], out=t2[:, 1], in0=Bim_p, in1=Ts, op=ALU.mult)
            P.op("dve", "tensor_tensor", reads=[pk, "tw"], writes=[t2k], out=t2[:, 2], in0=Bre_p, in1=Ts, op=ALU.mult)
            P.op("dve", "tensor_tensor", reads=[pk, "tw"], writes=[t2k], out=t2[:, 3], in0=Bim_p, in1=Tc, op=ALU.mult)
            P.op("pool", "tensor_tensor", reads=[t2k], writes=[Brek], out=Bre[:, c0:c0 + 2, :], in0=t2[:, 0], in1=t2[:, 1], op=ALU.subtract)
            P.op("pool", "tensor_tensor", reads=[t2k], writes=[Bimk], out=Bim[:, c0:c0 + 2, :], in0=t2[:, 2], in1=t2[:, 3], op=ALU.add)
        yo, yok = B.ring("yo", [64, CG, 128], F32, 1)
        for c0 in range(0, CG, 4):
            bo = B.bank()
            P.op("pe", "matmul", reads=[Brek, "dft"], writes=["ps%d" % bo], out=B.ps[bo][0:64, :], lhsT=dft[:, 0:64],
                 rhs=Bre[:, c0:c0 + 4, :].rearrange("p c j -> p (c j)"), start=True, stop=False)
            P.op("pe", "matmul", reads=[Bimk, "dft"], writes=["ps%d" % bo], out=B.ps[bo][0:64, :], lhsT=dft[:, 384:448],
                 rhs=Bim[:, c0:c0 + 4, :].rearrange("p c j -> p (c j)"), start=False, stop=True)
            P.op("act", "activation", reads=["ps%d" % bo], writes=[yok], out=yo[:, c0:c0 + 4, :].rearrange("p c j -> p (c j)"),
                 in_=B.ps[bo][0:64, :], func=AF.Identity, scale=1.0 / NFFT)
        P.dma("sp", reads=[yok], writes=["y"], out=yview[:, cs0:cs0 + CG, :], in_=yo[:])
    return B.finish()


def run_s2(u_all, l, hy_f_w1, hy_f_b1, hy_f_w2, hy_f_b2, hy_f_w3, hy_f_freq):
    nc = get_nc("s2")
    zpos, win, dft, tw = hyena_tables()
    in_maps = []
    for core in range(8):
        b, q = divmod(core, 4)
        w3 = np.concatenate([hy_f_w3[l][:, 128 * q:128 * (q + 1)], hy_f_w3[l][:, 512 + 128 * q:512 + 128 * (q + 1)]], axis=1)
        in_maps.append({
            "u": np.ascontiguousarray(u_all[b, 128 * q:128 * (q + 1)]), "hw1": hy_f_w1[l], "hb1": hy_f_b1[l], "hw2": hy_f_w2[l],
            "hb2": hy_f_b2[l], "hw3": np.ascontiguousarray(w3), "hfreq": hy_f_freq[l], "zpos": zpos,
            "win": np.ascontiguousarray(win[128 * q:128 * (q + 1)]), "dft": dft, "tw": tw,
        })
    res = run_bass_kernel_spmd(nc, in_maps, core_ids=list(range(8))).results
    y = np.stack([np.concatenate([res[4 * b + q]["y"] for q in range(4)], 0) for b in range(2)], 0)
    ssq = np.concatenate([res[q]["ssq"][:, 0] for q in range(4)], 0)
    return y, ssq


NTOK3 = TOK + CTX
HCOLS = TOK + 2 + CTX + 2


def hcol(t):
    return 1 + t if t < TOK else 3 + t


CHUNKS = [(512 * j, 512) for j in range(TOK // 512)] + [(TOK, CTX)]
TILES = [128 * i for i in range(NTOK3 // 128)]


def bcast_row(B, dram_vec, n, name):
    P = B.P
    row = B.sb("row_" + name, [1, n], F32)
    P.dma("sp", writes=["row_" + name], out=row[:], in_=dram_vec.rearrange("(o c) -> o c", o=1))
    t = B.sb("bcr_" + name, [128, n], F32)
    pos = 0
    while pos < n:
        m = min(512, n - pos)
        bk = B.bank()
        P.op("pe", "matmul", reads=["ones1", "row_" + name], writes=["ps%d" % bk], out=B.ps[bk][:, 0:m], lhsT=B.ones1[:, :],
             rhs=row[:, pos:pos + m], start=True, stop=True)
        P.op("act", "activation", reads=["ps%d" % bk], writes=["bcr_" + name], out=t[:, pos:pos + m], in_=B.ps[bk][:, 0:m],
             func=AF.Identity)
        pos += m
    return t, "bcr_" + name


def rstd_from_ssq(B, st_, sk, n, c_in, c_tmp, c_out, inv_n):
    P = B.P
    P.op("dve", "tensor_scalar", reads=[sk], writes=[sk], out=st_[0:n, c_tmp:c_tmp + 1], in0=st_[0:n, c_in:c_in + 1], scalar1=inv_n,
         scalar2=EPS, op0=ALU.mult, op1=ALU.add)
    P.op("act", "activation", reads=[sk], writes=[sk], out=st_[0:n, c_tmp:c_tmp + 1], in_=st_[0:n, c_tmp:c_tmp + 1], func=AF.Sqrt)
    P.op("dve", "reciprocal", reads=[sk], writes=[sk], out=st_[0:n, c_out:c_out + 1], in_=st_[0:n, c_tmp:c_tmp + 1])


def transpose_to_fm(B, src, srck, dst, dstk, t0, idb):
    P = B.P
    bk = B.bank()
    psb = B.ps[bk][:, :].bitcast(BF16)
    for k in range(4):
        P.op("pe", "transpose", reads=[srck, "ident_b"], writes=["ps%d" % bk], out=psb[:, k * 128:(k + 1) * 128],
             in_=src[:, k * 128:(k + 1) * 128], identity=idb[:, :])
    P.op("act", "activation", reads=["ps%d" % bk], writes=[dstk], out=dst[:, :, t0:t0 + 128],
         in_=psb[:, 0:512].rearrange("p (k t) -> p k t", t=128), func=AF.Identity)


def build_s3():
    B = Builder()
    P = B.P
    xs = B.din("xs", [TOK + 2, D]); xc = B.din("xc", [CTX, D])
    cvec = B.din("c", [D]); cctx = B.din("c_ctx", [D])
    ada_w = B.din("ada_w", [D, 3 * D]); ada_b = B.din("ada_b", [3 * D])
    npre = B.din("npre", [D]); npost = B.din("npost", [D])
    w_in = B.din("w_in", [D, COL_END]); wq_perm = B.din("wq_perm", [D, 512])
    hmask = B.din("hmask", [2, 1])
    cosT = B.din("cosT", [128, TOK]); sinT = B.din("sinT", [128, TOK])
    kt_all = B.din("kt_all", [4, 128, SEQ], BF16); v_all = B.din("v_all", [SEQ, 512], BF16)
    u_own = B.din("u_own", [512, TOK]); y_own = B.din("y_own", [512, TOK])
    hyc = B.din("hyc", [128, 4, 2])
    hy_w = B.din("hy_w", [3, 1536]); hy_b = B.din("hy_b", [1536])
    da_lam = B.din("da_lam", [256]); lam_init = B.din("lam_init", [1]); da_subln = B.din("da_subln", [128])
    hw1 = B.din("hw1", [33, 64]); hb1 = B.din("hb1", [64]); hw2 = B.din("hw2", [64, 64]); hb2 = B.din("hb2", [64])
    hw3 = B.din("hw3", [64, 1024]); hfreq = B.din("hfreq", [2, 64])
    zposc = B.din("zposc", [33, 512]); winc = B.din("winc", [512, 512])
    gm_g = B.din("gm_g", [512]); gm_b = B.din("gm_b", [512]); gm_ws = B.din("gm_ws", [8, 128, 128]); gm_bs = B.din("gm_bs", [8, 128])
    w_br = B.din("w_br", [3, 512, D]); w_out = B.din("w_out", [D, D])
    x_new = B.dout("x_new", [TOK, D]); xc_new = B.dout("xc_new", [CTX, D])
    B.init_psum()
    idf, idb = make_identity(B)
    B.ones1 = B.sb("ones1", [1, 128], F32)
    P.op("dve", "memset", writes=["ones1"], ap=B.ones1[:], constant=1.0)
    hT = B.sb("hT", [128, 8, HCOLS], BF16)
    Gbc = B.sb("bc_G", [128, 1024], F32)
    Gcbc = B.sb("bc_Gc", [128, 1024], F32)
    gaT = B.sb("gaT", [128, 4, NTOK3], BF16)
    gbT = B.sb("gbT", [128, 4, NTOK3], BF16)
    gcT = B.sb("gcT", [128, 4, NTOK3], BF16)
    mt = B.sb("hmask", [2, 1], F32)
    P.dma("sp", writes=["hmask"], out=mt[:], in_=hmask[:, :])
    m0 = B.mark()
    Abc = B.sb("bc_A", [128, 1024], F32); Bbc = B.sb("bc_Bm", [128, 1024], F32)
    Acbc = B.sb("bc_Ac", [128, 1024], F32); Bcbc = B.sb("bc_Bc", [128, 1024], F32)
    m1 = B.mark()
    modulation(B, cvec, ada_w, ada_b, npre, npost, ("A", "Bm", "G"), {"A": Abc, "Bm": Bbc, "G": Gbc})
    B.release(m1)
    modulation(B, cctx, ada_w, ada_b, npre, npost, ("Ac", "Bc", "Gc"), {"Ac": Acbc, "Bc": Bcbc, "Gc": Gcbc})
    B.release(m1)
    for i in range(NT):
        compute_h_tile(B, [(xs[1 + 128 * i: 1 + 128 * (i + 1), :], 0, 128)], 128, Abc, Bbc, "bcA", "bcBm", hT, "hT", 1 + 128 * i, idb)
    hh = B.sb("hTh", [128, 8, 2], BF16)
    compute_h_tile(B, [(xs[0:1, :], 0, 1), (xs[TOK + 1:TOK + 2, :], 1, 1)], 2, Abc, Bbc, "bcA", "bcBm", hh, "hTh", 0, idb,
                   mask=(mt, "hmask"))
    P.op("pool", "tensor_copy", reads=["hTh"], writes=["hT"], out=hT[:, :, 0:1], in_=hh[:, :, 0:1])
    P.op("pool", "tensor_copy", reads=["hTh"], writes=["hT"], out=hT[:, :, TOK + 1:TOK + 2], in_=hh[:, :, 1:2])
    P.op("pool", "memset", writes=["hT"], ap=hT[:, :, TOK + 2:TOK + 3], constant=0.0)
    P.op("pool", "memset", writes=["hT"], ap=hT[:, :, HCOLS - 1:HCOLS], constant=0.0)
    for i in range(2):
        compute_h_tile(B, [(xc[128 * i:128 * (i + 1), :], 0, 128)], 128, Acbc, Bcbc, "bcAc", "bcBc", hT, "hT", TOK + 3 + 128 * i, idb)
    B.release(m0)

    m0 = B.mark()
    scw = load_shortconv(B, hy_w, hy_b)
    hyct = B.sb("hyct", [128, 4, 4], F32)
    P.dma("sp", writes=["hyct"], out=hyct[:, :, 0:2], in_=hyc[:, :, :])
    fw = load_filter_weights(B, hw1, hb1, hw2, hb2, hfreq)
    w3 = B.sb("fw3", [64, 1024], F32)
    P.dma("sp", writes=["fw"], out=w3[:], in_=hw3[:, :])
    zt = B.sb("ztc", [33, 512], F32)
    P.dma("sp", writes=["ztc"], out=zt[:], in_=zposc[:, :])
    hid2, hid2k = filter_mlp_chunk(B, fw, zt, "ztc", 512)
    hid2p = B.sb("hid2p", [64, 512], F32)
    P.op("pool", "tensor_copy", reads=[hid2k], writes=["hid2p"], out=hid2p[:], in_=hid2[0:64, :])
    segs_all = [(1, TOK, 0), (TOK + 3, CTX, TOK)]
    segs_ctx = [(TOK + 3, CTX, 0)]
    for ci in range(4):
        P.op("act", "activation", reads=["hyct"], writes=["hyct"], out=hyct[:, ci, 2:3], in_=hyct[:, ci, 0:1], func=AF.Sqrt)
        P.op("dve", "reciprocal", reads=["hyct"], writes=["hyct"], out=hyct[:, ci, 3:4], in_=hyct[:, ci, 2:3])
        x0s, x0k = B.ring("x0s", [128, NTOK3], F32, 1)
        hy_conv_tile(B, w_in, ci, hT, "hT", segs_all if do_ctx else segs_all[:1], scw, x0s, x0k)
        yb, ybk = B.ring("yb", [128, NTOK3], F32, 1)
        ut, utk = B.ring("ut", [128, TOK], F32, 1)
        P.dma("sp", writes=[ybk], out=yb[:, 0:TOK], in_=y_own[128 * ci:128 * (ci + 1), :])
        P.dma("sp", writes=[utk], out=ut[:], in_=u_own[128 * ci:128 * (ci + 1), :])
        P.op("dve", "tensor_scalar", reads=[ybk, "hyct"], writes=[ybk], out=yb[:, 0:TOK], in0=yb[:, 0:TOK], scalar1=hyct[:, ci, 3:4],
             scalar2=None, op0=ALU.mult)
        P.op("dve", "scalar_tensor_tensor", reads=[utk, "hyct", ybk], writes=[ybk], out=yb[:, 0:TOK], in0=ut[:], scalar=hyct[:, ci, 1:2],
             in1=yb[:, 0:TOK], op0=ALU.mult, op1=ALU.add)
        x1c, x1ck = B.ring("x1c", [128, CTX], F32, 1)
        vc_, vck = B.ring("vcc", [128, CTX], F32, 1)
        hy_conv_tile(B, w_in, 4 + ci, hT, "hT", segs_ctx, scw, x1c, x1ck)
        hy_conv_tile(B, w_in, 8 + ci, hT, "hT", segs_ctx, scw, vc_, vck)
        P.op("dve", "tensor_tensor", reads=[x1ck, vck], writes=[x1ck], out=x1c[:], in0=x1c[:], in1=vc_[:], op=ALU.mult)
        kc, kck = B.ring("kcf", [128, 512], F32, 1)
        wnc, wnck = B.ring("wnc", [128, 512], F32, 1)
        P.dma("sp", writes=[wnck], out=wnc[:], in_=winc[128 * ci:128 * (ci + 1), :])
        bk = B.bank()
        P.op("pe", "matmul", reads=["fw", "hid2p"], writes=["ps%d" % bk], out=B.ps[bk][:, 0:255], lhsT=w3[0:64, 512 + 128 * ci:512 + 128 * (ci + 1)],
             rhs=hid2p[0:64, 0:255], start=True, stop=True)
        P.op("pe", "matmul", reads=["fw", "hid2p"], writes=["ps%d" % bk], out=B.ps[bk][:, 255:512], lhsT=w3[0:64, 128 * ci:128 * (ci + 1)],
             rhs=hid2p[0:64, 255:512], start=True, stop=True)
        P.op("dve", "tensor_tensor", reads=["ps%d" % bk, wnck], writes=[kck], out=kc[:], in0=B.ps[bk][:, :], in1=wnc[:], op=ALU.mult)
        cst, cstk = B.ring("cst", [128, 4], F32, 2)
        jk_, jkk = B.ring("kjunk", [128, 512], F32, 1)
        P.op("act", "activation", reads=[kck], writes=[jkk, cstk], out=jk_[:], in_=kc[:], func=AF.Square, accum_out=cst[:, 0:1])
        P.op("act", "activation", reads=[cstk], writes=[cstk], out=cst[:, 1:2], in_=cst[:, 0:1], func=AF.Sqrt)
        P.op("dve", "reciprocal", reads=[cstk], writes=[cstk], out=cst[:, 2:3], in_=cst[:, 1:2])
        acc, acck = B.ring("cacc", [128, 2, CTX], F32, 1)
        P.op("pool", "memset", writes=[acck + "0"], ap=acc[:, 0, :], constant=0.0)
        P.op("pool", "memset", writes=[acck + "1"], ap=acc[:, 1, :], constant=0.0)
        for s in range(CTX):
            a = s % 2
            P.op("dve", "scalar_tensor_tensor", reads=[kck, x1ck, acck + str(a)], writes=[acck + str(a)], out=acc[:, a, :],
                 in0=kc[:, 255 - s:511 - s], scalar=x1c[:, s:s + 1], in1=acc[:, a, :], op0=ALU.mult, op1=ALU.add)
        P.op("dve", "tensor_tensor", reads=[acck + "0", acck + "1"], writes=[acck + "0"], out=acc[:, 0, :], in0=acc[:, 0, :], in1=acc[:, 1, :],
             op=ALU.add)
        P.op("dve", "tensor_scalar", reads=[acck + "0", cstk], writes=[ybk], out=yb[:, TOK:NTOK3], in0=acc[:, 0, :], scalar1=cst[:, 2:3],
             scalar2=None, op0=ALU.mult)
        P.op("dve", "scalar_tensor_tensor", reads=[x1ck, "hyct", ybk], writes=[ybk], out=yb[:, TOK:NTOK3], in0=x1c[:], scalar=hyct[:, ci, 1:2],
             in1=yb[:, TOK:NTOK3], op0=ALU.mult, op1=ALU.add)
        P.op("pool", "tensor_tensor", reads=[x0k, ybk], writes=[ybk], out=yb[:], in0=yb[:], in1=x0s[:], op=ALU.mult)
        wt, wk = wblock(B, w_in, COL_GB + 128 * ci, 128, "wblk128", 3, 128)
        for (t0, n) in CHUNKS:
            bk = B.bank()
            proj_fm(B, wt, wk, 0, hT, "hT", hcol(t0), n, bk)
            sg, sgk = B.ring("sgb", [128, 512], F32, 2)
            P.op("act", "activation", reads=["ps%d" % bk], writes=[sgk], out=sg[:, 0:n], in_=B.ps[bk][:, 0:n], func=AF.Silu)
            P.op("dve", "tensor_tensor", reads=[sgk, ybk], writes=["gbT"], out=gbT[:, ci, t0:t0 + n], in0=sg[:, 0:n], in1=yb[:, t0:t0 + n],
                 op=ALU.mult)
    B.release(m0)

    m0 = B.mark()
    lng, lngk = bcast_row(B, gm_g, 512, "lng")
    lnb, lnbk = bcast_row(B, gm_b, 512, "lnb")
    wsf = B.sb("wsf", [128, 8, 128], F32)
    P.dma("sp", writes=["wsf"], out=wsf[:], in_=gm_ws.rearrange("g p q -> p g q"))
    wsT = B.sb("wsT", [128, 8, 128], BF16)
    for g in range(8):
        bk = B.bank()
        P.op("pe", "transpose", reads=["wsf", "ident_f"], writes=["ps%d" % bk], out=B.ps[bk][:, 0:128], in_=wsf[:, g, :], identity=idf[:, :])
        P.op("act", "activation", reads=["ps%d" % bk], writes=["wsT"], out=wsT[:, g, :], in_=B.ps[bk][:, 0:128], func=AF.Identity)
    bsT = B.sb("bsT", [128, 8], F32)
    P.dma("sp", writes=["bsT"], out=bsT[:], in_=gm_bs.rearrange("g p -> p g"), allow_slow_non_contiguous=True)
    wu, wuk = wblock(B, w_in, COL_GM, 512, "wgm_u", 1)
    wv, wvk = wblock(B, w_in, COL_GM + 512, 512, "wgm_v", 1)
    wc, wck = wblock(B, w_in, COL_GC, 512, "wgm_c", 1)
    for t0 in TILES:
        ba, bb, bc_ = B.bank(), B.bank(), B.bank()
        proj_tm(B, wu, wuk, 0, 512, hT, "hT", hcol(t0), 128, ba)
        proj_tm(B, wv, wvk, 0, 512, hT, "hT", hcol(t0), 128, bb)
        proj_tm(B, wc, wck, 0, 512, hT, "hT", hcol(t0), 128, bc_)
        ug, ugk = B.ring("ug", [128, 512], F32, 2)
        vg, vgk = B.ring("vg", [128, 512], F32, 2)
        gs, gsk = B.ring("gs", [128, 512], F32, 2)
        P.op("act", "activation", reads=["ps%d" % ba], writes=[ugk], out=ug[:], in_=B.ps[ba][:, :], func=AF.Gelu)
        P.op("act", "activation", reads=["ps%d" % bb], writes=[vgk], out=vg[:], in_=B.ps[bb][:, :], func=AF.Gelu)
        P.op("act", "activation", reads=["ps%d" % bc_], writes=[gsk], out=gs[:], in_=B.ps[bc_][:, :], func=AF.Silu)
        s6, s6k = B.ring("s6", [128, 6], F32, 2)
        mv, mvk = B.ring("mv", [128, 4], F32, 2)
        P.op("dve", "bn_stats", reads=[vgk], writes=[s6k], out=s6[:], in_=vg[:])
        P.op("dve", "bn_aggr", reads=[s6k], writes=[mvk], out=mv[:, 0:2], in_=s6[:])
        P.op("dve", "tensor_scalar", reads=[mvk], writes=[mvk], out=mv[:, 2:3], in0=mv[:, 1:2], scalar1=EPS, scalar2=None, op0=ALU.add)
        P.op("act", "activation", reads=[mvk], writes=[mvk], out=mv[:, 2:3], in_=mv[:, 2:3], func=AF.Sqrt)
        P.op("dve", "reciprocal", reads=[mvk], writes=[mvk], out=mv[:, 3:4], in_=mv[:, 2:3])
        P.op("dve", "tensor_scalar", reads=[vgk, mvk], writes=[vgk], out=vg[:], in0=vg[:], scalar1=mv[:, 0:1], scalar2=mv[:, 3:4],
             op0=ALU.subtract, op1=ALU.mult)
        P.op("pool", "tensor_tensor", reads=[vgk, lngk], writes=[vgk], out=vg[:], in0=vg[:], in1=lng[:], op=ALU.mult)
        vnb, vnbk = B.ring("vnb", [128, 512], BF16, 2)
        P.op("pool", "tensor_tensor", reads=[vgk, lnbk], writes=[vnbk], out=vnb[:], in0=vg[:], in1=lnb[:], op=ALU.add)
        bm = B.bank()
        for g in range(8):
            P.op("pe", "matmul", reads=["wsT", vnbk], writes=["ps%d" % bm], out=B.ps[bm][:, 64 * g:64 * (g + 1)], lhsT=wsT[:, g, :],
                 rhs=vnb[:, 64 * g:64 * (g + 1)], start=(g == 0), stop=(g == 7), skip_group_check=True)
        for g in range(8):
            P.op("dve", "scalar_tensor_tensor", reads=["ps%d" % bm, "bsT", ugk], writes=[ugk], out=ug[:, 64 * g:64 * (g + 1)],
                 in0=B.ps[bm][:, 64 * g:64 * (g + 1)], scalar=bsT[:, g:g + 1], in1=ug[:, 64 * g:64 * (g + 1)], op0=ALU.add, op1=ALU.mult)
        yc, yck = B.ring("ycb", [128, 512], BF16, 2)
        P.op("pool", "tensor_tensor", reads=[ugk, gsk], writes=[yck], out=yc[:], in0=ug[:], in1=gs[:], op=ALU.mult)
        transpose_to_fm(B, yc, yck, gcT, "gcT", t0, idb)
    B.release(m0)
    build_s3_attn(B, locals())
    build_s3_merge(B, locals())
    return B.finish()


def build_s3_attn(B, L):
    P = B.P
    hT, idb, gaT = L["hT"], L["idb"], L["gaT"]
    w_in, wq_perm, cosT, sinT, kt_all, v_all = L["w_in"], L["wq_perm"], L["cosT"], L["sinT"], L["kt_all"], L["v_all"]
    m0 = B.mark()
    lamb, lambk = bcast_row(B, L["da_lam"], 256, "lam")
    lib, libk = bcast_row(B, L["lam_init"], 1, "li")
    gsub, gsubk = bcast_row(B, L["da_subln"], 128, "gsub")
    lt = B.sb("lamtmp", [128, 136], F32)
    P.op("dve", "tensor_tensor", reads=[lambk], writes=["lamtmp"], out=lt[:, 0:64], in0=lamb[:, 0:64], in1=lamb[:, 64:128], op=ALU.mult)
    P.op("dve", "tensor_tensor", reads=[lambk], writes=["lamtmp"], out=lt[:, 64:128], in0=lamb[:, 128:192], in1=lamb[:, 192:256], op=ALU.mult)
    P.op("dve", "tensor_reduce", reads=["lamtmp"], writes=["lamtmp"], out=lt[:, 128:129], in_=lt[:, 0:64], axis=AX.X, op=ALU.add)
    P.op("dve", "tensor_reduce", reads=["lamtmp"], writes=["lamtmp"], out=lt[:, 129:130], in_=lt[:, 64:128], axis=AX.X, op=ALU.add)
    P.op("act", "activation", reads=["lamtmp"], writes=["lamtmp"], out=lt[:, 130:132], in_=lt[:, 128:130], func=AF.Exp)
    P.op("dve", "tensor_tensor", reads=["lamtmp"], writes=["lamtmp"], out=lt[:, 132:133], in0=lt[:, 131:132], in1=lt[:, 130:131], op=ALU.subtract)
    P.op("dve", "tensor_tensor", reads=["lamtmp", libk], writes=["lamtmp"], out=lt[:, 133:134], in0=lt[:, 132:133], in1=lib[:, 0:1], op=ALU.subtract)
    neglam = lt[:, 133:134]
    P.op("dve", "tensor_scalar", reads=[libk], writes=["lamtmp"], out=lt[:, 134:135], in0=lib[:, 0:1], scalar1=-1.0, scalar2=1.0,
         op0=ALU.mult, op1=ALU.add)
    P.op("dve", "tensor_scalar", reads=[gsubk, "lamtmp"], writes=[gsubk], out=gsub[:], in0=gsub[:], scalar1=lt[:, 134:135], scalar2=None,
         op0=ALU.mult)
    QT = B.sb("QT", [128, 4, NTOK3], BF16)
    kcT = B.sb("kcT", [128, 4, CTX], BF16)
    vcx = B.sb("vcx", [128, 2, 4, 129], BF16)
    ya = B.sb("ya_tm", [128, NTOK3 // 128, 512], BF16)
    P.op("pool", "memset", writes=["vcx"], ap=vcx[:], constant=1.0)
    m1 = B.mark()
    cs = B.sb("cosT", [128, TOK], F32)
    sn = B.sb("sinT", [128, TOK], F32)
    P.dma("sp", writes=["cosT"], out=cs[:], in_=cosT[:, :])
    P.dma("sp", writes=["sinT"], out=sn[:], in_=sinT[:, :])
    wt, wk = wblock(B, w_in, COL_Q, 512, "wq")
    wp, wpk = wblock(B, wq_perm, 0, 512, "wq")
    for h in range(4):
        for j in range(TOK // 512):
            b1 = B.bank()
            proj_fm(B, wt, wk, 128 * h, hT, "hT", 1 + 512 * j, 512, b1)
            b2 = B.bank()
            proj_fm(B, wp, wpk, 128 * h, hT, "hT", 1 + 512 * j, 512, b2)
            t1, t1k = B.ring("rtmp1", [128, 512], F32, 2)
            t2, t2k = B.ring("rtmp2", [128, 512], F32, 2)
            P.op("dve", "tensor_tensor", reads=["ps%d" % b1, "cosT"], writes=[t1k], out=t1[:], in0=B.ps[b1][:, :],
                 in1=cs[:, 512 * j:512 * (j + 1)], op=ALU.mult)
            P.op("dve", "tensor_tensor", reads=["ps%d" % b2, "sinT"], writes=[t2k], out=t2[:], in0=B.ps[b2][:, :],
                 in1=sn[:, 512 * j:512 * (j + 1)], op=ALU.mult)
            P.op("pool", "tensor_tensor", reads=[t1k, t2k], writes=["QT"], out=QT[:, h, 512 * j:512 * (j + 1)], in0=t1[:], in1=t2[:], op=ALU.add)
        b1 = B.bank()
        proj_fm(B, wt, wk, 128 * h, hT, "hT", hcol(TOK), CTX, b1)
        P.op("act", "activation", reads=["ps%d" % b1], writes=["QT"], out=QT[:, h, TOK:NTOK3], in_=B.ps[b1][:, 0:CTX], func=AF.Identity)
    wt, wk = wblock(B, w_in, COL_K, 512, "wq")
    for h in range(4):
        b1 = B.bank()
        proj_fm(B, wt, wk, 128 * h, hT, "hT", hcol(TOK), CTX, b1)
        P.op("act", "activation", reads=["ps%d" % b1], writes=["kcT"], out=kcT[:, h, :], in_=B.ps[b1][:, 0:CTX], func=AF.Identity)
    wt, wk = wblock(B, w_in, COL_V, 512, "wq")
    for i in range(2):
        b1 = B.bank()
        proj_tm(B, wt, wk, 0, 512, hT, "hT", hcol(TOK + 128 * i), 128, b1)
        P.op("act", "activation", reads=["ps%d" % b1], writes=["vcx"], out=vcx[:, i, :, 0:128],
             in_=B.ps[b1][:, :].rearrange("p (h d) -> p h d", d=128), func=AF.Identity)
    B.release(m1)
    m2 = B.mark()
    kth = B.sb("kth", [128, SEQ], BF16)
    vh = B.sb("vh", [128, SEQ // 128, 129], BF16)
    P.op("pool", "memset", writes=["vh"], ap=vh[:], constant=1.0)
    vview = v_all.rearrange("(t p) c -> p t c", p=128) if v_all is not None else None
    scnt = 0
    for h in range(4):
        P.dma("sp", reads=L.get("kt_keys", []), writes=["kth"], out=L.get("kth_out", lambda k: k[:])(kth), in_=kt_all[h])
        if vview is not None:
            P.dma("sp", reads=L.get("v_keys", []), writes=["vh"], out=vh[:, :, 0:128], in_=vview[:, :, 128 * h:128 * (h + 1)])
        else:
            for k in range(2):
                for r in range(4):
                    P.dma("sp", reads=L.get("v_keys", []), writes=["vh"], out=vh[:, 16 * r + 8 * k:16 * r + 8 * k + 8, 0:128],
                          in_=L["v_gk"][k].rearrange("(r t p) c -> r p t c", r=4, p=128)[r][:, :, 128 * h:128 * (h + 1)])
        for (t0, n) in CHUNKS:
            nq = n // 128
            latent = t0 < TOK
            keys = ([("l", k) for k in range(SEQ // 128)] if latent else []) + [("c", 0), ("c", 1)]
            om, omk = B.ring("om", [128, 2, 4, 128], F32, 2)
            for m in range(2):
                for ki, (kind, kt) in enumerate(keys):
                    bS = 4 + (scnt % 4)
                    scnt += 1
                    if kind == "l":
                        lk, lkk = kth[64 * m:64 * (m + 1), 128 * kt:128 * (kt + 1)], "kth"
                        vt, vtk = vh[:, kt, :], "vh"
                    else:
                        lk, lkk = kcT[64 * m:64 * (m + 1), h, 128 * kt:128 * (kt + 1)], "kcT"
                        vt, vtk = vcx[:, kt, h, :], "vcx"
                    P.op("pe", "matmul", reads=[lkk, "QT"], writes=["ps%d" % bS], out=B.ps[bS][:, 0:n], lhsT=lk,
                         rhs=QT[64 * m:64 * (m + 1), h, t0:t0 + n], start=True, stop=True)
                    pt, ptk = B.ring("pt", [128, 512], BF16, 3)
                    P.op("act", "activation", reads=["ps%d" % bS], writes=[ptk], out=pt[:, 0:n], in_=B.ps[bS][:, 0:n], func=AF.Exp,
                         scale=0.125)
                    for qt in range(nq):
                        P.op("pe", "matmul", reads=[ptk, vtk], writes=["ps%d" % qt], out=B.ps[qt][:, 0:129],
                             lhsT=pt[:, 128 * qt:128 * (qt + 1)], rhs=vt, start=(ki == 0), stop=(ki == len(keys) - 1))
                for qt in range(nq):
                    rc, rck = B.ring("rc", [128, 2], F32, 4)
                    P.op("dve", "reciprocal", reads=["ps%d" % qt], writes=[rck], out=rc[:, 0:1], in_=B.ps[qt][:, 128:129])
                    if m == 1:
                        P.op("dve", "tensor_tensor", reads=[rck, "lamtmp"], writes=[rck], out=rc[:, 0:1], in0=rc[:, 0:1], in1=neglam, op=ALU.mult)
                    P.op("dve", "tensor_scalar", reads=["ps%d" % qt, rck], writes=[omk], out=om[:, m, qt, :], in0=B.ps[qt][:, 0:128],
                         scalar1=rc[:, 0:1], scalar2=None, op0=ALU.mult)
            P.op("pool", "tensor_tensor", reads=[omk], writes=[omk], out=om[:, 0, 0:nq, :], in0=om[:, 0, 0:nq, :], in1=om[:, 1, 0:nq, :], op=ALU.add)
            for qt in range(nq):
                st_, sk = B.ring("ast", [128, 4], F32, 4)
                jk_, jkk = B.ring("ajunk", [128, 128], F32, 2)
                P.op("act", "activation", reads=[omk], writes=[jkk, sk], out=jk_[:], in_=om[:, 0, qt, :], func=AF.Square, accum_out=st_[:, 0:1])
                rstd_from_ssq(B, st_, sk, 128, 0, 1, 2, 1.0 / 128)
                P.op("dve", "scalar_tensor_tensor", reads=[omk, sk, gsubk], writes=["ya_tm"], out=ya[:, t0 // 128 + qt, 128 * h:128 * (h + 1)],
                     in0=om[:, 0, qt, :], scalar=st_[:, 2:3], in1=gsub[:], op0=ALU.mult, op1=ALU.mult)
    B.release(m2)
    wt, wk = wblock(B, w_in, COL_GA, 512, "wq")
    for t0 in TILES:
        bk = B.bank()
        proj_tm(B, wt, wk, 0, 512, hT, "hT", hcol(t0), 128, bk)
        sg, sgk = B.ring("sga", [128, 512], F32, 2)
        P.op("act", "activation", reads=["ps%d" % bk], writes=[sgk], out=sg[:], in_=B.ps[bk][:, :], func=AF.Silu)
        yb_, ybk_ = B.ring("yab", [128, 512], BF16, 2)
        P.op("dve", "tensor_tensor", reads=[sgk, "ya_tm"], writes=[ybk_], out=yb_[:], in0=sg[:], in1=ya[:, t0 // 128, :], op=ALU.mult)
        transpose_to_fm(B, yb_, ybk_, gaT, "gaT", t0, idb)
    B.release(m0)


def build_s3_merge(B, L):
    P = B.P
    hT, w_in, w_br, w_out = L["hT"], L["w_in"], L["w_br"], L["w_out"]
    gT = [(L["gaT"], "gaT"), (L["gbT"], "gbT"), (L["gcT"], "gcT")]
    xs, xc, x_new, xc_new = L["xs"], L["xc"], L["x_new"], L["xc_new"]
    m0 = B.mark()
    wb = []
    for i in range(3):
        t = B.sb("wbr%d" % i, [128, 4, D], BF16)
        for j in range(4):
            load_cast(B, t[:, :, 256 * j:256 * (j + 1)], "wbr%d" % i, w_br[i].rearrange("(ct p) d -> p ct d", p=128)[:, :, 256 * j:256 * (j + 1)], (4, 256))
        wb.append(t)
    wo = B.sb("wo", [128, 8, D], BF16)
    for j in range(8):
        load_cast(B, wo[:, :, 128 * j:128 * (j + 1)], "wo", w_out.rearrange("(k p) d -> p k d", p=128)[:, :, 128 * j:128 * (j + 1)], (8, 128))
    for (t0, n) in L.get("chunks", CHUNKS):
        latent = t0 < TOK
        G, Gk = (L["Gbc"], "bcG") if latent else (L["Gcbc"], "bcGc")
        ob, obk = B.ring("outTb", [128, 8, 512], BF16, 1)
        for dt in range(8):
            acc, acck = B.ring("macc", [128, 512], F32, 2)
            for i in range(3):
                B.cast_eng = "act" if (dt * 3 + i) % 2 == 0 else "pool"
                wt, wk = wblock(B, w_in, COL_MG + 1024 * i + 128 * dt, 128, "wmg", 3, 128)
                B.cast_eng = "pool"
                bA = B.bank()
                proj_fm(B, wt, wk, 0, hT, "hT", hcol(t0), n, bA)
                bB = B.bank()
                for ct in range(4):
                    P.op("pe", "matmul", reads=["wbr%d" % i, gT[i][1]], writes=["ps%d" % bB], out=B.ps[bB][:, 0:n],
                         lhsT=wb[i][:, ct, 128 * dt:128 * (dt + 1)], rhs=gT[i][0][:, ct, t0:t0 + n], start=(ct == 0), stop=(ct == 3))
                sg, sgk = B.ring("msg", [128, 512], F32, 2)
                P.op("act", "activation", reads=["ps%d" % bA], writes=[sgk], out=sg[:, 0:n], in_=B.ps[bA][:, 0:n], func=AF.Sigmoid)
                if i == 0:
                    P.op("dve", "tensor_tensor", reads=[sgk, "ps%d" % bB], writes=[acck], out=acc[:, 0:n], in0=sg[:, 0:n], in1=B.ps[bB][:, 0:n], op=ALU.mult)
                else:
                    P.op("dve", "tensor_tensor", reads=[sgk, "ps%d" % bB], writes=[sgk], out=sg[:, 0:n], in0=sg[:, 0:n], in1=B.ps[bB][:, 0:n], op=ALU.mult)
                    if i == 1:
                        P.op("pool", "tensor_tensor", reads=[sgk, acck], writes=[acck], out=acc[:, 0:n], in0=acc[:, 0:n], in1=sg[:, 0:n], op=ALU.add)
                    else:
                        P.op("pool", "tensor_tensor", reads=[sgk, acck], writes=[obk], out=ob[:, dt, 0:n], in0=acc[:, 0:n], in1=sg[:, 0:n], op=ALU.add)
        for q in range(n // 128):
            tok = t0 + 128 * q
            bks = (B.bank(), B.bank())
            for half in range(2):
                for k in range(8):
                    P.op("pe", "matmul", reads=[obk, "wo"], writes=["ps%d" % bks[half]], out=B.ps[bks[half]][:, :], lhsT=ob[:, k, 128 * q:128 * (q + 1)],
                         rhs=wo[:, k, 512 * half:512 * (half + 1)], start=(k == 0), stop=(k == 7))
            st_, sk = B.ring("mst", [128, 8], F32, 4)
            for half in range(2):
                jk_, jkk = B.ring("mjunk", [128, 512], BF16, 2)
                P.op("act", "activation", reads=["ps%d" % bks[half]], writes=[jkk, sk], out=jk_[:], in_=B.ps[bks[half]][:, :], func=AF.Square,
                     accum_out=st_[:, half:half + 1])
            P.op("dve", "tensor_tensor", reads=[sk], writes=[sk], out=st_[:, 2:3], in0=st_[:, 0:1], in1=st_[:, 1:2], op=ALU.add)
            rstd_from_ssq(B, st_, sk, 128, 2, 3, 4, 1.0 / D)
            xt, xk = B.ring("mxt", [128, D], F32, 2)
            src = xs[1 + tok:1 + tok + 128, :] if latent else xc[tok - TOK:tok - TOK + 128, :]
            P.dma("sp", reads=L.get("x_keys", []), writes=[xk], out=xt[:], in_=src)
            ot, otk = B.ring("mot", [128, D], F32, 2)
            for half in range(2):
                P.op("dve", "scalar_tensor_tensor", reads=["ps%d" % bks[half], sk, Gk], writes=[otk], out=ot[:, 512 * half:512 * (half + 1)],
                     in0=B.ps[bks[half]][:, :], scalar=st_[:, 4:5], in1=G[:, 512 * half:512 * (half + 1)], op0=ALU.mult, op1=ALU.mult)
            P.op("pool", "tensor_tensor", reads=[otk, xk], writes=[otk], out=ot[:], in0=ot[:], in1=xt[:], op=ALU.add)
            if latent:
                P.dma("sp", reads=[otk], writes=[L.get("x_new_key", "x_new")], out=x_new[tok:tok + 128, :], in_=ot[:])
                if L.get("hal_src") is not None and tok == 0:
                    P.dma("sp", reads=[otk], writes=["hal_src"], out=L["hal_src"][0:1, :], in_=ot[0:1, :])
                if L.get("hal_src") is not None and tok == TOK - 128:
                    P.dma("sp", reads=[otk], writes=["hal_src"], out=L["hal_src"][1:2, :], in_=ot[127:128, :])
            elif xc_new is not None:
                P.dma("sp", reads=[otk], writes=[L.get("xc_new_key", "xc_new")], out=xc_new[tok - TOK:tok - TOK + 128, :], in_=ot[:])
    B.release(m0)


def ctx_tables():
    L = CTX
    i = np.arange(512)
    pos = np.minimum(np.abs(i - 255), L - 1)
    t_lin = np.linspace(0.0, 1.0, L, dtype=np.float32)
    wpos = ((2.0 * math.pi / L) * np.arange(L, dtype=np.float32)).astype(np.float32)
    bands = np.linspace(1e-4, 16 - 1, 16, dtype=np.float32)
    zfull = np.concatenate([t_lin[:, None], np.cos(bands[None, :] * wpos[:, None]), -np.sin(bands[None, :] * wpos[:, None])],
                           axis=-1).astype(np.float32)
    zposc = np.ascontiguousarray(zfull[pos].T)
    mn = math.log(1e-2) / 1.5
    mx = math.log(1e-2) / 0.3
    deltas = np.abs(np.linspace(mn, mx, 512, dtype=np.float32))
    winc = np.exp(-t_lin[pos][None, :] * deltas[:, None]).astype(np.float32)
    winc[:, 511] = 0.0
    return zposc, winc


def run_s3(x, xc, l, inp, kt_all, v_all, u_all, y_all, ssq):
    nc = get_nc("s3")
    cosT, sinT = rope_tables()
    zposc, winc = ctx_tables()
    w_in = inp["w_in"][l]
    wqp = perm_cols(w_in[:, COL_Q:COL_Q + 512])
    lam_init = np.array([0.8 - 0.6 * math.exp(-0.3 * l)], np.float32)
    hyc = np.ascontiguousarray(np.stack([ssq.reshape(4, 128).T, inp["hy_bias"][l].reshape(4, 128).T], axis=-1)).astype(np.float32)
    in_maps = []
    for core in range(8):
        b, r = divmod(core, 4)
        t0 = r * TOK
        xs = np.zeros((TOK + 2, D), np.float32)
        lo, hi = max(t0 - 1, 0), min(t0 + TOK + 1, SEQ)
        xs[lo - (t0 - 1): hi - (t0 - 1)] = x[b, lo:hi]
        hmask = np.array([[0.0 if t0 == 0 else 1.0], [0.0 if t0 + TOK == SEQ else 1.0]], np.float32)
        in_maps.append({
            "xs": xs, "xc": np.ascontiguousarray(xc[b]), "c": inp["c"][b], "c_ctx": inp["c_ctx"], "ada_w": inp["ada_w"][l], "ada_b": inp["ada_b"][l],
            "npre": inp["norm_pre"][l], "npost": inp["norm_post"][l], "w_in": w_in, "wq_perm": wqp, "hmask": hmask,
            "cosT": np.ascontiguousarray(cosT[:, t0:t0 + TOK]), "sinT": np.ascontiguousarray(sinT[:, t0:t0 + TOK]),
            "kt_all": kt_all[b], "v_all": v_all[b], "u_own": np.ascontiguousarray(u_all[b][:, t0:t0 + TOK]),
            "y_own": np.ascontiguousarray(y_all[b][:, t0:t0 + TOK]), "hyc": hyc, "hy_w": inp["hy_short_w"][l], "hy_b": inp["hy_short_b"][l],
            "da_lam": np.ascontiguousarray(inp["da_lambda"][l].reshape(256)), "lam_init": lam_init, "da_subln": inp["da_subln"][l],
            "hw1": inp["hy_f_w1"][l], "hb1": inp["hy_f_b1"][l], "hw2": inp["hy_f_w2"][l], "hb2": inp["hy_f_b2"][l],
            "hw3": inp["hy_f_w3"][l], "hfreq": inp["hy_f_freq"][l], "zposc": zposc, "winc": winc,
            "gm_g": inp["gm_ln_g"][l], "gm_b": inp["gm_ln_b"][l], "gm_ws": inp["gm_ws"][l], "gm_bs": inp["gm_bs"][l],
            "w_br": inp["w_branch"][l], "w_out": inp["w_out"][l],
        })
    res = run_bass_kernel_spmd(nc, in_maps, core_ids=list(range(8))).results
    x_new = np.stack([np.concatenate([res[4 * b + r]["x_new"] for r in range(4)], 0) for b in range(2)], 0)
    xc_new = np.stack([res[4 * b]["xc_new"] for b in range(2)], 0)
    return x_new, xc_new


def gather_s1(res):
    kt = np.stack([np.concatenate([np.asarray(res[4 * b + r]["o_kt"]) for r in range(4)], axis=2) for b in range(2)], 0)
    v = np.stack([np.concatenate([np.asarray(res[4 * b + r]["o_v"]) for r in range(4)], axis=0) for b in range(2)], 0)
    u = np.stack([np.concatenate([np.asarray(res[4 * b + r]["o_u"]) for r in range(4)], axis=1) for b in range(2)], 0)
    return np.ascontiguousarray(kt), np.ascontiguousarray(v), np.ascontiguousarray(u)


def kernel(**inputs):
    inp = {k: np.asarray(v) for k, v in inputs.items()}
    x = inp["x"].astype(np.float32, copy=False)
    xc = inp["ctx"].astype(np.float32, copy=False)
    for l in range(DEPTH):
        res1 = run_s1(x, inp["c"], l, inp["ada_w"], inp["ada_b"], inp["norm_pre"], inp["norm_post"], inp["w_in"], inp["hy_short_w"],
                      inp["hy_short_b"])
        kt_all, v_all, u_all = gather_s1(res1)
        y_all, ssq = run_s2(u_all, l, inp["hy_f_w1"], inp["hy_f_b1"], inp["hy_f_w2"], inp["hy_f_b2"], inp["hy_f_w3"], inp["hy_f_freq"])
        x, xc = run_s3(x, xc, l, inp, kt_all, v_all, u_all, y_all, ssq)
    return x.astype(np.float32)


RG4 = [[0, 1, 2, 3], [4, 5, 6, 7]]
FUSED_DEPTH = DEPTH
EXTRA_CC = 0
CGF = 16


_RANK_CACHE = {}


def rank_of(e):
    k = id(e)
    if k not in _RANK_CACHE:
        _RANK_CACHE[k] = (e, e.partition_id() % 4)
    return _RANK_CACHE[k][1]


def rank_nb(e, d):
    k = (id(e), d)
    if k not in _RANK_CACHE:
        _RANK_CACHE[k] = (e, (rank_of(e) + d) % 4)
    return _RANK_CACHE[k][1]


def f_products(B, E, l):
    P = B.P
    hT, w_in = E["hT"], E["w_in"][l]
    m0 = B.mark()
    cs = B.sb("cosT", [128, TOK], F32)
    sn = B.sb("sinT", [128, TOK], F32)
    P.dma("sp", writes=["cosT"], out=cs[:], in_=E["cosT"][:, :])
    P.dma("sp", writes=["sinT"], out=sn[:], in_=E["sinT"][:, :])
    wt, wk = wblock(B, w_in, COL_K, 512, "wblkK", 1)
    wp, wpk = wblock(B, E["wk_perm"][l], 0, 512, "wblkP", 1)
    for h in range(4):
        ko, kk = B.ring("kout", [128, TOK], BF16, 2)
        for j in range(TOK // 512):
            b1 = B.bank()
            proj_fm(B, wt, wk, 128 * h, hT, "hT", 1 + 512 * j, 512, b1)
            b2 = B.bank()
            proj_fm(B, wp, wpk, 128 * h, hT, "hT", 1 + 512 * j, 512, b2)
            t1, t1k = B.ring("rtmp1", [128, 512], F32, 2)
            t2, t2k = B.ring("rtmp2", [128, 512], F32, 2)
            P.op("dve", "tensor_tensor", reads=["ps%d" % b1, "cosT"], writes=[t1k], out=t1[:], in0=B.ps[b1][:, :],
                 in1=cs[:, 512 * j:512 * (j + 1)], op=ALU.mult)
            P.op("dve", "tensor_tensor", reads=["ps%d" % b2, "sinT"], writes=[t2k], out=t2[:], in0=B.ps[b2][:, :],
                 in1=sn[:, 512 * j:512 * (j + 1)], op=ALU.mult)
            P.op("pool", "tensor_tensor", reads=[t1k, t2k], writes=[kk], out=ko[:, 512 * j:512 * (j + 1)], in0=t1[:], in1=t2[:], op=ALU.add)
        P.dma("sp", reads=[kk], writes=["kt_src"], out=E["kt_src"][h // 2][128 * (h % 2):128 * (h % 2 + 1), :], in_=ko[:])
    B.release(m0)
    wt, wk = wblock(B, w_in, COL_V, 512, "wblkK", 1)
    for i in range(NT):
        bk = B.bank()
        proj_tm(B, wt, wk, 0, 512, hT, "hT", 1 + 128 * i, 128, bk)
        vo, vk = B.ring("vout", [128, 512], BF16, 3)
        P.op("act", "activation", reads=["ps%d" % bk], writes=[vk], out=vo[:], in_=B.ps[bk][:, :], func=AF.Identity)
        P.dma("sp", reads=[vk], writes=["v_src"], out=E["v_src"][i // 8][128 * (i % 8):128 * (i % 8 + 1), :], in_=vo[:])
    B.release(m0)
    scw = load_shortconv(B, E["hy_w"][l], E["hy_b"][l])
    for ci in range(4):
        x1s, x1k = B.ring("x1s", [128, TOK], F32, 2)
        vs, vsk = B.ring("vs", [128, TOK], F32, 2)
        hy_conv_tile(B, w_in, 4 + ci, hT, "hT", [(1, TOK, 0)], scw, x1s, x1k)
        hy_conv_tile(B, w_in, 8 + ci, hT, "hT", [(1, TOK, 0)], scw, vs, vsk)
        P.op("dve", "tensor_tensor", reads=[x1k, vsk], writes=[x1k], out=x1s[:], in0=x1s[:], in1=vs[:], op=ALU.mult)
        P.dma("sp", reads=[x1k], writes=["u_src"], out=E["u_src"][ci], in_=x1s[:])
    B.release(m0)
    par = l % 2
    for k in range(2):
        P.cc(reads=["kt_src"], writes=["kt_g%d" % par], kind="AllGather", op=ALU.bypass, replica_groups=RG4, ins=[E["kt_src"][k].opt()],
             outs=[E["kt_g"][par][k].opt()])
    for k in range(2):
        P.cc(reads=["v_src"], writes=["v_g%d" % par], kind="AllGather", op=ALU.bypass, replica_groups=RG4, ins=[E["v_src"][k].opt()],
             outs=[E["v_g"][par][k].opt()])
    for k in range(4):
        P.cc(reads=["u_src"], writes=["u_g"], kind="AllGather", op=ALU.bypass, replica_groups=RG4, ins=[E["u_src"][k].opt()],
             outs=[E["u_g"][par][k].opt()])


def f_conv(B, E, l):
    P = B.P
    par = l % 2
    dft, tw = E["dft"], E["tw"]
    kscr = E["kscr"]
    m0 = B.mark()
    fw = load_filter_weights(B, E["hw1"][l], E["hb1"][l], E["hw2"][l], E["hb2"][l], E["hfreq"][l])
    w3 = B.sb("fw3", [64, 256], F32)
    P.dma("sp", writes=["fw"], out=w3[:], in_=E["hw3q"][l])
    ssqp = B.sb("ssqp", [128, 33], F32)
    P.op("dve", "memset", writes=["ssqp"], ap=ssqp[:], constant=0.0)
    m1 = B.mark()
    for ch in range(NFFT // 512):
        zt, zk = B.ring("zt", [33, 512], F32, 2)
        P.dma("sp", writes=[zk], out=zt[:], in_=E["zpos"][:, ch * 512:(ch + 1) * 512])
        wn, wnk = B.ring("wn", [128, 512], F32, 2)
        P.dma("sp", writes=[wnk], out=wn[:], in_=E["win"][:, ch * 512:(ch + 1) * 512])
        h2, h2k = filter_mlp_chunk(B, fw, zt, zk, 512)
        bk = B.bank()
        half = 0 if ch * 512 < SEQ else 1
        P.op("pe", "matmul", reads=["fw", h2k], writes=["ps%d" % bk], out=B.ps[bk][:, :], lhsT=w3[0:64, 128 * half:128 * (half + 1)],
             rhs=h2[0:64, :], start=True, stop=True)
        kc, kck = B.ring("kc", [128, 512], F32, 2)
        P.op("dve", "tensor_tensor", reads=["ps%d" % bk, wnk], writes=[kck], out=kc[:], in0=B.ps[bk][:, :], in1=wn[:], op=ALU.mult)
        jk_, jkk = B.ring("kjunk", [128, 512], F32, 2)
        P.op("act", "activation", reads=[kck], writes=[jkk, "ssqp"], out=jk_[:], in_=kc[:], func=AF.Square, accum_out=ssqp[:, ch:ch + 1])
        P.dma("sp", reads=[kck], writes=["kscr"], out=kscr[:, ch * 512:(ch + 1) * 512], in_=kc[:])
    P.op("dve", "tensor_reduce", reads=["ssqp"], writes=["ssqp"], out=ssqp[:, 32:33], in_=ssqp[:, 0:32], axis=AX.X, op=ALU.add)
    P.dma("sp", reads=["ssqp"], writes=["ssq_src"], out=E["ssq_src"][:, 0:1], in_=ssqp[:, 32:33],
          allow_slow_non_contiguous=True)
    B.release(m1)
    kview = kscr.rearrange("c (p j) -> p c j", j=128)
    ugq = E["u_g"][par].rearrange("q (r c) t -> q r c t", r=4)
    P.dma("sp", reads=["u_g"], writes=["u_my"], out=E["u_my"], in_=(lambda e: ugq[rank_of(e)]))
    ugv = E["u_my"].rearrange("r c (pp j) -> r pp c j", j=128)
    yview = E["y_src"].rearrange("r c (pp j) -> r pp c j", j=128)
    for g in range(128 // CGF):
        cs0 = g * CGF
        kd, kdk = B.ring("kd", [128, CGF, 128], F32, 1)
        ud, udk = B.ring("ud", [64, CGF, 128], F32, 1)
        P.dma("sp", reads=["kscr"], writes=[kdk], out=kd[:], in_=kview[:, cs0:cs0 + CGF, :])
        for r in range(4):
            P.dma("sp", reads=["u_my"], writes=[udk], out=ud[16 * r:16 * (r + 1), :, :], in_=ugv[r][:, cs0:cs0 + CGF, :])
        KF, KFk = B.ring("KF", [128, CGF, 2, 128], F32, 1)
        Tc = tw[:, 0:128].unsqueeze(1).to_broadcast([128, 2, 128])
        Ts = tw[:, 128:256].unsqueeze(1).to_broadcast([128, 2, 128])

        def st1(src, srck, kdim, c0):
            bk = B.bank()
            for i in range(2):
                P.op("pe", "matmul", reads=[srck, "dft"], writes=["ps%d" % bk], out=B.ps[bk][:, 256 * i:256 * (i + 1)],
                     lhsT=src[0:kdim, c0 + i, :], rhs=dft[0:kdim, 256:512], start=True, stop=True)
            return bk

        def st2(bk):
            A = B.ps[bk][:, :].rearrange("p (c r k) -> p c r k", c=2, r=2)
            Are, Aim = A[:, :, 0, :], A[:, :, 1, :]
            tt, ttk = B.ring("fft_t", [128, 4, 2, 128], F32, 3)
            pk = "ps%d" % bk
            P.op("dve", "tensor_tensor", reads=[pk, "tw"], writes=[ttk], out=tt[:, 0], in0=Are, in1=Tc, op=ALU.mult)
            P.op("dve", "tensor_tensor", reads=[pk, "tw"], writes=[ttk], out=tt[:, 1], in0=Aim, in1=Ts, op=ALU.mult)
            P.op("dve", "tensor_tensor", reads=[pk, "tw"], writes=[ttk], out=tt[:, 2], in0=Aim, in1=Tc, op=ALU.mult)
            P.op("dve", "tensor_tensor", reads=[pk, "tw"], writes=[ttk], out=tt[:, 3], in0=Are, in1=Ts, op=ALU.mult)
            b1, b1k = B.ring("fft_b1", [128, 2, 2, 128], F32, 2)
            b2, b2k = B.ring("fft_b2", [128, 2, 2, 128], F32, 2)
            P.op("pool", "tensor_tensor", reads=[ttk], writes=[b1k], out=b1[:, :, 0, :], in0=tt[:, 0], in1=tt[:, 1], op=ALU.add)
            P.op("pool", "tensor_tensor", reads=[ttk], writes=[b1k], out=b1[:, :, 1, :], in0=tt[:, 2], in1=tt[:, 3], op=ALU.subtract)
            P.op("act", "activation", reads=[b1k], writes=[b2k], out=b2[:, :, 0, :], in_=b1[:, :, 1, :], func=AF.Identity)
            P.op("act", "activation", reads=[b1k], writes=[b2k], out=b2[:, :, 1, :], in_=b1[:, :, 0, :], func=AF.Identity, scale=-1.0)
            return (b1, b1k, b2, b2k)

        def st3(bb):
            b1, b1k, b2, b2k = bb
            bx = B.bank()
            P.op("pe", "matmul", reads=[b1k, "dft"], writes=["ps%d" % bx], out=B.ps[bx][:, :], lhsT=dft[:, 0:128],
                 rhs=b1[:].rearrange("p c r k -> p (c r k)"), start=True, stop=False)
            P.op("pe", "matmul", reads=[b2k, "dft"], writes=["ps%d" % bx], out=B.ps[bx][:, :], lhsT=dft[:, 128:256],
                 rhs=b2[:].rearrange("p c r k -> p (c r k)"), start=False, stop=True)
            return bx

        def st4f(bx, c0):
            P.op("act", "activation", reads=["ps%d" % bx], writes=[KFk], out=KF[:, c0:c0 + 2].rearrange("p c r k -> p (c r k)"),
                 in_=B.ps[bx][:, :], func=AF.Identity)

        def st4d(bx, c0):
            X = B.ps[bx][:, :].rearrange("p (c r k) -> p c r k", c=2, r=2)
            Xre, Xim = X[:, :, 0, :], X[:, :, 1, :]
            Kre, Kim = KF[:, c0:c0 + 2, 0, :], KF[:, c0:c0 + 2, 1, :]
            tt, ttk = B.ring("fft_t", [128, 4, 2, 128], F32, 3)
            pk = "ps%d" % bx
            P.op("dve", "tensor_tensor", reads=[pk, KFk], writes=[ttk], out=tt[:, 0], in0=Xre, in1=Kre, op=ALU.mult)
            P.op("dve", "tensor_tensor", reads=[pk, KFk], writes=[ttk], out=tt[:, 1], in0=Xim, in1=Kim, op=ALU.mult)
            P.op("dve", "tensor_tensor", reads=[pk, KFk], writes=[ttk], out=tt[:, 2], in0=Xre, in1=Kim, op=ALU.mult)
            P.op("dve", "tensor_tensor", reads=[pk, KFk], writes=[ttk], out=tt[:, 3], in0=Xim, in1=Kre, op=ALU.mult)
            Y, Yk = B.ring("Y", [128, 2, 2, 128], F32, 2)
            P.op("pool", "tensor_tensor", reads=[ttk], writes=[Yk], out=Y[:, :, 0, :], in0=tt[:, 0], in1=tt[:, 1], op=ALU.subtract)
            P.op("pool", "tensor_tensor", reads=[ttk], writes=[Yk], out=Y[:, :, 1, :], in0=tt[:, 2], in1=tt[:, 3], op=ALU.add)
            return (Y, Yk)

        def st5(yy):
            Y, Yk = yy
            bi = B.bank()
            for i in range(2):
                P.op("pe", "matmul", reads=[Yk, "dft"], writes=["ps%d" % bi], out=B.ps[bi][:, 256 * i:256 * (i + 1)],
                     lhsT=Y[:, i, 0, :], rhs=dft[:, 0:256], start=True, stop=False, skip_group_check=True)
                P.op("pe", "matmul", reads=[Yk, "dft"], writes=["ps%d" % bi], out=B.ps[bi][:, 256 * i:256 * (i + 1)],
                     lhsT=Y[:, i, 1, :], rhs=dft[:, 384:640], start=False, stop=True, skip_group_check=True)
            return bi

        def st6(bi, c0):
            Bm = B.ps[bi][:, :].rearrange("p (c r k) -> p c r k", c=2, r=2)
            Bre_p, Bim_p = Bm[:, :, 0, :], Bm[:, :, 1, :]
            t2, t2k = B.ring("fft_t", [128, 4, 2, 128], F32, 3)
            pk = "ps%d" % bi
            P.op("dve", "tensor_tensor", reads=[pk, "tw"], writes=[t2k], out=t2[:, 0], in0=Bre_p, in1=Tc, op=ALU.mult)
            P.op("dve", "tensor_tensor", reads=[pk, "tw"], writes=[t2k], out=t2[:, 1], in0=Bim_p, in1=Ts, op=ALU.mult)
            P.op("dve", "tensor_tensor", reads=[pk, "tw"], writes=[t2k], out=t2[:, 2], in0=Bre_p, in1=Ts, op=ALU.mult)
            P.op("dve", "tensor_tensor", reads=[pk, "tw"], writes=[t2k], out=t2[:, 3], in0=Bim_p, in1=Tc, op=ALU.mult)
            P.op("pool", "tensor_tensor", reads=[t2k], writes=[Brek], out=Bre[:, c0:c0 + 2, :], in0=t2[:, 0], in1=t2[:, 1], op=ALU.subtract)
            P.op("pool", "tensor_tensor", reads=[t2k], writes=[Bimk], out=Bim[:, c0:c0 + 2, :], in0=t2[:, 2], in1=t2[:, 3], op=ALU.add)

        pairs = list(range(0, CGF, 2))
        for q0 in range(0, len(pairs), 2):
            cs_ = pairs[q0:q0 + 2]
            bks = [st1(kd, kdk, 128, c0) for c0 in cs_]
            bbs = [st2(bk) for bk in bks]
            bxs = [st3(bb) for bb in bbs]
            for bx, c0 in zip(bxs, cs_):
                st4f(bx, c0)
        Bre, Brek = B.ring("Bre", [128, CGF, 128], F32, 1)
        Bim, Bimk = B.ring("Bim", [128, CGF, 128], F32, 1)
        for q0 in range(0, len(pairs), 2):
            cs_ = pairs[q0:q0 + 2]
            bks = [st1(ud, udk, 64, c0) for c0 in cs_]
            bbs = [st2(bk) for bk in bks]
            bxs = [st3(bb) for bb in bbs]
            yys = [st4d(bx, c0) for bx, c0 in zip(bxs, cs_)]
            bis = [st5(yy) for yy in yys]
            for bi, c0 in zip(bis, cs_):
                st6(bi, c0)
        yo, yok = B.ring("yo", [64, CGF, 128], F32, 1)
        for c0 in range(0, CGF, 4):
            bo = B.bank()
            P.op("pe", "matmul", reads=[Brek, "dft"], writes=["ps%d" % bo], out=B.ps[bo][0:64, :], lhsT=dft[:, 0:64],
                 rhs=Bre[:, c0:c0 + 4, :].rearrange("p c j -> p (c j)"), start=True, stop=False)
            P.op("pe", "matmul", reads=[Bimk, "dft"], writes=["ps%d" % bo], out=B.ps[bo][0:64, :], lhsT=dft[:, 384:448],
                 rhs=Bim[:, c0:c0 + 4, :].rearrange("p c j -> p (c j)"), start=False, stop=True)
            P.op("act", "activation", reads=["ps%d" % bo], writes=[yok], out=yo[:, c0:c0 + 4, :].rearrange("p c j -> p (c j)"),
                 in_=B.ps[bo][0:64, :], func=AF.Identity, scale=1.0 / NFFT)
        for r in range(4):
            P.dma("sp", reads=[yok], writes=["y_src"], out=yview[r][:, cs0:cs0 + CGF, :], in_=yo[16 * r:16 * (r + 1), :, :])
    B.release(m0)
    def issue_y_gather():
        for k in range(4):
            P.cc(reads=["y_src"], writes=["y_g"], kind="AllGather", op=ALU.bypass, replica_groups=RG4, ins=[E["y_src"][k].opt()],
                 outs=[E["y_g"][par][k].opt()])
        P.cc(reads=["ssq_src"], writes=["ssq_g%d" % par], kind="AllGather", op=ALU.bypass, replica_groups=RG4, ins=[E["ssq_src"].opt()],
             outs=[E["ssq_g"][par].opt()])
    return issue_y_gather


def f_hyena_gate(B, E, l, do_ctx=True):
    P = B.P
    par = l % 2
    hT, w_in, gbT = E["hT"], E["w_in"][l], E["gbT"]
    m0 = B.mark()
    scw = load_shortconv(B, E["hy_w"][l], E["hy_b"][l])
    hyct = B.sb("hyct", [128, 4, 4], F32)
    P.dma("sp", reads=["ssq_g%d" % par], writes=["hyct"], out=hyct[:, :, 0:1],
          in_=E["ssq_g"][par][:, 0:1].rearrange("(ci p) o -> p ci o", p=128), allow_slow_non_contiguous=True)
    P.dma("sp", writes=["hyct"], out=hyct[:, :, 1:2], in_=E["hyb"][l].rearrange("p (ci o) -> p ci o", o=1), allow_slow_non_contiguous=True)
    fw = load_filter_weights(B, E["hw1"][l], E["hb1"][l], E["hw2"][l], E["hb2"][l], E["hfreq"][l])
    w3 = B.sb("fw3", [64, 1024], F32)
    P.dma("sp", writes=["fw"], out=w3[:], in_=E["hw3"][l])
    zt = B.sb("ztc", [33, 512], F32)
    P.dma("sp", writes=["ztc"], out=zt[:], in_=E["zposc"][:, :])
    hid2, hid2k = filter_mlp_chunk(B, fw, zt, "ztc", 512)
    hid2p = B.sb("hid2p", [64, 512], F32)
    P.op("pool", "tensor_copy", reads=[hid2k], writes=["hid2p"], out=hid2p[:], in_=hid2[0:64, :])
    segs_all = [(1, TOK, 0), (TOK + 3, CTX, TOK)]
    segs_ctx = [(TOK + 3, CTX, 0)]
    ygr = E["y_g"][par]
    P.dma("sp", reads=["y_g"], writes=["y_my"], out=E["y_my"], in_=(lambda e: ygr[rank_of(e)]))
    if do_ctx:
        idf = E["idf"]
        uc_all = B.sb("uc_all", [128, 4, CTX], F32)
        kc_all = B.sb("kc_all", [128, 4, 512], F32)
        ycv = B.sb("ycv_all", [128, 4, CTX], F32)
        cst = B.sb("cst_all", [128, 4, 4], F32)
        mA = B.mark()
        for ci in range(4):
            x1c, x1ck = B.ring("x1c", [128, CTX], F32, 1)
            vc_, vck = B.ring("vcc", [128, CTX], F32, 1)
            hy_conv_tile(B, w_in, 4 + ci, hT, "hT", segs_ctx, scw, x1c, x1ck)
            hy_conv_tile(B, w_in, 8 + ci, hT, "hT", segs_ctx, scw, vc_, vck)
            P.op("dve", "tensor_tensor", reads=[x1ck, vck], writes=["uc_all"], out=uc_all[:, ci, :], in0=x1c[:], in1=vc_[:], op=ALU.mult)
            wnc, wnck = B.ring("wnc", [128, 512], F32, 1)
            P.dma("sp", writes=[wnck], out=wnc[:], in_=E["winc"][128 * ci:128 * (ci + 1), :])
            bk = B.bank()
            P.op("pe", "matmul", reads=["fw", "hid2p"], writes=["ps%d" % bk], out=B.ps[bk][:, 0:255], lhsT=w3[0:64, 512 + 128 * ci:512 + 128 * (ci + 1)],
                 rhs=hid2p[0:64, 0:255], start=True, stop=True)
            P.op("pe", "matmul", reads=["fw", "hid2p"], writes=["ps%d" % bk], out=B.ps[bk][:, 255:512], lhsT=w3[0:64, 128 * ci:128 * (ci + 1)],
                 rhs=hid2p[0:64, 255:512], start=True, stop=True)
            P.op("dve", "tensor_tensor", reads=["ps%d" % bk, wnck], writes=["kc_all"], out=kc_all[:, ci, :], in0=B.ps[bk][:, :], in1=wnc[:], op=ALU.mult)
            jk_, jkk = B.ring("kjunk", [128, 512], F32, 1)
            P.op("act", "activation", reads=["kc_all"], writes=[jkk, "cst_all"], out=jk_[:], in_=kc_all[:, ci, :], func=AF.Square, accum_out=cst[:, ci, 0:1])
            P.op("act", "activation", reads=["cst_all"], writes=["cst_all"], out=cst[:, ci, 1:2], in_=cst[:, ci, 0:1], func=AF.Sqrt)
            P.op("dve", "reciprocal", reads=["cst_all"], writes=["cst_all"], out=cst[:, ci, 2:3], in_=cst[:, ci, 1:2])
        B.release(mA)
        ftab = B.sb("ftab", [128, 4, 1024], F32)
        uT = B.sb("uT", [128, 2, 512], F32)
        kT = B.sb("kT", [128, 4, 512], F32)
        AB = B.sb("AB", [128, 2, 4, 512], F32)
        PP_ = B.sb("PP", [128, 2, 4, 512], F32)
        yT = uT
        for tt in range(2):
            bk = B.bank()
            for ci in range(4):
                P.op("pe", "transpose", reads=["uc_all", "ident_f"], writes=["ps%d" % bk], out=B.ps[bk][:, 128 * ci:128 * (ci + 1)],
                     in_=uc_all[:, ci, 128 * tt:128 * (tt + 1)], identity=idf[:, :])
            P.op("act", "activation", reads=["ps%d" % bk], writes=["uT"], out=uT[:, tt, :], in_=B.ps[bk][:, :], func=AF.Identity)
        for it_ in range(4):
            bk = B.bank()
            for ci in range(4):
                P.op("pe", "transpose", reads=["kc_all", "ident_f"], writes=["ps%d" % bk], out=B.ps[bk][:, 128 * ci:128 * (ci + 1)],
                     in_=kc_all[:, ci, 128 * it_:128 * (it_ + 1)], identity=idf[:, :])
            P.op("act", "activation", reads=["ps%d" % bk], writes=["kT"], out=kT[:, it_, :], in_=B.ps[bk][:, :], func=AF.Identity)
        P.dma("sp", writes=["ftab"], out=ftab[:, 0:2, :], in_=E["FU"].rearrange("(a p) k -> p a k", p=128))
        for kt in range(4):
            for j in range(2):
                bk = B.bank()
                for tt in range(2):
                    P.op("pe", "matmul", reads=["ftab", "uT"], writes=["ps%d" % bk], out=B.ps[bk][:, :], lhsT=ftab[:, tt, 512 * j + 128 * kt:512 * j + 128 * (kt + 1)],
                         rhs=uT[:, tt, :], start=(tt == 0), stop=(tt == 1))
                P.op("act", "activation", reads=["ps%d" % bk], writes=["AB"], out=AB[:, j, kt, :], in_=B.ps[bk][:, :], func=AF.Identity)
        P.dma("sp", reads=[], writes=["ftab"], out=ftab[:], in_=E["FK"].rearrange("(a p) k -> p a k", p=128))
        for kt in range(4):
            bks = []
            for j in range(2):
                bk = B.bank()
                for it_ in range(4):
                    P.op("pe", "matmul", reads=["ftab", "kT"], writes=["ps%d" % bk], out=B.ps[bk][:, :], lhsT=ftab[:, it_, 512 * j + 128 * kt:512 * j + 128 * (kt + 1)],
                         rhs=kT[:, it_, :], start=(it_ == 0), stop=(it_ == 3))
                bks.append(bk)
            kcb, ksb = bks
            tq, tqk = B.ring("ctt", [128, 512], F32, 2)
            P.op("dve", "tensor_tensor", reads=["ps%d" % kcb, "AB"], writes=["PP0"], out=PP_[:, 0, kt, :], in0=B.ps[kcb][:, :], in1=AB[:, 0, kt, :], op=ALU.mult)
            P.op("dve", "tensor_tensor", reads=["ps%d" % ksb, "AB"], writes=[tqk], out=tq[:], in0=B.ps[ksb][:, :], in1=AB[:, 1, kt, :], op=ALU.mult)
            P.op("pool", "tensor_tensor", reads=[tqk, "PP0"], writes=["PP0"], out=PP_[:, 0, kt, :], in0=PP_[:, 0, kt, :], in1=tq[:], op=ALU.subtract)
            tq2, tq2k = B.ring("ctt", [128, 512], F32, 2)
            P.op("dve", "tensor_tensor", reads=["ps%d" % ksb, "AB"], writes=["PP1"], out=PP_[:, 1, kt, :], in0=B.ps[ksb][:, :], in1=AB[:, 0, kt, :], op=ALU.mult)
            P.op("dve", "tensor_tensor", reads=["ps%d" % kcb, "AB"], writes=[tq2k], out=tq2[:], in0=B.ps[kcb][:, :], in1=AB[:, 1, kt, :], op=ALU.mult)
            P.op("pool", "tensor_tensor", reads=[tq2k, "PP1"], writes=["PP1"], out=PP_[:, 1, kt, :], in0=PP_[:, 1, kt, :], in1=tq2[:], op=ALU.add)
        P.dma("sp", writes=["ftab"], out=ftab[:, :, 0:512], in_=E["FI"].rearrange("(a p) k -> p a k", p=128))
        for tt in range(2):
            bk = B.bank()
            n_ = 0
            for j in range(2):
                for kt in range(4):
                    P.op("pe", "matmul", reads=["ftab", "PP0", "PP1"], writes=["ps%d" % bk], out=B.ps[bk][:, :], lhsT=ftab[:, kt, 256 * j + 128 * tt:256 * j + 128 * (tt + 1)],
                         rhs=PP_[:, j, kt, :], start=(n_ == 0), stop=(n_ == 7))
                    n_ += 1
            P.op("act", "activation", reads=["ps%d" % bk], writes=["uT"], out=yT[:, tt, :], in_=B.ps[bk][:, :], func=AF.Identity, scale=1.0 / 512)
        for ci in range(4):
            bk = B.bank()
            for tt in range(2):
                P.op("pe", "transpose", reads=["uT", "ident_f"], writes=["ps%d" % bk], out=B.ps[bk][:, 128 * tt:128 * (tt + 1)],
                     in_=yT[:, tt, 128 * ci:128 * (ci + 1)], identity=idf[:, :])
            P.op("act", "activation", reads=["ps%d" % bk], writes=["ycv_all"], out=ycv[:, ci, :], in_=B.ps[bk][:, 0:CTX], func=AF.Identity)
        B.release(mA)
    for ci in range(4):
        P.op("act", "activation", reads=["hyct"], writes=["hyct"], out=hyct[:, ci, 2:3], in_=hyct[:, ci, 0:1], func=AF.Sqrt)
        P.op("dve", "reciprocal", reads=["hyct"], writes=["hyct"], out=hyct[:, ci, 3:4], in_=hyct[:, ci, 2:3])
        x0s, x0k = B.ring("x0s", [128, NTOK3], F32, 1)
        hy_conv_tile(B, w_in, ci, hT, "hT", segs_all if do_ctx else segs_all[:1], scw, x0s, x0k)
        yb, ybk = B.ring("yb", [128, NTOK3], F32, 1)
        ut, utk = B.ring("ut", [128, TOK], F32, 1)
        P.dma("sp", reads=["y_my"], writes=[ybk], out=yb[:, 0:TOK], in_=E["y_my"][128 * ci:128 * (ci + 1), :])
        P.dma("sp", reads=["u_src"], writes=[utk], out=ut[:], in_=E["u_src"][ci])
        P.op("dve", "tensor_scalar", reads=[ybk, "hyct"], writes=[ybk], out=yb[:, 0:TOK], in0=yb[:, 0:TOK], scalar1=hyct[:, ci, 3:4],
             scalar2=None, op0=ALU.mult)
        P.op("dve", "scalar_tensor_tensor", reads=[utk, "hyct", ybk], writes=[ybk], out=yb[:, 0:TOK], in0=ut[:], scalar=hyct[:, ci, 1:2],
             in1=yb[:, 0:TOK], op0=ALU.mult, op1=ALU.add)
        if do_ctx:
            P.op("dve", "tensor_scalar", reads=["ycv_all", "cst_all"], writes=[ybk], out=yb[:, TOK:NTOK3], in0=ycv[:, ci, :], scalar1=cst[:, ci, 2:3],
                 scalar2=None, op0=ALU.mult)
            P.op("dve", "scalar_tensor_tensor", reads=["uc_all", "hyct", ybk], writes=[ybk], out=yb[:, TOK:NTOK3], in0=uc_all[:, ci, :], scalar=hyct[:, ci, 1:2],
                 in1=yb[:, TOK:NTOK3], op0=ALU.mult, op1=ALU.add)
        nuse = NTOK3 if do_ctx else TOK
        P.op("pool", "tensor_tensor", reads=[x0k, ybk], writes=[ybk], out=yb[:, 0:nuse], in0=yb[:, 0:nuse], in1=x0s[:, 0:nuse], op=ALU.mult)
        wt, wk = wblock(B, w_in, COL_GB + 128 * ci, 128, "wblk128", 3, 128)
        for (t0, n) in (CHUNKS if do_ctx else CHUNKS[:-1]):
            bk = B.bank()
            proj_fm(B, wt, wk, 0, hT, "hT", hcol(t0), n, bk)
            sg, sgk = B.ring("sgb", [128, 512], F32, 2)
            P.op("act", "activation", reads=["ps%d" % bk], writes=[sgk], out=sg[:, 0:n], in_=B.ps[bk][:, 0:n], func=AF.Silu)
            P.op("dve", "tensor_tensor", reads=[sgk, ybk], writes=["gbT"], out=gbT[:, ci, t0:t0 + n], in0=sg[:, 0:n], in1=yb[:, t0:t0 + n],
                 op=ALU.mult)
    B.release(m0)


def f_gmlp(B, E, l, tiles=None):
    P = B.P
    hT, w_in, gcT, idf, idb = E["hT"], E["w_in"][l], E["gcT"], E["idf"], E["idb"]
    m0 = B.mark()
    B.cast_eng = "dve"
    lng, lngk = bcast_row(B, E["gm_g"][l], 512, "lng")
    lnb, lnbk = bcast_row(B, E["gm_b"][l], 512, "lnb")
    wsf = B.sb("wsf", [128, 8, 128], F32)
    P.dma("sp", writes=["wsf"], out=wsf[:], in_=E["gm_ws"][l].rearrange("g p q -> p g q"))
    wsT = B.sb("wsT", [128, 8, 128], BF16)
    for g in range(8):
        bk = B.bank()
        P.op("pe", "transpose", reads=["wsf", "ident_f"], writes=["ps%d" % bk], out=B.ps[bk][:, 0:128], in_=wsf[:, g, :], identity=idf[:, :])
        P.op("act", "activation", reads=["ps%d" % bk], writes=["wsT"], out=wsT[:, g, :], in_=B.ps[bk][:, 0:128], func=AF.Identity)
    bsT = B.sb("bsT", [128, 8], F32)
    P.dma("sp", writes=["bsT"], out=bsT[:], in_=E["gm_bs"][l].rearrange("g p -> p g"), allow_slow_non_contiguous=True)
    wu, wuk = wblock(B, w_in, COL_GM, 512, "wgm_u", 1)
    wv, wvk = wblock(B, w_in, COL_GM + 512, 512, "wgm_v", 1)
    wc, wck = wblock(B, w_in, COL_GC, 512, "wgm_c", 1)
    def gA(t0):
        ba, bb, bc_ = B.bank(), B.bank(), B.bank()
        proj_tm(B, wu, wuk, 0, 512, hT, "hT", hcol(t0), 128, ba)
        proj_tm(B, wv, wvk, 0, 512, hT, "hT", hcol(t0), 128, bb)
        proj_tm(B, wc, wck, 0, 512, hT, "hT", hcol(t0), 128, bc_)
        return (ba, bb, bc_)

    def gB(bks):
        ba, bb, bc_ = bks
        ug, ugk = B.ring("ug", [128, 512], F32, 2)
        vg, vgk = B.ring("vg", [128, 512], F32, 2)
        gs, gsk = B.ring("gs", [128, 512], F32, 2)
        P.op("act", "activation", reads=["ps%d" % ba], writes=[ugk], out=ug[:], in_=B.ps[ba][:, :], func=AF.Gelu)
        P.op("act", "activation", reads=["ps%d" % bb], writes=[vgk], out=vg[:], in_=B.ps[bb][:, :], func=AF.Gelu)
        P.op("act", "activation", reads=["ps%d" % bc_], writes=[gsk], out=gs[:], in_=B.ps[bc_][:, :], func=AF.Silu)
        return (ug, ugk, vg, vgk, gs, gsk)

    def gC(st):
        ug, ugk, vg, vgk, gs, gsk = st
        s6, s6k = B.ring("s6", [128, 6], F32, 2)
        mv, mvk = B.ring("mv", [128, 4], F32, 2)
        P.op("dve", "bn_stats", reads=[vgk], writes=[s6k], out=s6[:], in_=vg[:])
        P.op("dve", "bn_aggr", reads=[s6k], writes=[mvk], out=mv[:, 0:2], in_=s6[:])
        P.op("dve", "tensor_scalar", reads=[mvk], writes=[mvk], out=mv[:, 2:3], in0=mv[:, 1:2], scalar1=EPS, scalar2=None, op0=ALU.add)
        P.op("act", "activation", reads=[mvk], writes=[mvk], out=mv[:, 2:3], in_=mv[:, 2:3], func=AF.Sqrt)
        P.op("dve", "reciprocal", reads=[mvk], writes=[mvk], out=mv[:, 3:4], in_=mv[:, 2:3])
        P.op("dve", "tensor_scalar", reads=[vgk, mvk], writes=[vgk], out=vg[:], in0=vg[:], scalar1=mv[:, 0:1], scalar2=mv[:, 3:4],
             op0=ALU.subtract, op1=ALU.mult)
        P.op("dve", "tensor_tensor", reads=[vgk, lngk], writes=[vgk], out=vg[:], in0=vg[:], in1=lng[:], op=ALU.mult)
        vnb, vnbk = B.ring("vnb", [128, 512], BF16, 2)
        P.op("dve", "tensor_tensor", reads=[vgk, lnbk], writes=[vnbk], out=vnb[:], in0=vg[:], in1=lnb[:], op=ALU.add)
        return (vnb, vnbk)

    def gD(vv):
        vnb, vnbk = vv
        bm = B.bank()
        for g in range(8):
            P.op("pe", "matmul", reads=["wsT", vnbk], writes=["ps%d" % bm], out=B.ps[bm][:, 64 * g:64 * (g + 1)], lhsT=wsT[:, g, :],
                 rhs=vnb[:, 64 * g:64 * (g + 1)], start=(g == 0), stop=(g == 7), skip_group_check=True)
        return bm

    def gE(bm, st, t0):
        ug, ugk, vg, vgk, gs, gsk = st
        for g in range(8):
            P.op("dve", "scalar_tensor_tensor", reads=["ps%d" % bm, "bsT", ugk], writes=[ugk], out=ug[:, 64 * g:64 * (g + 1)],
                 in0=B.ps[bm][:, 64 * g:64 * (g + 1)], scalar=bsT[:, g:g + 1], in1=ug[:, 64 * g:64 * (g + 1)], op0=ALU.add, op1=ALU.mult)
        yc, yck = B.ring("ycb", [128, 512], BF16, 2)
        P.op("dve", "tensor_tensor", reads=[ugk, gsk], writes=[yck], out=yc[:], in0=ug[:], in1=gs[:], op=ALU.mult)
        transpose_to_fm(B, yc, yck, gcT, "gcT", t0, idb)

    tiles = TILES if tiles is None else tiles
    for q0 in range(0, len(tiles), 2):
        ts_ = tiles[q0:q0 + 2]
        bk3 = [gA(t0) for t0 in ts_]
        sts = [gB(x_) for x_ in bk3]
        vvs = [gC(x_) for x_ in sts]
        bms = [gD(x_) for x_ in vvs]
        for bm, st, t0 in zip(bms, sts, ts_):
            gE(bm, st, t0)
    B.cast_eng = "pool"
    B.release(m0)


def build_fused():
    B = Builder()
    P = B.P
    E = {}
    for nm, shp in (("xs", [TOK + 2, D]), ("xc0", [CTX, D]), ("c", [D]), ("c_ctx", [D]), ("ada_w", [DEPTH, D, 3 * D]), ("ada_b", [DEPTH, 3 * D]),
                    ("npre", [DEPTH, D]), ("npost", [DEPTH, D]), ("w_in", [DEPTH, D, COL_END]), ("wk_perm", [DEPTH, D, 512]),
                    ("wq_perm", [DEPTH, D, 512]), ("hmask", [2, 1]), ("cosT", [128, TOK]), ("sinT", [128, TOK]),
                    ("hy_w", [DEPTH, 3, 1536]), ("hy_b", [DEPTH, 1536]), ("hw1", [DEPTH, 33, 64]), ("hb1", [DEPTH, 64]),
                    ("hw2", [DEPTH, 64, 64]), ("hb2", [DEPTH, 64]), ("hw3", [DEPTH, 64, 1024]), ("hw3q", [DEPTH, 64, 256]),
                    ("hfreq", [DEPTH, 2, 64]), ("zpos", [33, NFFT]), ("win", [128, NFFT]), ("dft_in", [128, 640]), ("tw_in", [128, 256]),
                    ("zposc", [33, 512]), ("winc", [512, 512]), ("FU", [256, 1024]), ("FK", [512, 1024]), ("FI", [512, 512]), ("hyb", [DEPTH, 128, 4]), ("da_lam", [DEPTH, 256]), ("lam_init", [DEPTH, 1]),
                    ("da_subln", [DEPTH, 128]), ("gm_g", [DEPTH, 512]), ("gm_b", [DEPTH, 512]), ("gm_ws", [DEPTH, 8, 128, 128]),
                    ("gm_bs", [DEPTH, 8, 128]), ("w_br", [DEPTH, 3, 512, D]), ("w_out", [DEPTH, D, D])):
        E[nm] = B.din(nm, shp)
    x_out = B.dout("x_out", [TOK, D])
    nc = B.nc
    def scr(name, shape, dt=F32):
        return nc.dram_tensor(name, list(shape), dt, kind="Internal").ap()
    E["xcur"] = [scr("xcur%d" % i, [TOK + 2, D]) for i in range(2)]
    E["xccur"] = [scr("xccur%d" % i, [CTX, D]) for i in range(2)]
    E["kt_src"] = scr("kt_src", [2, 256, TOK], BF16)
    E["v_src"] = scr("v_src", [2, TOK // 2, 512], BF16)
    E["u_src"] = scr("u_src", [4, 128, TOK])
    E["y_src"] = scr("y_src", [4, 128, TOK])
    E["ssq_src"] = scr("ssq_src", [128, 16])
    E["hal_src"] = scr("hal_src", [2, D])
    E["kt_g"] = [scr("kt_g%d" % i, [2, 4 * 256, TOK], BF16) for i in range(2)]
    E["v_g"] = [scr("v_g%d" % i, [2, 4 * (TOK // 2), 512], BF16) for i in range(2)]
    E["u_g"] = [scr("u_g0", [4, 4 * 128, TOK])] * 2
    E["y_g"] = [scr("y_g0", [4, 4 * 128, TOK])] * 2
    E["ssq_g"] = [scr("ssq_g%d" % i, [512, 16]) for i in range(2)]
    E["hal_g"] = [scr("hal_g0", [8, D])] * 2
    E["kscr"] = scr("kscr", [128, NFFT])
    E["u_my"] = scr("u_my", [4, 128, TOK])
    E["y_my"] = scr("y_my", [512, TOK])
    B.init_psum()
    idf, idb = make_identity(B)
    E["idf"], E["idb"] = idf, idb
    B.ones1 = B.sb("ones1", [1, 128], F32)
    P.op("dve", "memset", writes=["ones1"], ap=B.ones1[:], constant=1.0)
    dft = B.sb("dft", [128, 640], F32)
    tw = B.sb("tw", [128, 256], F32)
    P.dma("sp", writes=["dft"], out=dft[:], in_=E["dft_in"][:, :])
    P.dma("sp", writes=["tw"], out=tw[:], in_=E["tw_in"][:, :])
    E["dft"], E["tw"] = dft, tw
    hT = B.sb("hT", [128, 8, HCOLS], BF16)
    Gbc = B.sb("bc_G", [128, 1024], F32)
    Gcbc = B.sb("bc_Gc", [128, 1024], F32)
    gaT = B.sb("gaT", [128, 4, NTOK3], BF16)
    gbT = B.sb("gbT", [128, 4, NTOK3], BF16)
    gcT = B.sb("gcT", [128, 4, NTOK3], BF16)
    mt = B.sb("hmask", [2, 1], F32)
    P.dma("sp", writes=["hmask"], out=mt[:], in_=E["hmask"][:, :])
    E.update(hT=hT, gaT=gaT, gbT=gbT, gcT=gcT)
    P.op("pool", "memset", writes=["hT"], ap=hT[:, :, TOK + 2:TOK + 3], constant=0.0)
    P.op("pool", "memset", writes=["hT"], ap=hT[:, :, HCOLS - 1:HCOLS], constant=0.0)
    for i_ in range(EXTRA_CC):
        P.cc(reads=["hal_src"], writes=["hal_g"], kind="AllGather", op=ALU.bypass, replica_groups=RG4, ins=[E["hal_src"].opt()],
             outs=[E["hal_g"][0].opt()])
    for l in range(FUSED_DEPTH):
        last = l == FUSED_DEPTH - 1
        par = l % 2
        xsrc = E["xs"] if l == 0 else E["xcur"][par]
        xcsrc = E["xc0"] if l == 0 else E["xccur"][par]
        xkeys = [] if l == 0 else ["xcur%d" % par]
        xckeys = [] if l == 0 else ["xccur%d" % par]
        m0 = B.mark()
        Abc = B.sb("bc_A", [128, 1024], F32); Bbc = B.sb("bc_Bm", [128, 1024], F32)
        Acbc = B.sb("bc_Ac", [128, 1024], F32); Bcbc = B.sb("bc_Bc", [128, 1024], F32)
        m1 = B.mark()
        modulation2(B, E["c"], E["c_ctx"], E["ada_w"][l], E["ada_b"][l], E["npre"][l], E["npost"][l],
                    {"A": Abc, "Bm": Bbc, "G": Gbc, "Ac": Acbc, "Bc": Bcbc, "Gc": Gcbc})
        B.release(m1)
        for i in range(NT):
            compute_h_tile(B, [(xsrc[1 + 128 * i: 1 + 128 * (i + 1), :], 0, 128)], 128, Abc, Bbc, "bcA", "bcBm", hT, "hT", 1 + 128 * i, idb, xkeys=xkeys)
        hh = B.sb("hTh", [128, 8, 2], BF16)
        compute_h_tile(B, [(xsrc[0:1, :], 0, 1), (xsrc[TOK + 1:TOK + 2, :], 1, 1)], 2, Abc, Bbc, "bcA", "bcBm", hh, "hTh", 0, idb,
                       mask=(mt, "hmask"), xkeys=xkeys)
        P.op("pool", "tensor_copy", reads=["hTh"], writes=["hT"], out=hT[:, :, 0:1], in_=hh[:, :, 0:1])
        P.op("pool", "tensor_copy", reads=["hTh"], writes=["hT"], out=hT[:, :, TOK + 1:TOK + 2], in_=hh[:, :, 1:2])
        for i in range(2):
            compute_h_tile(B, [(xcsrc[128 * i:128 * (i + 1), :], 0, 128)], 128, Acbc, Bcbc, "bcAc", "bcBc", hT, "hT", TOK + 3 + 128 * i, idb, xkeys=xckeys)
        B.release(m0)
        f_products(B, E, l)
        f_gmlp(B, E, l, tiles=(TILES[:NT] if last else None))
        issue_y = f_conv(B, E, l)
        L = dict(after_q=issue_y, hT=hT, idb=idb, gaT=gaT, gbT=gbT, gcT=gcT, w_in=E["w_in"][l], wq_perm=E["wq_perm"][l], cosT=E["cosT"], sinT=E["sinT"],
                 kt_all=[E["kt_g"][par][h // 2].rearrange("(r h d) t -> h d r t", r=4, h=2)[h % 2] for h in range(4)], v_all=None,
                 v_gk=E["v_g"][par],
                 kth_out=(lambda k: k.rearrange("p (r t) -> p r t", r=4)), kt_keys=["kt_g%d" % par], v_keys=["v_g%d" % par],
                 da_lam=E["da_lam"][l], lam_init=E["lam_init"][l], da_subln=E["da_subln"][l],
                 w_br=E["w_br"][l], w_out=E["w_out"][l], xs=xsrc, xc=xcsrc, x_keys=xkeys + xckeys, Gbc=Gbc, Gcbc=Gcbc)
        if last:
            L["chunks"] = CHUNKS[:-1]
        f_attn(B, L)
        f_hyena_gate(B, E, l, do_ctx=not last)
        if last:
            L.update(x_new=x_out, x_new_key="x_out", xc_new=None, hal_src=None)
        else:
            L.update(x_new=E["xcur"][1 - par][1:TOK + 1, :], x_new_key="xcur%d" % (1 - par), xc_new=E["xccur"][1 - par],
                     xc_new_key="xccur%d" % (1 - par), hal_src=E["hal_src"])
        build_s3_merge(B, L)
        if not last:
            P.cc(reads=["hal_src"], writes=["hal_g"], kind="AllGather", op=ALU.bypass, replica_groups=RG4, ins=[E["hal_src"].opt()],
                 outs=[E["hal_g"][par].opt()])
            hg = E["hal_g"][par]
            xn = E["xcur"][1 - par]
            P.dma("sp", reads=["hal_g"], writes=["xcur%d" % (1 - par)], out=xn[0:1, :],
                  in_=(lambda e, hg=hg: hg.rearrange("(r two) d -> r two d", two=2)[rank_nb(e, 3)][1:2, :]))
            P.dma("sp", reads=["hal_g"], writes=["xcur%d" % (1 - par)], out=xn[TOK + 1:TOK + 2, :],
                  in_=(lambda e, hg=hg: hg.rearrange("(r two) d -> r two d", two=2)[rank_nb(e, 1)][0:1, :]))
    return B.finish()


def ctx_dft_tables():
    t = np.arange(256, dtype=np.float64)[:, None]
    k = np.arange(512, dtype=np.float64)[None, :]
    th = 2.0 * math.pi * t * k / 512.0
    FU = np.concatenate([np.cos(th), np.sin(th)], axis=1).astype(np.float32)
    i = np.arange(512, dtype=np.float64)[:, None]
    ph = 2.0 * math.pi * (i - 255.0) * k / 512.0
    FK = np.concatenate([np.cos(ph), np.sin(ph)], axis=1).astype(np.float32)
    kk = np.arange(512, dtype=np.float64)[:, None]
    tt = np.arange(256, dtype=np.float64)[None, :]
    ti = 2.0 * math.pi * kk * tt / 512.0
    FI = np.concatenate([np.cos(ti), np.sin(ti)], axis=1).astype(np.float32)
    return FU, FK, FI


def fused_inputs(inp):
    cosT, sinT = rope_tables()
    FU, FK, FI = ctx_dft_tables()
    zpos, win, dft, tw = hyena_tables()
    zposc, winc = ctx_tables()
    w_in = inp["w_in"]
    wkp = np.stack([perm_cols(w_in[l][:, COL_K:COL_K + 512]) for l in range(DEPTH)], 0)
    wqp = np.stack([perm_cols(w_in[l][:, COL_Q:COL_Q + 512]) for l in range(DEPTH)], 0)
    lam_init = np.array([[0.8 - 0.6 * math.exp(-0.3 * l)] for l in range(DEPTH)], np.float32)
    hyb = np.ascontiguousarray(inp["hy_bias"].reshape(DEPTH, 4, 128).transpose(0, 2, 1))
    shared = {
        "c_ctx": inp["c_ctx"], "ada_w": inp["ada_w"], "ada_b": inp["ada_b"], "npre": inp["norm_pre"], "npost": inp["norm_post"],
        "w_in": w_in, "wk_perm": wkp, "wq_perm": wqp, "hy_w": inp["hy_short_w"], "hy_b": inp["hy_short_b"],
        "hw1": inp["hy_f_w1"], "hb1": inp["hy_f_b1"], "hw2": inp["hy_f_w2"], "hb2": inp["hy_f_b2"], "hw3": inp["hy_f_w3"],
        "hfreq": inp["hy_f_freq"], "zpos": zpos, "dft_in": dft, "tw_in": tw, "zposc": zposc, "winc": winc, "hyb": hyb, "FU": FU, "FK": FK, "FI": FI,
        "da_lam": np.ascontiguousarray(inp["da_lambda"].reshape(DEPTH, 256)), "lam_init": lam_init, "da_subln": inp["da_subln"],
        "gm_g": inp["gm_ln_g"], "gm_b": inp["gm_ln_b"], "gm_ws": inp["gm_ws"], "gm_bs": inp["gm_bs"], "w_br": inp["w_branch"],
        "w_out": inp["w_out"],
    }
    x = inp["x"]
    in_maps = []
    for core in range(8):
        b, r = divmod(core, 4)
        t0 = r * TOK
        xs = np.zeros((TOK + 2, D), np.float32)
        lo, hi = max(t0 - 1, 0), min(t0 + TOK + 1, SEQ)
        xs[lo - (t0 - 1): hi - (t0 - 1)] = x[b, lo:hi]
        hmask = np.array([[0.0 if t0 == 0 else 1.0], [0.0 if t0 + TOK == SEQ else 1.0]], np.float32)
        hw3q = np.ascontiguousarray(np.concatenate([inp["hy_f_w3"][:, :, 128 * r:128 * (r + 1)],
                                                    inp["hy_f_w3"][:, :, 512 + 128 * r:512 + 128 * (r + 1)]], axis=2))
        m = dict(shared)
        m.update({"xs": xs, "xc0": np.ascontiguousarray(inp["ctx"][b]), "c": inp["c"][b], "hmask": hmask,
                  "cosT": np.ascontiguousarray(cosT[:, t0:t0 + TOK]), "sinT": np.ascontiguousarray(sinT[:, t0:t0 + TOK]),
                  "hw3q": hw3q, "win": np.ascontiguousarray(win[128 * r:128 * (r + 1)])})
        in_maps.append(m)
    return in_maps


def kernel(**inputs):
    inp = {k: np.ascontiguousarray(np.asarray(v), dtype=np.float32) for k, v in inputs.items()}
    nc = get_nc("fused")
    in_maps = fused_inputs(inp)
    res = run_bass_kernel_spmd(nc, in_maps, core_ids=list(range(8))).results
    out = np.stack([np.concatenate([np.asarray(res[4 * b + r]["x_out"]) for r in range(4)], 0) for b in range(2)], 0)
    return out.astype(np.float32)


def f_attn(B, L):
    P = B.P
    hT, gaT = L["hT"], L["gaT"]
    w_in, wq_perm, cosT, sinT, kt_all = L["w_in"], L["wq_perm"], L["cosT"], L["sinT"], L["kt_all"]
    m0 = B.mark()
    lamb, lambk = bcast_row(B, L["da_lam"], 256, "lam")
    lib, libk = bcast_row(B, L["lam_init"], 1, "li")
    lt = B.sb("lamtmp", [128, 140], F32)
    P.op("dve", "tensor_tensor", reads=[lambk], writes=["lamtmp"], out=lt[:, 0:64], in0=lamb[:, 0:64], in1=lamb[:, 64:128], op=ALU.mult)
    P.op("dve", "tensor_tensor", reads=[lambk], writes=["lamtmp"], out=lt[:, 64:128], in0=lamb[:, 128:192], in1=lamb[:, 192:256], op=ALU.mult)
    P.op("dve", "tensor_reduce", reads=["lamtmp"], writes=["lamtmp"], out=lt[:, 128:129], in_=lt[:, 0:64], axis=AX.X, op=ALU.add)
    P.op("dve", "tensor_reduce", reads=["lamtmp"], writes=["lamtmp"], out=lt[:, 129:130], in_=lt[:, 64:128], axis=AX.X, op=ALU.add)
    P.op("act", "activation", reads=["lamtmp"], writes=["lamtmp"], out=lt[:, 130:132], in_=lt[:, 128:130], func=AF.Exp)
    P.op("dve", "tensor_tensor", reads=["lamtmp"], writes=["lamtmp"], out=lt[:, 132:133], in0=lt[:, 131:132], in1=lt[:, 130:131], op=ALU.subtract)
    P.op("dve", "tensor_tensor", reads=["lamtmp", libk], writes=["lamtmp"], out=lt[:, 133:134], in0=lt[:, 132:133], in1=lib[:, 0:1], op=ALU.subtract)
    neglam = lt[:, 133:134]
    P.op("dve", "tensor_scalar", reads=[libk], writes=["lamtmp"], out=lt[:, 134:135], in0=lib[:, 0:1], scalar1=-1.0, scalar2=1.0,
         op0=ALU.mult, op1=ALU.add)
    P.dma("sp", writes=["lamtmp"], out=lt[:, 135:136], in_=L["da_subln"].rearrange("(p o) -> p o", o=1))
    P.op("dve", "tensor_tensor", reads=["lamtmp"], writes=["lamtmp"], out=lt[:, 136:137], in0=lt[:, 135:136], in1=lt[:, 134:135], op=ALU.mult)
    gcol = lt[:, 136:137]
    ones_b = B.sb("ones_b", [128, 128], BF16)
    ones_f = B.sb("ones_f", [128, 128], F32)
    P.op("pool", "memset", writes=["ones_b"], ap=ones_b[:], constant=1.0)
    P.op("pool", "memset", writes=["ones_f"], ap=ones_f[:], constant=1.0)
    QT = B.sb("QT", [128, 4, NTOK3], BF16)
    kcT = B.sb("kcT", [128, 4, CTX], BF16)
    vcx = B.sb("vcx", [128, 2, 4, 128], BF16)
    m1 = B.mark()
    cs = B.sb("cosT", [128, TOK], F32)
    sn = B.sb("sinT", [128, TOK], F32)
    P.dma("sp", writes=["cosT"], out=cs[:], in_=cosT[:, :])
    P.dma("sp", writes=["sinT"], out=sn[:], in_=sinT[:, :])
    wt, wk = wblock(B, w_in, COL_Q, 512, "wq")
    wp, wpk = wblock(B, wq_perm, 0, 512, "wq")
    for h in range(4):
        for j in range(TOK // 512):
            b1 = B.bank()
            proj_fm(B, wt, wk, 128 * h, hT, "hT", 1 + 512 * j, 512, b1)
            b2 = B.bank()
            proj_fm(B, wp, wpk, 128 * h, hT, "hT", 1 + 512 * j, 512, b2)
            t1, t1k = B.ring("rtmp1", [128, 512], F32, 2)
            t2, t2k = B.ring("rtmp2", [128, 512], F32, 2)
            P.op("dve", "tensor_tensor", reads=["ps%d" % b1, "cosT"], writes=[t1k], out=t1[:], in0=B.ps[b1][:, :],
                 in1=cs[:, 512 * j:512 * (j + 1)], op=ALU.mult)
            P.op("dve", "tensor_tensor", reads=["ps%d" % b2, "sinT"], writes=[t2k], out=t2[:], in0=B.ps[b2][:, :],
                 in1=sn[:, 512 * j:512 * (j + 1)], op=ALU.mult)
            P.op("pool", "tensor_tensor", reads=[t1k, t2k], writes=["QT"], out=QT[:, h, 512 * j:512 * (j + 1)], in0=t1[:], in1=t2[:], op=ALU.add)
        b1 = B.bank()
        proj_fm(B, wt, wk, 128 * h, hT, "hT", hcol(TOK), CTX, b1)
        P.op("act", "activation", reads=["ps%d" % b1], writes=["QT"], out=QT[:, h, TOK:NTOK3], in_=B.ps[b1][:, 0:CTX], func=AF.Identity)
    wt, wk = wblock(B, w_in, COL_K, 512, "wq")
    for h in range(4):
        b1 = B.bank()
        proj_fm(B, wt, wk, 128 * h, hT, "hT", hcol(TOK), CTX, b1)
        P.op("act", "activation", reads=["ps%d" % b1], writes=["kcT"], out=kcT[:, h, :], in_=B.ps[b1][:, 0:CTX], func=AF.Identity)
    wt, wk = wblock(B, w_in, COL_V, 512, "wq")
    for i in range(2):
        b1 = B.bank()
        proj_tm(B, wt, wk, 0, 512, hT, "hT", hcol(TOK + 128 * i), 128, b1)
        P.op("act", "activation", reads=["ps%d" % b1], writes=["vcx"], out=vcx[:, i, :, :],
             in_=B.ps[b1][:, :].rearrange("p (h d) -> p h d", d=128), func=AF.Identity)
    B.release(m1)
    kth = B.sb("kth", [128, SEQ], BF16)
    vh = B.sb("vh", [128, SEQ // 128, 128], BF16)
    wga, wgak = wblock(B, w_in, COL_GA, 512, "wga", 1)
    if L.get("after_q") is not None:
        L["after_q"]()
    ACC, RS = 0, 1
    pcnt = 0
    for h in range(4):
        P.dma("sp", reads=L.get("kt_keys", []), writes=["kth"], out=kth[:].rearrange("p (r t) -> p r t", r=4), in_=kt_all[h])
        for k in range(2):
            for r in range(4):
                P.dma("sp", reads=L.get("v_keys", []), writes=["vh"], out=vh[:, 16 * r + 8 * k:16 * r + 8 * k + 8, :],
                      in_=L["v_gk"][k].rearrange("(r t p) c -> r p t c", r=4, p=128)[r][:, :, 128 * h:128 * (h + 1)])
        for (t0, n) in L.get("chunks", CHUNKS):
            latent = t0 < TOK
            keys = ([("l", k) for k in range(SEQ // 128)] if latent else []) + [("c", 0), ("c", 1)]
            om, omk = B.ring("om", [128, 2, 512], F32, 1)
            iters = [(m, ki) for m in range(2) for ki in range(0, len(keys), 2)]
            state = {}

            def emit_S(it):
                nonlocal pcnt
                m, ki = it
                p = 1 + (pcnt % 3)
                pcnt += 1
                vts = []
                for i, (kind, kt) in enumerate(keys[ki:ki + 2]):
                    if kind == "l":
                        lk, lkk = kth[64 * m:64 * (m + 1), 128 * kt:128 * (kt + 1)], "kth"
                        vts.append((vh[:, kt, :], "vh"))
                    else:
                        lk, lkk = kcT[64 * m:64 * (m + 1), h, 128 * kt:128 * (kt + 1)], "kcT"
                        vts.append((vcx[:, kt, h, :], "vcx"))
                    P.op("pe", "matmul", reads=[lkk, "QT"], writes=["ps%d" % (2 * p + i)], out=B.ps[2 * p + i][:, 0:n], lhsT=lk,
                         rhs=QT[64 * m:64 * (m + 1), h, t0:t0 + n], start=True, stop=True)
                state[it] = (p, vts)

            def emit_PV(it):
                m, ki = it
                p, vts = state.pop(it)
                pt, ptk = B.ring("pt", [128, 2, 512], BF16, 3)
                P.op("act", "activation", reads=["ps%d" % (2 * p), "ps%d" % (2 * p + 1)], writes=[ptk], out=pt[:, :, 0:n],
                     in_=B.pp[p][:, :].rearrange("p (b c) -> p b c", b=2)[:, :, 0:n], func=AF.Exp, scale=0.125)
                for i, (vt, vtk) in enumerate(vts):
                    first = (ki == 0 and i == 0)
                    lastk = (ki + i == len(keys) - 1)
                    P.op("pe", "matmul", reads=[ptk, vtk], writes=["ps%d" % ACC], out=B.ps[ACC][:, 0:n], lhsT=vt, rhs=pt[:, i, 0:n],
                         start=first, stop=lastk)
                    P.op("pe", "matmul", reads=[ptk, "ones_b"], writes=["ps%d" % RS], out=B.ps[RS][:, 0:n], lhsT=ones_b[:, :], rhs=pt[:, i, 0:n],
                         start=first, stop=lastk)
                if ki + 2 >= len(keys):
                    rr, rrk = B.ring("rr", [128, 512], F32, 1)
                    P.op("dve", "reciprocal", reads=["ps%d" % RS], writes=[rrk], out=rr[:, 0:n], in_=B.ps[RS][:, 0:n])
                    if m == 1:
                        P.op("dve", "tensor_scalar", reads=[rrk, "lamtmp"], writes=[rrk], out=rr[:, 0:n], in0=rr[:, 0:n], scalar1=neglam, scalar2=None,
                             op0=ALU.mult)
                    P.op("dve", "tensor_tensor", reads=["ps%d" % ACC, rrk], writes=[omk + str(m)], out=om[:, m, 0:n], in0=B.ps[ACC][:, 0:n], in1=rr[:, 0:n],
                         op=ALU.mult)

            emit_S(iters[0])
            for idx, it in enumerate(iters):
                if idx + 1 < len(iters):
                    emit_S(iters[idx + 1])
                emit_PV(it)
            P.op("pool", "tensor_tensor", reads=[omk + "0", omk + "1"], writes=[omk + "0"], out=om[:, 0, 0:n], in0=om[:, 0, 0:n], in1=om[:, 1, 0:n], op=ALU.add)
            sq, sqk = B.ring("asq", [128, 512], F32, 1)
            P.op("act", "activation", reads=[omk + "0"], writes=[sqk], out=sq[:, 0:n], in_=om[:, 0, 0:n], func=AF.Square)
            bq = B.bank()
            while bq in (ACC, RS):
                bq = B.bank()
            P.op("pe", "matmul", reads=[sqk, "ones_f"], writes=["ps%d" % bq], out=B.ps[bq][:, 0:n], lhsT=ones_f[:, :], rhs=sq[:, 0:n], start=True, stop=True)
            P.op("dve", "tensor_scalar", reads=["ps%d" % bq], writes=[sqk], out=sq[:, 0:n], in0=B.ps[bq][:, 0:n], scalar1=1.0 / 128, scalar2=EPS,
                 op0=ALU.mult, op1=ALU.add)
            P.op("act", "activation", reads=[sqk], writes=[sqk], out=sq[:, 0:n], in_=sq[:, 0:n], func=AF.Sqrt)
            P.op("dve", "reciprocal", reads=[sqk], writes=[sqk], out=sq[:, 0:n], in_=sq[:, 0:n])
            P.op("dve", "scalar_tensor_tensor", reads=[omk + "0", "lamtmp", sqk], writes=[sqk], out=sq[:, 0:n], in0=om[:, 0, 0:n], scalar=gcol,
                 in1=sq[:, 0:n], op0=ALU.mult, op1=ALU.mult)
            bg = B.bank()
            while bg in (ACC, RS):
                bg = B.bank()
            proj_fm(B, wga, wgak, 128 * h, hT, "hT", hcol(t0), n, bg)
            sg, sgk = B.ring("sga", [128, 512], F32, 1)
            P.op("act", "activation", reads=["ps%d" % bg], writes=[sgk], out=sg[:, 0:n], in_=B.ps[bg][:, 0:n], func=AF.Silu)
            P.op("dve", "tensor_tensor", reads=[sgk, sqk], writes=["gaT"], out=gaT[:, h, t0:t0 + n], in0=sg[:, 0:n], in1=sq[:, 0:n], op=ALU.mult)
    B.release(m0)
```

```python
import contextlib
import math
import numpy as np
import ml_dtypes
import concourse.bass as bass
import concourse.mybir as mybir
from concourse.bass_utils import run_bass_kernel_spmd

F32 = mybir.dt.float32
BF16 = mybir.dt.bfloat16
AF = mybir.ActivationFunctionType
ALU = mybir.AluOpType
AX = mybir.AxisListType

D = 1024
SEQ = 8192
NB = 2
DEPTH = 4
CTX = 256
TOK = 2048
NT = TOK // 128
EPS = 1e-6
COL_K, COL_V, COL_Q, COL_GA, COL_HY, COL_GB, COL_GM, COL_GC, COL_MG, COL_END = (
    0, 512, 1024, 1536, 2048, 3584, 4096, 5120, 5632, 8704)
PI = math.pi


class Prog:
    ENG = ("pe", "act", "dve", "pool", "sp")

    def __init__(self, nc):
        self.nc = nc
        self.ops = {e: [] for e in self.ENG}
        self.cnt = {e: 0 for e in self.ENG}
        self.semidx = {e: 0 for e in self.ENG}
        self.known = {e: {} for e in self.ENG}
        self.lastw = {}
        self.reads = {}
        self.ndma = 0
        self.dma_uses = {}
        self.NDMASEM = 32
        self.semnames = set()

    def _need(self, eng, reads, writes):
        toks = []
        for b in reads:
            t = self.lastw.get(b)
            if t is not None:
                toks.append(t)
        for b in writes:
            t = self.lastw.get(b)
            if t is not None:
                toks.append(t)
            toks.extend(self.reads.get(b, ()))
        need = {}
        for (s, v, e) in toks:
            if e == "pe" and eng == "pe":
                continue
            if v > need.get(s, 0):
                need[s] = v
        kn = self.known[eng]
        out = []
        for s, v in need.items():
            if kn.get(s, 0) >= v:
                continue
            kn[s] = v
            out.append((s, v))
        return out

    def _commit(self, tok, reads, writes):
        for b in reads:
            lst = self.reads.setdefault(b, [])
            lst.append(tok)
            if len(lst) > 64:
                mx = {}
                for (s, v, e) in lst:
                    if v > mx.get(s, (0, None))[0]:
                        mx[s] = (v, e)
                self.reads[b] = [(s, v, e) for s, (v, e) in mx.items()]
        for b in writes:
            self.lastw[b] = tok
            self.reads[b] = []

    def op(self, eng, fname, reads=(), writes=(), **kw):
        waits = self._need(eng, reads, writes)
        self.cnt[eng] += 1
        if self.cnt[eng] > 30000:
            self.semidx[eng] += 1
            self.cnt[eng] = 1
        s = "c_%s%d" % (eng, self.semidx[eng])
        self.semnames.add(s)
        tok = (s, self.cnt[eng], eng)
        self.ops[eng].append((waits, fname, kw, (s, 1)))
        self._commit(tok, reads, writes)

    def dma(self, eng, reads=(), writes=(), _fname="dma_start", **kw):
        j = self.ndma % self.NDMASEM
        self.ndma += 1
        s = "d_%d" % j
        self.semnames.add(s)
        uses = self.dma_uses.get(s, 0)
        waits = self._need(eng, reads, writes)
        if uses > 0 and self.known[eng].get(s, 0) < 16 * uses:
            self.known[eng][s] = 16 * uses
            waits.append((s, 16 * uses))
        self.dma_uses[s] = uses + 1
        tok = (s, 16 * (uses + 1), "dma")
        self.ops[eng].append((waits, _fname, kw, (s, 16)))
        self._commit(tok, reads, writes)

    def cc(self, reads=(), writes=(), **kw):
        waits = self._need("pool", reads, writes)
        self.ncc = getattr(self, "ncc", 0) + 1
        s = "ccsem%d" % self.ncc
        self.semnames.add(s)
        tok = (s, 1, "cc")
        self.ops["pool"].append((waits, "collective_compute", kw, (s, 1)))
        self._commit(tok, reads, writes)

    def barrier(self):
        latest = []
        for e in self.ENG:
            if self.cnt[e] > 0:
                latest.append(("c_%s%d" % (e, self.semidx[e]), self.cnt[e]))
        for s, uses in self.dma_uses.items():
            latest.append((s, 16 * uses))

        for e in self.ENG:
            kn = self.known[e]
            waits = []
            for (s, v) in latest:
                if kn.get(s, 0) < v:
                    kn[s] = v
                    waits.append((s, v))
            if waits:
                self.ops[e].append((waits, None, None, None))

    def wait_all(self, eng, bufs):
        waits = self._need(eng, bufs, ())
        self.ops[eng].append((waits, None, None, None))

    def run(self):
        nc = self.nc
        with contextlib.ExitStack() as st:
            sems = {n: st.enter_context(nc.semaphore(n)) for n in sorted(self.semnames)}
            block = st.enter_context(nc.Block())

            def replay(name):
                def f(e):
                    for waits, fname, kw, inc in self.ops[name]:
                        for (s, v) in waits:
                            e.wait_ge(sems[s], v)
                        if fname is not None:
                            kw = {k_: (v_(e) if callable(v_) else v_) for k_, v_ in kw.items()}
                            try:
                                ins = getattr(e, fname)(**kw)
                            except Exception:
                                print("FAILED OP", name, fname, {k_: str(v_)[:300] for k_, v_ in kw.items()})
                                raise
                            ins.then_inc(sems[inc[0]], inc[1])
                return f
            block.tensor(replay("pe"))
            block.scalar(replay("act"))
            block.vector(replay("dve"))
            block.gpsimd(replay("pool"))
            block.sync(replay("sp"))


class Builder:
    def __init__(self):
        self.nc = bass.Bass("TRN2", target_bir_lowering=False)
        self.P = Prog(self.nc)
        self.st = contextlib.ExitStack()
        self.outs = []
        self.nbank = 0
        self.rings = {}

    def din(self, name, shape, dt=F32):
        return self.nc.dram_tensor(name, list(shape), dt, kind="ExternalInput").ap()

    def dout(self, name, shape, dt=F32):
        self.outs.append(name)
        return self.nc.dram_tensor(name, list(shape), dt, kind="ExternalOutput").ap()

    def dscratch(self, name, shape, dt=F32):
        return self.nc.dram_tensor(name, list(shape), dt, kind="Internal").ap()

    ARENA_WORDS = 51 * 1024

    def sb(self, name, shape, dt=F32):
        if not hasattr(self, "arena"):
            self.arena = self.st.enter_context(self.nc.sbuf_tensor("arena", [128, self.ARENA_WORDS], F32))
            self.top = 0
        nel = 1
        for d_ in shape[1:]:
            nel *= d_
        esz = 4 if dt == F32 else 2
        words = (nel * esz + 3) // 4
        assert self.top + words <= self.ARENA_WORDS, "arena overflow at %s: top=%d need=%d" % (name, self.top, words)
        v = self.arena[:, self.top:self.top + words]
        self.top += words
        if dt != F32:
            v = v.bitcast(dt)
        v = v[:, 0:nel]
        if len(shape) == 3:
            v = v.rearrange("p (a b) -> p a b", b=shape[2])
        elif len(shape) == 4:
            v = v.rearrange("p (a b c) -> p a b c", b=shape[2], c=shape[3])
        if shape[0] < 128:
            v = v[0:shape[0]]
        return v

    def mark(self):
        return (self.top, set(self.rings.keys()))

    def release(self, m):
        self.P.barrier()
        self.top = m[0]
        for k in list(self.rings.keys()):
            if k not in m[1]:
                del self.rings[k]

    def init_psum(self):
        self.pp = [self.st.enter_context(self.nc.psum_tensor("psp%d" % i, [128, 1024], F32)) for i in range(4)]
        self.ps = [self.pp[i // 2][:, 512 * (i % 2):512 * (i % 2 + 1)] for i in range(8)]

    def bank(self):
        i = self.nbank % 8
        self.nbank += 1
        return i

    def ring(self, name, shape, dt, n):
        if name not in self.rings:
            self.rings[name] = [[self.sb("%s_%d" % (name, i), shape, dt) for i in range(n)], 0]
        r = self.rings[name]
        i = r[1] % n
        r[1] += 1
        return r[0][i], "%s_%d" % (name, i)

    def finish(self):
        self.P.wait_all("sp", self.outs)
        self.P.run()
        self.st.close()
        return self.nc


def make_identity(B, dt=BF16):
    P = B.P
    idf = B.sb("ident_f", [128, 128], F32)
    P.op("pool", "memset", writes=["ident_f"], ap=idf[:], constant=0.0)
    P.op("pool", "affine_select", reads=["ident_f"], writes=["ident_f"], out=idf[:], in_=idf[:],
         compare_op=ALU.not_equal, fill=1.0, base=0, pattern=[[-1, 128]], channel_multiplier=1)
    idb = B.sb("ident_b", [128, 128], BF16)
    P.op("pool", "tensor_copy", reads=["ident_f"], writes=["ident_b"], out=idb[:], in_=idf[:])
    return idf, idb


def modulation(B, c_ap, ada_w, ada_b, npre, npost, names, dest):
    P = B.P
    nA, nB, nG = names
    cT = B.sb("cT_" + nA, [128, 8], F32)
    P.dma("sp", writes=["cT" + nA], out=cT[:], in_=c_ap.rearrange("(p k) -> p k", k=8))
    P.op("act", "activation", reads=["cT" + nA], writes=["cT" + nA], out=cT[:], in_=cT[:], func=AF.Silu)
    rows = B.sb("rows_" + nA, [1, 3072], F32)
    bro = B.sb("brow_" + nA, [1, 3072], F32)
    P.dma("sp", writes=["brow" + nA], out=bro[:], in_=ada_b.rearrange("(o c) -> o c", o=1))
    wv = ada_w.rearrange("(p k) c -> p k c", k=8)
    for ch in range(6):
        wt, wk = B.ring("adaw", [128, 8, 512], F32, 2)
        P.dma("sp", writes=[wk], out=wt[:], in_=wv[:, :, ch * 512:(ch + 1) * 512])
        bk = B.bank()
        for k in range(8):
            P.op("pe", "matmul", reads=[wk, "cT" + nA], writes=["ps%d" % bk], out=B.ps[bk][0:1, :], lhsT=cT[:, k:k + 1],
                 rhs=wt[:, k, :], start=(k == 0), stop=(k == 7))
        P.op("dve", "tensor_tensor", reads=["ps%d" % bk, "brow" + nA], writes=["rows" + nA],
             out=rows[:, ch * 512:(ch + 1) * 512], in0=B.ps[bk][0:1, :], in1=bro[:, ch * 512:(ch + 1) * 512], op=ALU.add)
    gp = B.sb("gp_" + nA, [1, 2048], F32)
    P.dma("sp", writes=["gp" + nA], out=gp[:, 0:1024], in_=npre.rearrange("(o c) -> o c", o=1))
    P.dma("sp", writes=["gp" + nA], out=gp[:, 1024:2048], in_=npost.rearrange("(o c) -> o c", o=1))
    P.op("dve", "scalar_tensor_tensor", reads=["rows" + nA, "gp" + nA], writes=["rows" + nA], out=rows[:, 1024:2048],
         in0=rows[:, 1024:2048], scalar=1.0, in1=gp[:, 0:1024], op0=ALU.add, op1=ALU.mult)
    P.op("dve", "tensor_tensor", reads=["rows" + nA, "gp" + nA], writes=["rows" + nA], out=rows[:, 2048:3072],
         in0=rows[:, 2048:3072], in1=gp[:, 1024:2048], op=ALU.mult)
    ones = B.sb("ones_" + nA, [1, 128], F32)
    P.op("dve", "memset", writes=["ones" + nA], ap=ones[:], constant=1.0)
    tiles = {}
    for nm, off in ((nB, 0), (nA, 1024), (nG, 2048)):
        if nm is None:
            continue
        t = dest[nm]
        for hh in range(2):
            bk = B.bank()
            P.op("pe", "matmul", reads=["ones" + nA, "rows" + nA], writes=["ps%d" % bk], out=B.ps[bk][:, :], lhsT=ones[:, :],
                 rhs=rows[:, off + hh * 512: off + (hh + 1) * 512], start=True, stop=True)
            P.op("act", "activation", reads=["ps%d" % bk], writes=["bc" + nm], out=t[:, hh * 512:(hh + 1) * 512],
                 in_=B.ps[bk][:, :], func=AF.Identity)
        tiles[nm] = t
    return tiles


def modulation2(B, c_ap, cctx_ap, ada_w, ada_b, npre, npost, dest):
    P = B.P
    cT = B.sb("cT2", [128, 8, 2], F32)
    P.dma("sp", writes=["cT2"], out=cT[:, :, 0:1], in_=c_ap.rearrange("(p k o) -> p k o", k=8, o=1), allow_slow_non_contiguous=True)
    P.dma("sp", writes=["cT2"], out=cT[:, :, 1:2], in_=cctx_ap.rearrange("(p k o) -> p k o", k=8, o=1), allow_slow_non_contiguous=True)
    P.op("act", "activation", reads=["cT2"], writes=["cT2"], out=cT[:], in_=cT[:], func=AF.Silu)
    rows = B.sb("rows2", [2, 3072], F32)
    bro = B.sb("brow2", [2, 3072], F32)
    gp = B.sb("gp2", [2, 2048], F32)
    for r in range(2):
        P.dma("sp", writes=["brow2"], out=bro[r:r + 1, :], in_=ada_b.rearrange("(o c) -> o c", o=1))
        P.dma("sp", writes=["gp2"], out=gp[r:r + 1, 0:1024], in_=npre.rearrange("(o c) -> o c", o=1))
        P.dma("sp", writes=["gp2"], out=gp[r:r + 1, 1024:2048], in_=npost.rearrange("(o c) -> o c", o=1))
    wv = ada_w.rearrange("(p k) c -> p k c", k=8)
    for ch in range(6):
        wt, wk = B.ring("adaw", [128, 8, 512], F32, 2)
        P.dma("sp", writes=[wk], out=wt[:], in_=wv[:, :, ch * 512:(ch + 1) * 512])
        bk = B.bank()
        for k in range(8):
            P.op("pe", "matmul", reads=[wk, "cT2"], writes=["ps%d" % bk], out=B.ps[bk][0:2, :], lhsT=cT[:, k, :],
                 rhs=wt[:, k, :], start=(k == 0), stop=(k == 7))
        P.op("dve", "tensor_tensor", reads=["ps%d" % bk, "brow2"], writes=["rows2"],
             out=rows[:, ch * 512:(ch + 1) * 512], in0=B.ps[bk][0:2, :], in1=bro[:, ch * 512:(ch + 1) * 512], op=ALU.add)
    P.op("dve", "scalar_tensor_tensor", reads=["rows2", "gp2"], writes=["rows2"], out=rows[:, 1024:2048],
         in0=rows[:, 1024:2048], scalar=1.0, in1=gp[:, 0:1024], op0=ALU.add, op1=ALU.mult)
    P.op("dve", "tensor_tensor", reads=["rows2", "gp2"], writes=["rows2"], out=rows[:, 2048:3072],
         in0=rows[:, 2048:3072], in1=gp[:, 1024:2048], op=ALU.mult)
    sel = B.sb("sel2", [2, 2, 128], F32)
    P.op("dve", "memset", writes=["sel2"], ap=sel[:, 0, :], constant=0.0)
    P.op("dve", "memset", writes=["sel2"], ap=sel[0:1, 0, :], constant=1.0)
    P.op("dve", "tensor_scalar", reads=["sel2"], writes=["sel2"], out=sel[:, 1, :], in0=sel[:, 0, :], scalar1=-1.0, scalar2=1.0,
         op0=ALU.mult, op1=ALU.add)
    for r, names in ((0, ("Bm", "A", "G")), (1, ("Bc", "Ac", "Gc"))):
        for nm, off in zip(names, (0, 1024, 2048)):
            t = dest[nm]
            for hh in range(2):
                bk = B.bank()
                P.op("pe", "matmul", reads=["sel2", "rows2"], writes=["ps%d" % bk], out=B.ps[bk][:, :], lhsT=sel[:, r, :],
                     rhs=rows[:, off + hh * 512: off + (hh + 1) * 512], start=True, stop=True)
                P.op("act", "activation", reads=["ps%d" % bk], writes=["bc" + nm], out=t[:, hh * 512:(hh + 1) * 512],
                     in_=B.ps[bk][:, :], func=AF.Identity)


def compute_h_tile(B, x_rows_aps, n, Abc, Bbc, keyA, keyB, hT, hkey, col0, idb, mask=None, keep_x=None, xkeys=()):
    P = B.P
    if keep_x is None:
        xt, xk = B.ring("xt", [128, 1024], F32, 3)
    else:
        xt, xk = keep_x
    for ap, r0, r in x_rows_aps:
        P.dma("sp", reads=list(xkeys), writes=[xk], out=xt[r0:r0 + r, :], in_=ap)
    junk, jk = B.ring("hjunk", [128, 1024], BF16, 2)
    st_, sk = B.ring("hstat", [128, 4], F32, 4)
    P.op("act", "activation", reads=[xk], writes=[jk, sk], out=junk[0:n, :], in_=xt[0:n, :], func=AF.Square,
         accum_out=st_[0:n, 0:1])
    P.op("dve", "tensor_scalar", reads=[sk], writes=[sk], out=st_[0:n, 1:2], in0=st_[0:n, 0:1], scalar1=1.0 / D,
         scalar2=EPS, op0=ALU.mult, op1=ALU.add)
    P.op("act", "activation", reads=[sk], writes=[sk], out=st_[0:n, 2:3], in_=st_[0:n, 1:2], func=AF.Sqrt)
    P.op("dve", "reciprocal", reads=[sk], writes=[sk], out=st_[0:n, 3:4], in_=st_[0:n, 2:3])
    hm, hk = B.ring("hm", [128, 1024], F32, 2)
    P.op("dve", "scalar_tensor_tensor", reads=[xk, sk, keyA], writes=[hk], out=hm[0:n, :], in0=xt[0:n, :],
         scalar=st_[0:n, 3:4], in1=Abc[0:n, :], op0=ALU.mult, op1=ALU.mult)
    hb, hbk = B.ring("hb", [128, 1024], BF16, 2)
    P.op("pool", "tensor_tensor", reads=[hk, keyB], writes=[hbk], out=hb[0:n, :], in0=hm[0:n, :], in1=Bbc[0:n, :], op=ALU.add)
    if mask is not None:
        mt, mk = mask
        P.op("pool", "tensor_scalar", reads=[hbk, mk], writes=[hbk], out=hb[0:n, :], in0=hb[0:n, :], scalar1=mt[0:n, 0:1],
             scalar2=None, op0=ALU.mult)
    bk = B.bank()
    psb = B.ps[bk][:, :].bitcast(BF16)
    for k in range(8):
        P.op("pe", "transpose", reads=[hbk, "ident_b"], writes=["ps%d" % bk], out=psb[:, k * 128:k * 128 + n],
             in_=hb[0:n, k * 128:(k + 1) * 128], identity=idb[0:n, 0:n])
    P.op("act", "activation", reads=["ps%d" % bk], writes=[hkey], out=hT[:, :, col0:col0 + n],
         in_=psb.rearrange("p (k t) -> p k t", t=128)[:, :, 0:n], func=AF.Identity)
    return st_, sk


def load_cast(B, dst, dstk, src, shape3):
    a, b = shape3
    st, sk = B.ring("wstage", [128, 1024], F32, getattr(B, "nstage", 2))
    sv = st[:, 0:a * b].rearrange("p (a b) -> p a b", b=b)
    B.P.dma("sp", writes=[sk], out=sv, in_=src)
    eng = getattr(B, "cast_eng", "pool")
    if eng == "act":
        B.P.op("act", "activation", reads=[sk], writes=[dstk], out=dst, in_=sv, func=AF.Identity)
    else:
        B.P.op(eng, "tensor_copy", reads=[sk], writes=[dstk], out=dst, in_=sv)


def wblock(B, w_ap, col0, ncols, ringname="wblk", nbuf=2, width=512):
    wt, wk = B.ring(ringname, [128, 8, width], BF16, nbuf)
    wv = w_ap.rearrange("(k p) c -> p k c", p=128)
    for c in range(0, ncols, 128):
        load_cast(B, wt[:, :, c:c + 128], wk, wv[:, :, col0 + c:col0 + c + 128], (8, 128))
    return wt, wk


def proj_fm(B, wt, wk, c0, hT, hkey, t0, nt, bk, M=128):
    for k in range(8):
        B.P.op("pe", "matmul", reads=[wk, hkey], writes=["ps%d" % bk], out=B.ps[bk][0:M, 0:nt], lhsT=wt[:, k, c0:c0 + M],
               rhs=hT[:, k, t0:t0 + nt], start=(k == 0), stop=(k == 7))


def proj_tm(B, wt, wk, c0, ncols, hT, hkey, t0, n, bk):
    for k in range(8):
        B.P.op("pe", "matmul", reads=[wk, hkey], writes=["ps%d" % bk], out=B.ps[bk][0:n, 0:ncols], lhsT=hT[:, k, t0:t0 + n],
               rhs=wt[:, k, c0:c0 + ncols], start=(k == 0), stop=(k == 7))


def load_shortconv(B, hy_w, hy_b):
    P = B.P
    t = B.sb("scw", [128, 12, 4], F32)
    for j in range(3):
        P.dma("sp", writes=["scw"], out=t[:, :, j:j + 1], in_=hy_w[j].rearrange("(t p o) -> p t o", p=128, o=1),
              allow_slow_non_contiguous=True)
    P.dma("sp", writes=["scw"], out=t[:, :, 3:4], in_=hy_b.rearrange("(t p o) -> p t o", p=128, o=1),
          allow_slow_non_contiguous=True)
    return t


def hy_conv_tile(B, w_in, ct, hT, hkey, segs, scw, outt, outk, eng="dve"):
    P = B.P
    wt, wk = wblock(B, w_in, COL_HY + ct * 128, 128, "wblk128", 3, 128)
    for (tc0, n, oc0) in segs:
        zr, zk = B.ring("zrow", [128, 2050], F32, 1)
        pos = tc0 - 1
        end = tc0 + n + 1
        while pos < end:
            m = min(512, end - pos)
            bk = B.bank()
            proj_fm(B, wt, wk, 0, hT, hkey, pos, m, bk)
            P.op("act", "activation", reads=["ps%d" % bk], writes=[zk], out=zr[:, pos - (tc0 - 1): pos - (tc0 - 1) + m],
                 in_=B.ps[bk][:, 0:m], func=AF.Identity)
            pos += m
        tmp, tk = B.ring("cvtmp", [128, 2048], F32, 1)
        P.op(eng, "tensor_scalar", reads=[zk, "scw"], writes=[tk], out=tmp[:, 0:n], in0=zr[:, 0:n], scalar1=scw[:, ct, 0:1],
             scalar2=scw[:, ct, 3:4], op0=ALU.mult, op1=ALU.add)
        P.op("dve", "scalar_tensor_tensor", reads=[zk, "scw", tk], writes=[tk], out=tmp[:, 0:n], in0=zr[:, 1:n + 1],
             scalar=scw[:, ct, 1:2], in1=tmp[:, 0:n], op0=ALU.mult, op1=ALU.add)
        P.op("dve", "scalar_tensor_tensor", reads=[zk, "scw", tk], writes=[outk], out=outt[:, oc0:oc0 + n], in0=zr[:, 2:n + 2],
             scalar=scw[:, ct, 2:3], in1=tmp[:, 0:n], op0=ALU.mult, op1=ALU.add)


def build_s1():
    B = Builder()
    P = B.P
    xs = B.din("xs", [TOK + 2, D])
    cvec = B.din("c", [D])
    ada_w = B.din("ada_w", [D, 3 * D])
    ada_b = B.din("ada_b", [3 * D])
    npre = B.din("npre", [D])
    npost = B.din("npost", [D])
    w_in = B.din("w_in", [D, COL_END])
    wk_perm = B.din("wk_perm", [D, 512])
    hy_w = B.din("hy_w", [3, 1536])
    hy_b = B.din("hy_b", [1536])
    cosT = B.din("cosT", [128, TOK])
    sinT = B.din("sinT", [128, TOK])
    hmask = B.din("hmask", [2, 1])
    o_kt = B.dout("o_kt", [4, 128, TOK], BF16)
    o_v = B.dout("o_v", [TOK, 512], BF16)
    o_u = B.dout("o_u", [512, TOK], F32)
    B.init_psum()
    idf, idb = make_identity(B)
    Abc = B.sb("bc_A", [128, 1024], F32)
    Bbc = B.sb("bc_Bm", [128, 1024], F32)
    hT = B.sb("hT", [128, 8, TOK + 2], BF16)
    mt = B.sb("hmask", [2, 1], F32)
    m0 = B.mark()
    modulation(B, cvec, ada_w, ada_b, npre, npost, ("A", "Bm", None), {"A": Abc, "Bm": Bbc})
    B.release(m0)
    P.dma("sp", writes=["hmask"], out=mt[:], in_=hmask[:, :])
    for i in range(NT):
        compute_h_tile(B, [(xs[1 + 128 * i: 1 + 128 * (i + 1), :], 0, 128)], 128, Abc, Bbc, "bcA", "bcBm", hT, "hT", 1 + 128 * i, idb)
    hh = B.sb("hTh", [128, 8, 2], BF16)
    compute_h_tile(B, [(xs[0:1, :], 0, 1), (xs[TOK + 1:TOK + 2, :], 1, 1)], 2, Abc, Bbc, "bcA", "bcBm", hh, "hTh", 0, idb,
                   mask=(mt, "hmask"))
    P.op("pool", "tensor_copy", reads=["hTh"], writes=["hT"], out=hT[:, :, 0:1], in_=hh[:, :, 0:1])
    P.op("pool", "tensor_copy", reads=["hTh"], writes=["hT"], out=hT[:, :, TOK + 1:TOK + 2], in_=hh[:, :, 1:2])
    B.release(m0)
    cs = B.sb("cosT", [128, TOK], F32)
    sn = B.sb("sinT", [128, TOK], F32)
    P.dma("sp", writes=["cosT"], out=cs[:], in_=cosT[:, :])
    P.dma("sp", writes=["sinT"], out=sn[:], in_=sinT[:, :])
    wt, wk = wblock(B, w_in, COL_K, 512, "wblkK")
    wp, wpk = wblock(B, wk_perm, 0, 512, "wblkK")
    for h in range(4):
        ko, kk = B.ring("kout", [128, TOK], BF16, 2)
        for j in range(TOK // 512):
            b1 = B.bank()
            proj_fm(B, wt, wk, 128 * h, hT, "hT", 1 + 512 * j, 512, b1)
            b2 = B.bank()
            proj_fm(B, wp, wpk, 128 * h, hT, "hT", 1 + 512 * j, 512, b2)
            t1, t1k = B.ring("rtmp1", [128, 512], F32, 2)
            t2, t2k = B.ring("rtmp2", [128, 512], F32, 2)
            P.op("dve", "tensor_tensor", reads=["ps%d" % b1, "cosT"], writes=[t1k], out=t1[:], in0=B.ps[b1][:, :],
                 in1=cs[:, 512 * j:512 * (j + 1)], op=ALU.mult)
            P.op("dve", "tensor_tensor", reads=["ps%d" % b2, "sinT"], writes=[t2k], out=t2[:], in0=B.ps[b2][:, :],
                 in1=sn[:, 512 * j:512 * (j + 1)], op=ALU.mult)
            P.op("pool", "tensor_tensor", reads=[t1k, t2k], writes=[kk], out=ko[:, 512 * j:512 * (j + 1)], in0=t1[:], in1=t2[:],
                 op=ALU.add)
        P.dma("sp", reads=[kk], writes=["o_kt"], out=o_kt[h], in_=ko[:])
    B.release(m0)
    wt, wk = wblock(B, w_in, COL_V, 512, "wblkK")
    for i in range(NT):
        bk = B.bank()
        proj_tm(B, wt, wk, 0, 512, hT, "hT", 1 + 128 * i, 128, bk)
        vo, vk = B.ring("vout", [128, 512], BF16, 3)
        P.op("act", "activation", reads=["ps%d" % bk], writes=[vk], out=vo[:], in_=B.ps[bk][:, :], func=AF.Identity)
        P.dma("sp", reads=[vk], writes=["o_v"], out=o_v[128 * i:128 * (i + 1), :], in_=vo[:])
    B.release(m0)
    scw = load_shortconv(B, hy_w, hy_b)
    for ci in range(4):
        x1s, x1k = B.ring("x1s", [128, TOK], F32, 2)
        vs, vsk = B.ring("vs", [128, TOK], F32, 2)
        hy_conv_tile(B, w_in, 4 + ci, hT, "hT", [(1, TOK, 0)], scw, x1s, x1k, "dve")
        hy_conv_tile(B, w_in, 8 + ci, hT, "hT", [(1, TOK, 0)], scw, vs, vsk, "pool")
        P.op("dve", "tensor_tensor", reads=[x1k, vsk], writes=[x1k], out=x1s[:], in0=x1s[:], in1=vs[:], op=ALU.mult)
        P.dma("sp", reads=[x1k], writes=["o_u"], out=o_u[128 * ci:128 * (ci + 1), :], in_=x1s[:])
    return B.finish()


def rope_tables():
    rows = SEQ // 64
    row = np.repeat(np.arange(rows, dtype=np.float32), 64)
    col = np.tile(np.arange(64, dtype=np.float32), rows)
    inv = (10000.0 ** (-np.arange(16, dtype=np.float32) / 16)).astype(np.float32)
    ang_r = row[None, :] * inv[:, None]
    ang_c = col[None, :] * inv[:, None]
    cos64 = np.concatenate([np.cos(ang_r), np.cos(ang_r), np.cos(ang_c), np.cos(ang_c)], 0)
    sin64 = np.concatenate([-np.sin(ang_r), np.sin(ang_r), -np.sin(ang_c), np.sin(ang_c)], 0)
    cosT = np.concatenate([cos64, cos64], 0).astype(np.float32)
    sinT = np.concatenate([sin64, sin64], 0).astype(np.float32)
    return cosT, sinT


def perm_cols(w):
    k, c = w.shape
    return np.ascontiguousarray(w.reshape(k, c // 32, 2, 16)[:, :, ::-1, :].reshape(k, c))


_NC = {}


def get_nc(name):
    if name not in _NC:
        _NC[name] = globals()["build_" + name]()
    return _NC[name]


def run_s1(x, c, l, ada_w, ada_b, norm_pre, norm_post, w_in, hy_short_w, hy_short_b):
    nc = get_nc("s1")
    cosT, sinT = rope_tables()
    wkp = perm_cols(w_in[l][:, COL_K:COL_K + 512])
    in_maps = []
    for core in range(8):
        b, r = divmod(core, 4)
        t0 = r * TOK
        xs = np.zeros((TOK + 2, D), np.float32)
        lo, hi = max(t0 - 1, 0), min(t0 + TOK + 1, SEQ)
        xs[lo - (t0 - 1): hi - (t0 - 1)] = x[b, lo:hi]
        hmask = np.array([[0.0 if t0 == 0 else 1.0], [0.0 if t0 + TOK == SEQ else 1.0]], np.float32)
        in_maps.append({
            "xs": xs, "c": c[b], "ada_w": ada_w[l], "ada_b": ada_b[l], "npre": norm_pre[l], "npost": norm_post[l],
            "w_in": w_in[l], "wk_perm": wkp, "hy_w": hy_short_w[l], "hy_b": hy_short_b[l],
            "cosT": np.ascontiguousarray(cosT[:, t0:t0 + TOK]), "sinT": np.ascontiguousarray(sinT[:, t0:t0 + TOK]),
            "hmask": hmask,
        })
    res = run_bass_kernel_spmd(nc, in_maps, core_ids=list(range(8)))
    return res.results


NFFT = 2 * SEQ
CG = 32


def hyena_tables():
    L = SEQ
    tau = np.arange(NFFT)
    pos = np.where(tau < L, tau, NFFT - tau).astype(np.int64)
    posc = np.minimum(pos, L - 1)
    t_lin = np.linspace(0.0, 1.0, L, dtype=np.float32)
    wpos = ((2.0 * math.pi / L) * np.arange(L, dtype=np.float32)).astype(np.float32)
    bands = np.linspace(1e-4, 16 - 1, 16, dtype=np.float32)
    zfull = np.concatenate([t_lin[:, None], np.cos(bands[None, :] * wpos[:, None]), -np.sin(bands[None, :] * wpos[:, None])],
                           axis=-1).astype(np.float32)
    zpos = np.ascontiguousarray(zfull[posc].T)
    mn = math.log(1e-2) / 1.5
    mx = math.log(1e-2) / 0.3
    deltas = np.abs(np.linspace(mn, mx, 512, dtype=np.float32))
    win = np.exp(-t_lin[posc][None, :] * deltas[:, None]).astype(np.float32)
    win[:, L] = 0.0
    a = np.arange(128, dtype=np.float64)
    ang = 2.0 * math.pi * np.outer(a, a) / 128.0
    C = np.cos(ang).astype(np.float32)
    S = np.sin(ang).astype(np.float32)
    angt = 2.0 * math.pi * np.outer(a, a) / NFFT
    Tc = np.cos(angt).astype(np.float32)
    Ts = np.sin(angt).astype(np.float32)
    dft = np.concatenate([C, S, C, -S, C], axis=1).astype(np.float32)
    tw = np.concatenate([Tc, Ts], axis=1).astype(np.float32)
    return zpos, win, dft, tw


def wrap_pi(B, a, ak, n, rows):
    P = B.P
    m, mk = B.ring("wrapm", [128, 512], F32, 2)
    P.op("dve", "tensor_scalar", reads=[ak], writes=[mk], out=m[0:rows, 0:n], in0=a[0:rows, 0:n], scalar1=PI, scalar2=-2.0 * PI,
         op0=ALU.is_gt, op1=ALU.mult)
    P.op("dve", "tensor_tensor", reads=[ak, mk], writes=[ak], out=a[0:rows, 0:n], in0=a[0:rows, 0:n], in1=m[0:rows, 0:n], op=ALU.add)
    P.op("dve", "tensor_scalar", reads=[ak], writes=[mk], out=m[0:rows, 0:n], in0=a[0:rows, 0:n], scalar1=-PI, scalar2=2.0 * PI,
         op0=ALU.is_lt, op1=ALU.mult)
    P.op("dve", "tensor_tensor", reads=[ak, mk], writes=[ak], out=a[0:rows, 0:n], in0=a[0:rows, 0:n], in1=m[0:rows, 0:n], op=ALU.add)


def filter_mlp_chunk(B, fw, zt, zk, n):
    P = B.P
    w1, w2, cols = fw["w1"], fw["w2"], fw["cols"]
    bk = B.bank()
    P.op("pe", "matmul", reads=["fw", zk], writes=["ps%d" % bk], out=B.ps[bk][0:64, 0:n], lhsT=w1[0:33, :], rhs=zt[0:33, 0:n],
         start=True, stop=True)
    a1, a1k = B.ring("fa", [128, 512], F32, 3)
    P.op("dve", "tensor_scalar", reads=["ps%d" % bk, "fw"], writes=[a1k], out=a1[0:64, 0:n], in0=B.ps[bk][0:64, 0:n],
         scalar1=cols[0:64, 0:1], scalar2=cols[0:64, 1:2], op0=ALU.add, op1=ALU.mult)
    wrap_pi(B, a1, a1k, n, 64)
    P.op("act", "activation", reads=[a1k], writes=[a1k], out=a1[0:64, 0:n], in_=a1[0:64, 0:n], func=AF.Sin)
    bk = B.bank()
    P.op("pe", "matmul", reads=["fw", a1k], writes=["ps%d" % bk], out=B.ps[bk][0:64, 0:n], lhsT=w2[0:64, :], rhs=a1[0:64, 0:n],
         start=True, stop=True)
    a2, a2k = B.ring("fa", [128, 512], F32, 3)
    P.op("dve", "tensor_scalar", reads=["ps%d" % bk, "fw"], writes=[a2k], out=a2[0:64, 0:n], in0=B.ps[bk][0:64, 0:n],
         scalar1=cols[0:64, 2:3], scalar2=cols[0:64, 3:4], op0=ALU.add, op1=ALU.mult)
    wrap_pi(B, a2, a2k, n, 64)
    P.op("act", "activation", reads=[a2k], writes=[a2k], out=a2[0:64, 0:n], in_=a2[0:64, 0:n], func=AF.Sin)
    return a2, a2k


def load_filter_weights(B, hw1, hb1, hw2, hb2, hfreq):
    P = B.P
    w1 = B.sb("fw1", [33, 64], F32)
    w2 = B.sb("fw2", [64, 64], F32)
    cols = B.sb("fcols", [64, 4], F32)
    P.dma("sp", writes=["fw"], out=w1[:], in_=hw1[:, :])
    P.dma("sp", writes=["fw"], out=w2[:], in_=hw2[:, :])
    P.dma("sp", writes=["fw"], out=cols[:, 0:1], in_=hb1.rearrange("(p o) -> p o", o=1))
    P.dma("sp", writes=["fw"], out=cols[:, 1:2], in_=hfreq[0].rearrange("(p o) -> p o", o=1))
    P.dma("sp", writes=["fw"], out=cols[:, 2:3], in_=hb2.rearrange("(p o) -> p o", o=1))
    P.dma("sp", writes=["fw"], out=cols[:, 3:4], in_=hfreq[1].rearrange("(p o) -> p o", o=1))
    return {"w1": w1, "w2": w2, "cols": cols}


def fft_pair_fwd(B, src, srck, kdim, c0, dft, tw):
    P = B.P
    bk = B.bank()
    for i in range(2):
        P.op("pe", "matmul", reads=[srck, "dft"], writes=["ps%d" % bk], out=B.ps[bk][:, 256 * i:256 * (i + 1)],
             lhsT=src[0:kdim, c0 + i, :], rhs=dft[0:kdim, 256:512], start=True, stop=True)
    A = B.ps[bk][:, :].rearrange("p (c r k) -> p c r k", c=2, r=2)
    Are, Aim = A[:, :, 0, :], A[:, :, 1, :]
    Tc = tw[:, 0:128].unsqueeze(1).to_broadcast([128, 2, 128])
    Ts = tw[:, 128:256].unsqueeze(1).to_broadcast([128, 2, 128])
    tt, ttk = B.ring("fft_t", [128, 4, 2, 128], F32, 2)
    pk = "ps%d" % bk
    P.op("dve", "tensor_tensor", reads=[pk, "tw"], writes=[ttk], out=tt[:, 0], in0=Are, in1=Tc, op=ALU.mult)
    P.op("dve", "tensor_tensor", reads=[pk, "tw"], writes=[ttk], out=tt[:, 1], in0=Aim, in1=Ts, op=ALU.mult)
    P.op("dve", "tensor_tensor", reads=[pk, "tw"], writes=[ttk], out=tt[:, 2], in0=Aim, in1=Tc, op=ALU.mult)
    P.op("dve", "tensor_tensor", reads=[pk, "tw"], writes=[ttk], out=tt[:, 3], in0=Are, in1=Ts, op=ALU.mult)
    b1, b1k = B.ring("fft_b1", [128, 2, 2, 128], F32, 2)
    b2, b2k = B.ring("fft_b2", [128, 2, 2, 128], F32, 2)
    P.op("pool", "tensor_tensor", reads=[ttk], writes=[b1k], out=b1[:, :, 0, :], in0=tt[:, 0], in1=tt[:, 1], op=ALU.add)
    P.op("pool", "tensor_tensor", reads=[ttk], writes=[b1k], out=b1[:, :, 1, :], in0=tt[:, 2], in1=tt[:, 3], op=ALU.subtract)
    P.op("act", "activation", reads=[b1k], writes=[b2k], out=b2[:, :, 0, :], in_=b1[:, :, 1, :], func=AF.Identity)
    P.op("act", "activation", reads=[b1k], writes=[b2k], out=b2[:, :, 1, :], in_=b1[:, :, 0, :], func=AF.Identity, scale=-1.0)
    bx = B.bank()
    P.op("pe", "matmul", reads=[b1k, "dft"], writes=["ps%d" % bx], out=B.ps[bx][:, :], lhsT=dft[:, 0:128],
         rhs=b1[:].rearrange("p c r k -> p (c r k)"), start=True, stop=False)
    P.op("pe", "matmul", reads=[b2k, "dft"], writes=["ps%d" % bx], out=B.ps[bx][:, :], lhsT=dft[:, 128:256],
         rhs=b2[:].rearrange("p c r k -> p (c r k)"), start=False, stop=True)
    return bx


def build_s2():
    B = Builder()
    P = B.P
    u_in = B.din("u", [128, SEQ])
    hw1 = B.din("hw1", [33, 64]); hb1 = B.din("hb1", [64]); hw2 = B.din("hw2", [64, 64]); hb2 = B.din("hb2", [64])
    hw3 = B.din("hw3", [64, 256]); hfreq = B.din("hfreq", [2, 64])
    zpos = B.din("zpos", [33, NFFT]); win = B.din("win", [128, NFFT])
    dft_in = B.din("dft", [128, 640]); tw_in = B.din("tw", [128, 256])
    y_out = B.dout("y", [128, SEQ])
    ssq_out = B.dout("ssq", [128, 1])
    kscr = B.dscratch("kscr", [128, NFFT])
    B.init_psum()
    dft = B.sb("dft", [128, 640], F32)
    tw = B.sb("tw", [128, 256], F32)
    P.dma("sp", writes=["dft"], out=dft[:], in_=dft_in[:, :])
    P.dma("sp", writes=["tw"], out=tw[:], in_=tw_in[:, :])
    fw = load_filter_weights(B, hw1, hb1, hw2, hb2, hfreq)
    w3 = B.sb("fw3", [64, 256], F32)
    P.dma("sp", writes=["fw"], out=w3[:], in_=hw3[:, :])
    ssqp = B.sb("ssqp", [128, 33], F32)
    P.op("dve", "memset", writes=["ssqp"], ap=ssqp[:], constant=0.0)
    m0 = B.mark()
    for ch in range(NFFT // 512):
        zt, zk = B.ring("zt", [33, 512], F32, 2)
        P.dma("sp", writes=[zk], out=zt[:], in_=zpos[:, ch * 512:(ch + 1) * 512])
        wn, wnk = B.ring("wn", [128, 512], F32, 2)
        P.dma("sp", writes=[wnk], out=wn[:], in_=win[:, ch * 512:(ch + 1) * 512])
        h2, h2k = filter_mlp_chunk(B, fw, zt, zk, 512)
        bk = B.bank()
        half = 0 if ch * 512 < SEQ else 1
        P.op("pe", "matmul", reads=["fw", h2k], writes=["ps%d" % bk], out=B.ps[bk][:, :], lhsT=w3[0:64, 128 * half:128 * (half + 1)],
             rhs=h2[0:64, :], start=True, stop=True)
        kc, kck = B.ring("kc", [128, 512], F32, 2)
        P.op("dve", "tensor_tensor", reads=["ps%d" % bk, wnk], writes=[kck], out=kc[:], in0=B.ps[bk][:, :], in1=wn[:], op=ALU.mult)
        jk_, jkk = B.ring("kjunk", [128, 512], F32, 2)
        P.op("act", "activation", reads=[kck], writes=[jkk, "ssqp"], out=jk_[:], in_=kc[:], func=AF.Square,
             accum_out=ssqp[:, ch:ch + 1])
        P.dma("sp", reads=[kck], writes=["kscr"], out=kscr[:, ch * 512:(ch + 1) * 512], in_=kc[:])
    P.op("dve", "tensor_reduce", reads=["ssqp"], writes=["ssqp"], out=ssqp[:, 32:33], in_=ssqp[:, 0:32], axis=AX.X, op=ALU.add)
    P.dma("sp", reads=["ssqp"], writes=["ssq"], out=ssq_out[:, :], in_=ssqp[:, 32:33])
    B.release(m0)
    kview = kscr.rearrange("c (p j) -> p c j", j=128)
    uview = u_in.rearrange("c (p j) -> p c j", j=128)
    yview = y_out.rearrange("c (p j) -> p c j", j=128)
    for g in range(128 // CG):
        cs0 = g * CG
        kd, kdk = B.ring("kd", [128, CG, 128], F32, 1)
        ud, udk = B.ring("ud", [64, CG, 128], F32, 1)
        P.dma("sp", reads=["kscr"], writes=[kdk], out=kd[:], in_=kview[:, cs0:cs0 + CG, :])
        P.dma("sp", writes=[udk], out=ud[:], in_=uview[:, cs0:cs0 + CG, :])
        KF, KFk = B.ring("KF", [128, CG, 2, 128], F32, 1)
        for c0 in range(0, CG, 2):
            bx = fft_pair_fwd(B, kd, kdk, 128, c0, dft, tw)
            P.op("act", "activation", reads=["ps%d" % bx], writes=[KFk], out=KF[:, c0:c0 + 2].rearrange("p c r k -> p (c r k)"),
                 in_=B.ps[bx][:, :], func=AF.Identity)
        Bre, Brek = B.ring("Bre", [128, CG, 128], F32, 1)
        Bim, Bimk = B.ring("Bim", [128, CG, 128], F32, 1)
        for c0 in range(0, CG, 2):
            bx = fft_pair_fwd(B, ud, udk, 64, c0, dft, tw)
            X = B.ps[bx][:, :].rearrange("p (c r k) -> p c r k", c=2, r=2)
            Xre, Xim = X[:, :, 0, :], X[:, :, 1, :]
            Kre, Kim = KF[:, c0:c0 + 2, 0, :], KF[:, c0:c0 + 2, 1, :]
            tt, ttk = B.ring("fft_t", [128, 4, 2, 128], F32, 2)
            pk = "ps%d" % bx
            P.op("dve", "tensor_tensor", reads=[pk, KFk], writes=[ttk], out=tt[:, 0], in0=Xre, in1=Kre, op=ALU.mult)
            P.op("dve", "tensor_tensor", reads=[pk, KFk], writes=[ttk], out=tt[:, 1], in0=Xim, in1=Kim, op=ALU.mult)
            P.op("dve", "tensor_tensor", reads=[pk, KFk], writes=[ttk], out=tt[:, 2], in0=Xre, in1=Kim, op=ALU.mult)
            P.op("dve", "tensor_tensor", reads=[pk, KFk], writes=[ttk], out=tt[:, 3], in0=Xim, in1=Kre, op=ALU.mult)
            Y, Yk = B.ring("Y", [128, 2, 2, 128], F32, 2)
            P.op("pool", "tensor_tensor", reads=[ttk], writes=[Yk], out=Y[:, :, 0, :], in0=tt[:, 0], in1=tt[:, 1], op=ALU.subtract)
            P.op("pool", "tensor_tensor", reads=[ttk], writes=[Yk], out=Y[:, :, 1, :], in0=tt[:, 2], in1=tt[:, 3], op=ALU.add)
            bi = B.bank()
            for i in range(2):
                P.op("pe", "matmul", reads=[Yk, "dft"], writes=["ps%d" % bi], out=B.ps[bi][:, 256 * i:256 * (i + 1)],
                     lhsT=Y[:, i, 0, :], rhs=dft[:, 0:256], start=True, stop=False, skip_group_check=True)
                P.op("pe", "matmul", reads=[Yk, "dft"], writes=["ps%d" % bi], out=B.ps[bi][:, 256 * i:256 * (i + 1)],
                     lhsT=Y[:, i, 1, :], rhs=dft[:, 384:640], start=False, stop=True, skip_group_check=True)
            Bm = B.ps[bi][:, :].rearrange("p (c r k) -> p c r k", c=2, r=2)
            Bre_p, Bim_p = Bm[:, :, 0, :], Bm[:, :, 1, :]
            Tc = tw[:, 0:128].unsqueeze(1).to_broadcast([128, 2, 128])
            Ts = tw[:, 128:256].unsqueeze(1).to_broadcast([128, 2, 128])
            t2, t2k = B.ring("fft_t", [128, 4, 2, 128], F32, 2)
            pk = "ps%d" % bi
            P.op("dve", "tensor_tensor", reads=[pk, "tw"], writes=[t2k], out=t2[:, 0], in0=Bre_p, in1=Tc, op=ALU.mult)
            P.op("dve", "tensor_tensor", reads=[pk, "tw"], writes=[t2k], out=t2[:, 1], in0=Bim_p, in1=Ts, op=ALU.mult)
            P.op("dve", "tensor_tensor", reads=[pk, "tw"], writes=[t2k], out=t2[:, 2], in0=Bre_p, in1=Ts, op=ALU.mult)
            P.op("dve", "tensor_tensor", reads=[pk, "tw"], writes=[t2k], out=t2[:, 3], in0=Bim_p, in1=Tc, op=ALU.mult)
            P.op("pool", "tensor_tensor", reads=[t2k], writes=[Brek], out=Bre[:, c0:c0 + 2, :], in0=t2[:, 0], in1=t2[:, 1], op=ALU.subtract)
            P.op("pool", "tensor_tensor", reads=[t2k], writes=[Bimk], out=Bim[:, c0:c0 + 2, :], in0=t2[:, 2], in1=t2[:, 3], op=ALU.add)
        yo, yok = B.ring("yo", [64, CG, 128], F32, 1)
        for c0 in range(0, CG, 4):
            bo = B.bank()
            P.op("pe", "matmul", reads=[Brek, "dft"], writes=["ps%d" % bo], out=B.ps[bo][0:64, :], lhsT=dft[:, 0:64],
                 rhs=Bre[:, c0:c0 + 4, :].rearrange("p c j -> p (c j)"), start=True, stop=False)
            P.op("pe", "matmul", reads=[Bimk, "dft"], writes=["ps%d" % bo], out=B.ps[bo][0:64, :], lhsT=dft[:, 384:448],
                 rhs=Bim[:, c0:c0 + 4, :].rearrange("p c j -> p (c j)"), start=False, stop=True)
            P.op("act", "activation", reads=["ps%d" % bo], writes=[yok], out=yo[:, c0:c0 + 4, :].rearrange("p c j -> p (c j)"),
                 in_=B.ps[bo][0:64, :], func=AF.Identity, scale=1.0 / NFFT)
        P.dma("sp", reads=[yok], writes=["y"], out=yview[:, cs0:cs0 + CG, :], in_=yo[:])
    return B.finish()


def run_s2(u_all, l, hy_f_w1, hy_f_b1, hy_f_w2, hy_f_b2, hy_f_w3, hy_f_freq):
    nc = get_nc("s2")
    zpos, win, dft, tw = hyena_tables()
    in_maps = []
    for core in range(8):
        b, q = divmod(core, 4)
        w3 = np.concatenate([hy_f_w3[l][:, 128 * q:128 * (q + 1)], hy_f_w3[l][:, 512 + 128 * q:512 + 128 * (q + 1)]], axis=1)
        in_maps.append({
            "u": np.ascontiguousarray(u_all[b, 128 * q:128 * (q + 1)]), "hw1": hy_f_w1[l], "hb1": hy_f_b1[l], "hw2": hy_f_w2[l],
            "hb2": hy_f_b2[l], "hw3": np.ascontiguousarray(w3), "hfreq": hy_f_freq[l], "zpos": zpos,
            "win": np.ascontiguousarray(win[128 * q:128 * (q + 1)]), "dft": dft, "tw": tw,
        })
    res = run_bass_kernel_spmd(nc, in_maps, core_ids=list(range(8))).results
    y = np.stack([np.concatenate([res[4 * b + q]["y"] for q in range(4)], 0) for b in range(2)], 0)
    ssq = np.concatenate([res[q]["ssq"][:, 0] for q in range(4)], 0)
    return y, ssq


NTOK3 = TOK + CTX
HCOLS = TOK + 2 + CTX + 2


def hcol(t):
    return 1 + t if t < TOK else 3 + t


CHUNKS = [(512 * j, 512) for j in range(TOK // 512)] + [(TOK, CTX)]
TILES = [128 * i for i in range(NTOK3 // 128)]


def bcast_row(B, dram_vec, n, name):
    P = B.P
    row = B.sb("row_" + name, [1, n], F32)
    P.dma("sp", writes=["row_" + name], out=row[:], in_=dram_vec.rearrange("(o c) -> o c", o=1))
    t = B.sb("bcr_" + name, [128, n], F32)
    pos = 0
    while pos < n:
        m = min(512, n - pos)
        bk = B.bank()
        P.op("pe", "matmul", reads=["ones1", "row_" + name], writes=["ps%d" % bk], out=B.ps[bk][:, 0:m], lhsT=B.ones1[:, :],
             rhs=row[:, pos:pos + m], start=True, stop=True)
        P.op("act", "activation", reads=["ps%d" % bk], writes=["bcr_" + name], out=t[:, pos:pos + m], in_=B.ps[bk][:, 0:m],
             func=AF.Identity)
        pos += m
    return t, "bcr_" + name


def rstd_from_ssq(B, st_, sk, n, c_in, c_tmp, c_out, inv_n):
    P = B.P
    P.op("dve", "tensor_scalar", reads=[sk], writes=[sk], out=st_[0:n, c_tmp:c_tmp + 1], in0=st_[0:n, c_in:c_in + 1], scalar1=inv_n,
         scalar2=EPS, op0=ALU.mult, op1=ALU.add)
    P.op("act", "activation", reads=[sk], writes=[sk], out=st_[0:n, c_tmp:c_tmp + 1], in_=st_[0:n, c_tmp:c_tmp + 1], func=AF.Sqrt)
    P.op("dve", "reciprocal", reads=[sk], writes=[sk], out=st_[0:n, c_out:c_out + 1], in_=st_[0:n, c_tmp:c_tmp + 1])


def transpose_to_fm(B, src, srck, dst, dstk, t0, idb):
    P = B.P
    bk = B.bank()
    psb = B.ps[bk][:, :].bitcast(BF16)
    for k in range(4):
        P.op("pe", "transpose", reads=[srck, "ident_b"], writes=["ps%d" % bk], out=psb[:, k * 128:(k + 1) * 128],
             in_=src[:, k * 128:(k + 1) * 128], identity=idb[:, :])
    P.op("act", "activation", reads=["ps%d" % bk], writes=[dstk], out=dst[:, :, t0:t0 + 128],
         in_=psb[:, 0:512].rearrange("p (k t) -> p k t", t=128), func=AF.Identity)


def build_s3():
    B = Builder()
    P = B.P
    xs = B.din("xs", [TOK + 2, D]); xc = B.din("xc", [CTX, D])
    cvec = B.din("c", [D]); cctx = B.din("c_ctx", [D])
    ada_w = B.din("ada_w", [D, 3 * D]); ada_b = B.din("ada_b", [3 * D])
    npre = B.din("npre", [D]); npost = B.din("npost", [D])
    w_in = B.din("w_in", [D, COL_END]); wq_perm = B.din("wq_perm", [D, 512])
    hmask = B.din("hmask", [2, 1])
    cosT = B.din("cosT", [128, TOK]); sinT = B.din("sinT", [128, TOK])
    kt_all = B.din("kt_all", [4, 128, SEQ], BF16); v_all = B.din("v_all", [SEQ, 512], BF16)
    u_own = B.din("u_own", [512, TOK]); y_own = B.din("y_own", [512, TOK])
    hyc = B.din("hyc", [128, 4, 2])
    hy_w = B.din("hy_w", [3, 1536]); hy_b = B.din("hy_b", [1536])
    da_lam = B.din("da_lam", [256]); lam_init = B.din("lam_init", [1]); da_subln = B.din("da_subln", [128])
    hw1 = B.din("hw1", [33, 64]); hb1 = B.din("hb1", [64]); hw2 = B.din("hw2", [64, 64]); hb2 = B.din("hb2", [64])
    hw3 = B.din("hw3", [64, 1024]); hfreq = B.din("hfreq", [2, 64])
    zposc = B.din("zposc", [33, 512]); winc = B.din("winc", [512, 512])
    gm_g = B.din("gm_g", [512]); gm_b = B.din("gm_b", [512]); gm_ws = B.din("gm_ws", [8, 128, 128]); gm_bs = B.din("gm_bs", [8, 128])
    w_br = B.din("w_br", [3, 512, D]); w_out = B.din("w_out", [D, D])
    x_new = B.dout("x_new", [TOK, D]); xc_new = B.dout("xc_new", [CTX, D])
    B.init_psum()
    idf, idb = make_identity(B)
    B.ones1 = B.sb("ones1", [1, 128], F32)
    P.op("dve", "memset", writes=["ones1"], ap=B.ones1[:], constant=1.0)
    hT = B.sb("hT", [128, 8, HCOLS], BF16)
    Gbc = B.sb("bc_G", [128, 1024], F32)
    Gcbc = B.sb("bc_Gc", [128, 1024], F32)
    gaT = B.sb("gaT", [128, 4, NTOK3], BF16)
    gbT = B.sb("gbT", [128, 4, NTOK3], BF16)
    gcT = B.sb("gcT", [128, 4, NTOK3], BF16)
    mt = B.sb("hmask", [2, 1], F32)
    P.dma("sp", writes=["hmask"], out=mt[:], in_=hmask[:, :])
    m0 = B.mark()
    Abc = B.sb("bc_A", [128, 1024], F32); Bbc = B.sb("bc_Bm", [128, 1024], F32)
    Acbc = B.sb("bc_Ac", [128, 1024], F32); Bcbc = B.sb("bc_Bc", [128, 1024], F32)
    m1 = B.mark()
    modulation(B, cvec, ada_w, ada_b, npre, npost, ("A", "Bm", "G"), {"A": Abc, "Bm": Bbc, "G": Gbc})
    B.release(m1)
    modulation(B, cctx, ada_w, ada_b, npre, npost, ("Ac", "Bc", "Gc"), {"Ac": Acbc, "Bc": Bcbc, "Gc": Gcbc})
    B.release(m1)
    for i in range(NT):
        compute_h_tile(B, [(xs[1 + 128 * i: 1 + 128 * (i + 1), :], 0, 128)], 128, Abc, Bbc, "bcA", "bcBm", hT, "hT", 1 + 128 * i, idb)
    hh = B.sb("hTh", [128, 8, 2], BF16)
    compute_h_tile(B, [(xs[0:1, :], 0, 1), (xs[TOK + 1:TOK + 2, :], 1, 1)], 2, Abc, Bbc, "bcA", "bcBm", hh, "hTh", 0, idb,
                   mask=(mt, "hmask"))
    P.op("pool", "tensor_copy", reads=["hTh"], writes=["hT"], out=hT[:, :, 0:1], in_=hh[:, :, 0:1])
    P.op("pool", "tensor_copy", reads=["hTh"], writes=["hT"], out=hT[:, :, TOK + 1:TOK + 2], in_=hh[:, :, 1:2])
    P.op("pool", "memset", writes=["hT"], ap=hT[:, :, TOK + 2:TOK + 3], constant=0.0)
    P.op("pool", "memset", writes=["hT"], ap=hT[:, :, HCOLS - 1:HCOLS], constant=0.0)
    for i in range(2):
        compute_h_tile(B, [(xc[128 * i:128 * (i + 1), :], 0, 128)], 128, Acbc, Bcbc, "bcAc", "bcBc", hT, "hT", TOK + 3 + 128 * i, idb)
    B.release(m0)

    m0 = B.mark()
    scw = load_shortconv(B, hy_w, hy_b)
    hyct = B.sb("hyct", [128, 4, 4], F32)
    P.dma("sp", writes=["hyct"], out=hyct[:, :, 0:2], in_=hyc[:, :, :])
    fw = load_filter_weights(B, hw1, hb1, hw2, hb2, hfreq)
    w3 = B.sb("fw3", [64, 1024], F32)
    P.dma("sp", writes=["fw"], out=w3[:], in_=hw3[:, :])
    zt = B.sb("ztc", [33, 512], F32)
    P.dma("sp", writes=["ztc"], out=zt[:], in_=zposc[:, :])
    hid2, hid2k = filter_mlp_chunk(B, fw, zt, "ztc", 512)
    hid2p = B.sb("hid2p", [64, 512], F32)
    P.op("pool", "tensor_copy", reads=[hid2k], writes=["hid2p"], out=hid2p[:], in_=hid2[0:64, :])
    segs_all = [(1, TOK, 0), (TOK + 3, CTX, TOK)]
    segs_ctx = [(TOK + 3, CTX, 0)]
    for ci in range(4):
        P.op("act", "activation", reads=["hyct"], writes=["hyct"], out=hyct[:, ci, 2:3], in_=hyct[:, ci, 0:1], func=AF.Sqrt)
        P.op("dve", "reciprocal", reads=["hyct"], writes=["hyct"], out=hyct[:, ci, 3:4], in_=hyct[:, ci, 2:3])
        x0s, x0k = B.ring("x0s", [128, NTOK3], F32, 1)
        hy_conv_tile(B, w_in, ci, hT, "hT", segs_all if do_ctx else segs_all[:1], scw, x0s, x0k)
        yb, ybk = B.ring("yb", [128, NTOK3], F32, 1)
        ut, utk = B.ring("ut", [128, TOK], F32, 1)
        P.dma("sp", writes=[ybk], out=yb[:, 0:TOK], in_=y_own[128 * ci:128 * (ci + 1), :])
        P.dma("sp", writes=[utk], out=ut[:], in_=u_own[128 * ci:128 * (ci + 1), :])
        P.op("dve", "tensor_scalar", reads=[ybk, "hyct"], writes=[ybk], out=yb[:, 0:TOK], in0=yb[:, 0:TOK], scalar1=hyct[:, ci, 3:4],
             scalar2=None, op0=ALU.mult)
        P.op("dve", "scalar_tensor_tensor", reads=[utk, "hyct", ybk], writes=[ybk], out=yb[:, 0:TOK], in0=ut[:], scalar=hyct[:, ci, 1:2],
             in1=yb[:, 0:TOK], op0=ALU.mult, op1=ALU.add)
        x1c, x1ck = B.ring("x1c", [128, CTX], F32, 1)
        vc_, vck = B.ring("vcc", [128, CTX], F32, 1)
        hy_conv_tile(B, w_in, 4 + ci, hT, "hT", segs_ctx, scw, x1c, x1ck)
        hy_conv_tile(B, w_in, 8 + ci, hT, "hT", segs_ctx, scw, vc_, vck)
        P.op("dve", "tensor_tensor", reads=[x1ck, vck], writes=[x1ck], out=x1c[:], in0=x1c[:], in1=vc_[:], op=ALU.mult)
        kc, kck = B.ring("kcf", [128, 512], F32, 1)
        wnc, wnck = B.ring("wnc", [128, 512], F32, 1)
        P.dma("sp", writes=[wnck], out=wnc[:], in_=winc[128 * ci:128 * (ci + 1), :])
        bk = B.bank()
        P.op("pe", "matmul", reads=["fw", "hid2p"], writes=["ps%d" % bk], out=B.ps[bk][:, 0:255], lhsT=w3[0:64, 512 + 128 * ci:512 + 128 * (ci + 1)],
             rhs=hid2p[0:64, 0:255], start=True, stop=True)
        P.op("pe", "matmul", reads=["fw", "hid2p"], writes=["ps%d" % bk], out=B.ps[bk][:, 255:512], lhsT=w3[0:64, 128 * ci:128 * (ci + 1)],
             rhs=hid2p[0:64, 255:512], start=True, stop=True)
        P.op("dve", "tensor_tensor", reads=["ps%d" % bk, wnck], writes=[kck], out=kc[:], in0=B.ps[bk][:, :], in1=wnc[:], op=ALU.mult)
        cst, cstk = B.ring("cst", [128, 4], F32, 2)
        jk_, jkk = B.ring("kjunk", [128, 512], F32, 1)
        P.op("act", "activation", reads=[kck], writes=[jkk, cstk], out=jk_[:], in_=kc[:], func=AF.Square, accum_out=cst[:, 0:1])
        P.op("act", "activation", reads=[cstk], writes=[cstk], out=cst[:, 1:2], in_=cst[:, 0:1], func=AF.Sqrt)
        P.op("dve", "reciprocal", reads=[cstk], writes=[cstk], out=cst[:, 2:3], in_=cst[:, 1:2])
        acc, acck = B.ring("cacc", [128, 2, CTX], F32, 1)
        P.op("pool", "memset", writes=[acck + "0"], ap=acc[:, 0, :], constant=0.0)
        P.op("pool", "memset", writes=[acck + "1"], ap=acc[:, 1, :], constant=0.0)
        for s in range(CTX):
            a = s % 2
            P.op("dve", "scalar_tensor_tensor", reads=[kck, x1ck, acck + str(a)], writes=[acck + str(a)], out=acc[:, a, :],
                 in0=kc[:, 255 - s:511 - s], scalar=x1c[:, s:s + 1], in1=acc[:, a, :], op0=ALU.mult, op1=ALU.add)
        P.op("dve", "tensor_tensor", reads=[acck + "0", acck + "1"], writes=[acck + "0"], out=acc[:, 0, :], in0=acc[:, 0, :], in1=acc[:, 1, :],
             op=ALU.add)
        P.op("dve", "tensor_scalar", reads=[acck + "0", cstk], writes=[ybk], out=yb[:, TOK:NTOK3], in0=acc[:, 0, :], scalar1=cst[:, 2:3],
             scalar2=None, op0=ALU.mult)
        P.op("dve", "scalar_tensor_tensor", reads=[x1ck, "hyct", ybk], writes=[ybk], out=yb[:, TOK:NTOK3], in0=x1c[:], scalar=hyct[:, ci, 1:2],
             in1=yb[:, TOK:NTOK3], op0=ALU.mult, op1=ALU.add)
        P.op("pool", "tensor_tensor", reads=[x0k, ybk], writes=[ybk], out=yb[:], in0=yb[:], in1=x0s[:], op=ALU.mult)
        wt, wk = wblock(B, w_in, COL_GB + 128 * ci, 128, "wblk128", 3, 128)
        for (t0, n) in CHUNKS:
            bk = B.bank()
            proj_fm(B, wt, wk, 0, hT, "hT", hcol(t0), n, bk)
            sg, sgk = B.ring("sgb", [128, 512], F32, 2)
            P.op("act", "activation", reads=["ps%d" % bk], writes=[sgk], out=sg[:, 0:n], in_=B.ps[bk][:, 0:n], func=AF.Silu)
            P.op("dve", "tensor_tensor", reads=[sgk, ybk], writes=["gbT"], out=gbT[:, ci, t0:t0 + n], in0=sg[:, 0:n], in1=yb[:, t0:t0 + n],
                 op=ALU.mult)
    B.release(m0)

    m0 = B.mark()
    lng, lngk = bcast_row(B, gm_g, 512, "lng")
    lnb, lnbk = bcast_row(B, gm_b, 512, "lnb")
    wsf = B.sb("wsf", [128, 8, 128], F32)
    P.dma("sp", writes=["wsf"], out=wsf[:], in_=gm_ws.rearrange("g p q -> p g q"))
    wsT = B.sb("wsT", [128, 8, 128], BF16)
    for g in range(8):
        bk = B.bank()
        P.op("pe", "transpose", reads=["wsf", "ident_f"], writes=["ps%d" % bk], out=B.ps[bk][:, 0:128], in_=wsf[:, g, :], identity=idf[:, :])
        P.op("act", "activation", reads=["ps%d" % bk], writes=["wsT"], out=wsT[:, g, :], in_=B.ps[bk][:, 0:128], func=AF.Identity)
    bsT = B.sb("bsT", [128, 8], F32)
    P.dma("sp", writes=["bsT"], out=bsT[:], in_=gm_bs.rearrange("g p -> p g"), allow_slow_non_contiguous=True)
    wu, wuk = wblock(B, w_in, COL_GM, 512, "wgm_u", 1)
    wv, wvk = wblock(B, w_in, COL_GM + 512, 512, "wgm_v", 1)
    wc, wck = wblock(B, w_in, COL_GC, 512, "wgm_c", 1)
    for t0 in TILES:
        ba, bb, bc_ = B.bank(), B.bank(), B.bank()
        proj_tm(B, wu, wuk, 0, 512, hT, "hT", hcol(t0), 128, ba)
        proj_tm(B, wv, wvk, 0, 512, hT, "hT", hcol(t0), 128, bb)
        proj_tm(B, wc, wck, 0, 512, hT, "hT", hcol(t0), 128, bc_)
        ug, ugk = B.ring("ug", [128, 512], F32, 2)
        vg, vgk = B.ring("vg", [128, 512], F32, 2)
        gs, gsk = B.ring("gs", [128, 512], F32, 2)
        P.op("act", "activation", reads=["ps%d" % ba], writes=[ugk], out=ug[:], in_=B.ps[ba][:, :], func=AF.Gelu)
        P.op("act", "activation", reads=["ps%d" % bb], writes=[vgk], out=vg[:], in_=B.ps[bb][:, :], func=AF.Gelu)
        P.op("act", "activation", reads=["ps%d" % bc_], writes=[gsk], out=gs[:], in_=B.ps[bc_][:, :], func=AF.Silu)
        s6, s6k = B.ring("s6", [128, 6], F32, 2)
        mv, mvk = B.ring("mv", [128, 4], F32, 2)
        P.op("dve", "bn_stats", reads=[vgk], writes=[s6k], out=s6[:], in_=vg[:])
        P.op("dve", "bn_aggr", reads=[s6k], writes=[mvk], out=mv[:, 0:2], in_=s6[:])
        P.op("dve", "tensor_scalar", reads=[mvk], writes=[mvk], out=mv[:, 2:3], in0=mv[:, 1:2], scalar1=EPS, scalar2=None, op0=ALU.add)
        P.op("act", "activation", reads=[mvk], writes=[mvk], out=mv[:, 2:3], in_=mv[:, 2:3], func=AF.Sqrt)
        P.op("dve", "reciprocal", reads=[mvk], writes=[mvk], out=mv[:, 3:4], in_=mv[:, 2:3])
        P.op("dve", "tensor_scalar", reads=[vgk, mvk], writes=[vgk], out=vg[:], in0=vg[:], scalar1=mv[:, 0:1], scalar2=mv[:, 3:4],
             op0=ALU.subtract, op1=ALU.mult)
        P.op("pool", "tensor_tensor", reads=[vgk, lngk], writes=[vgk], out=vg[:], in0=vg[:], in1=lng[:], op=ALU.mult)
        vnb, vnbk = B.ring("vnb", [128, 512], BF16, 2)
        P.op("pool", "tensor_tensor", reads=[vgk, lnbk], writes=[vnbk], out=vnb[:], in0=vg[:], in1=lnb[:], op=ALU.add)
        bm = B.bank()
        for g in range(8):
            P.op("pe", "matmul", reads=["wsT", vnbk], writes=["ps%d" % bm], out=B.ps[bm][:, 64 * g:64 * (g + 1)], lhsT=wsT[:, g, :],
                 rhs=vnb[:, 64 * g:64 * (g + 1)], start=(g == 0), stop=(g == 7), skip_group_check=True)
        for g in range(8):
            P.op("dve", "scalar_tensor_tensor", reads=["ps%d" % bm, "bsT", ugk], writes=[ugk], out=ug[:, 64 * g:64 * (g + 1)],
                 in0=B.ps[bm][:, 64 * g:64 * (g + 1)], scalar=bsT[:, g:g + 1], in1=ug[:, 64 * g:64 * (g + 1)], op0=ALU.add, op1=ALU.mult)
        yc, yck = B.ring("ycb", [128, 512], BF16, 2)
        P.op("pool", "tensor_tensor", reads=[ugk, gsk], writes=[yck], out=yc[:], in0=ug[:], in1=gs[:], op=ALU.mult)
        transpose_to_fm(B, yc, yck, gcT, "gcT", t0, idb)
    B.release(m0)
    build_s3_attn(B, locals())
    build_s3_merge(B, locals())
    return B.finish()


def build_s3_attn(B, L):
    P = B.P
    hT, idb, gaT = L["hT"], L["idb"], L["gaT"]
    w_in, wq_perm, cosT, sinT, kt_all, v_all = L["w_in"], L["wq_perm"], L["cosT"], L["sinT"], L["kt_all"], L["v_all"]
    m0 = B.mark()
    lamb, lambk = bcast_row(B, L["da_lam"], 256, "lam")
    lib, libk = bcast_row(B, L["lam_init"], 1, "li")
    gsub, gsubk = bcast_row(B, L["da_subln"], 128, "gsub")
    lt = B.sb("lamtmp", [128, 136], F32)
    P.op("dve", "tensor_tensor", reads=[lambk], writes=["lamtmp"], out=lt[:, 0:64], in0=lamb[:, 0:64], in1=lamb[:, 64:128], op=ALU.mult)
    P.op("dve", "tensor_tensor", reads=[lambk], writes=["lamtmp"], out=lt[:, 64:128], in0=lamb[:, 128:192], in1=lamb[:, 192:256], op=ALU.mult)
    P.op("dve", "tensor_reduce", reads=["lamtmp"], writes=["lamtmp"], out=lt[:, 128:129], in_=lt[:, 0:64], axis=AX.X, op=ALU.add)
    P.op("dve", "tensor_reduce", reads=["lamtmp"], writes=["lamtmp"], out=lt[:, 129:130], in_=lt[:, 64:128], axis=AX.X, op=ALU.add)
    P.op("act", "activation", reads=["lamtmp"], writes=["lamtmp"], out=lt[:, 130:132], in_=lt[:, 128:130], func=AF.Exp)
    P.op("dve", "tensor_tensor", reads=["lamtmp"], writes=["lamtmp"], out=lt[:, 132:133], in0=lt[:, 131:132], in1=lt[:, 130:131], op=ALU.subtract)
    P.op("dve", "tensor_tensor", reads=["lamtmp", libk], writes=["lamtmp"], out=lt[:, 133:134], in0=lt[:, 132:133], in1=lib[:, 0:1], op=ALU.subtract)
    neglam = lt[:, 133:134]
    P.op("dve", "tensor_scalar", reads=[libk], writes=["lamtmp"], out=lt[:, 134:135], in0=lib[:, 0:1], scalar1=-1.0, scalar2=1.0,
         op0=ALU.mult, op1=ALU.add)
    P.op("dve", "tensor_scalar", reads=[gsubk, "lamtmp"], writes=[gsubk], out=gsub[:], in0=gsub[:], scalar1=lt[:, 134:135], scalar2=None,
         op0=ALU.mult)
    QT = B.sb("QT", [128, 4, NTOK3], BF16)
    kcT = B.sb("kcT", [128, 4, CTX], BF16)
    vcx = B.sb("vcx", [128, 2, 4, 129], BF16)
    ya = B.sb("ya_tm", [128, NTOK3 // 128, 512], BF16)
    P.op("pool", "memset", writes=["vcx"], ap=vcx[:], constant=1.0)
    m1 = B.mark()
    cs = B.sb("cosT", [128, TOK], F32)
    sn = B.sb("sinT", [128, TOK], F32)
    P.dma("sp", writes=["cosT"], out=cs[:], in_=cosT[:, :])
    P.dma("sp", writes=["sinT"], out=sn[:], in_=sinT[:, :])
    wt, wk = wblock(B, w_in, COL_Q, 512, "wq")
    wp, wpk = wblock(B, wq_perm, 0, 512, "wq")
    for h in range(4):
        for j in range(TOK // 512):
            b1 = B.bank()
            proj_fm(B, wt, wk, 128 * h, hT, "hT", 1 + 512 * j, 512, b1)
            b2 = B.bank()
            proj_fm(B, wp, wpk, 128 * h, hT, "hT", 1 + 512 * j, 512, b2)
            t1, t1k = B.ring("rtmp1", [128, 512], F32, 2)
            t2, t2k = B.ring("rtmp2", [128, 512], F32, 2)
            P.op("dve", "tensor_tensor", reads=["ps%d" % b1, "cosT"], writes=[t1k], out=t1[:], in0=B.ps[b1][:, :],
                 in1=cs[:, 512 * j:512 * (j + 1)], op=ALU.mult)
            P.op("dve", "tensor_tensor", reads=["ps%d" % b2, "sinT"], writes=[t2k], out=t2[:], in0=B.ps[b2][:, :],
                 in1=sn[:, 512 * j:512 * (j + 1)], op=ALU.mult)
            P.op("pool", "tensor_tensor", reads=[t1k, t2k], writes=["QT"], out=QT[:, h, 512 * j:512 * (j + 1)], in0=t1[:], in1=t2[:], op=ALU.add)
        b1 = B.bank()
        proj_fm(B, wt, wk, 128 * h, hT, "hT", hcol(TOK), CTX, b1)
        P.op("act", "activation", reads=["ps%d" % b1], writes=["QT"], out=QT[:, h, TOK:NTOK3], in_=B.ps[b1][:, 0:CTX], func=AF.Identity)
    wt, wk = wblock(B, w_in, COL_K, 512, "wq")
    for h in range(4):
        b1 = B.bank()
        proj_fm(B, wt, wk, 128 * h, hT, "hT", hcol(TOK), CTX, b1)
        P.op("act", "activation", reads=["ps%d" % b1], writes=["kcT"], out=kcT[:, h, :], in_=B.ps[b1][:, 0:CTX], func=AF.Identity)
    wt, wk = wblock(B, w_in, COL_V, 512, "wq")
    for i in range(2):
        b1 = B.bank()
        proj_tm(B, wt, wk, 0, 512, hT, "hT", hcol(TOK + 128 * i), 128, b1)
        P.op("act", "activation", reads=["ps%d" % b1], writes=["vcx"], out=vcx[:, i, :, 0:128],
             in_=B.ps[b1][:, :].rearrange("p (h d) -> p h d", d=128), func=AF.Identity)
    B.release(m1)
    m2 = B.mark()
    kth = B.sb("kth", [128, SEQ], BF16)
    vh = B.sb("vh", [128, SEQ // 128, 129], BF16)
    P.op("pool", "memset", writes=["vh"], ap=vh[:], constant=1.0)
    vview = v_all.rearrange("(t p) c -> p t c", p=128) if v_all is not None else None
    scnt = 0
    for h in range(4):
        P.dma("sp", reads=L.get("kt_keys", []), writes=["kth"], out=L.get("kth_out", lambda k: k[:])(kth), in_=kt_all[h])
        if vview is not None:
            P.dma("sp", reads=L.get("v_keys", []), writes=["vh"], out=vh[:, :, 0:128], in_=vview[:, :, 128 * h:128 * (h + 1)])
        else:
            for k in range(2):
                for r in range(4):
                    P.dma("sp", reads=L.get("v_keys", []), writes=["vh"], out=vh[:, 16 * r + 8 * k:16 * r + 8 * k + 8, 0:128],
                          in_=L["v_gk"][k].rearrange("(r t p) c -> r p t c", r=4, p=128)[r][:, :, 128 * h:128 * (h + 1)])
        for (t0, n) in CHUNKS:
            nq = n // 128
            latent = t0 < TOK
            keys = ([("l", k) for k in range(SEQ // 128)] if latent else []) + [("c", 0), ("c", 1)]
            om, omk = B.ring("om", [128, 2, 4, 128], F32, 2)
            for m in range(2):
                for ki, (kind, kt) in enumerate(keys):
                    bS = 4 + (scnt % 4)
                    scnt += 1
                    if kind == "l":
                        lk, lkk = kth[64 * m:64 * (m + 1), 128 * kt:128 * (kt + 1)], "kth"
                        vt, vtk = vh[:, kt, :], "vh"
                    else:
                        lk, lkk = kcT[64 * m:64 * (m + 1), h, 128 * kt:128 * (kt + 1)], "kcT"
                        vt, vtk = vcx[:, kt, h, :], "vcx"
                    P.op("pe", "matmul", reads=[lkk, "QT"], writes=["ps%d" % bS], out=B.ps[bS][:, 0:n], lhsT=lk,
                         rhs=QT[64 * m:64 * (m + 1), h, t0:t0 + n], start=True, stop=True)
                    pt, ptk = B.ring("pt", [128, 512], BF16, 3)
                    P.op("act", "activation", reads=["ps%d" % bS], writes=[ptk], out=pt[:, 0:n], in_=B.ps[bS][:, 0:n], func=AF.Exp,
                         scale=0.125)
                    for qt in range(nq):
                        P.op("pe", "matmul", reads=[ptk, vtk], writes=["ps%d" % qt], out=B.ps[qt][:, 0:129],
                             lhsT=pt[:, 128 * qt:128 * (qt + 1)], rhs=vt, start=(ki == 0), stop=(ki == len(keys) - 1))
                for qt in range(nq):
                    rc, rck = B.ring("rc", [128, 2], F32, 4)
                    P.op("dve", "reciprocal", reads=["ps%d" % qt], writes=[rck], out=rc[:, 0:1], in_=B.ps[qt][:, 128:129])
                    if m == 1:
                        P.op("dve", "tensor_tensor", reads=[rck, "lamtmp"], writes=[rck], out=rc[:, 0:1], in0=rc[:, 0:1], in1=neglam, op=ALU.mult)
                    P.op("dve", "tensor_scalar", reads=["ps%d" % qt, rck], writes=[omk], out=om[:, m, qt, :], in0=B.ps[qt][:, 0:128],
                         scalar1=rc[:, 0:1], scalar2=None, op0=ALU.mult)
            P.op("pool", "tensor_tensor", reads=[omk], writes=[omk], out=om[:, 0, 0:nq, :], in0=om[:, 0, 0:nq, :], in1=om[:, 1, 0:nq, :], op=ALU.add)
            for qt in range(nq):
                st_, sk = B.ring("ast", [128, 4], F32, 4)
                jk_, jkk = B.ring("ajunk", [128, 128], F32, 2)
                P.op("act", "activation", reads=[omk], writes=[jkk, sk], out=jk_[:], in_=om[:, 0, qt, :], func=AF.Square, accum_out=st_[:, 0:1])
                rstd_from_ssq(B, st_, sk, 128, 0, 1, 2, 1.0 / 128)
                P.op("dve", "scalar_tensor_tensor", reads=[omk, sk, gsubk], writes=["ya_tm"], out=ya[:, t0 // 128 + qt, 128 * h:128 * (h + 1)],
                     in0=om[:, 0, qt, :], scalar=st_[:, 2:3], in1=gsub[:], op0=ALU.mult, op1=ALU.mult)
    B.release(m2)
    wt, wk = wblock(B, w_in, COL_GA, 512, "wq")
    for t0 in TILES:
        bk = B.bank()
        proj_tm(B, wt, wk, 0, 512, hT, "hT", hcol(t0), 128, bk)
        sg, sgk = B.ring("sga", [128, 512], F32, 2)
        P.op("act", "activation", reads=["ps%d" % bk], writes=[sgk], out=sg[:], in_=B.ps[bk][:, :], func=AF.Silu)
        yb_, ybk_ = B.ring("yab", [128, 512], BF16, 2)
        P.op("dve", "tensor_tensor", reads=[sgk, "ya_tm"], writes=[ybk_], out=yb_[:], in0=sg[:], in1=ya[:, t0 // 128, :], op=ALU.mult)
        transpose_to_fm(B, yb_, ybk_, gaT, "gaT", t0, idb)
    B.release(m0)


def build_s3_merge(B, L):
    P = B.P
    hT, w_in, w_br, w_out = L["hT"], L["w_in"], L["w_br"], L["w_out"]
    gT = [(L["gaT"], "gaT"), (L["gbT"], "gbT"), (L["gcT"], "gcT")]
    xs, xc, x_new, xc_new = L["xs"], L["xc"], L["x_new"], L["xc_new"]
    m0 = B.mark()
    B.nstage = 4
    wb = []
    for i in range(3):
        t = B.sb("wbr%d" % i, [128, 4, D], BF16)
        for j in range(4):
            load_cast(B, t[:, :, 256 * j:256 * (j + 1)], "wbr%d" % i, w_br[i].rearrange("(ct p) d -> p ct d", p=128)[:, :, 256 * j:256 * (j + 1)], (4, 256))
        wb.append(t)
    wo = B.sb("wo", [128, 8, D], BF16)
    for j in range(8):
        load_cast(B, wo[:, :, 128 * j:128 * (j + 1)], "wo", w_out.rearrange("(k p) d -> p k d", p=128)[:, :, 128 * j:128 * (j + 1)], (8, 128))
    for (t0, n) in L.get("chunks", CHUNKS):
        latent = t0 < TOK
        G, Gk = (L["Gbc"], "bcG") if latent else (L["Gcbc"], "bcGc")
        ob, obk = B.ring("outTb", [128, 8, 512], BF16, 1)
        for dt in range(8):
            acc, acck = B.ring("macc", [128, 512], F32, 2)
            for i in range(3):
                B.cast_eng = "act" if (dt * 3 + i) % 2 == 0 else "pool"
                wt, wk = wblock(B, w_in, COL_MG + 1024 * i + 128 * dt, 128, "wmg", 4, 128)
                B.cast_eng = "pool"
                bA = B.bank()
                proj_fm(B, wt, wk, 0, hT, "hT", hcol(t0), n, bA)
                bB = B.bank()
                for ct in range(4):
                    P.op("pe", "matmul", reads=["wbr%d" % i, gT[i][1]], writes=["ps%d" % bB], out=B.ps[bB][:, 0:n],
                         lhsT=wb[i][:, ct, 128 * dt:128 * (dt + 1)], rhs=gT[i][0][:, ct, t0:t0 + n], start=(ct == 0), stop=(ct == 3))
                sg, sgk = B.ring("msg", [128, 512], F32, 2)
                P.op("act", "activation", reads=["ps%d" % bA], writes=[sgk], out=sg[:, 0:n], in_=B.ps[bA][:, 0:n], func=AF.Sigmoid)
                if i == 0:
                    P.op("dve", "tensor_tensor", reads=[sgk, "ps%d" % bB], writes=[acck], out=acc[:, 0:n], in0=sg[:, 0:n], in1=B.ps[bB][:, 0:n], op=ALU.mult)
                else:
                    P.op("dve", "tensor_tensor", reads=[sgk, "ps%d" % bB], writes=[sgk], out=sg[:, 0:n], in0=sg[:, 0:n], in1=B.ps[bB][:, 0:n], op=ALU.mult)
                    if i == 1:
                        P.op("pool", "tensor_tensor", reads=[sgk, acck], writes=[acck], out=acc[:, 0:n], in0=acc[:, 0:n], in1=sg[:, 0:n], op=ALU.add)
                    else:
                        P.op("pool", "tensor_tensor", reads=[sgk, acck], writes=[obk], out=ob[:, dt, 0:n], in0=acc[:, 0:n], in1=sg[:, 0:n], op=ALU.add)
        for q in range(n // 128):
            tok = t0 + 128 * q
            bks = (B.bank(), B.bank())
            for half in range(2):
                for k in range(8):
                    P.op("pe", "matmul", reads=[obk, "wo"], writes=["ps%d" % bks[half]], out=B.ps[bks[half]][:, :], lhsT=ob[:, k, 128 * q:128 * (q + 1)],
                         rhs=wo[:, k, 512 * half:512 * (half + 1)], start=(k == 0), stop=(k == 7))
            st_, sk = B.ring("mst", [128, 8], F32, 4)
            for half in range(2):
                jk_, jkk = B.ring("mjunk", [128, 512], BF16, 2)
                P.op("act", "activation", reads=["ps%d" % bks[half]], writes=[jkk, sk], out=jk_[:], in_=B.ps[bks[half]][:, :], func=AF.Square,
                     accum_out=st_[:, half:half + 1])
            P.op("dve", "tensor_tensor", reads=[sk], writes=[sk], out=st_[:, 2:3], in0=st_[:, 0:1], in1=st_[:, 1:2], op=ALU.add)
            rstd_from_ssq(B, st_, sk, 128, 2, 3, 4, 1.0 / D)
            xt, xk = B.ring("mxt", [128, D], F32, 2)
            src = xs[1 + tok:1 + tok + 128, :] if latent else xc[tok - TOK:tok - TOK + 128, :]
            P.dma("sp", reads=L.get("x_keys", []), writes=[xk], out=xt[:], in_=src)
            ot, otk = B.ring("mot", [128, D], F32, 2)
            for half in range(2):
                P.op("dve", "scalar_tensor_tensor", reads=["ps%d" % bks[half], sk, Gk], writes=[otk], out=ot[:, 512 * half:512 * (half + 1)],
                     in0=B.ps[bks[half]][:, :], scalar=st_[:, 4:5], in1=G[:, 512 * half:512 * (half + 1)], op0=ALU.mult, op1=ALU.mult)
            P.op("pool", "tensor_tensor", reads=[otk, xk], writes=[otk], out=ot[:], in0=ot[:], in1=xt[:], op=ALU.add)
            if latent:
                P.dma("sp", reads=[otk], writes=[L.get("x_new_key", "x_new")], out=x_new[tok:tok + 128, :], in_=ot[:])
                if L.get("hal_src") is not None and tok == 0:
                    P.dma("sp", reads=[otk], writes=["hal_src"], out=L["hal_src"][0:1, :], in_=ot[0:1, :])
                if L.get("hal_src") is not None and tok == TOK - 128:
                    P.dma("sp", reads=[otk], writes=["hal_src"], out=L["hal_src"][1:2, :], in_=ot[127:128, :])
            elif xc_new is not None:
                P.dma("sp", reads=[otk], writes=[L.get("xc_new_key", "xc_new")], out=xc_new[tok - TOK:tok - TOK + 128, :], in_=ot[:])
    B.nstage = 2
    B.release(m0)


def ctx_tables():
    L = CTX
    i = np.arange(512)
    pos = np.minimum(np.abs(i - 255), L - 1)
    t_lin = np.linspace(0.0, 1.0, L, dtype=np.float32)
    wpos = ((2.0 * math.pi / L) * np.arange(L, dtype=np.float32)).astype(np.float32)
    bands = np.linspace(1e-4, 16 - 1, 16, dtype=np.float32)
    zfull = np.concatenate([t_lin[:, None], np.cos(bands[None, :] * wpos[:, None]), -np.sin(bands[None, :] * wpos[:, None])],
                           axis=-1).astype(np.float32)
    zposc = np.ascontiguousarray(zfull[pos].T)
    mn = math.log(1e-2) / 1.5
    mx = math.log(1e-2) / 0.3
    deltas = np.abs(np.linspace(mn, mx, 512, dtype=np.float32))
    winc = np.exp(-t_lin[pos][None, :] * deltas[:, None]).astype(np.float32)
    winc[:, 511] = 0.0
    return zposc, winc


def run_s3(x, xc, l, inp, kt_all, v_all, u_all, y_all, ssq):
    nc = get_nc("s3")
    cosT, sinT = rope_tables()
    zposc, winc = ctx_tables()
    w_in = inp["w_in"][l]
    wqp = perm_cols(w_in[:, COL_Q:COL_Q + 512])
    lam_init = np.array([0.8 - 0.6 * math.exp(-0.3 * l)], np.float32)
    hyc = np.ascontiguousarray(np.stack([ssq.reshape(4, 128).T, inp["hy_bias"][l].reshape(4, 128).T], axis=-1)).astype(np.float32)
    in_maps = []
    for core in range(8):
        b, r = divmod(core, 4)
        t0 = r * TOK
        xs = np.zeros((TOK + 2, D), np.float32)
        lo, hi = max(t0 - 1, 0), min(t0 + TOK + 1, SEQ)
        xs[lo - (t0 - 1): hi - (t0 - 1)] = x[b, lo:hi]
        hmask = np.array([[0.0 if t0 == 0 else 1.0], [0.0 if t0 + TOK == SEQ else 1.0]], np.float32)
        in_maps.append({
            "xs": xs, "xc": np.ascontiguousarray(xc[b]), "c": inp["c"][b], "c_ctx": inp["c_ctx"], "ada_w": inp["ada_w"][l], "ada_b": inp["ada_b"][l],
            "npre": inp["norm_pre"][l], "npost": inp["norm_post"][l], "w_in": w_in, "wq_perm": wqp, "hmask": hmask,
            "cosT": np.ascontiguousarray(cosT[:, t0:t0 + TOK]), "sinT": np.ascontiguousarray(sinT[:, t0:t0 + TOK]),
            "kt_all": kt_all[b], "v_all": v_all[b], "u_own": np.ascontiguousarray(u_all[b][:, t0:t0 + TOK]),
            "y_own": np.ascontiguousarray(y_all[b][:, t0:t0 + TOK]), "hyc": hyc, "hy_w": inp["hy_short_w"][l], "hy_b": inp["hy_short_b"][l],
            "da_lam": np.ascontiguousarray(inp["da_lambda"][l].reshape(256)), "lam_init": lam_init, "da_subln": inp["da_subln"][l],
            "hw1": inp["hy_f_w1"][l], "hb1": inp["hy_f_b1"][l], "hw2": inp["hy_f_w2"][l], "hb2": inp["hy_f_b2"][l],
            "hw3": inp["hy_f_w3"][l], "hfreq": inp["hy_f_freq"][l], "zposc": zposc, "winc": winc,
            "gm_g": inp["gm_ln_g"][l], "gm_b": inp["gm_ln_b"][l], "gm_ws": inp["gm_ws"][l], "gm_bs": inp["gm_bs"][l],
            "w_br": inp["w_branch"][l], "w_out": inp["w_out"][l],
        })
    res = run_bass_kernel_spmd(nc, in_maps, core_ids=list(range(8))).results
    x_new = np.stack([np.concatenate([res[4 * b + r]["x_new"] for r in range(4)], 0) for b in range(2)], 0)
    xc_new = np.stack([res[4 * b]["xc_new"] for b in range(2)], 0)
    return x_new, xc_new


def gather_s1(res):
    kt = np.stack([np.concatenate([np.asarray(res[4 * b + r]["o_kt"]) for r in range(4)], axis=2) for b in range(2)], 0)
    v = np.stack([np.concatenate([np.asarray(res[4 * b + r]["o_v"]) for r in range(4)], axis=0) for b in range(2)], 0)
    u = np.stack([np.concatenate([np.asarray(res[4 * b + r]["o_u"]) for r in range(4)], axis=1) for b in range(2)], 0)
    return np.ascontiguousarray(kt), np.ascontiguousarray(v), np.ascontiguousarray(u)


def kernel(**inputs):
    inp = {k: np.asarray(v) for k, v in inputs.items()}
    x = inp["x"].astype(np.float32, copy=False)
    xc = inp["ctx"].astype(np.float32, copy=False)
    for l in range(DEPTH):
        res1 = run_s1(x, inp["c"], l, inp["ada_w"], inp["ada_b"], inp["norm_pre"], inp["norm_post"], inp["w_in"], inp["hy_short_w"],
                      inp["hy_short_b"])
        kt_all, v_all, u_all = gather_s1(res1)
        y_all, ssq = run_s2(u_all, l, inp["hy_f_w1"], inp["hy_f_b1"], inp["hy_f_w2"], inp["hy_f_b2"], inp["hy_f_w3"], inp["hy_f_freq"])
        x, xc = run_s3(x, xc, l, inp, kt_all, v_all, u_all, y_all, ssq)
    return x.astype(np.float32)


RG4 = [[0, 1, 2, 3], [4, 5, 6, 7]]
FUSED_DEPTH = DEPTH
EXTRA_CC = 0
CGF = 16


_RANK_CACHE = {}


def rank_of(e):
    k = id(e)
    if k not in _RANK_CACHE:
        _RANK_CACHE[k] = (e, e.partition_id() % 4)
    return _RANK_CACHE[k][1]


def rank_nb(e, d):
    k = (id(e), d)
    if k not in _RANK_CACHE:
        _RANK_CACHE[k] = (e, (rank_of(e) + d) % 4)
    return _RANK_CACHE[k][1]


def f_products(B, E, l):
    P = B.P
    hT, w_in = E["hT"], E["w_in"][l]
    m0 = B.mark()
    cs = B.sb("cosT", [128, TOK], F32)
    sn = B.sb("sinT", [128, TOK], F32)
    P.dma("sp", writes=["cosT"], out=cs[:], in_=E["cosT"][:, :])
    P.dma("sp", writes=["sinT"], out=sn[:], in_=E["sinT"][:, :])
    wt, wk = wblock(B, w_in, COL_K, 512, "wblkK", 1)
    wp, wpk = wblock(B, E["wk_perm"][l], 0, 512, "wblkP", 1)
    for h in range(4):
        ko, kk = B.ring("kout", [128, TOK], BF16, 2)
        for j in range(TOK // 512):
            b1 = B.bank()
            proj_fm(B, wt, wk, 128 * h, hT, "hT", 1 + 512 * j, 512, b1)
            b2 = B.bank()
            proj_fm(B, wp, wpk, 128 * h, hT, "hT", 1 + 512 * j, 512, b2)
            t1, t1k = B.ring("rtmp1", [128, 512], F32, 2)
            t2, t2k = B.ring("rtmp2", [128, 512], F32, 2)
            P.op("dve", "tensor_tensor", reads=["ps%d" % b1, "cosT"], writes=[t1k], out=t1[:], in0=B.ps[b1][:, :],
                 in1=cs[:, 512 * j:512 * (j + 1)], op=ALU.mult)
            P.op("dve", "tensor_tensor", reads=["ps%d" % b2, "sinT"], writes=[t2k], out=t2[:], in0=B.ps[b2][:, :],
                 in1=sn[:, 512 * j:512 * (j + 1)], op=ALU.mult)
            P.op("pool", "tensor_tensor", reads=[t1k, t2k], writes=[kk], out=ko[:, 512 * j:512 * (j + 1)], in0=t1[:], in1=t2[:], op=ALU.add)
        P.dma("sp", reads=[kk], writes=["kt_src"], out=E["kt_src"][h // 2][128 * (h % 2):128 * (h % 2 + 1), :], in_=ko[:])
    B.release(m0)
    wt, wk = wblock(B, w_in, COL_V, 512, "wblkK", 1)
    for i in range(NT):
        bk = B.bank()
        proj_tm(B, wt, wk, 0, 512, hT, "hT", 1 + 128 * i, 128, bk)
        vo, vk = B.ring("vout", [128, 512], BF16, 3)
        P.op("act", "activation", reads=["ps%d" % bk], writes=[vk], out=vo[:], in_=B.ps[bk][:, :], func=AF.Identity)
        P.dma("sp", reads=[vk], writes=["v_src"], out=E["v_src"][i // 8][128 * (i % 8):128 * (i % 8 + 1), :], in_=vo[:])
    B.release(m0)
    scw = load_shortconv(B, E["hy_w"][l], E["hy_b"][l])
    for ci in range(4):
        x1s, x1k = B.ring("x1s", [128, TOK], F32, 2)
        vs, vsk = B.ring("vs", [128, TOK], F32, 2)
        hy_conv_tile(B, w_in, 4 + ci, hT, "hT", [(1, TOK, 0)], scw, x1s, x1k)
        hy_conv_tile(B, w_in, 8 + ci, hT, "hT", [(1, TOK, 0)], scw, vs, vsk)
        P.op("dve", "tensor_tensor", reads=[x1k, vsk], writes=[x1k], out=x1s[:], in0=x1s[:], in1=vs[:], op=ALU.mult)
        P.dma("sp", reads=[x1k], writes=["u_src"], out=E["u_src"][ci], in_=x1s[:])
    B.release(m0)
    par = l % 2
    for k in range(2):
        P.cc(reads=["kt_src"], writes=["kt_g%d" % par], kind="AllGather", op=ALU.bypass, replica_groups=RG4, ins=[E["kt_src"][k].opt()],
             outs=[E["kt_g"][par][k].opt()])
    for k in range(2):
        P.cc(reads=["v_src"], writes=["v_g%d" % par], kind="AllGather", op=ALU.bypass, replica_groups=RG4, ins=[E["v_src"][k].opt()],
             outs=[E["v_g"][par][k].opt()])
    for k in range(4):
        P.cc(reads=["u_src"], writes=["u_g"], kind="AllGather", op=ALU.bypass, replica_groups=RG4, ins=[E["u_src"][k].opt()],
             outs=[E["u_g"][par][k].opt()])


def f_conv(B, E, l):
    P = B.P
    par = l % 2
    dft, tw = E["dft"], E["tw"]
    kscr = E["kscr"]
    m0 = B.mark()
    fw = load_filter_weights(B, E["hw1"][l], E["hb1"][l], E["hw2"][l], E["hb2"][l], E["hfreq"][l])
    w3 = B.sb("fw3", [64, 256], F32)
    P.dma("sp", writes=["fw"], out=w3[:], in_=E["hw3q"][l])
    ssqp = B.sb("ssqp", [128, 33], F32)
    P.op("dve", "memset", writes=["ssqp"], ap=ssqp[:], constant=0.0)
    m1 = B.mark()
    for ch in range(NFFT // 512):
        zt, zk = B.ring("zt", [33, 512], F32, 2)
        P.dma("sp", writes=[zk], out=zt[:], in_=E["zpos"][:, ch * 512:(ch + 1) * 512])
        wn, wnk = B.ring("wn", [128, 512], F32, 2)
        P.dma("sp", writes=[wnk], out=wn[:], in_=E["win"][:, ch * 512:(ch + 1) * 512])
        h2, h2k = filter_mlp_chunk(B, fw, zt, zk, 512)
        bk = B.bank()
        half = 0 if ch * 512 < SEQ else 1
        P.op("pe", "matmul", reads=["fw", h2k], writes=["ps%d" % bk], out=B.ps[bk][:, :], lhsT=w3[0:64, 128 * half:128 * (half + 1)],
             rhs=h2[0:64, :], start=True, stop=True)
        kc, kck = B.ring("kc", [128, 512], F32, 2)
        P.op("dve", "tensor_tensor", reads=["ps%d" % bk, wnk], writes=[kck], out=kc[:], in0=B.ps[bk][:, :], in1=wn[:], op=ALU.mult)
        jk_, jkk = B.ring("kjunk", [128, 512], F32, 2)
        P.op("act", "activation", reads=[kck], writes=[jkk, "ssqp"], out=jk_[:], in_=kc[:], func=AF.Square, accum_out=ssqp[:, ch:ch + 1])
        P.dma("sp", reads=[kck], writes=["kscr"], out=kscr[:, ch * 512:(ch + 1) * 512], in_=kc[:])
    P.op("dve", "tensor_reduce", reads=["ssqp"], writes=["ssqp"], out=ssqp[:, 32:33], in_=ssqp[:, 0:32], axis=AX.X, op=ALU.add)
    P.dma("sp", reads=["ssqp"], writes=["ssq_src"], out=E["ssq_src"][:, 0:1], in_=ssqp[:, 32:33],
          allow_slow_non_contiguous=True)
    B.release(m1)
    kview = kscr.rearrange("c (p j) -> p c j", j=128)
    ugq = E["u_g"][par].rearrange("q (r c) t -> q r c t", r=4)
    P.dma("sp", reads=["u_g"], writes=["u_my"], out=E["u_my"], in_=(lambda e: ugq[rank_of(e)]))
    ugv = E["u_my"].rearrange("r c (pp j) -> r pp c j", j=128)
    yview = E["y_src"].rearrange("r c (pp j) -> r pp c j", j=128)
    for g in range(128 // CGF):
        cs0 = g * CGF
        kd, kdk = B.ring("kd", [128, CGF, 128], F32, 1)
        ud, udk = B.ring("ud", [64, CGF, 128], F32, 1)
        P.dma("sp", reads=["kscr"], writes=[kdk], out=kd[:], in_=kview[:, cs0:cs0 + CGF, :])
        for r in range(4):
            P.dma("sp", reads=["u_my"], writes=[udk], out=ud[16 * r:16 * (r + 1), :, :], in_=ugv[r][:, cs0:cs0 + CGF, :])
        KF, KFk = B.ring("KF", [128, CGF, 2, 128], F32, 1)
        Tc = tw[:, 0:128].unsqueeze(1).to_broadcast([128, 2, 128])
        Ts = tw[:, 128:256].unsqueeze(1).to_broadcast([128, 2, 128])

        def st1(src, srck, kdim, c0):
            bk = B.bank()
            for i in range(2):
                P.op("pe", "matmul", reads=[srck, "dft"], writes=["ps%d" % bk], out=B.ps[bk][:, 256 * i:256 * (i + 1)],
                     lhsT=src[0:kdim, c0 + i, :], rhs=dft[0:kdim, 256:512], start=True, stop=True)
            return bk

        def st2(bk):
            A = B.ps[bk][:, :].rearrange("p (c r k) -> p c r k", c=2, r=2)
            Are, Aim = A[:, :, 0, :], A[:, :, 1, :]
            tt, ttk = B.ring("fft_t", [128, 4, 2, 128], F32, 3)
            pk = "ps%d" % bk
            P.op("dve", "tensor_tensor", reads=[pk, "tw"], writes=[ttk], out=tt[:, 0], in0=Are, in1=Tc, op=ALU.mult)
            P.op("dve", "tensor_tensor", reads=[pk, "tw"], writes=[ttk], out=tt[:, 1], in0=Aim, in1=Ts, op=ALU.mult)
            P.op("dve", "tensor_tensor", reads=[pk, "tw"], writes=[ttk], out=tt[:, 2], in0=Aim, in1=Tc, op=ALU.mult)
            P.op("dve", "tensor_tensor", reads=[pk, "tw"], writes=[ttk], out=tt[:, 3], in0=Are, in1=Ts, op=ALU.mult)
            b1, b1k = B.ring("fft_b1", [128, 2, 2, 128], F32, 2)
            b2, b2k = B.ring("fft_b2", [128, 2, 2, 128], F32, 2)
            P.op("pool", "tensor_tensor", reads=[ttk], writes=[b1k], out=b1[:, :, 0, :], in0=tt[:, 0], in1=tt[:, 1], op=ALU.add)
            P.op("pool", "tensor_tensor", reads=[ttk], writes=[b1k], out=b1[:, :, 1, :], in0=tt[:, 2], in1=tt[:, 3], op=ALU.subtract)
            P.op("act", "activation", reads=[b1k], writes=[b2k], out=b2[:, :, 0, :], in_=b1[:, :, 1, :], func=AF.Identity)
            P.op("act", "activation", reads=[b1k], writes=[b2k], out=b2[:, :, 1, :], in_=b1[:, :, 0, :], func=AF.Identity, scale=-1.0)
            return (b1, b1k, b2, b2k)

        def st3(bb):
            b1, b1k, b2, b2k = bb
            bx = B.bank()
            P.op("pe", "matmul", reads=[b1k, "dft"], writes=["ps%d" % bx], out=B.ps[bx][:, :], lhsT=dft[:, 0:128],
                 rhs=b1[:].rearrange("p c r k -> p (c r k)"), start=True, stop=False)
            P.op("pe", "matmul", reads=[b2k, "dft"], writes=["ps%d" % bx], out=B.ps[bx][:, :], lhsT=dft[:, 128:256],
                 rhs=b2[:].rearrange("p c r k -> p (c r k)"), start=False, stop=True)
            return bx

        def st4f(bx, c0):
            P.op("act", "activation", reads=["ps%d" % bx], writes=[KFk], out=KF[:, c0:c0 + 2].rearrange("p c r k -> p (c r k)"),
                 in_=B.ps[bx][:, :], func=AF.Identity)

        def st4d(bx, c0):
            X = B.ps[bx][:, :].rearrange("p (c r k) -> p c r k", c=2, r=2)
            Xre, Xim = X[:, :, 0, :], X[:, :, 1, :]
            Kre, Kim = KF[:, c0:c0 + 2, 0, :], KF[:, c0:c0 + 2, 1, :]
            tt, ttk = B.ring("fft_t", [128, 4, 2, 128], F32, 3)
            pk = "ps%d" % bx
            P.op("dve", "tensor_tensor", reads=[pk, KFk], writes=[ttk], out=tt[:, 0], in0=Xre, in1=Kre, op=ALU.mult)
            P.op("dve", "tensor_tensor", reads=[pk, KFk], writes=[ttk], out=tt[:, 1], in0=Xim, in1=Kim, op=ALU.mult)
            P.op("dve", "tensor_tensor", reads=[pk, KFk], writes=[ttk], out=tt[:, 2], in0=Xre, in1=Kim, op=ALU.mult)
            P.op("dve", "tensor_tensor", reads=[pk, KFk], writes=[ttk], out=tt[:, 3], in0=Xim, in1=Kre, op=ALU.mult)
            Y, Yk = B.ring("Y", [128, 2, 2, 128], F32, 2)
            P.op("pool", "tensor_tensor", reads=[ttk], writes=[Yk], out=Y[:, :, 0, :], in0=tt[:, 0], in1=tt[:, 1], op=ALU.subtract)
            P.op("pool", "tensor_tensor", reads=[ttk], writes=[Yk], out=Y[:, :, 1, :], in0=tt[:, 2], in1=tt[:, 3], op=ALU.add)
            return (Y, Yk)

        def st5(yy):
            Y, Yk = yy
            bi = B.bank()
            for i in range(2):
                P.op("pe", "matmul", reads=[Yk, "dft"], writes=["ps%d" % bi], out=B.ps[bi][:, 256 * i:256 * (i + 1)],
                     lhsT=Y[:, i, 0, :], rhs=dft[:, 0:256], start=True, stop=False, skip_group_check=True)
                P.op("pe", "matmul", reads=[Yk, "dft"], writes=["ps%d" % bi], out=B.ps[bi][:, 256 * i:256 * (i + 1)],
                     lhsT=Y[:, i, 1, :], rhs=dft[:, 384:640], start=False, stop=True, skip_group_check=True)
            return bi

        def st6(bi, c0):
            Bm = B.ps[bi][:, :].rearrange("p (c r k) -> p c r k", c=2, r=2)
            Bre_p, Bim_p = Bm[:, :, 0, :], Bm[:, :, 1, :]
            t2, t2k = B.ring("fft_t", [128, 4, 2, 128], F32, 3)
            pk = "ps%d" % bi
            P.op("dve", "tensor_tensor", reads=[pk, "tw"], writes=[t2k], out=t2[:, 0], in0=Bre_p, in1=Tc, op=ALU.mult)
            P.op("dve", "tensor_tensor", reads=[pk, "tw"], writes=[t2k], out=t2[:, 1], in0=Bim_p, in1=Ts, op=ALU.mult)
            P.op("dve", "tensor_tensor", reads=[pk, "tw"], writes=[t2k], out=t2[:, 2], in0=Bre_p, in1=Ts, op=ALU.mult)
            P.op("dve", "tensor_tensor", reads=[pk, "tw"], writes=[t2k], out=t2[:, 3], in0=Bim_p, in1=Tc, op=ALU.mult)
            P.op("pool", "tensor_tensor", reads=[t2k], writes=[Brek], out=Bre[:, c0:c0 + 2, :], in0=t2[:, 0], in1=t2[:, 1], op=ALU.subtract)
            P.op("pool", "tensor_tensor", reads=[t2k], writes=[Bimk], out=Bim[:, c0:c0 + 2, :], in0=t2[:, 2], in1=t2[:, 3], op=ALU.add)

        pairs = list(range(0, CGF, 2))
        for q0 in range(0, len(pairs), 2):
            cs_ = pairs[q0:q0 + 2]
            bks = [st1(kd, kdk, 128, c0) for c0 in cs_]
            bbs = [st2(bk) for bk in bks]
            bxs = [st3(bb) for bb in bbs]
            for bx, c0 in zip(bxs, cs_):
                st4f(bx, c0)
        Bre, Brek = B.ring("Bre", [128, CGF, 128], F32, 1)
        Bim, Bimk = B.ring("Bim", [128, CGF, 128], F32, 1)
        for q0 in range(0, len(pairs), 2):
            cs_ = pairs[q0:q0 + 2]
            bks = [st1(ud, udk, 64, c0) for c0 in cs_]
            bbs = [st2(bk) for bk in bks]
            bxs = [st3(bb) for bb in bbs]
            yys = [st4d(bx, c0) for bx, c0 in zip(bxs, cs_)]
            bis = [st5(yy) for yy in yys]
            for bi, c0 in zip(bis, cs_):
                st6(bi, c0)
        yo, yok = B.ring("yo", [64, CGF, 128], F32, 1)
        for c0 in range(0, CGF, 4):
            bo = B.bank()
            P.op("pe", "matmul", reads=[Brek, "dft"], writes=["ps%d" % bo], out=B.ps[bo][0:64, :], lhsT=dft[:, 0:64],
                 rhs=Bre[:, c0:c0 + 4, :].rearrange("p c j -> p (c j)"), start=True, stop=False)
            P.op("pe", "matmul", reads=[Bimk, "dft"], writes=["ps%d" % bo], out=B.ps[bo][0:64, :], lhsT=dft[:, 384:448],
                 rhs=Bim[:, c0:c0 + 4, :].rearrange("p c j -> p (c j)"), start=False, stop=True)
            P.op("act", "activation", reads=["ps%d" % bo], writes=[yok], out=yo[:, c0:c0 + 4, :].rearrange("p c j -> p (c j)"),
                 in_=B.ps[bo][0:64, :], func=AF.Identity, scale=1.0 / NFFT)
        for r in range(4):
            P.dma("sp", reads=[yok], writes=["y_src"], out=yview[r][:, cs0:cs0 + CGF, :], in_=yo[16 * r:16 * (r + 1), :, :])
    B.release(m0)
    def issue_y_gather():
        for k in range(4):
            P.cc(reads=["y_src"], writes=["y_g"], kind="AllGather", op=ALU.bypass, replica_groups=RG4, ins=[E["y_src"][k].opt()],
                 outs=[E["y_g"][par][k].opt()])
        P.cc(reads=["ssq_src"], writes=["ssq_g%d" % par], kind="AllGather", op=ALU.bypass, replica_groups=RG4, ins=[E["ssq_src"].opt()],
             outs=[E["ssq_g"][par].opt()])
    return issue_y_gather


def f_hyena_gate(B, E, l, do_ctx=True):
    P = B.P
    par = l % 2
    hT, w_in, gbT = E["hT"], E["w_in"][l], E["gbT"]
    m0 = B.mark()
    scw = load_shortconv(B, E["hy_w"][l], E["hy_b"][l])
    hyct = B.sb("hyct", [128, 4, 4], F32)
    P.dma("sp", reads=["ssq_g%d" % par], writes=["hyct"], out=hyct[:, :, 0:1],
          in_=E["ssq_g"][par][:, 0:1].rearrange("(ci p) o -> p ci o", p=128), allow_slow_non_contiguous=True)
    P.dma("sp", writes=["hyct"], out=hyct[:, :, 1:2], in_=E["hyb"][l].rearrange("p (ci o) -> p ci o", o=1), allow_slow_non_contiguous=True)
    fw = load_filter_weights(B, E["hw1"][l], E["hb1"][l], E["hw2"][l], E["hb2"][l], E["hfreq"][l])
    w3 = B.sb("fw3", [64, 1024], F32)
    P.dma("sp", writes=["fw"], out=w3[:], in_=E["hw3"][l])
    zt = B.sb("ztc", [33, 512], F32)
    P.dma("sp", writes=["ztc"], out=zt[:], in_=E["zposc"][:, :])
    hid2, hid2k = filter_mlp_chunk(B, fw, zt, "ztc", 512)
    hid2p = B.sb("hid2p", [64, 512], F32)
    P.op("pool", "tensor_copy", reads=[hid2k], writes=["hid2p"], out=hid2p[:], in_=hid2[0:64, :])
    segs_all = [(1, TOK, 0), (TOK + 3, CTX, TOK)]
    segs_ctx = [(TOK + 3, CTX, 0)]
    ygr = E["y_g"][par]
    P.dma("sp", reads=["y_g"], writes=["y_my"], out=E["y_my"], in_=(lambda e: ygr[rank_of(e)]))
    if do_ctx:
        idf = E["idf"]
        uc_all = B.sb("uc_all", [128, 4, CTX], F32)
        kc_all = B.sb("kc_all", [128, 4, 512], F32)
        ycv = B.sb("ycv_all", [128, 4, CTX], F32)
        cst = B.sb("cst_all", [128, 4, 4], F32)
        mA = B.mark()
        for ci in range(4):
            x1c, x1ck = B.ring("x1c", [128, CTX], F32, 1)
            vc_, vck = B.ring("vcc", [128, CTX], F32, 1)
            hy_conv_tile(B, w_in, 4 + ci, hT, "hT", segs_ctx, scw, x1c, x1ck)
            hy_conv_tile(B, w_in, 8 + ci, hT, "hT", segs_ctx, scw, vc_, vck)
            P.op("dve", "tensor_tensor", reads=[x1ck, vck], writes=["uc_all"], out=uc_all[:, ci, :], in0=x1c[:], in1=vc_[:], op=ALU.mult)
            wnc, wnck = B.ring("wnc", [128, 512], F32, 1)
            P.dma("sp", writes=[wnck], out=wnc[:], in_=E["winc"][128 * ci:128 * (ci + 1), :])
            bk = B.bank()
            P.op("pe", "matmul", reads=["fw", "hid2p"], writes=["ps%d" % bk], out=B.ps[bk][:, 0:255], lhsT=w3[0:64, 512 + 128 * ci:512 + 128 * (ci + 1)],
                 rhs=hid2p[0:64, 0:255], start=True, stop=True)
            P.op("pe", "matmul", reads=["fw", "hid2p"], writes=["ps%d" % bk], out=B.ps[bk][:, 255:512], lhsT=w3[0:64, 128 * ci:128 * (ci + 1)],
                 rhs=hid2p[0:64, 255:512], start=True, stop=True)
            P.op("dve", "tensor_tensor", reads=["ps%d" % bk, wnck], writes=["kc_all"], out=kc_all[:, ci, :], in0=B.ps[bk][:, :], in1=wnc[:], op=ALU.mult)
            jk_, jkk = B.ring("kjunk", [128, 512], F32, 1)
            P.op("act", "activation", reads=["kc_all"], writes=[jkk, "cst_all"], out=jk_[:], in_=kc_all[:, ci, :], func=AF.Square, accum_out=cst[:, ci, 0:1])
            P.op("act", "activation", reads=["cst_all"], writes=["cst_all"], out=cst[:, ci, 1:2], in_=cst[:, ci, 0:1], func=AF.Sqrt)
            P.op("dve", "reciprocal", reads=["cst_all"], writes=["cst_all"], out=cst[:, ci, 2:3], in_=cst[:, ci, 1:2])
        B.release(mA)
        ftab = B.sb("ftab", [128, 4, 1024], F32)
        uT = B.sb("uT", [128, 2, 512], F32)
        kT = B.sb("kT", [128, 4, 512], F32)
        AB = B.sb("AB", [128, 2, 4, 512], F32)
        PP_ = B.sb("PP", [128, 2, 4, 512], F32)
        yT = uT
        for tt in range(2):
            bk = B.bank()
            for ci in range(4):
                P.op("pe", "transpose", reads=["uc_all", "ident_f"], writes=["ps%d" % bk], out=B.ps[bk][:, 128 * ci:128 * (ci + 1)],
                     in_=uc_all[:, ci, 128 * tt:128 * (tt + 1)], identity=idf[:, :])
            P.op("act", "activation", reads=["ps%d" % bk], writes=["uT"], out=uT[:, tt, :], in_=B.ps[bk][:, :], func=AF.Identity)
        for it_ in range(4):
            bk = B.bank()
            for ci in range(4):
                P.op("pe", "transpose", reads=["kc_all", "ident_f"], writes=["ps%d" % bk], out=B.ps[bk][:, 128 * ci:128 * (ci + 1)],
                     in_=kc_all[:, ci, 128 * it_:128 * (it_ + 1)], identity=idf[:, :])
            P.op("act", "activation", reads=["ps%d" % bk], writes=["kT"], out=kT[:, it_, :], in_=B.ps[bk][:, :], func=AF.Identity)
        P.dma("sp", writes=["ftab"], out=ftab[:, 0:2, :], in_=E["FU"].rearrange("(a p) k -> p a k", p=128))
        for kt in range(4):
            for j in range(2):
                bk = B.bank()
                for tt in range(2):
                    P.op("pe", "matmul", reads=["ftab", "uT"], writes=["ps%d" % bk], out=B.ps[bk][:, :], lhsT=ftab[:, tt, 512 * j + 128 * kt:512 * j + 128 * (kt + 1)],
                         rhs=uT[:, tt, :], start=(tt == 0), stop=(tt == 1))
                P.op("act", "activation", reads=["ps%d" % bk], writes=["AB"], out=AB[:, j, kt, :], in_=B.ps[bk][:, :], func=AF.Identity)
        P.dma("sp", reads=[], writes=["ftab"], out=ftab[:], in_=E["FK"].rearrange("(a p) k -> p a k", p=128))
        for kt in range(4):
            bks = []
            for j in range(2):
                bk = B.bank()
                for it_ in range(4):
                    P.op("pe", "matmul", reads=["ftab", "kT"], writes=["ps%d" % bk], out=B.ps[bk][:, :], lhsT=ftab[:, it_, 512 * j + 128 * kt:512 * j + 128 * (kt + 1)],
                         rhs=kT[:, it_, :], start=(it_ == 0), stop=(it_ == 3))
                bks.append(bk)
            kcb, ksb = bks
            tq, tqk = B.ring("ctt", [128, 512], F32, 2)
            P.op("dve", "tensor_tensor", reads=["ps%d" % kcb, "AB"], writes=["PP0"], out=PP_[:, 0, kt, :], in0=B.ps[kcb][:, :], in1=AB[:, 0, kt, :], op=ALU.mult)
            P.op("dve", "tensor_tensor", reads=["ps%d" % ksb, "AB"], writes=[tqk], out=tq[:], in0=B.ps[ksb][:, :], in1=AB[:, 1, kt, :], op=ALU.mult)
            P.op("pool", "tensor_tensor", reads=[tqk, "PP0"], writes=["PP0"], out=PP_[:, 0, kt, :], in0=PP_[:, 0, kt, :], in1=tq[:], op=ALU.subtract)
            tq2, tq2k = B.ring("ctt", [128, 512], F32, 2)
            P.op("dve", "tensor_tensor", reads=["ps%d" % ksb, "AB"], writes=["PP1"], out=PP_[:, 1, kt, :], in0=B.ps[ksb][:, :], in1=AB[:, 0, kt, :], op=ALU.mult)
            P.op("dve", "tensor_tensor", reads=["ps%d" % kcb, "AB"], writes=[tq2k], out=tq2[:], in0=B.ps[kcb][:, :], in1=AB[:, 1, kt, :], op=ALU.mult)
            P.op("pool", "tensor_tensor", reads=[tq2k, "PP1"], writes=["PP1"], out=PP_[:, 1, kt, :], in0=PP_[:, 1, kt, :], in1=tq2[:], op=ALU.add)
        P.dma("sp", writes=["ftab"], out=ftab[:, :, 0:512], in_=E["FI"].rearrange("(a p) k -> p a k", p=128))
        for tt in range(2):
            bk = B.bank()
            n_ = 0
            for j in range(2):
                for kt in range(4):
                    P.op("pe", "matmul", reads=["ftab", "PP0", "PP1"], writes=["ps%d" % bk], out=B.ps[bk][:, :], lhsT=ftab[:, kt, 256 * j + 128 * tt:256 * j + 128 * (tt + 1)],
                         rhs=PP_[:, j, kt, :], start=(n_ == 0), stop=(n_ == 7))
                    n_ += 1
            P.op("act", "activation", reads=["ps%d" % bk], writes=["uT"], out=yT[:, tt, :], in_=B.ps[bk][:, :], func=AF.Identity, scale=1.0 / 512)
        for ci in range(4):
            bk = B.bank()
            for tt in range(2):
                P.op("pe", "transpose", reads=["uT", "ident_f"], writes=["ps%d" % bk], out=B.ps[bk][:, 128 * tt:128 * (tt + 1)],
                     in_=yT[:, tt, 128 * ci:128 * (ci + 1)], identity=idf[:, :])
            P.op("act", "activation", reads=["ps%d" % bk], writes=["ycv_all"], out=ycv[:, ci, :], in_=B.ps[bk][:, 0:CTX], func=AF.Identity)
        B.release(mA)
    for ci in range(4):
        P.op("act", "activation", reads=["hyct"], writes=["hyct"], out=hyct[:, ci, 2:3], in_=hyct[:, ci, 0:1], func=AF.Sqrt)
        P.op("dve", "reciprocal", reads=["hyct"], writes=["hyct"], out=hyct[:, ci, 3:4], in_=hyct[:, ci, 2:3])
        x0s, x0k = B.ring("x0s", [128, NTOK3], F32, 1)
        hy_conv_tile(B, w_in, ci, hT, "hT", segs_all if do_ctx else segs_all[:1], scw, x0s, x0k)
        yb, ybk = B.ring("yb", [128, NTOK3], F32, 1)
        ut, utk = B.ring("ut", [128, TOK], F32, 1)
        P.dma("sp", reads=["y_my"], writes=[ybk], out=yb[:, 0:TOK], in_=E["y_my"][128 * ci:128 * (ci + 1), :])
        P.dma("sp", reads=["u_src"], writes=[utk], out=ut[:], in_=E["u_src"][ci])
        P.op("dve", "tensor_scalar", reads=[ybk, "hyct"], writes=[ybk], out=yb[:, 0:TOK], in0=yb[:, 0:TOK], scalar1=hyct[:, ci, 3:4],
             scalar2=None, op0=ALU.mult)
        P.op("dve", "scalar_tensor_tensor", reads=[utk, "hyct", ybk], writes=[ybk], out=yb[:, 0:TOK], in0=ut[:], scalar=hyct[:, ci, 1:2],
             in1=yb[:, 0:TOK], op0=ALU.mult, op1=ALU.add)
        if do_ctx:
            P.op("dve", "tensor_scalar", reads=["ycv_all", "cst_all"], writes=[ybk], out=yb[:, TOK:NTOK3], in0=ycv[:, ci, :], scalar1=cst[:, ci, 2:3],
                 scalar2=None, op0=ALU.mult)
            P.op("dve", "scalar_tensor_tensor", reads=["uc_all", "hyct", ybk], writes=[ybk], out=yb[:, TOK:NTOK3], in0=uc_all[:, ci, :], scalar=hyct[:, ci, 1:2],
                 in1=yb[:, TOK:NTOK3], op0=ALU.mult, op1=ALU.add)
        nuse = NTOK3 if do_ctx else TOK
        P.op("pool", "tensor_tensor", reads=[x0k, ybk], writes=[ybk], out=yb[:, 0:nuse], in0=yb[:, 0:nuse], in1=x0s[:, 0:nuse], op=ALU.mult)
        wt, wk = wblock(B, w_in, COL_GB + 128 * ci, 128, "wblk128", 3, 128)
        for (t0, n) in (CHUNKS if do_ctx else CHUNKS[:-1]):
            bk = B.bank()
            proj_fm(B, wt, wk, 0, hT, "hT", hcol(t0), n, bk)
            sg, sgk = B.ring("sgb", [128, 512], F32, 2)
            P.op("act", "activation", reads=["ps%d" % bk], writes=[sgk], out=sg[:, 0:n], in_=B.ps[bk][:, 0:n], func=AF.Silu)
            P.op("dve", "tensor_tensor", reads=[sgk, ybk], writes=["gbT"], out=gbT[:, ci, t0:t0 + n], in0=sg[:, 0:n], in1=yb[:, t0:t0 + n],
                 op=ALU.mult)
    B.release(m0)


def f_gmlp(B, E, l, tiles=None):
    P = B.P
    hT, w_in, gcT, idf, idb = E["hT"], E["w_in"][l], E["gcT"], E["idf"], E["idb"]
    m0 = B.mark()
    B.cast_eng = "dve"
    lng, lngk = bcast_row(B, E["gm_g"][l], 512, "lng")
    lnb, lnbk = bcast_row(B, E["gm_b"][l], 512, "lnb")
    wsf = B.sb("wsf", [128, 8, 128], F32)
    P.dma("sp", writes=["wsf"], out=wsf[:], in_=E["gm_ws"][l].rearrange("g p q -> p g q"))
    wsT = B.sb("wsT", [128, 8, 128], BF16)
    for g in range(8):
        bk = B.bank()
        P.op("pe", "transpose", reads=["wsf", "ident_f"], writes=["ps%d" % bk], out=B.ps[bk][:, 0:128], in_=wsf[:, g, :], identity=idf[:, :])
        P.op("act", "activation", reads=["ps%d" % bk], writes=["wsT"], out=wsT[:, g, :], in_=B.ps[bk][:, 0:128], func=AF.Identity)
    bsT = B.sb("bsT", [128, 8], F32)
    P.dma("sp", writes=["bsT"], out=bsT[:], in_=E["gm_bs"][l].rearrange("g p -> p g"), allow_slow_non_contiguous=True)
    wu, wuk = wblock(B, w_in, COL_GM, 512, "wgm_u", 1)
    wv, wvk = wblock(B, w_in, COL_GM + 512, 512, "wgm_v", 1)
    wc, wck = wblock(B, w_in, COL_GC, 512, "wgm_c", 1)
    def gA(t0):
        ba, bb, bc_ = B.bank(), B.bank(), B.bank()
        proj_tm(B, wu, wuk, 0, 512, hT, "hT", hcol(t0), 128, ba)
        proj_tm(B, wv, wvk, 0, 512, hT, "hT", hcol(t0), 128, bb)
        proj_tm(B, wc, wck, 0, 512, hT, "hT", hcol(t0), 128, bc_)
        return (ba, bb, bc_)

    def gB(bks):
        ba, bb, bc_ = bks
        ug, ugk = B.ring("ug", [128, 512], F32, 2)
        vg, vgk = B.ring("vg", [128, 512], F32, 2)
        gs, gsk = B.ring("gs", [128, 512], F32, 2)
        P.op("act", "activation", reads=["ps%d" % ba], writes=[ugk], out=ug[:], in_=B.ps[ba][:, :], func=AF.Gelu)
        P.op("act", "activation", reads=["ps%d" % bb], writes=[vgk], out=vg[:], in_=B.ps[bb][:, :], func=AF.Gelu)
        P.op("act", "activation", reads=["ps%d" % bc_], writes=[gsk], out=gs[:], in_=B.ps[bc_][:, :], func=AF.Silu)
        return (ug, ugk, vg, vgk, gs, gsk)

    def gC(st):
        ug, ugk, vg, vgk, gs, gsk = st
        s6, s6k = B.ring("s6", [128, 6], F32, 2)
        mv, mvk = B.ring("mv", [128, 4], F32, 2)
        P.op("dve", "bn_stats", reads=[vgk], writes=[s6k], out=s6[:], in_=vg[:])
        P.op("dve", "bn_aggr", reads=[s6k], writes=[mvk], out=mv[:, 0:2], in_=s6[:])
        P.op("dve", "tensor_scalar", reads=[mvk], writes=[mvk], out=mv[:, 2:3], in0=mv[:, 1:2], scalar1=EPS, scalar2=None, op0=ALU.add)
        P.op("act", "activation", reads=[mvk], writes=[mvk], out=mv[:, 2:3], in_=mv[:, 2:3], func=AF.Sqrt)
        P.op("dve", "reciprocal", reads=[mvk], writes=[mvk], out=mv[:, 3:4], in_=mv[:, 2:3])
        P.op("dve", "tensor_scalar", reads=[vgk, mvk], writes=[vgk], out=vg[:], in0=vg[:], scalar1=mv[:, 0:1], scalar2=mv[:, 3:4],
             op0=ALU.subtract, op1=ALU.mult)
        P.op("dve", "tensor_tensor", reads=[vgk, lngk], writes=[vgk], out=vg[:], in0=vg[:], in1=lng[:], op=ALU.mult)
        vnb, vnbk = B.ring("vnb", [128, 512], BF16, 2)
        P.op("dve", "tensor_tensor", reads=[vgk, lnbk], writes=[vnbk], out=vnb[:], in0=vg[:], in1=lnb[:], op=ALU.add)
        return (vnb, vnbk)

    def gD(vv):
        vnb, vnbk = vv
        bm = B.bank()
        for g in range(8):
            P.op("pe", "matmul", reads=["wsT", vnbk], writes=["ps%d" % bm], out=B.ps[bm][:, 64 * g:64 * (g + 1)], lhsT=wsT[:, g, :],
                 rhs=vnb[:, 64 * g:64 * (g + 1)], start=(g == 0), stop=(g == 7), skip_group_check=True)
        return bm

    def gE(bm, st, t0):
        ug, ugk, vg, vgk, gs, gsk = st
        for g in range(8):
            P.op("dve", "scalar_tensor_tensor", reads=["ps%d" % bm, "bsT", ugk], writes=[ugk], out=ug[:, 64 * g:64 * (g + 1)],
                 in0=B.ps[bm][:, 64 * g:64 * (g + 1)], scalar=bsT[:, g:g + 1], in1=ug[:, 64 * g:64 * (g + 1)], op0=ALU.add, op1=ALU.mult)
        yc, yck = B.ring("ycb", [128, 512], BF16, 2)
        P.op("dve", "tensor_tensor", reads=[ugk, gsk], writes=[yck], out=yc[:], in0=ug[:], in1=gs[:], op=ALU.mult)
        transpose_to_fm(B, yc, yck, gcT, "gcT", t0, idb)

    tiles = TILES if tiles is None else tiles
    for q0 in range(0, len(tiles), 2):
        ts_ = tiles[q0:q0 + 2]
        bk3 = [gA(t0) for t0 in ts_]
        sts = [gB(x_) for x_ in bk3]
        vvs = [gC(x_) for x_ in sts]
        bms = [gD(x_) for x_ in vvs]
        for bm, st, t0 in zip(bms, sts, ts_):
            gE(bm, st, t0)
    B.cast_eng = "pool"
    B.release(m0)


def build_fused():
    B = Builder()
    P = B.P
    E = {}
    for nm, shp in (("xs", [TOK + 2, D]), ("xc0", [CTX, D]), ("c", [D]), ("c_ctx", [D]), ("ada_w", [DEPTH, D, 3 * D]), ("ada_b", [DEPTH, 3 * D]),
                    ("npre", [DEPTH, D]), ("npost", [DEPTH, D]), ("w_in", [DEPTH, D, COL_END]), ("wk_perm", [DEPTH, D, 512]),
                    ("wq_perm", [DEPTH, D, 512]), ("hmask", [2, 1]), ("cosT", [128, TOK]), ("sinT", [128, TOK]),
                    ("hy_w", [DEPTH, 3, 1536]), ("hy_b", [DEPTH, 1536]), ("hw1", [DEPTH, 33, 64]), ("hb1", [DEPTH, 64]),
                    ("hw2", [DEPTH, 64, 64]), ("hb2", [DEPTH, 64]), ("hw3", [DEPTH, 64, 1024]), ("hw3q", [DEPTH, 64, 256]),
                    ("hfreq", [DEPTH, 2, 64]), ("zpos", [33, NFFT]), ("win", [128, NFFT]), ("dft_in", [128, 640]), ("tw_in", [128, 256]),
                    ("zposc", [33, 512]), ("winc", [512, 512]), ("FU", [256, 1024]), ("FK", [512, 1024]), ("FI", [512, 512]), ("hyb", [DEPTH, 128, 4]), ("da_lam", [DEPTH, 256]), ("lam_init", [DEPTH, 1]),
                    ("da_subln", [DEPTH, 128]), ("gm_g", [DEPTH, 512]), ("gm_b", [DEPTH, 512]), ("gm_ws", [DEPTH, 8, 128, 128]),
                    ("gm_bs", [DEPTH, 8, 128]), ("w_br", [DEPTH, 3, 512, D]), ("w_out", [DEPTH, D, D])):
        E[nm] = B.din(nm, shp)
    x_out = B.dout("x_out", [TOK, D])
    nc = B.nc
    def scr(name, shape, dt=F32):
        return nc.dram_tensor(name, list(shape), dt, kind="Internal").ap()
    E["xcur"] = [scr("xcur%d" % i, [TOK + 2, D]) for i in range(2)]
    E["xccur"] = [scr("xccur%d" % i, [CTX, D]) for i in range(2)]
    E["kt_src"] = scr("kt_src", [2, 256, TOK], BF16)
    E["v_src"] = scr("v_src", [2, TOK // 2, 512], BF16)
    E["u_src"] = scr("u_src", [4, 128, TOK])
    E["y_src"] = scr("y_src", [4, 128, TOK])
    E["ssq_src"] = scr("ssq_src", [128, 16])
    E["hal_src"] = scr("hal_src", [2, D])
    E["kt_g"] = [scr("kt_g%d" % i, [2, 4 * 256, TOK], BF16) for i in range(2)]
    E["v_g"] = [scr("v_g%d" % i, [2, 4 * (TOK // 2), 512], BF16) for i in range(2)]
    E["u_g"] = [scr("u_g0", [4, 4 * 128, TOK])] * 2
    E["y_g"] = [scr("y_g0", [4, 4 * 128, TOK])] * 2
    E["ssq_g"] = [scr("ssq_g%d" % i, [512, 16]) for i in range(2)]
    E["hal_g"] = [scr("hal_g0", [8, D])] * 2
    E["kscr"] = scr("kscr", [128, NFFT])
    E["u_my"] = scr("u_my", [4, 128, TOK])
    E["y_my"] = scr("y_my", [512, TOK])
    B.init_psum()
    idf, idb = make_identity(B)
    E["idf"], E["idb"] = idf, idb
    B.ones1 = B.sb("ones1", [1, 128], F32)
    P.op("dve", "memset", writes=["ones1"], ap=B.ones1[:], constant=1.0)
    dft = B.sb("dft", [128, 640], F32)
    tw = B.sb("tw", [128, 256], F32)
    P.dma("sp", writes=["dft"], out=dft[:], in_=E["dft_in"][:, :])
    P.dma("sp", writes=["tw"], out=tw[:], in_=E["tw_in"][:, :])
    E["dft"], E["tw"] = dft, tw
    hT = B.sb("hT", [128, 8, HCOLS], BF16)
    Gbc = B.sb("bc_G", [128, 1024], F32)
    Gcbc = B.sb("bc_Gc", [128, 1024], F32)
    gaT = B.sb("gaT", [128, 4, NTOK3], BF16)
    gbT = B.sb("gbT", [128, 4, NTOK3], BF16)
    gcT = B.sb("gcT", [128, 4, NTOK3], BF16)
    mt = B.sb("hmask", [2, 1], F32)
    P.dma("sp", writes=["hmask"], out=mt[:], in_=E["hmask"][:, :])
    E.update(hT=hT, gaT=gaT, gbT=gbT, gcT=gcT)
    P.op("pool", "memset", writes=["hT"], ap=hT[:, :, TOK + 2:TOK + 3], constant=0.0)
    P.op("pool", "memset", writes=["hT"], ap=hT[:, :, HCOLS - 1:HCOLS], constant=0.0)
    for i_ in range(EXTRA_CC):
        P.cc(reads=["hal_src"], writes=["hal_g"], kind="AllGather", op=ALU.bypass, replica_groups=RG4, ins=[E["hal_src"].opt()],
             outs=[E["hal_g"][0].opt()])
    for l in range(FUSED_DEPTH):
        last = l == FUSED_DEPTH - 1
        par = l % 2
        xsrc = E["xs"] if l == 0 else E["xcur"][par]
        xcsrc = E["xc0"] if l == 0 else E["xccur"][par]
        xkeys = [] if l == 0 else ["xcur%d" % par]
        xckeys = [] if l == 0 else ["xccur%d" % par]
        m0 = B.mark()
        Abc = B.sb("bc_A", [128, 1024], F32); Bbc = B.sb("bc_Bm", [128, 1024], F32)
        Acbc = B.sb("bc_Ac", [128, 1024], F32); Bcbc = B.sb("bc_Bc", [128, 1024], F32)
        m1 = B.mark()
        modulation2(B, E["c"], E["c_ctx"], E["ada_w"][l], E["ada_b"][l], E["npre"][l], E["npost"][l],
                    {"A": Abc, "Bm": Bbc, "G": Gbc, "Ac": Acbc, "Bc": Bcbc, "Gc": Gcbc})
        B.release(m1)
        for i in range(NT):
            compute_h_tile(B, [(xsrc[1 + 128 * i: 1 + 128 * (i + 1), :], 0, 128)], 128, Abc, Bbc, "bcA", "bcBm", hT, "hT", 1 + 128 * i, idb, xkeys=xkeys)
        hh = B.sb("hTh", [128, 8, 2], BF16)
        compute_h_tile(B, [(xsrc[0:1, :], 0, 1), (xsrc[TOK + 1:TOK + 2, :], 1, 1)], 2, Abc, Bbc, "bcA", "bcBm", hh, "hTh", 0, idb,
                       mask=(mt, "hmask"), xkeys=xkeys)
        P.op("pool", "tensor_copy", reads=["hTh"], writes=["hT"], out=hT[:, :, 0:1], in_=hh[:, :, 0:1])
        P.op("pool", "tensor_copy", reads=["hTh"], writes=["hT"], out=hT[:, :, TOK + 1:TOK + 2], in_=hh[:, :, 1:2])
        for i in range(2):
            compute_h_tile(B, [(xcsrc[128 * i:128 * (i + 1), :], 0, 128)], 128, Acbc, Bcbc, "bcAc", "bcBc", hT, "hT", TOK + 3 + 128 * i, idb, xkeys=xckeys)
        B.release(m0)
        f_products(B, E, l)
        f_gmlp(B, E, l, tiles=(TILES[:NT] if last else None))
        issue_y = f_conv(B, E, l)
        L = dict(after_q=issue_y, hT=hT, idb=idb, gaT=gaT, gbT=gbT, gcT=gcT, w_in=E["w_in"][l], wq_perm=E["wq_perm"][l], cosT=E["cosT"], sinT=E["sinT"],
                 kt_all=[E["kt_g"][par][h // 2].rearrange("(r h d) t -> h d r t", r=4, h=2)[h % 2] for h in range(4)], v_all=None,
                 v_gk=E["v_g"][par],
                 kth_out=(lambda k: k.rearrange("p (r t) -> p r t", r=4)), kt_keys=["kt_g%d" % par], v_keys=["v_g%d" % par],
                 da_lam=E["da_lam"][l], lam_init=E["lam_init"][l], da_subln=E["da_subln"][l],
                 w_br=E["w_br"][l], w_out=E["w_out"][l], xs=xsrc, xc=xcsrc, x_keys=xkeys + xckeys, Gbc=Gbc, Gcbc=Gcbc)
        if last:
            L["chunks"] = CHUNKS[:-1]
        f_attn(B, L)
        f_hyena_gate(B, E, l, do_ctx=not last)
        if last:
            L.update(x_new=x_out, x_new_key="x_out", xc_new=None, hal_src=None)
        else:
            L.update(x_new=E["xcur"][1 - par][1:TOK + 1, :], x_new_key="xcur%d" % (1 - par), xc_new=E["xccur"][1 - par],
                     xc_new_key="xccur%d" % (1 - par), hal_src=E["hal_src"])
        build_s3_merge(B, L)
        if not last:
            P.cc(reads=["hal_src"], writes=["hal_g"], kind="AllGather", op=ALU.bypass, replica_groups=RG4, ins=[E["hal_src"].opt()],
                 outs=[E["hal_g"][par].opt()])
            hg = E["hal_g"][par]
            xn = E["xcur"][1 - par]
            P.dma("sp", reads=["hal_g"], writes=["xcur%d" % (1 - par)], out=xn[0:1, :],
                  in_=(lambda e, hg=hg: hg.rearrange("(r two) d -> r two d", two=2)[rank_nb(e, 3)][1:2, :]))
            P.dma("sp", reads=["hal_g"], writes=["xcur%d" % (1 - par)], out=xn[TOK + 1:TOK + 2, :],
                  in_=(lambda e, hg=hg: hg.rearrange("(r two) d -> r two d", two=2)[rank_nb(e, 1)][0:1, :]))
    return B.finish()


def ctx_dft_tables():
    t = np.arange(256, dtype=np.float64)[:, None]
    k = np.arange(512, dtype=np.float64)[None, :]
    th = 2.0 * math.pi * t * k / 512.0
    FU = np.concatenate([np.cos(th), np.sin(th)], axis=1).astype(np.float32)
    i = np.arange(512, dtype=np.float64)[:, None]
    ph = 2.0 * math.pi * (i - 255.0) * k / 512.0
    FK = np.concatenate([np.cos(ph), np.sin(ph)], axis=1).astype(np.float32)
    kk = np.arange(512, dtype=np.float64)[:, None]
    tt = np.arange(256, dtype=np.float64)[None, :]
    ti = 2.0 * math.pi * kk * tt / 512.0
    FI = np.concatenate([np.cos(ti), np.sin(ti)], axis=1).astype(np.float32)
    return FU, FK, FI


def fused_inputs(inp):
    cosT, sinT = rope_tables()
    FU, FK, FI = ctx_dft_tables()
    zpos, win, dft, tw = hyena_tables()
    zposc, winc = ctx_tables()
    w_in = inp["w_in"]
    wkp = np.stack([perm_cols(w_in[l][:, COL_K:COL_K + 512]) for l in range(DEPTH)], 0)
    wqp = np.stack([perm_cols(w_in[l][:, COL_Q:COL_Q + 512]) for l in range(DEPTH)], 0)
    lam_init = np.array([[0.8 - 0.6 * math.exp(-0.3 * l)] for l in range(DEPTH)], np.float32)
    hyb = np.ascontiguousarray(inp["hy_bias"].reshape(DEPTH, 4, 128).transpose(0, 2, 1))
    shared = {
        "c_ctx": inp["c_ctx"], "ada_w": inp["ada_w"], "ada_b": inp["ada_b"], "npre": inp["norm_pre"], "npost": inp["norm_post"],
        "w_in": w_in, "wk_perm": wkp, "wq_perm": wqp, "hy_w": inp["hy_short_w"], "hy_b": inp["hy_short_b"],
        "hw1": inp["hy_f_w1"], "hb1": inp["hy_f_b1"], "hw2": inp["hy_f_w2"], "hb2": inp["hy_f_b2"], "hw3": inp["hy_f_w3"],
        "hfreq": inp["hy_f_freq"], "zpos": zpos, "dft_in": dft, "tw_in": tw, "zposc": zposc, "winc": winc, "hyb": hyb, "FU": FU, "FK": FK, "FI": FI,
        "da_lam": np.ascontiguousarray(inp["da_lambda"].reshape(DEPTH, 256)), "lam_init": lam_init, "da_subln": inp["da_subln"],
        "gm_g": inp["gm_ln_g"], "gm_b": inp["gm_ln_b"], "gm_ws": inp["gm_ws"], "gm_bs": inp["gm_bs"], "w_br": inp["w_branch"],
        "w_out": inp["w_out"],
    }
    x = inp["x"]
    in_maps = []
    for core in range(8):
        b, r = divmod(core, 4)
        t0 = r * TOK
        xs = np.zeros((TOK + 2, D), np.float32)
        lo, hi = max(t0 - 1, 0), min(t0 + TOK + 1, SEQ)
        xs[lo - (t0 - 1): hi - (t0 - 1)] = x[b, lo:hi]
        hmask = np.array([[0.0 if t0 == 0 else 1.0], [0.0 if t0 + TOK == SEQ else 1.0]], np.float32)
        hw3q = np.ascontiguousarray(np.concatenate([inp["hy_f_w3"][:, :, 128 * r:128 * (r + 1)],
                                                    inp["hy_f_w3"][:, :, 512 + 128 * r:512 + 128 * (r + 1)]], axis=2))
        m = dict(shared)
        m.update({"xs": xs, "xc0": np.ascontiguousarray(inp["ctx"][b]), "c": inp["c"][b], "hmask": hmask,
                  "cosT": np.ascontiguousarray(cosT[:, t0:t0 + TOK]), "sinT": np.ascontiguousarray(sinT[:, t0:t0 + TOK]),
                  "hw3q": hw3q, "win": np.ascontiguousarray(win[128 * r:128 * (r + 1)])})
        in_maps.append(m)
    return in_maps


def kernel(**inputs):
    inp = {k: np.ascontiguousarray(np.asarray(v), dtype=np.float32) for k, v in inputs.items()}
    nc = get_nc("fused")
    in_maps = fused_inputs(inp)
    res = run_bass_kernel_spmd(nc, in_maps, core_ids=list(range(8))).results
    out = np.stack([np.concatenate([np.asarray(res[4 * b + r]["x_out"]) for r in range(4)], 0) for b in range(2)], 0)
    return out.astype(np.float32)


def f_attn(B, L):
    P = B.P
    hT, gaT = L["hT"], L["gaT"]
    w_in, wq_perm, cosT, sinT, kt_all = L["w_in"], L["wq_perm"], L["cosT"], L["sinT"], L["kt_all"]
    m0 = B.mark()
    lamb, lambk = bcast_row(B, L["da_lam"], 256, "lam")
    lib, libk = bcast_row(B, L["lam_init"], 1, "li")
    lt = B.sb("lamtmp", [128, 140], F32)
    P.op("dve", "tensor_tensor", reads=[lambk], writes=["lamtmp"], out=lt[:, 0:64], in0=lamb[:, 0:64], in1=lamb[:, 64:128], op=ALU.mult)
    P.op("dve", "tensor_tensor", reads=[lambk], writes=["lamtmp"], out=lt[:, 64:128], in0=lamb[:, 128:192], in1=lamb[:, 192:256], op=ALU.mult)
    P.op("dve", "tensor_reduce", reads=["lamtmp"], writes=["lamtmp"], out=lt[:, 128:129], in_=lt[:, 0:64], axis=AX.X, op=ALU.add)
    P.op("dve", "tensor_reduce", reads=["lamtmp"], writes=["lamtmp"], out=lt[:, 129:130], in_=lt[:, 64:128], axis=AX.X, op=ALU.add)
    P.op("act", "activation", reads=["lamtmp"], writes=["lamtmp"], out=lt[:, 130:132], in_=lt[:, 128:130], func=AF.Exp)
    P.op("dve", "tensor_tensor", reads=["lamtmp"], writes=["lamtmp"], out=lt[:, 132:133], in0=lt[:, 131:132], in1=lt[:, 130:131], op=ALU.subtract)
    P.op("dve", "tensor_tensor", reads=["lamtmp", libk], writes=["lamtmp"], out=lt[:, 133:134], in0=lt[:, 132:133], in1=lib[:, 0:1], op=ALU.subtract)
    neglam = lt[:, 133:134]
    P.op("dve", "tensor_scalar", reads=[libk], writes=["lamtmp"], out=lt[:, 134:135], in0=lib[:, 0:1], scalar1=-1.0, scalar2=1.0,
         op0=ALU.mult, op1=ALU.add)
    P.dma("sp", writes=["lamtmp"], out=lt[:, 135:136], in_=L["da_subln"].rearrange("(p o) -> p o", o=1))
    P.op("dve", "tensor_tensor", reads=["lamtmp"], writes=["lamtmp"], out=lt[:, 136:137], in0=lt[:, 135:136], in1=lt[:, 134:135], op=ALU.mult)
    gcol = lt[:, 136:137]
    ones_b = B.sb("ones_b", [128, 128], BF16)
    ones_f = B.sb("ones_f", [128, 128], F32)
    P.op("pool", "memset", writes=["ones_b"], ap=ones_b[:], constant=1.0)
    P.op("pool", "memset", writes=["ones_f"], ap=ones_f[:], constant=1.0)
    QT = B.sb("QT", [128, 4, NTOK3], BF16)
    kcT = B.sb("kcT", [128, 4, CTX], BF16)
    vcx = B.sb("vcx", [128, 2, 4, 128], BF16)
    m1 = B.mark()
    cs = B.sb("cosT", [128, TOK], F32)
    sn = B.sb("sinT", [128, TOK], F32)
    P.dma("sp", writes=["cosT"], out=cs[:], in_=cosT[:, :])
    P.dma("sp", writes=["sinT"], out=sn[:], in_=sinT[:, :])
    wt, wk = wblock(B, w_in, COL_Q, 512, "wq")
    wp, wpk = wblock(B, wq_perm, 0, 512, "wq")
    for h in range(4):
        for j in range(TOK // 512):
            b1 = B.bank()
            proj_fm(B, wt, wk, 128 * h, hT, "hT", 1 + 512 * j, 512, b1)
            b2 = B.bank()
            proj_fm(B, wp, wpk, 128 * h, hT, "hT", 1 + 512 * j, 512, b2)
            t1, t1k = B.ring("rtmp1", [128, 512], F32, 2)
            t2, t2k = B.ring("rtmp2", [128, 512], F32, 2)
            P.op("dve", "tensor_tensor", reads=["ps%d" % b1, "cosT"], writes=[t1k], out=t1[:], in0=B.ps[b1][:, :],
                 in1=cs[:, 512 * j:512 * (j + 1)], op=ALU.mult)
            P.op("dve", "tensor_tensor", reads=["ps%d" % b2, "sinT"], writes=[t2k], out=t2[:], in0=B.ps[b2][:, :],
                 in1=sn[:, 512 * j:512 * (j + 1)], op=ALU.mult)
            P.op("pool", "tensor_tensor", reads=[t1k, t2k], writes=["QT"], out=QT[:, h, 512 * j:512 * (j + 1)], in0=t1[:], in1=t2[:], op=ALU.add)
        b1 = B.bank()
        proj_fm(B, wt, wk, 128 * h, hT, "hT", hcol(TOK), CTX, b1)
        P.op("act", "activation", reads=["ps%d" % b1], writes=["QT"], out=QT[:, h, TOK:NTOK3], in_=B.ps[b1][:, 0:CTX], func=AF.Identity)
    wt, wk = wblock(B, w_in, COL_K, 512, "wq")
    for h in range(4):
        b1 = B.bank()
        proj_fm(B, wt, wk, 128 * h, hT, "hT", hcol(TOK), CTX, b1)
        P.op("act", "activation", reads=["ps%d" % b1], writes=["kcT"], out=kcT[:, h, :], in_=B.ps[b1][:, 0:CTX], func=AF.Identity)
    wt, wk = wblock(B, w_in, COL_V, 512, "wq")
    for i in range(2):
        b1 = B.bank()
        proj_tm(B, wt, wk, 0, 512, hT, "hT", hcol(TOK + 128 * i), 128, b1)
        P.op("act", "activation", reads=["ps%d" % b1], writes=["vcx"], out=vcx[:, i, :, :],
             in_=B.ps[b1][:, :].rearrange("p (h d) -> p h d", d=128), func=AF.Identity)
    B.release(m1)
    kth = B.sb("kth", [128, SEQ], BF16)
    vh = B.sb("vh", [128, SEQ // 128, 128], BF16)
    wga, wgak = wblock(B, w_in, COL_GA, 512, "wga", 1)
    if L.get("after_q") is not None:
        L["after_q"]()
    ACC, RS = 0, 1
    pcnt = 0
    for h in range(4):
        P.dma("sp", reads=L.get("kt_keys", []), writes=["kth"], out=kth[:].rearrange("p (r t) -> p r t", r=4), in_=kt_all[h])
        for k in range(2):
            for r in range(4):
                P.dma("sp", reads=L.get("v_keys", []), writes=["vh"], out=vh[:, 16 * r + 8 * k:16 * r + 8 * k + 8, :],
                      in_=L["v_gk"][k].rearrange("(r t p) c -> r p t c", r=4, p=128)[r][:, :, 128 * h:128 * (h + 1)])
        for (t0, n) in L.get("chunks", CHUNKS):
            latent = t0 < TOK
            keys = ([("l", k) for k in range(SEQ // 128)] if latent else []) + [("c", 0), ("c", 1)]
            om, omk = B.ring("om", [128, 2, 512], F32, 1)
            iters = [(m, ki) for m in range(2) for ki in range(0, len(keys), 2)]
            state = {}

            def emit_S(it):
                nonlocal pcnt
                m, ki = it
                p = 1 + (pcnt % 3)
                pcnt += 1
                vts = []
                for i, (kind, kt) in enumerate(keys[ki:ki + 2]):
                    if kind == "l":
                        lk, lkk = kth[64 * m:64 * (m + 1), 128 * kt:128 * (kt + 1)], "kth"
                        vts.append((vh[:, kt, :], "vh"))
                    else:
                        lk, lkk = kcT[64 * m:64 * (m + 1), h, 128 * kt:128 * (kt + 1)], "kcT"
                        vts.append((vcx[:, kt, h, :], "vcx"))
                    P.op("pe", "matmul", reads=[lkk, "QT"], writes=["ps%d" % (2 * p + i)], out=B.ps[2 * p + i][:, 0:n], lhsT=lk,
                         rhs=QT[64 * m:64 * (m + 1), h, t0:t0 + n], start=True, stop=True)
                state[it] = (p, vts)

            def emit_PV(it):
                m, ki = it
                p, vts = state.pop(it)
                pt, ptk = B.ring("pt", [128, 2, 512], BF16, 3)
                P.op("act", "activation", reads=["ps%d" % (2 * p), "ps%d" % (2 * p + 1)], writes=[ptk], out=pt[:, :, 0:n],
                     in_=B.pp[p][:, :].rearrange("p (b c) -> p b c", b=2)[:, :, 0:n], func=AF.Exp, scale=0.125)
                for i, (vt, vtk) in enumerate(vts):
                    first = (ki == 0 and i == 0)
                    lastk = (ki + i == len(keys) - 1)
                    P.op("pe", "matmul", reads=[ptk, vtk], writes=["ps%d" % ACC], out=B.ps[ACC][:, 0:n], lhsT=vt, rhs=pt[:, i, 0:n],
                         start=first, stop=lastk)
                    P.op("pe", "matmul", reads=[ptk, "ones_b"], writes=["ps%d" % RS], out=B.ps[RS][:, 0:n], lhsT=ones_b[:, :], rhs=pt[:, i, 0:n],
                         start=first, stop=lastk)
                if ki + 2 >= len(keys):
                    rr, rrk = B.ring("rr", [128, 512], F32, 1)
                    P.op("dve", "reciprocal", reads=["ps%d" % RS], writes=[rrk], out=rr[:, 0:n], in_=B.ps[RS][:, 0:n])
                    if m == 1:
                        P.op("dve", "tensor_scalar", reads=[rrk, "lamtmp"], writes=[rrk], out=rr[:, 0:n], in0=rr[:, 0:n], scalar1=neglam, scalar2=None,
                             op0=ALU.mult)
                    P.op("dve", "tensor_tensor", reads=["ps%d" % ACC, rrk], writes=[omk + str(m)], out=om[:, m, 0:n], in0=B.ps[ACC][:, 0:n], in1=rr[:, 0:n],
                         op=ALU.mult)

            emit_S(iters[0])
            for idx, it in enumerate(iters):
                if idx + 1 < len(iters):
                    emit_S(iters[idx + 1])
                emit_PV(it)
            P.op("pool", "tensor_tensor", reads=[omk + "0", omk + "1"], writes=[omk + "0"], out=om[:, 0, 0:n], in0=om[:, 0, 0:n], in1=om[:, 1, 0:n], op=ALU.add)
            sq, sqk = B.ring("asq", [128, 512], F32, 1)
            P.op("act", "activation", reads=[omk + "0"], writes=[sqk], out=sq[:, 0:n], in_=om[:, 0, 0:n], func=AF.Square)
            bq = B.bank()
            while bq in (ACC, RS):
                bq = B.bank()
            P.op("pe", "matmul", reads=[sqk, "ones_f"], writes=["ps%d" % bq], out=B.ps[bq][:, 0:n], lhsT=ones_f[:, :], rhs=sq[:, 0:n], start=True, stop=True)
            P.op("dve", "tensor_scalar", reads=["ps%d" % bq], writes=[sqk], out=sq[:, 0:n], in0=B.ps[bq][:, 0:n], scalar1=1.0 / 128, scalar2=EPS,
                 op0=ALU.mult, op1=ALU.add)
            P.op("act", "activation", reads=[sqk], writes=[sqk], out=sq[:, 0:n], in_=sq[:, 0:n], func=AF.Sqrt)
            P.op("dve", "reciprocal", reads=[sqk], writes=[sqk], out=sq[:, 0:n], in_=sq[:, 0:n])
            P.op("dve", "scalar_tensor_tensor", reads=[omk + "0", "lamtmp", sqk], writes=[sqk], out=sq[:, 0:n], in0=om[:, 0, 0:n], scalar=gcol,
                 in1=sq[:, 0:n], op0=ALU.mult, op1=ALU.mult)
            bg = B.bank()
            while bg in (ACC, RS):
                bg = B.bank()
            proj_fm(B, wga, wgak, 128 * h, hT, "hT", hcol(t0), n, bg)
            sg, sgk = B.ring("sga", [128, 512], F32, 1)
            P.op("act", "activation", reads=["ps%d" % bg], writes=[sgk], out=sg[:, 0:n], in_=B.ps[bg][:, 0:n], func=AF.Silu)
            P.op("dve", "tensor_tensor", reads=[sgk, sqk], writes=["gaT"], out=gaT[:, h, t0:t0 + n], in0=sg[:, 0:n], in1=sq[:, 0:n], op=ALU.mult)
    B.release(m0)
```
